# Optimizing a Trainium2 kernel written in Bass

```python
import jax
import jax.numpy as jnp
from jax import lax
import numpy as np

D_MODEL = 2048
BATCH = 16
SEQ = 256
DEPTH = 4
DEC_BATCH = 8
DEC_SEQ = 1024
PAST_LEN = 512

GRID_W = 64
N_EVEN = (DEPTH + 1) // 2
N_ODD = DEPTH // 2
N_MOD = 9
D_FF = 5632
EPS = 1e-6
Q_BLOCK = 128
ROPE_THETA = 10000.0

LRU_W = D_MODEL // 2
LRU_BLOCKS = 16
LRU_BW = LRU_W // LRU_BLOCKS
CONV_W = 4
LRU_C = 8.0
NA_HEADS = 16
NA_DH = 64
NA_W = NA_HEADS * NA_DH
NA_KR = 8
NA_KC = 16
GQA_HEADS = 16
GQA_KV_HEADS = 4
GQA_DH = 64
GQA_GROUP = GQA_HEADS // GQA_KV_HEADS
GQA_Q_W = GQA_HEADS * GQA_DH
GQA_KV_W = GQA_KV_HEADS * GQA_DH
MLA_HEADS = 8
MLA_Q_RANK = 512
MLA_KV_RANK = 512
MLA_NOPE = 128
MLA_ROPE = 64
MLA_V = 128
MLA_QK = MLA_NOPE + MLA_ROPE

AB_IN = 2 * LRU_W + 3 * NA_W
AB_OUT = LRU_W + NA_W
CD_IN = GQA_Q_W + 2 * GQA_KV_W + MLA_Q_RANK + MLA_KV_RANK + MLA_ROPE
CD_OUT = GQA_Q_W + MLA_HEADS * MLA_V

kernel_name = 'hybrid_diffusion_trunk_ctx_and_denoise_step'


def rmsnorm(x, gain):
    xf = x.astype(jnp.float32)
    y = xf * lax.rsqrt(jnp.mean(xf * xf, axis=-1, keepdims=True) + EPS)
    return (y * gain.astype(jnp.float32)).astype(x.dtype)


def adaln(x, gain, shift, scale):
    return rmsnorm(x, gain) * (1 + scale) + shift


def modulation(cond, w_mod, b_mod):
    m = jax.nn.silu(cond) @ w_mod + b_mod
    m = m.reshape(cond.shape[0], 1, N_MOD, D_MODEL)
    return tuple(m[:, :, j] for j in range(N_MOD))


def swiglu(h, w_in, w_out):
    g, u = jnp.split(h @ w_in, 2, axis=-1)
    return (jax.nn.silu(g) * u) @ w_out


def axial_rope(x):
    seq_len, dim = x.shape[1], x.shape[-1]
    half = dim // 2
    nf = half // 2
    t = jnp.arange(seq_len)
    inv_freq = 1.0 / (ROPE_THETA ** (jnp.arange(nf, dtype=jnp.float32) / nf))

    def rotate(xs, pos):
        ang = pos.astype(jnp.float32)[:, None] * inv_freq[None, :]
        cos = jnp.cos(ang)[None, :, None, :]
        sin = jnp.sin(ang)[None, :, None, :]
        xf = xs.astype(jnp.float32)
        x1, x2 = xf[..., :nf], xf[..., nf:]
        return jnp.concatenate([x1 * cos - x2 * sin, x2 * cos + x1 * sin], axis=-1)

    out = jnp.concatenate([rotate(x[..., :half], t // GRID_W), rotate(x[..., half:], t % GRID_W)], axis=-1)
    return out.astype(x.dtype)


def block_attention(q, k, v, scale):
    b, lq, hk, g, dq = q.shape
    nb = lq // Q_BLOCK
    qb = q.reshape(b, nb, Q_BLOCK, hk, g, dq).transpose(1, 0, 2, 3, 4, 5)

    def one_block(qblk):
        s = jnp.einsum('bqhgd,bkhd->bhgqk', qblk, k).astype(jnp.float32) * scale
        p = jax.nn.softmax(s, axis=-1).astype(v.dtype)
        return jnp.einsum('bhgqk,bkhd->bqhgd', p, v)

    o = lax.map(one_block, qb)
    return o.transpose(1, 0, 2, 3, 4, 5).reshape(b, lq, hk * g * v.shape[-1])


def centred_dwconv(x, w, b):
    seq_len = x.shape[1]
    left = CONV_W // 2
    xp = jnp.pad(x, ((0, 0), (left, CONV_W - 1 - left), (0, 0)))
    y = b
    for j in range(CONV_W):
        y = y + xp[:, j:j + seq_len] * w[j]
    return y


def rglru(xc, h0, wa, ba, wx, bx, lam, reverse):
    b, l, _ = xc.shape
    xb = xc.reshape(b, l, LRU_BLOCKS, LRU_BW)
    r = jax.nn.sigmoid(jnp.einsum('blni,nij->blnj', xb, wa).reshape(b, l, LRU_W) + ba)
    gi = jax.nn.sigmoid(jnp.einsum('blni,nij->blnj', xb, wx).reshape(b, l, LRU_W) + bx)
    log_a = -LRU_C * r.astype(jnp.float32) * jax.nn.softplus(-lam.astype(jnp.float32))
    a = jnp.exp(log_a)
    u = jnp.sqrt(-jnp.expm1(2.0 * log_a)) * (gi * xc).astype(jnp.float32)

    def combine(left, right):
        return (left[0] * right[0], right[0] * left[1] + right[1])

    a_cum, u_cum = lax.associative_scan(combine, (a, u), axis=1, reverse=reverse)
    h = a_cum * h0.astype(jnp.float32)[:, None, :] + u_cum
    last = h[:, 0] if reverse else h[:, -1]
    return h.astype(xc.dtype), last.astype(xc.dtype)


def lru_mixer(xa, ga, h0f, h0b, lp):
    xc = centred_dwconv(xa, lp['conv_w'], lp['conv_b'])
    hf, lf = rglru(xc, h0f, lp['wa'][0], lp['ba'][0], lp['wx'][0], lp['bx'][0], lp['lam'][0], False)
    hb, lb = rglru(xc, h0b, lp['wa'][1], lp['ba'][1], lp['wx'][1], lp['bx'][1], lp['lam'][1], True)
    return (hf + hb) * jax.nn.gelu(ga), lf, lb


def neighbourhood_attention(q, k, v, k_ctx, v_ctx, bias_tab):
    b, l, h, dh = q.shape
    rows_n = l // GRID_W
    kr = min(NA_KR, rows_n)
    scale = dh ** -0.5
    rows = jnp.arange(rows_n)
    r_start = jnp.clip(rows - kr // 2, 0, rows_n - kr)
    row_idx = r_start[:, None] + jnp.arange(kr)[None, :]
    qg = q.reshape(b, rows_n, GRID_W, h, dh)
    kg = k.reshape(b, rows_n, GRID_W, h, dh)[:, row_idx]
    vg = v.reshape(b, rows_n, GRID_W, h, dh)[:, row_idx]
    cols = jnp.arange(GRID_W)
    c_start = jnp.clip(cols - NA_KC // 2, 0, GRID_W - NA_KC)
    col_ok = (cols[None, :] >= c_start[:, None]) & (cols[None, :] < c_start[:, None] + NA_KC)
    rel_r = row_idx - rows[:, None] + (NA_KR - 1)
    rel_c = jnp.clip(cols[None, :] - cols[:, None] + (NA_KC - 1), 0, 2 * NA_KC - 2)
    bias = bias_tab[:, rel_r[:, None, :, None], rel_c[None, :, None, :]].astype(jnp.float32)
    s_loc = jnp.einsum('brqhd,brikhd->bhrqik', qg, kg).astype(jnp.float32) * scale + bias[None]
    s_loc = jnp.where(col_ok[None, None, None, :, None, :], s_loc, -jnp.inf)
    s_loc = s_loc.reshape(b, h, rows_n, GRID_W, kr * GRID_W)
    s_ctx = jnp.einsum('brqhd,bchd->bhrqc', qg, k_ctx).astype(jnp.float32) * scale
    p = jax.nn.softmax(jnp.concatenate([s_loc, s_ctx], axis=-1), axis=-1).astype(v.dtype)
    p_loc = p[..., :kr * GRID_W].reshape(b, h, rows_n, GRID_W, kr, GRID_W)
    p_ctx = p[..., kr * GRID_W:]
    o = jnp.einsum('bhrqik,brikhd->brqhd', p_loc, vg) + jnp.einsum('bhrqc,bchd->brqhd', p_ctx, v_ctx)
    return o.reshape(b, l, h * dh)


def ab_split(h, w_in):
    b, l, _ = h.shape
    xa, ga, q, k, v = jnp.split(h @ w_in, [LRU_W, 2 * LRU_W, 2 * LRU_W + NA_W, 2 * LRU_W + 2 * NA_W], axis=-1)
    return (xa, ga, q.reshape(b, l, NA_HEADS, NA_DH), k.reshape(b, l, NA_HEADS, NA_DH),
            v.reshape(b, l, NA_HEADS, NA_DH))


def ab_context(h, lp):
    xa, ga, q, k, v = ab_split(h, lp['w_in'])
    zeros = jnp.zeros((h.shape[0], LRU_W), h.dtype)
    ya, sf, sb = lru_mixer(xa, ga, zeros, zeros, lp)
    yb = block_attention(q[:, :, :, None], k, v, NA_DH ** -0.5)
    y = jnp.concatenate([ya, yb], axis=-1) @ lp['w_out']
    return y, (sf, sb, k, v)


def ab_latent(h, lp, sf, sb, k_ctx, v_ctx):
    xa, ga, q, k, v = ab_split(h, lp['w_in'])
    ya, _, _ = lru_mixer(xa, ga, sf, sb, lp)
    yb = neighbourhood_attention(q, k, v, k_ctx, v_ctx, lp['na_bias'])
    return jnp.concatenate([ya, yb], axis=-1) @ lp['w_out'], None


def cd_split(h, lp):
    b, l, _ = h.shape
    idx = [GQA_Q_W, GQA_Q_W + GQA_KV_W, GQA_Q_W + 2 * GQA_KV_W, GQA_Q_W + 2 * GQA_KV_W + MLA_Q_RANK,
           GQA_Q_W + 2 * GQA_KV_W + MLA_Q_RANK + MLA_KV_RANK]
    qc, kc, vc, qa, ckv, kr = jnp.split(h @ lp['w_in'], idx, axis=-1)
    qc = rmsnorm(qc.reshape(b, l, GQA_HEADS, GQA_DH), lp['q_gain'])
    kc = rmsnorm(kc.reshape(b, l, GQA_KV_HEADS, GQA_DH), lp['k_gain'])
    vc = vc.reshape(b, l, GQA_KV_HEADS, GQA_DH)
    qd = (rmsnorm(qa, lp['mla_q_gain']) @ lp['w_uq']).reshape(b, l, MLA_HEADS, MLA_QK)
    ckv = rmsnorm(ckv, lp['mla_kv_gain'])
    return qc, kc, vc, qd, ckv, kr


def mla_kv(ckv, kr, lp):
    b, l, _ = ckv.shape
    k_nope = (ckv @ lp['w_uk']).reshape(b, l, MLA_HEADS, MLA_NOPE)
    v = (ckv @ lp['w_uv']).reshape(b, l, MLA_HEADS, MLA_V)
    k = jnp.concatenate([k_nope, jnp.broadcast_to(kr[:, :, None, :], (b, l, MLA_HEADS, MLA_ROPE))], axis=-1)
    return k, v


def cd_context(h, lp):
    qc, kc, vc, qd, ckv, kr = cd_split(h, lp)
    b, l = h.shape[0], h.shape[1]
    yc = block_attention(qc.reshape(b, l, GQA_KV_HEADS, GQA_GROUP, GQA_DH), kc, vc, GQA_DH ** -0.5)
    kd, vd = mla_kv(ckv, kr, lp)
    yd = block_attention(qd[:, :, :, None], kd, vd, MLA_QK ** -0.5)
    y = jnp.concatenate([yc, yd], axis=-1) @ lp['w_out']
    return y, (kc, vc, ckv, kr)


def cd_latent(h, lp, kc_ctx, vc_ctx, ckv_ctx, kr_ctx):
    qc, kc, vc, qd, ckv, kr = cd_split(h, lp)
    b, l = h.shape[0], h.shape[1]
    qc = axial_rope(qc)
    kc = axial_rope(kc)
    qd = jnp.concatenate([qd[..., :MLA_NOPE], axial_rope(qd[..., MLA_NOPE:])], axis=-1)
    kr = axial_rope(kr[:, :, None, :])[:, :, 0]
    k_all = jnp.concatenate([kc, kc_ctx], axis=1)
    v_all = jnp.concatenate([vc, vc_ctx], axis=1)
    yc = block_attention(qc.reshape(b, l, GQA_KV_HEADS, GQA_GROUP, GQA_DH), k_all, v_all, GQA_DH ** -0.5)
    kd, vd = mla_kv(jnp.concatenate([ckv, ckv_ctx], axis=1), jnp.concatenate([kr, kr_ctx], axis=1), lp)
    yd = block_attention(qd[:, :, :, None], kd, vd, MLA_QK ** -0.5)
    return jnp.concatenate([yc, yd], axis=-1) @ lp['w_out'], None


def trunk_layer(x, mods, gains, ffn_in, ffn_out, mixer):
    sh1, sc1, g1, sh2, sc2, g2, sh3, sc3, g3 = mods
    x = x + 0.5 * g1 * swiglu(adaln(x, gains[0], sh1, sc1), ffn_in[0], ffn_out[0])
    y, ctx_tensors = mixer(adaln(x, gains[1], sh2, sc2))
    x = x + g2 * y
    x = x + 0.5 * g3 * swiglu(adaln(x, gains[2], sh3, sc3), ffn_in[1], ffn_out[1])
    return x, ctx_tensors


def setup_inputs(seed: int = 0) -> dict:
    key = jax.random.key(seed)
    keys = jax.random.split(key, 64)
    counter = [0]

    def nxt():
        k = keys[counter[0]]
        counter[0] += 1
        return k

    def nrm(shape, std):
        return jax.random.normal(nxt(), shape, jnp.float32) * std

    def gain(shape):
        return 1.0 + nrm(shape, 0.05)

    u = jax.random.uniform(nxt(), (N_EVEN, 2, LRU_W), jnp.float32, 0.9, 0.999)
    base = u ** (1.0 / LRU_C)
    lru_lambda = jnp.log(base) - jnp.log1p(-base)
    return {
        'x_prompt': nrm((BATCH, SEQ, D_MODEL), 1.0),
        'x_sample': nrm((DEC_BATCH, DEC_SEQ, D_MODEL), 1.0),
        'state_lru_fwd': nrm((DEC_BATCH, N_EVEN, LRU_W), 0.5),
        'state_lru_bwd': nrm((DEC_BATCH, N_EVEN, LRU_W), 0.5),
        'cache_na_k': nrm((DEC_BATCH, N_EVEN, PAST_LEN, NA_HEADS, NA_DH), 1.0),
        'cache_na_v': nrm((DEC_BATCH, N_EVEN, PAST_LEN, NA_HEADS, NA_DH), 1.0),
        'cache_gqa_k': nrm((DEC_BATCH, N_ODD, PAST_LEN, GQA_KV_HEADS, GQA_DH), 1.0),
        'cache_gqa_v': nrm((DEC_BATCH, N_ODD, PAST_LEN, GQA_KV_HEADS, GQA_DH), 1.0),
        'cache_mla_ckv': nrm((DEC_BATCH, N_ODD, PAST_LEN, MLA_KV_RANK), 1.0),
        'cache_mla_krope': nrm((DEC_BATCH, N_ODD, PAST_LEN, MLA_ROPE), 1.0),
        'c': nrm((DEC_BATCH, D_MODEL), 1.0),
        'c_ctx': nrm((D_MODEL,), 1.0),
        'w_mod': nrm((DEPTH, D_MODEL, N_MOD * D_MODEL), 0.5 * D_MODEL ** -0.5),
        'b_mod': nrm((DEPTH, N_MOD * D_MODEL), 0.02),
        'norm_gain': gain((DEPTH, 3, D_MODEL)),
        'w_ffn_in': nrm((DEPTH, 2, D_MODEL, 2 * D_FF), D_MODEL ** -0.5),
        'w_ffn_out': nrm((DEPTH, 2, D_FF, D_MODEL), D_FF ** -0.5),
        'w_in_ab': nrm((N_EVEN, D_MODEL, AB_IN), D_MODEL ** -0.5),
        'conv_w': nrm((N_EVEN, CONV_W, LRU_W), CONV_W ** -0.5),
        'conv_b': nrm((N_EVEN, LRU_W), 0.02),
        'lru_wa': nrm((N_EVEN, 2, LRU_BLOCKS, LRU_BW, LRU_BW), LRU_BW ** -0.5),
        'lru_ba': nrm((N_EVEN, 2, LRU_W), 0.1),
        'lru_wx': nrm((N_EVEN, 2, LRU_BLOCKS, LRU_BW, LRU_BW), LRU_BW ** -0.5),
        'lru_bx': nrm((N_EVEN, 2, LRU_W), 0.1),
        'lru_lambda': lru_lambda,
        'na_bias': nrm((N_EVEN, NA_HEADS, 2 * NA_KR - 1, 2 * NA_KC - 1), 0.5),
        'w_out_ab': nrm((N_EVEN, AB_OUT, D_MODEL), AB_OUT ** -0.5),
        'w_in_cd': nrm((N_ODD, D_MODEL, CD_IN), D_MODEL ** -0.5),
        'gqa_q_gain': gain((N_ODD, GQA_DH)),
        'gqa_k_gain': gain((N_ODD, GQA_DH)),
        'mla_q_gain': gain((N_ODD, MLA_Q_RANK)),
        'mla_kv_gain': gain((N_ODD, MLA_KV_RANK)),
        'mla_w_uq': nrm((N_ODD, MLA_Q_RANK, MLA_HEADS * MLA_QK), MLA_Q_RANK ** -0.5),
        'mla_w_uk': nrm((N_ODD, MLA_KV_RANK, MLA_HEADS * MLA_NOPE), MLA_KV_RANK ** -0.5),
        'mla_w_uv': nrm((N_ODD, MLA_KV_RANK, MLA_HEADS * MLA_V), MLA_KV_RANK ** -0.5),
        'w_out_cd': nrm((N_ODD, CD_OUT, D_MODEL), CD_OUT ** -0.5),
        'final_gain': gain((D_MODEL,)),
    }


def reference(x_prompt, x_sample, state_lru_fwd, state_lru_bwd, cache_na_k, cache_na_v, cache_gqa_k,
              cache_gqa_v, cache_mla_ckv, cache_mla_krope, c, c_ctx, w_mod, b_mod, norm_gain, w_ffn_in,
              w_ffn_out, w_in_ab, conv_w, conv_b, lru_wa, lru_ba, lru_wx, lru_bx, lru_lambda, na_bias,
              w_out_ab, w_in_cd, gqa_q_gain, gqa_k_gain, mla_q_gain, mla_kv_gain, mla_w_uq, mla_w_uk,
              mla_w_uv, w_out_cd, final_gain):
    xp, xs = x_prompt, x_sample
    st_f, st_b, na_k, na_v, gq_k, gq_v, ml_c, ml_r = [], [], [], [], [], [], [], []
    for layer in range(DEPTH):
        mod_p = modulation(c_ctx[None, :], w_mod[layer], b_mod[layer])
        mod_s = modulation(c, w_mod[layer], b_mod[layer])
        gains, f_in, f_out = norm_gain[layer], w_ffn_in[layer], w_ffn_out[layer]
        j = layer // 2
        if layer % 2 == 0:
            lp = {'w_in': w_in_ab[j], 'conv_w': conv_w[j], 'conv_b': conv_b[j], 'wa': lru_wa[j],
                  'ba': lru_ba[j], 'wx': lru_wx[j], 'bx': lru_bx[j], 'lam': lru_lambda[j],
                  'na_bias': na_bias[j], 'w_out': w_out_ab[j]}
            xp, ctx = trunk_layer(xp, mod_p, gains, f_in, f_out, lambda h: ab_context(h, lp))
            xs, _ = trunk_layer(xs, mod_s, gains, f_in, f_out,
                                lambda h: ab_latent(h, lp, state_lru_fwd[:, j], state_lru_bwd[:, j],
                                                    cache_na_k[:, j], cache_na_v[:, j]))
            st_f.append(ctx[0])
            st_b.append(ctx[1])
            na_k.append(ctx[2])
            na_v.append(ctx[3])
        else:
            lp = {'w_in': w_in_cd[j], 'q_gain': gqa_q_gain[j], 'k_gain': gqa_k_gain[j],
                  'mla_q_gain': mla_q_gain[j], 'mla_kv_gain': mla_kv_gain[j], 'w_uq': mla_w_uq[j],
                  'w_uk': mla_w_uk[j], 'w_uv': mla_w_uv[j], 'w_out': w_out_cd[j]}
            xp, ctx = trunk_layer(xp, mod_p, gains, f_in, f_out, lambda h: cd_context(h, lp))
            xs, _ = trunk_layer(xs, mod_s, gains, f_in, f_out,
                                lambda h: cd_latent(h, lp, cache_gqa_k[:, j], cache_gqa_v[:, j],
                                                    cache_mla_ckv[:, j], cache_mla_krope[:, j]))
            gq_k.append(ctx[0])
            gq_v.append(ctx[1])
            ml_c.append(ctx[2])
            ml_r.append(ctx[3])
    y_prompt = rmsnorm(xp, final_gain)
    y_sample = rmsnorm(xs, final_gain)
    new_state_lru_fwd = jnp.stack(st_f, axis=1)
    new_state_lru_bwd = jnp.stack(st_b, axis=1)
    new_cache_na_k = jnp.stack(na_k, axis=1)
    new_cache_na_v = jnp.stack(na_v, axis=1)
    new_cache_gqa_k = jnp.stack(gq_k, axis=1)
    new_cache_gqa_v = jnp.stack(gq_v, axis=1)
    new_cache_mla_ckv = jnp.stack(ml_c, axis=1)
    new_cache_mla_krope = jnp.stack(ml_r, axis=1)
    return (y_prompt, y_sample, new_state_lru_fwd, new_state_lru_bwd, new_cache_na_k, new_cache_na_v,
            new_cache_gqa_k, new_cache_gqa_v, new_cache_mla_ckv, new_cache_mla_krope)
```

```python
import numpy as np
from contextlib import ExitStack
import concourse.bass as bass
import concourse.mybir as mybir
from concourse.bass_utils import run_bass_kernel_spmd

F32 = mybir.dt.float32
BF16 = mybir.dt.bfloat16
AF = mybir.ActivationFunctionType
ALU = mybir.AluOpType

D = 2048
NCH = 16
TOK = 1536
NT = 3
DFF = 5632
NJ = 44
EPS = 1e-6
SEMCH = 30000


class T:
    __slots__ = ("w", "r", "x")

    def __init__(self, x=False):
        self.w = None
        self.r = []
        self.x = x


class Slot:
    def __init__(self, sem):
        self.sem = sem
        self.count = 0


class Builder:
    def __init__(self):
        self.nc = bass.Bass("TRN2", target_bir_lowering=False)
        self.es = ExitStack()
        nc = self.nc
        self.eng = {"pe": nc.tensor, "act": nc.scalar, "dve": nc.vector, "pool": nc.gpsimd, "sp": nc.sync}
        self.cnt = {e: 0 for e in self.eng}
        self.sems = {e: [] for e in self.eng}
        self.seen = {e: {} for e in self.eng}
        self.nsem = 0
        self.slots = []

    def new_sem(self, name):
        self.nsem += 1
        return self.es.enter_context(self.nc.semaphore(f"{name}_{self.nsem}"))

    def slot(self, name="s"):
        sl = Slot(self.new_sem(name))
        self.slots.append(sl)
        return sl

    def barrier(self):
        for e in self.eng:
            for e2 in self.eng:
                if e2 != e and self.cnt[e2] > 0:
                    self._wait(e, ("e", e2, self.cnt[e2]))
            for sl in self.slots:
                if sl.count:
                    self._wait(e, ("d", sl, sl.count))

    def sbuf(self, name, shape, dt):
        return self.es.enter_context(self.nc.sbuf_tensor(name, shape, dt))

    def psum(self, name, shape, dt):
        return self.es.enter_context(self.nc.psum_tensor(name, shape, dt))

    def _engsem(self, e, idx):
        ch = (idx - 1) // SEMCH
        while len(self.sems[e]) <= ch:
            self.sems[e].append(self.new_sem(f"e_{e}"))
        return self.sems[e][ch], idx - ch * SEMCH

    def _wait(self, e, ev):
        if ev[0] == "e":
            _, e2, idx = ev
            if e2 == e and e == "pe":
                return
            key = ("e", e2)
            if self.seen[e].get(key, 0) >= idx:
                return
            self.seen[e][key] = idx
            sem, val = self._engsem(e2, idx)
        else:
            _, sl, val = ev
            key = ("d", id(sl))
            if self.seen[e].get(key, 0) >= val:
                return
            self.seen[e][key] = val
            sem = sl.sem
        self.eng[e].wait_ge(sem, val)

    def _deps(self, e, reads, writes):
        for t in reads:
            if t.w is not None:
                self._wait(e, t.w)
            if t.x:
                for ev in t.r:
                    if not (ev[0] == "e" and ev[1] == e):
                        self._wait(e, ev)
        for t in writes:
            if t.w is not None:
                self._wait(e, t.w)
            for ev in t.r:
                self._wait(e, ev)

    def _commit(self, ev, reads, writes):
        for t in reads:
            t.r.append(ev)
            if len(t.r) > 24:
                t.r = t.r[-24:] if False else t.r
        for t in writes:
            t.w = ev
            t.r = []

    def op(self, e, fn, reads=(), writes=()):
        self._deps(e, reads, writes)
        ins = fn(self.eng[e])
        self.cnt[e] += 1
        idx = self.cnt[e]
        sem, _ = self._engsem(e, idx)
        ins.then_inc(sem, 1)
        self._commit(("e", e, idx), reads, writes)

    def group(self, e, fns, reads=(), writes=()):
        self._deps(e, reads, writes)
        ins = None
        for fn in fns:
            ins = fn(self.eng[e])
        self.cnt[e] += 1
        idx = self.cnt[e]
        sem, _ = self._engsem(e, idx)
        ins.then_inc(sem, 1)
        self._commit(("e", e, idx), reads, writes)

    def dma(self, e, pairs, reads, writes, slot, **kw):
        self._deps(e, reads, writes)
        for (o, i) in pairs:
            self.eng[e].dma_start(out=o, in_=i, **kw).then_inc(slot.sem, 16)
            slot.count += 16
        self._commit(("d", slot, slot.count), reads, writes)

    def finish(self, slots):
        for sl in slots:
            if sl.count:
                self.eng["sp"].wait_ge(sl.sem, sl.count)


def rev(ap2d):
    (ps, pn), (fs, fn) = ap2d.ap
    return bass.AP(ap2d.tensor, ap2d.offset + (fn - 1) * fs, [[ps, pn], [-fs, fn]])


NA_SCALE = 0.125
MLA_SCALE = 192 ** -0.5
NEG = -30000.0
DBG = {}


def build(nu, layers, mixers=True):
    b = Builder()
    nc = b.nc
    ncols = 1 + nu
    L = len(layers)
    TM = 1024

    def din(name, shape, dt=F32):
        return nc.dram_tensor(name, list(shape), dt, kind="ExternalInput").ap()

    def dout(name, shape, dt=F32):
        return nc.dram_tensor(name, list(shape), dt, kind="ExternalOutput").ap()

    xc = din("xc", [nu, NCH, 128, 512])
    xl = din("xl", [nu, NCH, 128, 1024])
    yc = dout("yc", [nu, NCH, 128, 512])
    yl = dout("yl", [nu, NCH, 128, 1024])
    condT = din("condT", [128, NCH * ncols])
    cst = din("cst", [128, 18])
    cosd = din("cosd", [128, 1024])
    sind = din("sind", [128, 1024])
    matd = din("matd", [128, 3 * 128])
    W = {}
    O = {}
    for l in layers:
        W[f"wmod{l}"] = din(f"wmod{l}", [36, 128, 16 * 512])
        W[f"bmod{l}"] = din(f"bmod{l}", [128, 144])
        W[f"gain{l}"] = din(f"gain{l}", [128, 48])
        for f in range(2):
            W[f"win{l}_{f}"] = din(f"win{l}_{f}", [NJ, 128, 16 * 256])
            W[f"wout{l}_{f}"] = din(f"wout{l}_{f}", [11, 128, 4 * 2048])
        if not mixers:
            continue
        if l % 2 == 0:
            W[f"wab{l}"] = din(f"wab{l}", [40, 128, 2048])
            W[f"wabo{l}"] = din(f"wabo{l}", [16, 128, 2048])
            W[f"bd{l}"] = din(f"bd{l}", [8, 128, 512])
            W[f"lc{l}"] = din(f"lc{l}", [128, 8 * 11])
            W[f"bz{l}"] = din(f"bz{l}", [16, 15, 64, 64])
            W[f"st{l}"] = din(f"st{l}", [nu, 128, 16])
            W[f"nakc{l}"] = din(f"nakc{l}", [nu, 8, 128, 512])
            W[f"navc{l}"] = din(f"navc{l}", [nu, 4, 128, 1024])
            O[f"ost{l}"] = dout(f"ost{l}", [nu, 128, 32])
            O[f"onak{l}"] = dout(f"onak{l}", [nu, 8, 128, 512])
            O[f"onav{l}"] = dout(f"onav{l}", [nu, 8, 128, 512])
        else:
            W[f"wcd{l}"] = din(f"wcd{l}", [23, 128, 2048])
            W[f"wvd{l}"] = din(f"wvd{l}", [128, 16 * 512])
            W[f"wcdo{l}"] = din(f"wcdo{l}", [16, 128, 2048])
            W[f"cdc{l}"] = din(f"cdc{l}", [128, 10])
            W[f"wuqn{l}"] = din(f"wuqn{l}", [128, 4 * 1024])
            W[f"wuqr{l}"] = din(f"wuqr{l}", [128, 4 * 512])
            W[f"wuk{l}"] = din(f"wuk{l}", [128, 4 * 1024])
            W[f"wuv{l}"] = din(f"wuv{l}", [128, 4 * 1024])
            W[f"gqk{l}"] = din(f"gqk{l}", [nu, 4, 128, 512])
            W[f"gqv{l}"] = din(f"gqv{l}", [nu, 4, 128, 512])
            W[f"mlc{l}"] = din(f"mlc{l}", [nu, 4, 128, 512])
            W[f"mlr{l}"] = din(f"mlr{l}", [nu, 128, 512])
            O[f"ogk{l}"] = dout(f"ogk{l}", [nu, 4, 64, 512])
            O[f"ogv{l}"] = dout(f"ogv{l}", [nu, 2, 128, 512])
            O[f"omc{l}"] = dout(f"omc{l}", [nu, 4, 128, 512])
            O[f"omr{l}"] = dout(f"omr{l}", [nu, 64, 512])

    X = b.sbuf("X", [128, NCH, TM], F32)
    H = b.sbuf("H", [128, NCH, TM], BF16)
    RS = b.sbuf("RS", [128, TM], F32)
    ones = b.sbuf("ones", [128, 128], BF16)
    MATS = b.sbuf("MATS", [128, 384], BF16)
    perm = MATS[:, 0:128]
    bones = MATS[:, 128:256]
    HM = b.sbuf("HM", [128, 2], F32)
    COS = b.sbuf("COS", [128, 1024], F32)
    SIN = b.sbuf("SIN", [128, 1024], F32)
    DER = b.sbuf("DER", [128, L * 3 * ncols * 3 * NCH], F32)
    GAIN = b.sbuf("GAIN", [128, L, 48], F32)
    FG = b.sbuf("FG", [128, NCH], F32)
    CT = b.sbuf("CT", [128, NCH * ncols], F32)
    CB = b.sbuf("CB", [128, NCH * ncols], BF16)
    ABYTES = 92160
    AR = b.sbuf("ARENA", [128, ABYTES // 2], BF16)
    PS = [b.psum(f"ps{i}", [128, 512], F32) for i in range(8)]

    class Carver:
        def __init__(self, base=0):
            self.off = base

        def take(self, nbytes, dt=BF16, shape=None):
            assert self.off % 4 == 0
            a = AR[:, self.off // 2:(self.off + nbytes) // 2]
            self.off += nbytes
            assert self.off <= ABYTES, self.off
            if dt == F32:
                a = a.bitcast(F32)
            if shape is not None:
                names = " ".join(f"d{i}" for i in range(len(shape)))
                a = a.rearrange(f"p ({names}) -> p {names}", **{f"d{i}": s for i, s in enumerate(shape)})
            return a

    tX = [[T() for _ in range(2)] for _ in range(NCH)]
    tH = [[T() for _ in range(2)] for _ in range(NCH)]
    tRS = [T() for _ in range(2)]
    tPS = [T(True) for _ in range(8)]
    tC = T()
    tDER = T()
    allX = [tX[m][t] for m in range(NCH) for t in range(2)]
    allH = [tH[m][t] for m in range(NCH) for t in range(2)]

    def der(li, n, col, k, m):
        off = ((((li * 3 + n) * ncols + col) * 3 + k) * NCH) + m
        return DER[:, off:off + 1]

    cv0 = Carver()
    SG = [cv0.take(1024) for _ in range(2)]
    TMP = [cv0.take(2048, F32) for _ in range(2)]
    tSG = [T(), T()]
    tTMP = [T(), T()]
    BASE = cv0.off
    sMisc = b.slot("misc")
    sX = b.slot("x")
    sY = b.slot("y")
    outslots = [sY]

    b.op("dve", lambda e: e.memset(ones[:], 1.0), writes=[tC])
    pairs = [(CT[:], condT[:, :]), (FG[:], cst[:, 0:16]), (HM[:], cst[:, 16:18]), (COS[:], cosd[:, :]), (SIN[:], sind[:, :])]
    for li, l in enumerate(layers):
        pairs.append((GAIN[:, li, :], W[f"gain{l}"][:, :]))
    b.dma("sp", pairs, [], [tC], sMisc)
    sMat = b.slot("mat")
    tMat = T()
    b.dma("pool", [(MATS[:], matd[:, :])], [], [tMat], sMat)
    b.op("act", lambda e: e.activation(out=CB[:], in_=CT[:], func=AF.Silu), reads=[tC], writes=[tC])

    cv = Carver(BASE)
    WM = [cv.take(16384) for _ in range(2)]
    MODT = cv.take(144 * ncols * 4, F32)
    BM = cv.take(144 * 4, F32)
    tWM = [T(), T()]
    tMODT = T()
    tBM = T()
    sWM = [b.slot("wm"), b.slot("wm")]
    sBM = b.slot("bm")
    k = 0
    for li, l in enumerate(layers):
        pm = PS[7]
        b.dma("sp", [(BM, W[f"bmod{l}"][:, :])], [], [tBM], sBM)
        for s in range(36):
            sl = k % 2
            k += 1
            b.dma("pool", [(WM[sl], W[f"wmod{l}"][s])], [], [tWM[sl]], sWM[sl], max_dma_last_dim=8192)
            fns = []
            for q in range(4):
                oc = 4 * s + q
                for kc in range(16):
                    fns.append(lambda e, q=q, kc=kc, oc=oc, sl=sl: e.matmul(
                        pm[:, oc * ncols:(oc + 1) * ncols],
                        WM[sl][:, kc * 512 + q * 128: kc * 512 + (q + 1) * 128],
                        CB[:, kc * ncols:(kc + 1) * ncols], start=(kc == 0), stop=(kc == 15)))
            b.group("pe", fns, reads=[tWM[sl], tC], writes=[tPS[7]])
        for col in range(ncols):
            b.op("dve", lambda e, col=col: e.tensor_tensor(
                out=MODT[:, col:144 * ncols:ncols], in0=pm[:, col:144 * ncols:ncols], in1=BM, op=ALU.add),
                reads=[tPS[7], tBM], writes=[tMODT])
        for n in range(3):
            for col in range(ncols):
                sh0 = ((3 * n + 0) * 16) * ncols + col
                sc0 = ((3 * n + 1) * 16) * ncols + col
                g0 = ((3 * n + 2) * 16) * ncols + col
                o = ((li * 3 + n) * ncols + col) * 3 * NCH
                b.op("dve", lambda e, sc0=sc0, o=o, n=n, li=li: e.scalar_tensor_tensor(
                    out=DER[:, o:o + NCH], in0=MODT[:, sc0:sc0 + 15 * ncols + 1:ncols], scalar=1.0,
                    in1=GAIN[:, li, n * 16:(n + 1) * 16], op0=ALU.add, op1=ALU.mult),
                    reads=[tMODT, tC], writes=[tDER])
                b.op("dve", lambda e, g0=g0, o=o, n=n: e.tensor_scalar(
                    out=DER[:, o + NCH:o + 2 * NCH], in0=MODT[:, g0:g0 + 15 * ncols + 1:ncols],
                    scalar1=(1.0 if n == 1 else 0.5), scalar2=None, op0=ALU.mult),
                    reads=[tMODT], writes=[tDER])
                b.op("dve", lambda e, sh0=sh0, o=o: e.tensor_copy(
                    out=DER[:, o + 2 * NCH:o + 3 * NCH], in_=MODT[:, sh0:sh0 + 15 * ncols + 1:ncols]),
                    reads=[tMODT], writes=[tDER])

    def TS(t):
        return slice(t * 512, (t + 1) * 512)

    def norm_rstd(nt):
        for t in range(nt):
            pb = 6
            for m in range(NCH):
                b.op("act", lambda e, m=m: e.activation(out=SG[m % 2], in_=X[:, m, TS(t)], func=AF.Square),
                     reads=[tX[m][t]], writes=[tSG[m % 2]])
                b.group("pe", [lambda e, m=m: e.matmul(PS[pb][:], ones[:], SG[m % 2], start=(m == 0), stop=(m == 15))],
                        reads=[tSG[m % 2], tC], writes=[tPS[pb]])
            b.op("act", lambda e: e.activation(out=RS[:, TS(t)], in_=PS[pb][:], func=AF.Sqrt, bias=EPS, scale=1.0 / D),
                 reads=[tPS[pb]], writes=[tRS[t]])
            b.op("dve", lambda e: e.reciprocal(out=RS[:, TS(t)], in_=RS[:, TS(t)]), reads=[tRS[t]], writes=[tRS[t]])

    def adaln(li, n, col, nt):
        norm_rstd(nt)
        for t in range(nt):
            for m in range(NCH):
                b.op("dve", lambda e, m=m: e.tensor_tensor(out=TMP[m % 2], in0=X[:, m, TS(t)], in1=RS[:, TS(t)], op=ALU.mult),
                     reads=[tX[m][t], tRS[t]], writes=[tTMP[m % 2]])
                b.op("act", lambda e, m=m: e.activation(
                    out=H[:, m, TS(t)], in_=TMP[m % 2], func=AF.Identity, bias=der(li, n, col, 2, m), scale=der(li, n, col, 0, m)),
                    reads=[tTMP[m % 2], tDER], writes=[tH[m][t]])

    pcount = [0]

    def ffn(li, l, f, col, nt):
        n = 0 if f == 0 else 2
        Tn = nt * 512
        b.barrier()
        adaln(li, n, col, nt)
        cv = Carver(BASE)
        WIN = [cv.take(8192) for _ in range(2)]
        WOUT = [cv.take(16384) for _ in range(2)]
        HID = cv.take(8192)
        tWIN = [T(), T()]
        tWOUT = [T(), T()]
        tHID = [T() for _ in range(2)]
        win = W[f"win{l}_{f}"]
        wout = W[f"wout{l}_{f}"]
        for g in range(11):
            gs = g % 2
            b.dma("pool", [(WOUT[gs], wout[g])], [], [tWOUT[gs]], sFW[2 + gs], max_dma_last_dim=8192)
            for c in range(4):
                j = 4 * g + c
                ws = j % 2
                b.dma("pool", [(WIN[ws], win[j])], [], [tWIN[ws]], sFW[ws], max_dma_last_dim=8192)
                for t in range(nt):
                    kk = pcount[0]
                    pcount[0] += 1
                    pg = kk % 2
                    pu = 2 + kk % 2
                    rd = [tWIN[ws]] + [tH[m][t] for m in range(NCH)]
                    b.group("pe", [lambda e, kc=kc: e.matmul(PS[pg][:], WIN[ws][:, kc * 256: kc * 256 + 128], H[:, kc, TS(t)],
                                                              start=(kc == 0), stop=(kc == 15)) for kc in range(16)],
                            reads=rd, writes=[tPS[pg]])
                    b.group("pe", [lambda e, kc=kc: e.matmul(PS[pu][:], WIN[ws][:, kc * 256 + 128: kc * 256 + 256], H[:, kc, TS(t)],
                                                              start=(kc == 0), stop=(kc == 15)) for kc in range(16)],
                            reads=rd, writes=[tPS[pu]])
                    b.op("act", lambda e: e.activation(out=SG[kk % 2], in_=PS[pg][:], func=AF.Silu),
                         reads=[tPS[pg]], writes=[tSG[kk % 2]])
                    b.op("dve", lambda e: e.tensor_tensor(
                        out=HID[:, c * Tn + t * 512: c * Tn + (t + 1) * 512], in0=SG[kk % 2], in1=PS[pu][:], op=ALU.mult),
                        reads=[tSG[kk % 2], tPS[pu]], writes=[tHID[t]])
            for m in range(NCH):
                for t in range(nt):
                    kk = pcount[0]
                    pcount[0] += 1
                    po = 4 + kk % 2
                    b.group("pe", [lambda e, c=c: e.matmul(PS[po][:], WOUT[gs][:, c * 2048 + m * 128: c * 2048 + (m + 1) * 128],
                                                            HID[:, c * Tn + t * 512: c * Tn + (t + 1) * 512],
                                                            start=(c == 0), stop=(c == 3)) for c in range(4)],
                            reads=[tWOUT[gs], tHID[t]], writes=[tPS[po]])
                    b.op("dve", lambda e: e.scalar_tensor_tensor(
                        out=X[:, m, TS(t)], in0=PS[po][:], scalar=der(li, n, col, 1, m), in1=X[:, m, TS(t)],
                        op0=ALU.mult, op1=ALU.add), reads=[tPS[po], tDER, tX[m][t]], writes=[tX[m][t]])

    sFW = [b.slot("fw") for _ in range(4)]
    sWP = [b.slot("wp") for _ in range(3)]
    sIO = [b.slot("io") for _ in range(6)]
    wpk = [0]

    def proj(WP, tWP, wchunk, nt, evac, tiles=None):
        sl = wpk[0] % len(WP)
        wpk[0] += 1
        b.dma("pool", [(WP[sl], wchunk)], [], [tWP[sl]], sWP[sl], max_dma_last_dim=8192)
        for t in (range(nt) if tiles is None else tiles):
            kk = pcount[0]
            pcount[0] += 1
            pb = kk % 2
            b.group("pe", [lambda e, kc=kc: e.matmul(PS[pb][:], WP[sl][:, kc * 128:(kc + 1) * 128], H[:, kc, TS(t)],
                                                      start=(kc == 0), stop=(kc == 15)) for kc in range(16)],
                    reads=[tWP[sl]] + [tH[m][t] for m in range(NCH)], writes=[tPS[pb]])
            evac(pb, t)

    def outproj(li, col, nt, Y, tY, WP, tWP, wo, kcs=range(16)):
        for m in range(NCH):
            sl = wpk[0] % len(WP)
            wpk[0] += 1
            b.dma("pool", [(WP[sl], wo[m])], [], [tWP[sl]], sWP[sl], max_dma_last_dim=8192)
            for t in range(nt):
                kk = pcount[0]
                pcount[0] += 1
                po = 4 + kk % 2
                k0 = kcs[0]
                b.group("pe", [lambda e, kc=kc: e.matmul(PS[po][:], WP[sl][:, kc * 128:(kc + 1) * 128], Y[:, kc - k0, TS(t)],
                                                          start=(kc == kcs[0]), stop=(kc == kcs[-1])) for kc in kcs],
                        reads=[tWP[sl], tY], writes=[tPS[po]])
                b.op("dve", lambda e: e.scalar_tensor_tensor(
                    out=X[:, m, TS(t)], in0=PS[po][:], scalar=der(li, 1, col, 1, m), in1=X[:, m, TS(t)],
                    op0=ALU.mult, op1=ALU.add), reads=[tPS[po], tDER, tX[m][t]], writes=[tX[m][t]])

    def attention(qparts, kparts, vfn, nkb, q0s, scale, rows, ydst, rtiles, wtile, PT, tPT, RD, tRD, tab=None):
        for (q0, n) in q0s:
            for kb in range(nkb):
                kk = pcount[0]
                pcount[0] += 1
                sb = kk % 2
                b.group("pe", [lambda e, i=i: e.matmul(PS[sb][:, 0:n], kparts[i](kb), qparts[i](q0, n),
                                                        start=(i == 0), stop=(i == len(qparts) - 1)) for i in range(len(qparts))],
                        reads=rtiles, writes=[tPS[sb]])
                pt = PT[kk % 2]
                if tab is not None and tab(kb, q0, n) is not None:
                    tap, ttile = tab(kb, q0, n)
                    b.op("dve", lambda e: e.scalar_tensor_tensor(out=TMP[kk % 2][:, 0:n], in0=PS[sb][:, 0:n], scalar=scale,
                                                                 in1=tap, op0=ALU.mult, op1=ALU.add),
                         reads=[tPS[sb], ttile], writes=[tTMP[kk % 2]])
                    b.op("act", lambda e: e.activation(out=pt[:, 0:n], in_=TMP[kk % 2][:, 0:n], func=AF.Exp),
                         reads=[tTMP[kk % 2]], writes=[tPT[kk % 2]])
                else:
                    b.op("act", lambda e: e.activation(out=pt[:, 0:n], in_=PS[sb][:, 0:n], func=AF.Exp, scale=scale),
                         reads=[tPS[sb]], writes=[tPT[kk % 2]])
                b.group("pe", [lambda e: e.matmul(PS[2][:, 0:n], vfn(kb), pt[:, 0:n], start=(kb == 0), stop=(kb == nkb - 1)),
                               lambda e: e.matmul(PS[3][:, 0:n], ones[:], pt[:, 0:n], start=(kb == 0), stop=(kb == nkb - 1))],
                        reads=[tPT[kk % 2], tC] + rtiles, writes=[tPS[2], tPS[3]])
            b.op("dve", lambda e: e.reciprocal(out=RD[rows, 0:n], in_=PS[3][rows, 0:n]), reads=[tPS[3]], writes=[tRD])
            b.op("dve", lambda e: e.tensor_tensor(out=ydst(q0, n), in0=PS[2][rows, 0:n], in1=RD[rows, 0:n], op=ALU.mult),
                 reads=[tPS[2], tRD], writes=[wtile])

    def mixer_ab(li, l, kind, ui, col, nt):
        Tn = nt * 512
        lat = kind == "lat"
        seqs = [(0, 1024)] if lat else [(0, 256), (256, 512)]
        b.barrier()
        adaln(li, 1, col, nt)
        cv = Carver(BASE)
        Y = cv.take(16 * Tn * 2, BF16, [16, Tn])
        tY = T()
        WP = [cv.take(4096) for _ in range(3)]
        tWP = [T() for _ in range(3)]
        LC = cv.take(8 * 11 * 4, F32, [8, 11])
        NSP = cv.take(8 * 4 * 4, F32, [8, 4])
        ST = cv.take(64, F32)
        SO = cv.take(128, F32)
        tLC = T()
        tSO = T()
        wab = W[f"wab{l}"]
        prs = [(LC, W[f"lc{l}"][:, :].rearrange("p (c k) -> p c k", k=11))]
        if lat:
            prs.append((ST, W[f"st{l}"][ui]))
        b.dma("sp", prs, [], [tLC], sIO[0])
        if DBG.get('skip_nsp'):
            return
        b.op("act", lambda e: e.activation(out=NSP[:, :, 0:2], in_=LC[:, :, 9:11], func=AF.Exp, scale=-1.0), reads=[tLC], writes=[tLC])
        b.op("act", lambda e: e.activation(out=NSP[:, :, 0:2], in_=NSP[:, :, 0:2], func=AF.Ln, bias=1.0), reads=[tLC], writes=[tLC])
        b.op("dve", lambda e: e.tensor_scalar(out=NSP[:, :, 2:4], in0=NSP[:, :, 0:2], scalar1=-16.0, scalar2=None, op0=ALU.mult), reads=[tLC], writes=[tLC])
        b.op("dve", lambda e: e.tensor_scalar(out=NSP[:, :, 0:2], in0=NSP[:, :, 0:2], scalar1=-8.0, scalar2=None, op0=ALU.mult), reads=[tLC], writes=[tLC])
        mark = cv.off
        XA = cv.take(Tn * 4, F32)
        GA = cv.take(Tn * 4, F32)
        XC = cv.take(Tn * 4, F32)
        R = cv.take(Tn * 4, F32)
        GI = cv.take(Tn * 4, F32)
        HF = cv.take(Tn * 4, F32)
        HB = cv.take(Tn * 4, F32)
        XCB = cv.take(Tn * 2)
        BD = cv.take(1024, BF16, [4, 128])
        tXA, tGA, tXC, tR, tGI, tHF, tHB, tXCB, tBD = [T() for _ in range(9)]
        Sb, tS = XA, tXA
        for c in range(8 if not DBG.get('skip_lru') else 0):
            b.dma("pool", [(BD, W[f"bd{l}"][c].rearrange("p (k n) -> p k n", n=128))], [], [tBD], sIO[1], max_dma_last_dim=8192)
            proj(WP, tWP, wab[c], nt, lambda pb, t: b.op("act", lambda e: e.activation(out=XA[:, TS(t)], in_=PS[pb][:], func=AF.Identity),
                                                       reads=[tPS[pb]], writes=[tXA]))
            proj(WP, tWP, wab[8 + c], nt, lambda pb, t: b.op("act", lambda e: e.activation(out=GA[:, TS(t)], in_=PS[pb][:], func=AF.Identity),
                                                           reads=[tPS[pb]], writes=[tGA]))
            for (s0, s1) in seqs:
                b.op("act", lambda e: e.activation(out=XC[:, s0:s1], in_=XA[:, s0:s1], func=AF.Identity,
                                                   bias=LC[:, c, 4:5], scale=LC[:, c, 2:3]), reads=[tXA, tLC], writes=[tXC])
                for (jj, sh) in ((0, -2), (1, -1), (3, 1)):
                    if sh < 0:
                        o_ = XC[:, s0 - sh:s1]
                        i_ = XA[:, s0:s1 + sh]
                    else:
                        o_ = XC[:, s0:s1 - sh]
                        i_ = XA[:, s0 + sh:s1]
                    b.op("dve", lambda e, o_=o_, i_=i_, jj=jj: e.scalar_tensor_tensor(
                        out=o_, in0=i_, scalar=LC[:, c, jj:jj + 1], in1=o_, op0=ALU.mult, op1=ALU.add),
                        reads=[tXA, tXC, tLC], writes=[tXC])
            b.op("dve", lambda e: e.tensor_copy(out=XCB, in_=XC), reads=[tXC], writes=[tXCB])
            for d in range(2):
                Hd, tHd = (HF, tHF) if d == 0 else (HB, tHB)
                for t in range(nt):
                    for (a, dst, tdst, bcol) in ((0, R, tR, 5 + d), (1, GI, tGI, 7 + d)):
                        kk = pcount[0]
                        pcount[0] += 1
                        pb = kk % 2
                        b.group("pe", [lambda e: e.matmul(PS[pb][:], BD[:, d * 2 + a, :], XCB[:, TS(t)], start=True, stop=True)],
                                reads=[tBD, tXCB], writes=[tPS[pb]])
                        b.op("act", lambda e: e.activation(out=dst[:, TS(t)], in_=PS[pb][:], func=AF.Sigmoid, bias=LC[:, c, bcol:bcol + 1]),
                             reads=[tPS[pb], tLC], writes=[tdst])
                b.op("act", lambda e: e.activation(out=Sb, in_=R, func=AF.Exp, scale=NSP[:, c, 2 + d:3 + d]), reads=[tR, tLC, tXC], writes=[tS])
                b.op("act", lambda e: e.activation(out=Sb, in_=Sb, func=AF.Sqrt, bias=1.0, scale=-1.0), reads=[tS], writes=[tS])
                b.op("act", lambda e: e.activation(out=R, in_=R, func=AF.Exp, scale=NSP[:, c, d:d + 1]), reads=[tR, tLC], writes=[tR])
                b.op("dve", lambda e: e.tensor_tensor(out=GI, in0=GI, in1=Sb, op=ALU.mult), reads=[tGI, tS], writes=[tGI])
                b.op("dve", lambda e: e.tensor_tensor(out=GI, in0=GI, in1=XC, op=ALU.mult), reads=[tGI, tXC], writes=[tGI])
                for si, (s0, s1) in enumerate(seqs):
                    init = ST[:, c * 2 + d:c * 2 + d + 1] if lat else 0.0
                    if d == 0:
                        b.op("dve", lambda e: e.tensor_tensor_scan(out=Hd[:, s0:s1], data0=R[:, s0:s1], data1=GI[:, s0:s1],
                                                                   initial=init, op0=ALU.mult, op1=ALU.add),
                             reads=[tR, tGI, tLC], writes=[tHd])
                    else:
                        b.op("dve", lambda e: e.tensor_tensor_scan(out=rev(Hd[:, s0:s1]), data0=rev(R[:, s0:s1]), data1=rev(GI[:, s0:s1]),
                                                                   initial=init, op0=ALU.mult, op1=ALU.add),
                             reads=[tR, tGI, tLC], writes=[tHd])
                    if not lat:
                        src = Hd[:, s1 - 1:s1] if d == 0 else Hd[:, s0:s0 + 1]
                        k_ = (c * 2 + d) * 2 + si
                        b.op("act", lambda e: e.activation(out=SO[:, k_:k_ + 1], in_=src, func=AF.Identity), reads=[tHd], writes=[tSO])
            b.op("act", lambda e: e.activation(out=R, in_=GA, func=AF.Square), reads=[tGA, tGI], writes=[tR])
            b.op("dve", lambda e: e.tensor_scalar(out=R, in0=R, scalar1=0.044715, scalar2=1.0, op0=ALU.mult, op1=ALU.add), reads=[tR], writes=[tR])
            b.op("dve", lambda e: e.tensor_tensor(out=R, in0=R, in1=GA, op=ALU.mult), reads=[tR, tGA], writes=[tR])
            b.op("act", lambda e: e.activation(out=R, in_=R, func=AF.Sigmoid, scale=1.5957691216057308), reads=[tR], writes=[tR])
            b.op("dve", lambda e: e.tensor_tensor(out=R, in0=R, in1=GA, op=ALU.mult), reads=[tR, tGA], writes=[tR])
            b.op("dve", lambda e: e.tensor_tensor(out=HF, in0=HF, in1=HB, op=ALU.add), reads=[tHF, tHB], writes=[tHF])
            b.op("dve", lambda e: e.tensor_tensor(out=Y[:, c, :], in0=HF, in1=R, op=ALU.mult), reads=[tHF, tR], writes=[tY])
        if not lat:
            b.dma("sp", [(O[f"ost{l}"][ui], SO)], [tSO], [], sOut[0])
        b.barrier()
        cv.off = mark
        QT = cv.take(Tn * 2)
        KT = cv.take((Tn + 512) * 2)
        QM = cv.take(Tn * 2)
        KF = cv.take(Tn * 4, F32) if not lat else None
        VF = cv.take(Tn * 4, F32) if not lat else None
        nvb = (Tn + (512 if lat else 0)) // 128
        VT = cv.take(nvb * 128 * 2, BF16, [nvb, 128])
        PT = [cv.take(1024) for _ in range(2)]
        RD = cv.take(2048, F32)
        tQT, tKT, tQM, tKF, tVF, tVT, tRD = [T() for _ in range(7)]
        tPT = [T(), T()]
        if lat:
            TAB = [cv.take(2048, BF16) for _ in range(8)]
            tTAB = [T() for _ in range(8)]
            for kb in range(8):
                b.op("pool", lambda e, kb=kb: e.memset(TAB[kb], NEG), writes=[tTAB[kb]])
        for c in range(8 if not DBG.get('skip_qkv') else 0):
            proj(WP, tWP, wab[16 + c], nt, lambda pb, t: b.op("act", lambda e: e.activation(out=QT[:, TS(t)], in_=PS[pb][:], func=AF.Identity),
                                                            reads=[tPS[pb]], writes=[tQT]))

            def evk(pb, t):
                if not lat:
                    b.op("act", lambda e: e.activation(out=KF[:, TS(t)], in_=PS[pb][:], func=AF.Identity), reads=[tPS[pb]], writes=[tKF])
                    b.op("dve", lambda e: e.tensor_copy(out=KT[:, TS(t)], in_=KF[:, TS(t)]), reads=[tKF], writes=[tKT])
                else:
                    b.op("dve", lambda e: e.tensor_copy(out=KT[:, TS(t)], in_=PS[pb][:]), reads=[tPS[pb]], writes=[tKT])
            proj(WP, tWP, wab[24 + c], nt, evk)
            if not lat:
                proj(WP, tWP, wab[32 + c], nt, lambda pb, t: b.op("act", lambda e: e.activation(out=VF[:, TS(t)], in_=PS[pb][:], func=AF.Identity),
                                                                reads=[tPS[pb]], writes=[tVF]))
                b.dma("sp", [(O[f"onak{l}"][ui, c], KF), (O[f"onav{l}"][ui, c], VF)], [tKF, tVF], [], sOut[1])
            sl = wpk[0] % 3
            wpk[0] += 1
            b.dma("pool", [(WP[sl], wab[32 + c])], [], [tWP[sl]], sWP[sl], max_dma_last_dim=8192)
            for tb in range(Tn // 128 if not DBG.get('skip_vt') else 0):
                kk = pcount[0]
                pcount[0] += 1
                pb = kk % 2
                t = tb // 4
                b.group("pe", [lambda e, kc=kc: e.matmul(PS[pb][:, 0:128], H[:, kc, tb * 128:(tb + 1) * 128], WP[sl][:, kc * 128:(kc + 1) * 128],
                                                          start=(kc == 0), stop=(kc == 15)) for kc in range(16)],
                        reads=[tWP[sl]] + [tH[m][t] for m in range(NCH)], writes=[tPS[pb]])
                b.op("act", lambda e: e.activation(out=VT[:, tb, :], in_=PS[pb][:, 0:128], func=AF.Identity), reads=[tPS[pb]], writes=[tVT])
            if lat:
                b.dma("pool", [(KT[:, Tn:Tn + 512], W[f"nakc{l}"][ui, c]),
                               (VT[:, 8:12, :], W[f"navc{l}"][ui][:, :, c * 128:(c + 1) * 128].rearrange("k p n -> p k n"))],
                      [], [tKT, tVT], sIO[2], max_dma_last_dim=8192)
            for hh in range(2 if not DBG.get('skip_att') else 0):
                rows = slice(hh * 64, (hh + 1) * 64)
                b.op("dve", lambda e: e.tensor_scalar(out=QM, in0=QT, scalar1=HM[:, hh:hh + 1], scalar2=None, op0=ALU.mult),
                     reads=[tQT, tC], writes=[tQM])
                if lat:
                    hd = 2 * c + hh
                    bz = W[f"bz{l}"]
                    for i in range(16):
                        r_lo = 0 if i <= 7 else i - 3
                        r_hi = 15 if i >= 8 else i + 4
                        nr = r_hi - r_lo + 1
                        rr0 = r_lo - i + 7
                        kb = i // 2
                        src = bass.AP(bz.tensor, bz.offset + ((hd * 15 + rr0) * 64) * 64, [[64, 64], [4096, nr], [1, 64]])
                        dst = TAB[kb][(i % 2) * 64:(i % 2) * 64 + 64, r_lo * 64:(r_hi + 1) * 64].rearrange("p (r q) -> p r q", q=64)
                        b.dma("pool", [(dst, src)], [], [tTAB[kb]], sTAB[kb])
                    attention([lambda q0, n: QM[:, q0:q0 + n]], [lambda kb: KT[:, kb * 128:(kb + 1) * 128]],
                              lambda kb: VT[:, kb, :], 12, [(0, 512), (512, 512)], NA_SCALE, rows,
                              lambda q0, n: Y[rows, 8 + c, q0:q0 + n], [tQM, tKT, tVT], tY, PT, tPT, RD, tRD,
                              tab=lambda kb, q0, n: (TAB[kb][:, q0:q0 + n], tTAB[kb]) if kb < 8 else None)
                else:
                    for (s0, s1) in seqs:
                        kb0 = s0 // 128
                        attention([lambda q0, n: QM[:, q0:q0 + n]], [lambda kb: KT[:, (kb0 + kb) * 128:(kb0 + kb + 1) * 128]],
                                  lambda kb: VT[:, kb0 + kb, :], 2, [(s0, 256)], NA_SCALE, rows,
                                  lambda q0, n: Y[rows, 8 + c, q0:q0 + n], [tQM, tKT, tVT], tY, PT, tPT, RD, tRD)
        if not DBG.get('skip_outproj'):
            outproj(li, col, nt, Y, tY, WP, tWP, W[f"wabo{l}"])

    sOut = [b.slot("out") for _ in range(6)]
    outslots.extend(sOut)
    sTAB = [b.slot("tab") for _ in range(8)]

    def mixer_cd(li, l, kind, ui, col, nt):
        Tn = nt * 512
        lat = kind == "lat"
        Tk = Tn + (512 if lat else 0)
        seqs = [(0, 1024)] if lat else [(0, 256), (256, 512)]
        wcd = W[f"wcd{l}"]
        b.barrier()
        adaln(li, 1, col, nt)
        cv = Carver(BASE)
        Y8 = cv.take(8 * Tn * 2, BF16, [8, Tn])
        tY = T()
        WP = [cv.take(4096) for _ in range(3)]
        tWP = [T() for _ in range(3)]
        CDC = cv.take(40, F32)
        tCDC = T()
        b.dma("sp", [(CDC, W[f"cdc{l}"][:, :])], [], [tCDC], sIO[0])
        mark = cv.off

        def rope_from(src, t0, n, dst, rd, wr):
            b.op("act", lambda e: e.activation(out=SG[0][:, 0:n], in_=src, func=AF.Identity), reads=rd, writes=[tSG[0]])
            b.group("pe", [lambda e: e.matmul(PS[5][:, 0:n], perm, SG[0][:, 0:n], start=True, stop=True)], reads=[tSG[0], tMat], writes=[tPS[5]])
            b.op("dve", lambda e: e.tensor_tensor(out=TMP[0][:, 0:n], in0=src, in1=COS[:, t0:t0 + n], op=ALU.mult), reads=rd + [tC], writes=[tTMP[0]])
            b.op("dve", lambda e: e.tensor_tensor(out=TMP[1][:, 0:n], in0=PS[5][:, 0:n], in1=SIN[:, t0:t0 + n], op=ALU.mult), reads=[tPS[5], tC], writes=[tTMP[1]])
            b.op("dve", lambda e: e.tensor_tensor(out=dst, in0=TMP[0][:, 0:n], in1=TMP[1][:, 0:n], op=ALU.add), reads=[tTMP[0], tTMP[1]], writes=wr)

        K2 = [cv.take(Tk * 2) for _ in range(4)]
        tK2 = [T() for _ in range(4)]
        nkbT = Tk // 128
        VD = cv.take(nkbT * 512 * 2, BF16, [nkbT, 512])
        tVD = T()
        WV = cv.take(16384)
        tWV = T()
        QT = cv.take(Tn * 2)
        QM = cv.take(Tn * 2)
        STG = cv.take(2048, F32)
        RSQ = cv.take(2048, F32)
        PT = [cv.take(1024) for _ in range(2)]
        RD = cv.take(2048, F32)
        tQT, tQM, tSTG, tRSQ, tRD = [T() for _ in range(5)]
        tPT = [T(), T()]

        def headnorm(pb, t, gcol, dst, wr, rope, f32dma=None):
            b.op("act", lambda e: e.activation(out=STG, in_=PS[pb][:], func=AF.Identity), reads=[tPS[pb]], writes=[tSTG])
            b.op("act", lambda e: e.activation(out=SG[1], in_=PS[pb][:], func=AF.Square), reads=[tPS[pb]], writes=[tSG[1]])
            b.group("pe", [lambda e: e.matmul(PS[6][:], bones, SG[1], start=True, stop=True)], reads=[tSG[1], tMat], writes=[tPS[6]])
            b.op("act", lambda e: e.activation(out=RSQ, in_=PS[6][:], func=AF.Sqrt, bias=EPS, scale=1.0 / 64), reads=[tPS[6]], writes=[tRSQ])
            b.op("dve", lambda e: e.reciprocal(out=RSQ, in_=RSQ), reads=[tRSQ], writes=[tRSQ])
            if rope or f32dma is not None:
                b.op("dve", lambda e: e.scalar_tensor_tensor(out=STG, in0=STG, scalar=CDC[:, gcol:gcol + 1], in1=RSQ, op0=ALU.mult, op1=ALU.mult),
                     reads=[tSTG, tRSQ, tCDC], writes=[tSTG])
                if f32dma is not None:
                    f32dma()
                if rope:
                    rope_from(STG, t * 512, 512, dst, [tSTG], wr)
                else:
                    b.op("act", lambda e: e.activation(out=dst, in_=STG, func=AF.Identity), reads=[tSTG], writes=wr)
            else:
                b.op("dve", lambda e: e.scalar_tensor_tensor(out=dst, in0=STG, scalar=CDC[:, gcol:gcol + 1], in1=RSQ, op0=ALU.mult, op1=ALU.mult),
                     reads=[tSTG, tRSQ, tCDC], writes=wr)

        for kvh in range(4):
            def evk(pb, t, kvh=kvh):
                f = None
                if not lat:
                    f = lambda: b.dma("sp", [(O[f"ogk{l}"][ui, kvh], STG[0:64, :])], [tSTG], [], sOut[2])
                headnorm(pb, t, 1, K2[kvh][:, TS(t)], [tK2[kvh]], lat, f)
            proj(WP, tWP, wcd[8 + kvh], nt, evk)
        if lat:
            b.dma("pool", [(K2[kvh][:, Tn:Tn + 512], W[f"gqk{l}"][ui, kvh]) for kvh in range(4)]
                  + [(VD[:, 8:12, :], W[f"gqv{l}"][ui].rearrange("k p n -> p k n"))], [], tK2 + [tVD], sIO[2], max_dma_last_dim=8192)
        else:
            for i in range(2):
                def evv(pb, t, i=i):
                    b.op("act", lambda e: e.activation(out=STG, in_=PS[pb][:], func=AF.Identity), reads=[tPS[pb]], writes=[tSTG])
                    b.dma("sp", [(O[f"ogv{l}"][ui, i], STG)], [tSTG], [], sOut[2])
                proj(WP, tWP, wcd[12 + i], nt, evv)
        b.dma("pool", [(WV, W[f"wvd{l}"][:, :])], [], [tWV], sIO[3], max_dma_last_dim=8192)
        for tb in range(Tn // 128):
            kk = pcount[0]
            pcount[0] += 1
            pb = kk % 2
            b.group("pe", [lambda e, kc=kc: e.matmul(PS[pb][:], H[:, kc, tb * 128:(tb + 1) * 128], WV[:, kc * 512:(kc + 1) * 512],
                                                      start=(kc == 0), stop=(kc == 15)) for kc in range(16)],
                    reads=[tWV] + [tH[m][tb // 4] for m in range(NCH)], writes=[tPS[pb]])
            b.op("act", lambda e: e.activation(out=VD[:, tb, :], in_=PS[pb][:], func=AF.Identity), reads=[tPS[pb]], writes=[tVD])
        for c in range(8):
            kvh = c // 2
            proj(WP, tWP, wcd[c], nt, lambda pb, t: headnorm(pb, t, 0, QT[:, TS(t)], [tQT], lat))
            for hh in range(2):
                rows = slice(hh * 64, (hh + 1) * 64)
                b.op("dve", lambda e: e.tensor_scalar(out=QM, in0=QT, scalar1=HM[:, hh:hh + 1], scalar2=None, op0=ALU.mult),
                     reads=[tQT, tC], writes=[tQM])
                if lat:
                    attention([lambda q0, n: QM[:, q0:q0 + n]], [lambda kb: K2[kvh][:, kb * 128:(kb + 1) * 128]],
                              lambda kb: VD[:, kb, kvh * 128:(kvh + 1) * 128], 12, [(0, 512), (512, 512)], NA_SCALE, rows,
                              lambda q0, n: Y8[rows, c, q0:q0 + n], [tQM, tK2[kvh], tVD], tY, PT, tPT, RD, tRD)
                else:
                    for (s0, s1) in seqs:
                        kb0 = s0 // 128
                        attention([lambda q0, n: QM[:, q0:q0 + n]], [lambda kb: K2[kvh][:, (kb0 + kb) * 128:(kb0 + kb + 1) * 128]],
                                  lambda kb: VD[:, kb0 + kb, kvh * 128:(kvh + 1) * 128], 2, [(s0, 256)], NA_SCALE, rows,
                                  lambda q0, n: Y8[rows, c, q0:q0 + n], [tQM, tK2[kvh], tVD], tY, PT, tPT, RD, tRD)
        outproj(li, col, nt, Y8, tY, WP, tWP, W[f"wcdo{l}"], kcs=range(0, 8))

        b.barrier()
        cv.off = mark
        STG4 = cv.take(8192, F32, [4, 512])
        QAN = cv.take(4 * Tn * 2, BF16, [4, Tn])
        CKB = cv.take(4 * Tk * 2, BF16, [4, Tk])
        KR2 = cv.take(Tk * 2)
        RS2 = cv.take(2048, F32)
        WH = cv.take(4096, BF16, [4, 4, 128])
        QN = cv.take(Tn * 2)
        QRh = cv.take(Tn * 2)
        KN = cv.take(Tk * 2)
        VM = cv.take(nkbT * 128 * 2, BF16, [nkbT, 128])
        PT = [cv.take(1024) for _ in range(2)]
        RD = cv.take(2048, F32)
        tSTG4, tQAN, tCKB, tKR2, tRS2, tWH, tQN, tQRh, tKN, tVM, tRD = [T() for _ in range(11)]
        tPT = [T(), T()]

        def norm512(wbase, gcol0, t, after):
            for i in range(4):
                def ev(pb, t_, i=i):
                    b.op("act", lambda e: e.activation(out=STG4[:, i, :], in_=PS[pb][:], func=AF.Identity), reads=[tPS[pb]], writes=[tSTG4])
                    b.op("act", lambda e: e.activation(out=SG[i % 2], in_=PS[pb][:], func=AF.Square), reads=[tPS[pb]], writes=[tSG[i % 2]])
                    b.group("pe", [lambda e: e.matmul(PS[6][:], ones[:], SG[i % 2], start=(i == 0), stop=(i == 3))],
                            reads=[tSG[i % 2], tC], writes=[tPS[6]])
                proj(WP, tWP, wcd[wbase + i], nt, ev, tiles=[t])
            b.op("act", lambda e: e.activation(out=RS2, in_=PS[6][:], func=AF.Sqrt, bias=EPS, scale=1.0 / 512), reads=[tPS[6]], writes=[tRS2])
            b.op("dve", lambda e: e.reciprocal(out=RS2, in_=RS2), reads=[tRS2], writes=[tRS2])
            for i in range(4):
                b.op("dve", lambda e: e.scalar_tensor_tensor(out=STG4[:, i, :], in0=STG4[:, i, :], scalar=CDC[:, gcol0 + i:gcol0 + i + 1], in1=RS2,
                                                             op0=ALU.mult, op1=ALU.mult), reads=[tSTG4, tRS2, tCDC], writes=[tSTG4])
                after(i)

        for t in range(nt):
            norm512(14, 2, t, lambda i: b.op("act", lambda e: e.activation(out=QAN[:, i, TS(t)], in_=STG4[:, i, :], func=AF.Identity),
                                             reads=[tSTG4], writes=[tQAN]))

            def after_ckv(i):
                if not lat:
                    b.dma("sp", [(O[f"omc{l}"][ui, i], STG4[:, i, :])], [tSTG4], [], sOut[3])
                b.op("act", lambda e: e.activation(out=CKB[:, i, TS(t)], in_=STG4[:, i, :], func=AF.Identity), reads=[tSTG4], writes=[tCKB])
            norm512(18, 6, t, after_ckv)

        def evkr(pb, t):
            b.op("act", lambda e: e.activation(out=RS2, in_=PS[pb][:], func=AF.Identity), reads=[tPS[pb]], writes=[tRS2])
            if lat:
                rope_from(RS2, t * 512, 512, KR2[:, TS(t)], [tRS2], [tKR2])
            else:
                b.dma("sp", [(O[f"omr{l}"][ui], RS2[0:64, :])], [tRS2], [], sOut[4])
                b.op("act", lambda e: e.activation(out=KR2[:, TS(t)], in_=RS2, func=AF.Identity), reads=[tRS2], writes=[tKR2])
        proj(WP, tWP, wcd[22], nt, evkr)
        if lat:
            b.dma("pool", [(CKB[:, :, Tn:Tn + 512], W[f"mlc{l}"][ui].rearrange("k p n -> p k n")), (KR2[:, Tn:Tn + 512], W[f"mlr{l}"][ui])],
                  [], [tCKB, tKR2], sIO[4], max_dma_last_dim=8192)
        wuqn = W[f"wuqn{l}"][:, :].rearrange("p (k n) -> p k n", n=1024)
        wuk = W[f"wuk{l}"][:, :].rearrange("p (k n) -> p k n", n=1024)
        wuv = W[f"wuv{l}"][:, :].rearrange("p (k n) -> p k n", n=1024)
        wuqr = W[f"wuqr{l}"][:, :].rearrange("p (k n) -> p k n", n=512)
        for h in range(8):
            hs = slice(h * 128, (h + 1) * 128)
            b.op("pool", lambda e: e.memset(WH[:, 3, :, :], 0.0), writes=[tWH])
            b.dma("pool", [(WH[:, 0, :, :], wuqn[:, :, hs]), (WH[:, 1, :, :], wuk[:, :, hs]), (WH[:, 2, :, :], wuv[:, :, hs]),
                           (WH[:, 3, :, (h % 2) * 64:(h % 2) * 64 + 64], wuqr[:, :, h * 64:(h + 1) * 64])], [], [tWH], sIO[5])
            for t in range(nt):
                for (wi, dst, tdst, rp) in ((0, QN, tQN, False), (3, QRh, tQRh, lat)):
                    kk = pcount[0]
                    pcount[0] += 1
                    pb = kk % 2
                    b.group("pe", [lambda e, kc=kc: e.matmul(PS[pb][:], WH[:, wi, kc, :], QAN[:, kc, TS(t)], start=(kc == 0), stop=(kc == 3)) for kc in range(4)],
                            reads=[tWH, tQAN], writes=[tPS[pb]])
                    if rp:
                        b.op("act", lambda e: e.activation(out=RS2, in_=PS[pb][:], func=AF.Identity), reads=[tPS[pb]], writes=[tRS2])
                        rope_from(RS2, t * 512, 512, dst[:, TS(t)], [tRS2], [tdst])
                    else:
                        b.op("act", lambda e: e.activation(out=dst[:, TS(t)], in_=PS[pb][:], func=AF.Identity), reads=[tPS[pb]], writes=[tdst])
            for kt in range(Tk // 512):
                kk = pcount[0]
                pcount[0] += 1
                pb = kk % 2
                b.group("pe", [lambda e, kc=kc: e.matmul(PS[pb][:], WH[:, 1, kc, :], CKB[:, kc, TS(kt)], start=(kc == 0), stop=(kc == 3)) for kc in range(4)],
                        reads=[tWH, tCKB], writes=[tPS[pb]])
                b.op("act", lambda e: e.activation(out=KN[:, TS(kt)], in_=PS[pb][:], func=AF.Identity), reads=[tPS[pb]], writes=[tKN])
            for kb in range(nkbT):
                kk = pcount[0]
                pcount[0] += 1
                pb = kk % 2
                b.group("pe", [lambda e, kc=kc: e.matmul(PS[pb][:, 0:128], CKB[:, kc, kb * 128:(kb + 1) * 128], WH[:, 2, kc, :], start=(kc == 0), stop=(kc == 3)) for kc in range(4)],
                        reads=[tWH, tCKB], writes=[tPS[pb]])
                b.op("act", lambda e: e.activation(out=VM[:, kb, :], in_=PS[pb][:, 0:128], func=AF.Identity), reads=[tPS[pb]], writes=[tVM])
            rows = slice(0, 128)
            rt = [tQN, tQRh, tKN, tKR2, tVM]
            if lat:
                attention([lambda q0, n: QN[:, q0:q0 + n], lambda q0, n: QRh[:, q0:q0 + n]],
                          [lambda kb: KN[:, kb * 128:(kb + 1) * 128], lambda kb: KR2[:, kb * 128:(kb + 1) * 128]],
                          lambda kb: VM[:, kb, :], 12, [(0, 512), (512, 512)], MLA_SCALE, rows,
                          lambda q0, n: Y8[:, h, q0:q0 + n], rt, tY, PT, tPT, RD, tRD)
            else:
                for (s0, s1) in seqs:
                    kb0 = s0 // 128
                    attention([lambda q0, n: QN[:, q0:q0 + n], lambda q0, n: QRh[:, q0:q0 + n]],
                              [lambda kb: KN[:, (kb0 + kb) * 128:(kb0 + kb + 1) * 128], lambda kb: KR2[:, (kb0 + kb) * 128:(kb0 + kb + 1) * 128]],
                              lambda kb: VM[:, kb0 + kb, :], 2, [(s0, 256)], MLA_SCALE, rows,
                              lambda q0, n: Y8[:, h, q0:q0 + n], rt, tY, PT, tPT, RD, tRD)
        outproj(li, col, nt, Y8, tY, WP, tWP, W[f"wcdo{l}"], kcs=range(8, 16))

    for ui in range(nu):
        for kind in ("ctx", "lat"):
            if DBG.get('only_' + ('lat' if kind == 'ctx' else 'ctx')):
                continue
            nt = 1 if kind == "ctx" else 2
            Tn = nt * 512
            col = 0 if kind == "ctx" else 1 + ui
            src = xc if kind == "ctx" else xl
            dst = yc if kind == "ctx" else yl
            b.dma("sp", [(X[:, m, 0:Tn], src[ui, m]) for m in range(NCH)], [], allX, sX)
            for li, l in enumerate(layers):
                ffn(li, l, 0, col, nt)
                if mixers:
                    if l % 2 == 0:
                        mixer_ab(li, l, kind, ui, col, nt)
                    else:
                        mixer_cd(li, l, kind, ui, col, nt)
                    ffn(li, l, 1, col, nt)
            norm_rstd(nt)
            for t in range(nt):
                for m in range(NCH):
                    b.op("dve", lambda e, m=m: e.scalar_tensor_tensor(
                        out=X[:, m, TS(t)], in0=X[:, m, TS(t)], scalar=FG[:, m:m + 1], in1=RS[:, TS(t)], op0=ALU.mult, op1=ALU.mult),
                        reads=[tX[m][t], tRS[t], tC], writes=[tX[m][t]])
            b.dma("sp", [(dst[ui, m], X[:, m, 0:Tn]) for m in range(NCH)], allX, [], sY)
    b.finish(outslots)
    return b


def _chunks(w, nchunk):
    return np.ascontiguousarray(w.reshape(16, 128, nchunk, 128).transpose(2, 1, 0, 3)).reshape(nchunk, 128, 2048)


def _rows4(w, nk, ncol):
    return np.ascontiguousarray(w.reshape(nk, 128, ncol).transpose(1, 0, 2)).reshape(128, nk * ncol)


def _const_tables():
    t = np.arange(1024)
    nf = 16
    inv = 1.0 / (10000.0 ** (np.arange(nf, dtype=np.float32) / nf))
    d = np.arange(128) % 64
    half = d // 32
    i = d % 16
    pos = np.where(half[:, None] == 0, (t // 64)[None, :], (t % 64)[None, :]).astype(np.float32)
    ang = pos * inv[i][:, None]
    cos = np.cos(ang).astype(np.float32)
    sin = np.sin(ang).astype(np.float32)
    perm = np.zeros((128, 128), np.float32)
    for m in range(128):
        j = (m % 64) % 32
        if j < 16:
            perm[m + 16, m] = -1.0
        else:
            perm[m - 16, m] = 1.0
    bones = np.zeros((128, 128), np.float32)
    bones[:64, :64] = 1.0
    bones[64:, 64:] = 1.0
    matd = np.concatenate([perm, bones, np.zeros((128, 128), np.float32)], axis=1)
    hm = np.zeros((128, 2), np.float32)
    hm[:64, 0] = 1.0
    hm[64:, 1] = 1.0
    return cos, sin, matd, hm


def _layout_common(inp, layers, mixers=True):
    out = {}
    cos, sin, matd, hm = _const_tables()
    out["cosd"], out["sind"], out["matd"] = cos, sin, matd
    fg = np.asarray(inp["final_gain"]).reshape(16, 128).T
    out["cst"] = np.ascontiguousarray(np.concatenate([fg, hm], axis=1).astype(np.float32))
    for l in layers:
        j = l // 2
        wm = np.asarray(inp["w_mod"][l])
        out[f"wmod{l}"] = np.ascontiguousarray(wm.reshape(16, 128, 36, 512).transpose(2, 1, 0, 3)).reshape(36, 128, 8192)
        out[f"bmod{l}"] = np.ascontiguousarray(np.asarray(inp["b_mod"][l]).reshape(144, 128).T)
        out[f"gain{l}"] = np.ascontiguousarray(np.asarray(inp["norm_gain"][l]).reshape(3, 16, 128).transpose(2, 0, 1)).reshape(128, 48)
        for f in range(2):
            w = np.asarray(inp["w_ffn_in"][l][f])
            g = w[:, :DFF].reshape(16, 128, NJ, 128)
            u = w[:, DFF:].reshape(16, 128, NJ, 128)
            cat = np.concatenate([g, u], axis=-1)
            out[f"win{l}_{f}"] = np.ascontiguousarray(cat.transpose(2, 1, 0, 3)).reshape(NJ, 128, 4096)
            wo = np.asarray(inp["w_ffn_out"][l][f])
            out[f"wout{l}_{f}"] = np.ascontiguousarray(wo.reshape(11, 4, 128, 2048).transpose(0, 2, 1, 3)).reshape(11, 128, 8192)
        if not mixers:
            continue
        if l % 2 == 0:
            out[f"wab{l}"] = _chunks(np.asarray(inp["w_in_ab"][j]), 40)
            out[f"wabo{l}"] = _chunks(np.asarray(inp["w_out_ab"][j]), 16)
            wa = np.asarray(inp["lru_wa"][j])
            wx = np.asarray(inp["lru_wx"][j])
            bd = np.zeros((8, 128, 4, 128), np.float32)
            for c in range(8):
                for d in range(2):
                    for a, ww in enumerate((wa, wx)):
                        bd[c, :64, d * 2 + a, :64] = ww[d, 2 * c]
                        bd[c, 64:, d * 2 + a, 64:] = ww[d, 2 * c + 1]
            out[f"bd{l}"] = bd.reshape(8, 128, 512)
            cols = [np.asarray(inp["conv_w"][j])[k] for k in range(4)] + [np.asarray(inp["conv_b"][j])]
            cols += [np.asarray(inp["lru_ba"][j])[0], np.asarray(inp["lru_ba"][j])[1],
                     np.asarray(inp["lru_bx"][j])[0], np.asarray(inp["lru_bx"][j])[1],
                     np.asarray(inp["lru_lambda"][j])[0], np.asarray(inp["lru_lambda"][j])[1]]
            lc = np.stack(cols, axis=-1)
            out[f"lc{l}"] = np.ascontiguousarray(lc.reshape(8, 128, 11).transpose(1, 0, 2)).reshape(128, 88)
            nb = np.asarray(inp["na_bias"][j])
            kc = np.arange(64)[:, None]
            qc = np.arange(64)[None, :]
            relc = np.clip(kc - qc + 15, 0, 30)
            cstart = np.clip(qc - 8, 0, 48)
            ok = (kc >= cstart) & (kc < cstart + 16)
            bz = np.empty((16, 15, 64, 64), np.float32)
            for rr in range(15):
                g = nb[:, 14 - rr][:, relc]
                bz[:, rr] = np.where(ok[None], g, np.float32(NEG))
            out[f"bz{l}"] = bz
        else:
            w = np.asarray(inp["w_in_cd"][j])
            cols = [w[:, c * 128:(c + 1) * 128] for c in range(8)]
            for kvh in range(4):
                kk = w[:, 1024 + kvh * 64:1024 + (kvh + 1) * 64]
                cols.append(np.concatenate([kk, kk], axis=1))
            cols += [w[:, 1280:1408], w[:, 1408:1536]]
            cols += [w[:, 1536 + i * 128:1536 + (i + 1) * 128] for i in range(4)]
            cols += [w[:, 2048 + i * 128:2048 + (i + 1) * 128] for i in range(4)]
            cols.append(np.concatenate([w[:, 2560:2624], w[:, 2560:2624]], axis=1))
            out[f"wcd{l}"] = _chunks(np.concatenate(cols, axis=1), 23)
            vd = np.concatenate([np.concatenate([w[:, 1280 + k * 64:1280 + (k + 1) * 64]] * 2, axis=1) for k in range(4)], axis=1)
            out[f"wvd{l}"] = _rows4(vd, 16, 512)
            out[f"wcdo{l}"] = _chunks(np.asarray(inp["w_out_cd"][j]), 16)
            gq = np.tile(np.asarray(inp["gqa_q_gain"][j]), 2)[:, None]
            gk = np.tile(np.asarray(inp["gqa_k_gain"][j]), 2)[:, None]
            mq = np.asarray(inp["mla_q_gain"][j]).reshape(4, 128).T
            mk = np.asarray(inp["mla_kv_gain"][j]).reshape(4, 128).T
            out[f"cdc{l}"] = np.ascontiguousarray(np.concatenate([gq, gk, mq, mk], axis=1).astype(np.float32))
            uq = np.asarray(inp["mla_w_uq"][j]).reshape(512, 8, 192)
            out[f"wuqn{l}"] = _rows4(np.ascontiguousarray(uq[:, :, :128]).reshape(512, 1024), 4, 1024)
            out[f"wuqr{l}"] = _rows4(np.ascontiguousarray(uq[:, :, 128:]).reshape(512, 512), 4, 512)
            out[f"wuk{l}"] = _rows4(np.asarray(inp["mla_w_uk"][j]), 4, 1024)
            out[f"wuv{l}"] = _rows4(np.asarray(inp["mla_w_uv"][j]), 4, 1024)
    return out


def _layout_unit(inp, units, layers, mixers=True):
    nu = len(units)
    xp = np.asarray(inp["x_prompt"])
    xs = np.asarray(inp["x_sample"])
    c = np.asarray(inp["c"])
    cond = np.empty((1 + nu, D), np.float32)
    cond[0] = np.asarray(inp["c_ctx"])
    xc = np.empty((nu, NCH, 128, 512), np.float32)
    xl = np.empty((nu, NCH, 128, 1024), np.float32)
    for k, u in enumerate(units):
        tok = np.concatenate([xp[2 * u], xp[2 * u + 1]], axis=0)
        xc[k] = tok.T.reshape(NCH, 128, 512)
        xl[k] = xs[u].T.reshape(NCH, 128, 1024)
        cond[1 + k] = c[u]
    ncols = 1 + nu
    out = {"xc": xc, "xl": xl,
           "condT": np.ascontiguousarray(cond.T.reshape(NCH, 128, ncols).transpose(1, 0, 2)).reshape(128, NCH * ncols)}
    if not mixers:
        return out
    for l in layers:
        j = l // 2
        if l % 2 == 0:
            sf = np.asarray(inp["state_lru_fwd"])[units, j]
            sb = np.asarray(inp["state_lru_bwd"])[units, j]
            st = np.stack([sf, sb], axis=-1).reshape(nu, 8, 128, 2).transpose(0, 2, 1, 3)
            out[f"st{l}"] = np.ascontiguousarray(st).reshape(nu, 128, 16)
            kk = np.asarray(inp["cache_na_k"])[units, j].reshape(nu, 512, 1024)
            out[f"nakc{l}"] = np.ascontiguousarray(kk.transpose(0, 2, 1)).reshape(nu, 8, 128, 512)
            out[f"navc{l}"] = np.ascontiguousarray(np.asarray(inp["cache_na_v"])[units, j].reshape(nu, 4, 128, 1024))
        else:
            gk = np.asarray(inp["cache_gqa_k"])[units, j]
            kt = gk.transpose(0, 2, 3, 1)
            out[f"gqk{l}"] = np.ascontiguousarray(np.concatenate([kt, kt], axis=2))
            gv = np.asarray(inp["cache_gqa_v"])[units, j]
            out[f"gqv{l}"] = np.ascontiguousarray(np.concatenate([gv, gv], axis=3)).reshape(nu, 4, 128, 512)
            mc = np.asarray(inp["cache_mla_ckv"])[units, j]
            out[f"mlc{l}"] = np.ascontiguousarray(mc.transpose(0, 2, 1)).reshape(nu, 4, 128, 512)
            mr = np.asarray(inp["cache_mla_krope"])[units, j].transpose(0, 2, 1)
            out[f"mlr{l}"] = np.ascontiguousarray(np.concatenate([mr, mr], axis=1))
    return out


NCORES = 4


def kernel(**inp):
    ncores = NCORES
    nu = 8 // ncores
    layers = [0, 1, 2, 3]
    bld = build(nu, layers, True)
    common = _layout_common(inp, layers, True)
    in_maps = []
    for core in range(ncores):
        d = dict(common)
        d.update(_layout_unit(inp, [core * nu + k for k in range(nu)], layers, True))
        in_maps.append(d)
    res = run_bass_kernel_spmd(bld.nc, in_maps, core_ids=list(range(ncores)))
    y_prompt = np.empty((16, 256, D), np.float32)
    y_sample = np.empty((8, 1024, D), np.float32)
    st_f = np.empty((16, 2, 1024), np.float32)
    st_b = np.empty((16, 2, 1024), np.float32)
    na_k = np.empty((16, 2, 256, 16, 64), np.float32)
    na_v = np.empty((16, 2, 256, 16, 64), np.float32)
    gq_k = np.empty((16, 2, 256, 4, 64), np.float32)
    gq_v = np.empty((16, 2, 256, 4, 64), np.float32)
    ml_c = np.empty((16, 2, 256, 512), np.float32)
    ml_r = np.empty((16, 2, 256, 64), np.float32)
    for core in range(ncores):
        r = res.results[core]
        for k in range(nu):
            u = core * nu + k
            yp = r["yc"][k].reshape(D, 512).T
            y_prompt[2 * u] = yp[:256]
            y_prompt[2 * u + 1] = yp[256:]
            y_sample[u] = r["yl"][k].reshape(D, 1024).T
            for l in layers:
                j = l // 2
                if l % 2 == 0:
                    so = r[f"ost{l}"][k].reshape(128, 8, 2, 2).transpose(3, 2, 1, 0).reshape(2, 2, 1024)
                    kk = r[f"onak{l}"][k].reshape(1024, 512).T.reshape(2, 256, 16, 64)
                    vv = r[f"onav{l}"][k].reshape(1024, 512).T.reshape(2, 256, 16, 64)
                    for s in range(2):
                        st_f[2 * u + s, j] = so[s, 0]
                        st_b[2 * u + s, j] = so[s, 1]
                        na_k[2 * u + s, j] = kk[s]
                        na_v[2 * u + s, j] = vv[s]
                else:
                    kk = r[f"ogk{l}"][k].reshape(256, 512).T.reshape(2, 256, 4, 64)
                    vv = r[f"ogv{l}"][k].reshape(256, 512).T.reshape(2, 256, 4, 64)
                    cc = r[f"omc{l}"][k].reshape(512, 512).T.reshape(2, 256, 512)
                    rr = r[f"omr{l}"][k].T.reshape(2, 256, 64)
                    for s in range(2):
                        gq_k[2 * u + s, j] = kk[s]
                        gq_v[2 * u + s, j] = vv[s]
                        ml_c[2 * u + s, j] = cc[s]
                        ml_r[2 * u + s, j] = rr[s]
    return (y_prompt, y_sample, st_f, st_b, na_k, na_v, gq_k, gq_v, ml_c, ml_r)
```

```python
import numpy as np
from contextlib import ExitStack
import concourse.bass as bass
import concourse.mybir as mybir
from concourse.bass_utils import run_bass_kernel_spmd

F32 = mybir.dt.float32
BF16 = mybir.dt.bfloat16
AF = mybir.ActivationFunctionType
ALU = mybir.AluOpType

D = 2048
NCH = 16
TOK = 1536
NT = 3
DFF = 5632
NJ = 44
EPS = 1e-6
SEMCH = 30000


class T:
    __slots__ = ("w", "r", "x")

    def __init__(self, x=False):
        self.w = None
        self.r = []
        self.x = x


class Slot:
    def __init__(self, sem):
        self.sem = sem
        self.count = 0


class Builder:
    def __init__(self):
        self.nc = bass.Bass("TRN2", target_bir_lowering=False)
        self.es = ExitStack()
        nc = self.nc
        self.eng = {"pe": nc.tensor, "act": nc.scalar, "dve": nc.vector, "pool": nc.gpsimd, "sp": nc.sync}
        self.cnt = {e: 0 for e in self.eng}
        self.sems = {e: [] for e in self.eng}
        self.seen = {e: {} for e in self.eng}
        self.nsem = 0
        self.slots = []

    def new_sem(self, name):
        self.nsem += 1
        return self.es.enter_context(self.nc.semaphore(f"{name}_{self.nsem}"))

    def slot(self, name="s"):
        sl = Slot(self.new_sem(name))
        self.slots.append(sl)
        return sl

    def barrier(self):
        for e in self.eng:
            for e2 in self.eng:
                if e2 != e and self.cnt[e2] > 0:
                    self._wait(e, ("e", e2, self.cnt[e2]))
            for sl in self.slots:
                if sl.count:
                    self._wait(e, ("d", sl, sl.count))

    def sbuf(self, name, shape, dt):
        return self.es.enter_context(self.nc.sbuf_tensor(name, shape, dt))

    def psum(self, name, shape, dt):
        return self.es.enter_context(self.nc.psum_tensor(name, shape, dt))

    def _engsem(self, e, idx):
        ch = (idx - 1) // SEMCH
        while len(self.sems[e]) <= ch:
            self.sems[e].append(self.new_sem(f"e_{e}"))
        return self.sems[e][ch], idx - ch * SEMCH

    def _wait(self, e, ev):
        if ev[0] == "e":
            _, e2, idx = ev
            if e2 == e and e == "pe":
                return
            key = ("e", e2)
            if self.seen[e].get(key, 0) >= idx:
                return
            self.seen[e][key] = idx
            sem, val = self._engsem(e2, idx)
        else:
            _, sl, val = ev
            key = ("d", id(sl))
            if self.seen[e].get(key, 0) >= val:
                return
            self.seen[e][key] = val
            sem = sl.sem
        self.eng[e].wait_ge(sem, val)

    def _deps(self, e, reads, writes):
        for t in reads:
            if t.w is not None:
                self._wait(e, t.w)
            if t.x:
                for ev in t.r:
                    if not (ev[0] == "e" and ev[1] == e):
                        self._wait(e, ev)
        for t in writes:
            if t.w is not None:
                self._wait(e, t.w)
            for ev in t.r:
                self._wait(e, ev)

    def _commit(self, ev, reads, writes):
        for t in reads:
            t.r.append(ev)
            if len(t.r) > 24:
                t.r = t.r[-24:] if False else t.r
        for t in writes:
            t.w = ev
            t.r = []

    def op(self, e, fn, reads=(), writes=()):
        self._deps(e, reads, writes)
        ins = fn(self.eng[e])
        self.cnt[e] += 1
        idx = self.cnt[e]
        sem, _ = self._engsem(e, idx)
        ins.then_inc(sem, 1)
        self._commit(("e", e, idx), reads, writes)

    def group(self, e, fns, reads=(), writes=()):
        self._deps(e, reads, writes)
        ins = None
        for fn in fns:
            ins = fn(self.eng[e])
        self.cnt[e] += 1
        idx = self.cnt[e]
        sem, _ = self._engsem(e, idx)
        ins.then_inc(sem, 1)
        self._commit(("e", e, idx), reads, writes)

    def dma(self, e, pairs, reads, writes, slot, **kw):
        self._deps(e, reads, writes)
        for (o, i) in pairs:
            self.eng[e].dma_start(out=o, in_=i, **kw).then_inc(slot.sem, 16)
            slot.count += 16
        self._commit(("d", slot, slot.count), reads, writes)

    def finish(self, slots):
        for sl in slots:
            if sl.count:
                self.eng["sp"].wait_ge(sl.sem, sl.count)


def rev(ap2d):
    (ps, pn), (fs, fn) = ap2d.ap
    return bass.AP(ap2d.tensor, ap2d.offset + (fn - 1) * fs, [[ps, pn], [-fs, fn]])


NA_SCALE = 0.125
MLA_SCALE = 192 ** -0.5
NEG = -30000.0
DBG = {}


def build(nu, layers, mixers=True):
    b = Builder()
    nc = b.nc
    ncols = 1 + nu
    L = len(layers)
    TM = 1024

    def din(name, shape, dt=F32):
        return nc.dram_tensor(name, list(shape), dt, kind="ExternalInput").ap()

    def dout(name, shape, dt=F32):
        return nc.dram_tensor(name, list(shape), dt, kind="ExternalOutput").ap()

    xc = din("xc", [nu, NCH, 128, 512])
    xl = din("xl", [nu, NCH, 128, 1024])
    yc = dout("yc", [nu, NCH, 128, 512])
    yl = dout("yl", [nu, NCH, 128, 1024])
    condT = din("condT", [128, NCH * ncols])
    cst = din("cst", [128, 18])
    cosd = din("cosd", [128, 1024])
    sind = din("sind", [128, 1024])
    matd = din("matd", [128, 3 * 128])
    W = {}
    O = {}
    for l in layers:
        W[f"wmod{l}"] = din(f"wmod{l}", [36, 128, 16 * 512])
        W[f"bmod{l}"] = din(f"bmod{l}", [128, 144])
        W[f"gain{l}"] = din(f"gain{l}", [128, 48])
        for f in range(2):
            W[f"win{l}_{f}"] = din(f"win{l}_{f}", [NJ, 128, 16 * 256])
            W[f"wout{l}_{f}"] = din(f"wout{l}_{f}", [11, 128, 4 * 2048])
        if not mixers:
            continue
        if l % 2 == 0:
            W[f"wab{l}"] = din(f"wab{l}", [40, 128, 2048])
            W[f"wabo{l}"] = din(f"wabo{l}", [16, 128, 2048])
            W[f"bd{l}"] = din(f"bd{l}", [8, 128, 512])
            W[f"lc{l}"] = din(f"lc{l}", [128, 8 * 11])
            W[f"bz{l}"] = din(f"bz{l}", [16, 15, 64, 64])
            W[f"st{l}"] = din(f"st{l}", [nu, 128, 16])
            W[f"nakc{l}"] = din(f"nakc{l}", [nu, 8, 128, 512])
            W[f"navc{l}"] = din(f"navc{l}", [nu, 4, 128, 1024])
            O[f"ost{l}"] = dout(f"ost{l}", [nu, 128, 32])
            O[f"onak{l}"] = dout(f"onak{l}", [nu, 8, 128, 512])
            O[f"onav{l}"] = dout(f"onav{l}", [nu, 8, 128, 512])
        else:
            W[f"wcd{l}"] = din(f"wcd{l}", [23, 128, 2048])
            W[f"wvd{l}"] = din(f"wvd{l}", [128, 16 * 512])
            W[f"wcdo{l}"] = din(f"wcdo{l}", [16, 128, 2048])
            W[f"cdc{l}"] = din(f"cdc{l}", [128, 10])
            W[f"wuqn{l}"] = din(f"wuqn{l}", [128, 4 * 1024])
            W[f"wuqr{l}"] = din(f"wuqr{l}", [128, 4 * 512])
            W[f"wuk{l}"] = din(f"wuk{l}", [128, 4 * 1024])
            W[f"wuv{l}"] = din(f"wuv{l}", [128, 4 * 1024])
            W[f"gqk{l}"] = din(f"gqk{l}", [nu, 4, 128, 512])
            W[f"gqv{l}"] = din(f"gqv{l}", [nu, 4, 128, 512])
            W[f"mlc{l}"] = din(f"mlc{l}", [nu, 4, 128, 512])
            W[f"mlr{l}"] = din(f"mlr{l}", [nu, 128, 512])
            O[f"ogk{l}"] = dout(f"ogk{l}", [nu, 4, 64, 512])
            O[f"ogv{l}"] = dout(f"ogv{l}", [nu, 2, 128, 512])
            O[f"omc{l}"] = dout(f"omc{l}", [nu, 4, 128, 512])
            O[f"omr{l}"] = dout(f"omr{l}", [nu, 64, 512])

    X = b.sbuf("X", [128, NCH, TM], F32)
    H = b.sbuf("H", [128, NCH, TM], BF16)
    RS = b.sbuf("RS", [128, TM], F32)
    ones = b.sbuf("ones", [128, 128], BF16)
    MATS = b.sbuf("MATS", [128, 384], BF16)
    perm = MATS[:, 0:128]
    bones = MATS[:, 128:256]
    HM = b.sbuf("HM", [128, 2], F32)
    COS = b.sbuf("COS", [128, 1024], F32)
    SIN = b.sbuf("SIN", [128, 1024], F32)
    DER = b.sbuf("DER", [128, L * 3 * ncols * 3 * NCH], F32)
    GAIN = b.sbuf("GAIN", [128, L, 48], F32)
    FG = b.sbuf("FG", [128, NCH], F32)
    CT = b.sbuf("CT", [128, NCH * ncols], F32)
    CB = b.sbuf("CB", [128, NCH * ncols], BF16)
    ABYTES = 92160
    AR = b.sbuf("ARENA", [128, ABYTES // 2], BF16)
    PS = [b.psum(f"ps{i}", [128, 512], F32) for i in range(8)]

    class Carver:
        def __init__(self, base=0):
            self.off = base

        def take(self, nbytes, dt=BF16, shape=None):
            assert self.off % 4 == 0
            a = AR[:, self.off // 2:(self.off + nbytes) // 2]
            self.off += nbytes
            assert self.off <= ABYTES, self.off
            if dt == F32:
                a = a.bitcast(F32)
            if shape is not None:
                names = " ".join(f"d{i}" for i in range(len(shape)))
                a = a.rearrange(f"p ({names}) -> p {names}", **{f"d{i}": s for i, s in enumerate(shape)})
            return a

    tX = [[T() for _ in range(2)] for _ in range(NCH)]
    tH = [[T() for _ in range(2)] for _ in range(NCH)]
    tRS = [T() for _ in range(2)]
    tPS = [T(True) for _ in range(8)]
    tC = T()
    tDER = T()
    allX = [tX[m][t] for m in range(NCH) for t in range(2)]
    allH = [tH[m][t] for m in range(NCH) for t in range(2)]

    def der(li, n, col, k, m):
        off = ((((li * 3 + n) * ncols + col) * 3 + k) * NCH) + m
        return DER[:, off:off + 1]

    cv0 = Carver()
    SG = [cv0.take(1024) for _ in range(2)]
    TMP = [cv0.take(2048, F32) for _ in range(2)]
    tSG = [T(), T()]
    tTMP = [T(), T()]
    BASE = cv0.off
    sMisc = b.slot("misc")
    sX = b.slot("x")
    sY = b.slot("y")
    outslots = [sY]

    b.op("dve", lambda e: e.memset(ones[:], 1.0), writes=[tC])
    pairs = [(CT[:], condT[:, :]), (FG[:], cst[:, 0:16]), (HM[:], cst[:, 16:18]), (COS[:], cosd[:, :]), (SIN[:], sind[:, :])]
    for li, l in enumerate(layers):
        pairs.append((GAIN[:, li, :], W[f"gain{l}"][:, :]))
    b.dma("sp", pairs, [], [tC], sMisc)
    sMat = b.slot("mat")
    tMat = T()
    b.dma("pool", [(MATS[:], matd[:, :])], [], [tMat], sMat)
    b.op("act", lambda e: e.activation(out=CB[:], in_=CT[:], func=AF.Silu), reads=[tC], writes=[tC])

    cv = Carver(BASE)
    WM = [cv.take(16384) for _ in range(2)]
    MODT = cv.take(144 * ncols * 4, F32)
    BM = cv.take(144 * 4, F32)
    tWM = [T(), T()]
    tMODT = T()
    tBM = T()
    sWM = [b.slot("wm"), b.slot("wm")]
    sBM = b.slot("bm")
    k = 0
    for li, l in enumerate(layers):
        pm = PS[7]
        b.dma("sp", [(BM, W[f"bmod{l}"][:, :])], [], [tBM], sBM)
        for s in range(36):
            sl = k % 2
            k += 1
            b.dma("pool", [(WM[sl], W[f"wmod{l}"][s])], [], [tWM[sl]], sWM[sl], max_dma_last_dim=8192)
            fns = []
            for q in range(4):
                oc = 4 * s + q
                for kc in range(16):
                    fns.append(lambda e, q=q, kc=kc, oc=oc, sl=sl: e.matmul(
                        pm[:, oc * ncols:(oc + 1) * ncols],
                        WM[sl][:, kc * 512 + q * 128: kc * 512 + (q + 1) * 128],
                        CB[:, kc * ncols:(kc + 1) * ncols], start=(kc == 0), stop=(kc == 15)))
            b.group("pe", fns, reads=[tWM[sl], tC], writes=[tPS[7]])
        for col in range(ncols):
            b.op("dve", lambda e, col=col: e.tensor_tensor(
                out=MODT[:, col:144 * ncols:ncols], in0=pm[:, col:144 * ncols:ncols], in1=BM, op=ALU.add),
                reads=[tPS[7], tBM], writes=[tMODT])
        for n in range(3):
            for col in range(ncols):
                sh0 = ((3 * n + 0) * 16) * ncols + col
                sc0 = ((3 * n + 1) * 16) * ncols + col
                g0 = ((3 * n + 2) * 16) * ncols + col
                o = ((li * 3 + n) * ncols + col) * 3 * NCH
                b.op("dve", lambda e, sc0=sc0, o=o, n=n, li=li: e.scalar_tensor_tensor(
                    out=DER[:, o:o + NCH], in0=MODT[:, sc0:sc0 + 15 * ncols + 1:ncols], scalar=1.0,
                    in1=GAIN[:, li, n * 16:(n + 1) * 16], op0=ALU.add, op1=ALU.mult),
                    reads=[tMODT, tC], writes=[tDER])
                b.op("dve", lambda e, g0=g0, o=o, n=n: e.tensor_scalar(
                    out=DER[:, o + NCH:o + 2 * NCH], in0=MODT[:, g0:g0 + 15 * ncols + 1:ncols],
                    scalar1=(1.0 if n == 1 else 0.5), scalar2=None, op0=ALU.mult),
                    reads=[tMODT], writes=[tDER])
                b.op("dve", lambda e, sh0=sh0, o=o: e.tensor_copy(
                    out=DER[:, o + 2 * NCH:o + 3 * NCH], in_=MODT[:, sh0:sh0 + 15 * ncols + 1:ncols]),
                    reads=[tMODT], writes=[tDER])

    def TS(t):
        return slice(t * 512, (t + 1) * 512)

    def norm_rstd(nt):
        for t in range(nt):
            pb = 6
            for m in range(NCH):
                b.op("act", lambda e, m=m: e.activation(out=SG[m % 2], in_=X[:, m, TS(t)], func=AF.Square),
                     reads=[tX[m][t]], writes=[tSG[m % 2]])
                b.group("pe", [lambda e, m=m: e.matmul(PS[pb][:], ones[:], SG[m % 2], start=(m == 0), stop=(m == 15))],
                        reads=[tSG[m % 2], tC], writes=[tPS[pb]])
            b.op("act", lambda e: e.activation(out=RS[:, TS(t)], in_=PS[pb][:], func=AF.Sqrt, bias=EPS, scale=1.0 / D),
                 reads=[tPS[pb]], writes=[tRS[t]])
            b.op("dve", lambda e: e.reciprocal(out=RS[:, TS(t)], in_=RS[:, TS(t)]), reads=[tRS[t]], writes=[tRS[t]])

    def adaln(li, n, col, nt):
        norm_rstd(nt)
        for t in range(nt):
            for m in range(NCH):
                b.op("dve", lambda e, m=m: e.tensor_tensor(out=TMP[m % 2], in0=X[:, m, TS(t)], in1=RS[:, TS(t)], op=ALU.mult),
                     reads=[tX[m][t], tRS[t]], writes=[tTMP[m % 2]])
                b.op("act", lambda e, m=m: e.activation(
                    out=H[:, m, TS(t)], in_=TMP[m % 2], func=AF.Identity, bias=der(li, n, col, 2, m), scale=der(li, n, col, 0, m)),
                    reads=[tTMP[m % 2], tDER], writes=[tH[m][t]])

    pcount = [0]

    def ffn(li, l, f, col, nt):
        n = 0 if f == 0 else 2
        Tn = nt * 512
        b.barrier()
        adaln(li, n, col, nt)
        cv = Carver(BASE)
        WIN = [cv.take(8192) for _ in range(2)]
        WOUT = [cv.take(16384) for _ in range(2)]
        HID = cv.take(8192)
        tWIN = [T(), T()]
        tWOUT = [T(), T()]
        tHID = [T() for _ in range(2)]
        win = W[f"win{l}_{f}"]
        wout = W[f"wout{l}_{f}"]
        for g in range(11):
            gs = g % 2
            b.dma("pool", [(WOUT[gs], wout[g])], [], [tWOUT[gs]], sFW[2 + gs], max_dma_last_dim=8192)
            for c in range(4):
                j = 4 * g + c
                ws = j % 2
                b.dma("pool", [(WIN[ws], win[j])], [], [tWIN[ws]], sFW[ws], max_dma_last_dim=8192)
                for t in range(nt):
                    kk = pcount[0]
                    pcount[0] += 1
                    pg = kk % 2
                    pu = 2 + kk % 2
                    rd = [tWIN[ws]] + [tH[m][t] for m in range(NCH)]
                    b.group("pe", [lambda e, kc=kc: e.matmul(PS[pg][:], WIN[ws][:, kc * 256: kc * 256 + 128], H[:, kc, TS(t)],
                                                              start=(kc == 0), stop=(kc == 15)) for kc in range(16)],
                            reads=rd, writes=[tPS[pg]])
                    b.group("pe", [lambda e, kc=kc: e.matmul(PS[pu][:], WIN[ws][:, kc * 256 + 128: kc * 256 + 256], H[:, kc, TS(t)],
                                                              start=(kc == 0), stop=(kc == 15)) for kc in range(16)],
                            reads=rd, writes=[tPS[pu]])
                    b.op("act", lambda e: e.activation(out=SG[kk % 2], in_=PS[pg][:], func=AF.Silu),
                         reads=[tPS[pg]], writes=[tSG[kk % 2]])
                    b.op("dve", lambda e: e.tensor_tensor(
                        out=HID[:, c * Tn + t * 512: c * Tn + (t + 1) * 512], in0=SG[kk % 2], in1=PS[pu][:], op=ALU.mult),
                        reads=[tSG[kk % 2], tPS[pu]], writes=[tHID[t]])
            for m in range(NCH):
                for t in range(nt):
                    kk = pcount[0]
                    pcount[0] += 1
                    po = 4 + kk % 2
                    b.group("pe", [lambda e, c=c: e.matmul(PS[po][:], WOUT[gs][:, c * 2048 + m * 128: c * 2048 + (m + 1) * 128],
                                                            HID[:, c * Tn + t * 512: c * Tn + (t + 1) * 512],
                                                            start=(c == 0), stop=(c == 3)) for c in range(4)],
                            reads=[tWOUT[gs], tHID[t]], writes=[tPS[po]])
                    b.op("dve", lambda e: e.scalar_tensor_tensor(
                        out=X[:, m, TS(t)], in0=PS[po][:], scalar=der(li, n, col, 1, m), in1=X[:, m, TS(t)],
                        op0=ALU.mult, op1=ALU.add), reads=[tPS[po], tDER, tX[m][t]], writes=[tX[m][t]])

    sFW = [b.slot("fw") for _ in range(4)]
    sWP = [b.slot("wp") for _ in range(3)]
    sIO = [b.slot("io") for _ in range(6)]
    wpk = [0]

    def proj(WP, tWP, wchunk, nt, evac, tiles=None):
        sl = wpk[0] % len(WP)
        wpk[0] += 1
        b.dma("pool", [(WP[sl], wchunk)], [], [tWP[sl]], sWP[sl], max_dma_last_dim=8192)
        for t in (range(nt) if tiles is None else tiles):
            kk = pcount[0]
            pcount[0] += 1
            pb = kk % 2
            b.group("pe", [lambda e, kc=kc: e.matmul(PS[pb][:], WP[sl][:, kc * 128:(kc + 1) * 128], H[:, kc, TS(t)],
                                                      start=(kc == 0), stop=(kc == 15)) for kc in range(16)],
                    reads=[tWP[sl]] + [tH[m][t] for m in range(NCH)], writes=[tPS[pb]])
            evac(pb, t)

    def outproj(li, col, nt, Y, tY, WP, tWP, wo, kcs=range(16)):
        for m in range(NCH):
            sl = wpk[0] % len(WP)
            wpk[0] += 1
            b.dma("pool", [(WP[sl], wo[m])], [], [tWP[sl]], sWP[sl], max_dma_last_dim=8192)
            for t in range(nt):
                kk = pcount[0]
                pcount[0] += 1
                po = 4 + kk % 2
                k0 = kcs[0]
                b.group("pe", [lambda e, kc=kc: e.matmul(PS[po][:], WP[sl][:, kc * 128:(kc + 1) * 128], Y[:, kc - k0, TS(t)],
                                                          start=(kc == kcs[0]), stop=(kc == kcs[-1])) for kc in kcs],
                        reads=[tWP[sl], tY], writes=[tPS[po]])
                b.op("dve", lambda e: e.scalar_tensor_tensor(
                    out=X[:, m, TS(t)], in0=PS[po][:], scalar=der(li, 1, col, 1, m), in1=X[:, m, TS(t)],
                    op0=ALU.mult, op1=ALU.add), reads=[tPS[po], tDER, tX[m][t]], writes=[tX[m][t]])

    def attention(qparts, kparts, vfn, nkb, q0s, scale, rows, ydst, rtiles, wtile, PT, tPT, RD, tRD, tab=None):
        for (q0, n) in q0s:
            for kb in range(nkb):
                kk = pcount[0]
                pcount[0] += 1
                sb = kk % 2
                b.group("pe", [lambda e, i=i: e.matmul(PS[sb][:, 0:n], kparts[i](kb), qparts[i](q0, n),
                                                        start=(i == 0), stop=(i == len(qparts) - 1)) for i in range(len(qparts))],
                        reads=rtiles, writes=[tPS[sb]])
                pt = PT[kk % 2]
                if tab is not None and tab(kb, q0, n) is not None:
                    tap, ttile = tab(kb, q0, n)
                    b.op("dve", lambda e: e.scalar_tensor_tensor(out=TMP[kk % 2][:, 0:n], in0=PS[sb][:, 0:n], scalar=scale,
                                                                 in1=tap, op0=ALU.mult, op1=ALU.add),
                         reads=[tPS[sb], ttile], writes=[tTMP[kk % 2]])
                    b.op("act", lambda e: e.activation(out=pt[:, 0:n], in_=TMP[kk % 2][:, 0:n], func=AF.Exp),
                         reads=[tTMP[kk % 2]], writes=[tPT[kk % 2]])
                else:
                    b.op("act", lambda e: e.activation(out=pt[:, 0:n], in_=PS[sb][:, 0:n], func=AF.Exp, scale=scale),
                         reads=[tPS[sb]], writes=[tPT[kk % 2]])
                b.group("pe", [lambda e: e.matmul(PS[2][:, 0:n], vfn(kb), pt[:, 0:n], start=(kb == 0), stop=(kb == nkb - 1)),
                               lambda e: e.matmul(PS[3][:, 0:n], ones[:], pt[:, 0:n], start=(kb == 0), stop=(kb == nkb - 1))],
                        reads=[tPT[kk % 2], tC] + rtiles, writes=[tPS[2], tPS[3]])
            b.op("dve", lambda e: e.reciprocal(out=RD[rows, 0:n], in_=PS[3][rows, 0:n]), reads=[tPS[3]], writes=[tRD])
            b.op("dve", lambda e: e.tensor_tensor(out=ydst(q0, n), in0=PS[2][rows, 0:n], in1=RD[rows, 0:n], op=ALU.mult),
                 reads=[tPS[2], tRD], writes=[wtile])

    def mixer_ab(li, l, kind, ui, col, nt):
        Tn = nt * 512
        lat = kind == "lat"
        seqs = [(0, 1024)] if lat else [(0, 256), (256, 512)]
        b.barrier()
        adaln(li, 1, col, nt)
        cv = Carver(BASE)
        Y = cv.take(16 * Tn * 2, BF16, [16, Tn])
        tY = T()
        WP = [cv.take(4096) for _ in range(3)]
        tWP = [T() for _ in range(3)]
        LC = cv.take(8 * 11 * 4, F32, [8, 11])
        NSP = cv.take(8 * 4 * 4, F32, [8, 4])
        ST = cv.take(64, F32)
        SO = cv.take(128, F32)
        tLC = T()
        tSO = T()
        wab = W[f"wab{l}"]
        prs = [(LC, W[f"lc{l}"][:, :].rearrange("p (c k) -> p c k", k=11))]
        if lat:
            prs.append((ST, W[f"st{l}"][ui]))
        b.dma("sp", prs, [], [tLC], sIO[0])
        if DBG.get('skip_nsp'):
            return
        b.op("act", lambda e: e.activation(out=NSP[:, :, 0:2], in_=LC[:, :, 9:11], func=AF.Exp, scale=-1.0), reads=[tLC], writes=[tLC])
        b.op("act", lambda e: e.activation(out=NSP[:, :, 0:2], in_=NSP[:, :, 0:2], func=AF.Ln, bias=1.0), reads=[tLC], writes=[tLC])
        b.op("dve", lambda e: e.tensor_scalar(out=NSP[:, :, 2:4], in0=NSP[:, :, 0:2], scalar1=-16.0, scalar2=None, op0=ALU.mult), reads=[tLC], writes=[tLC])
        b.op("dve", lambda e: e.tensor_scalar(out=NSP[:, :, 0:2], in0=NSP[:, :, 0:2], scalar1=-8.0, scalar2=None, op0=ALU.mult), reads=[tLC], writes=[tLC])
        mark = cv.off
        XA = cv.take(Tn * 4, F32)
        GA = cv.take(Tn * 4, F32)
        XC = cv.take(Tn * 4, F32)
        R = cv.take(Tn * 4, F32)
        GI = cv.take(Tn * 4, F32)
        HF = cv.take(Tn * 4, F32)
        HB = cv.take(Tn * 4, F32)
        XCB = cv.take(Tn * 2)
        BD = cv.take(1024, BF16, [4, 128])
        tXA, tGA, tXC, tR, tGI, tHF, tHB, tXCB, tBD = [T() for _ in range(9)]
        Sb, tS = XA, tXA
        for c in range(8 if not DBG.get('skip_lru') else 0):
            b.dma("pool", [(BD, W[f"bd{l}"][c].rearrange("p (k n) -> p k n", n=128))], [], [tBD], sIO[1], max_dma_last_dim=8192)
            proj(WP, tWP, wab[c], nt, lambda pb, t: b.op("act", lambda e: e.activation(out=XA[:, TS(t)], in_=PS[pb][:], func=AF.Identity),
                                                       reads=[tPS[pb]], writes=[tXA]))
            proj(WP, tWP, wab[8 + c], nt, lambda pb, t: b.op("act", lambda e: e.activation(out=GA[:, TS(t)], in_=PS[pb][:], func=AF.Identity),
                                                           reads=[tPS[pb]], writes=[tGA]))
            for (s0, s1) in seqs:
                b.op("act", lambda e: e.activation(out=XC[:, s0:s1], in_=XA[:, s0:s1], func=AF.Identity,
                                                   bias=LC[:, c, 4:5], scale=LC[:, c, 2:3]), reads=[tXA, tLC], writes=[tXC])
                for (jj, sh) in ((0, -2), (1, -1), (3, 1)):
                    if sh < 0:
                        o_ = XC[:, s0 - sh:s1]
                        i_ = XA[:, s0:s1 + sh]
                    else:
                        o_ = XC[:, s0:s1 - sh]
                        i_ = XA[:, s0 + sh:s1]
                    b.op("dve", lambda e, o_=o_, i_=i_, jj=jj: e.scalar_tensor_tensor(
                        out=o_, in0=i_, scalar=LC[:, c, jj:jj + 1], in1=o_, op0=ALU.mult, op1=ALU.add),
                        reads=[tXA, tXC, tLC], writes=[tXC])
            b.op("dve", lambda e: e.tensor_copy(out=XCB, in_=XC), reads=[tXC], writes=[tXCB])
            for d in range(2):
                Hd, tHd = (HF, tHF) if d == 0 else (HB, tHB)
                for t in range(nt):
                    for (a, dst, tdst, bcol) in ((0, R, tR, 5 + d), (1, GI, tGI, 7 + d)):
                        kk = pcount[0]
                        pcount[0] += 1
                        pb = kk % 2
                        b.group("pe", [lambda e: e.matmul(PS[pb][:], BD[:, d * 2 + a, :], XCB[:, TS(t)], start=True, stop=True)],
                                reads=[tBD, tXCB], writes=[tPS[pb]])
                        b.op("act", lambda e: e.activation(out=dst[:, TS(t)], in_=PS[pb][:], func=AF.Sigmoid, bias=LC[:, c, bcol:bcol + 1]),
                             reads=[tPS[pb], tLC], writes=[tdst])
                b.op("act", lambda e: e.activation(out=Sb, in_=R, func=AF.Exp, scale=NSP[:, c, 2 + d:3 + d]), reads=[tR, tLC, tXC], writes=[tS])
                b.op("act", lambda e: e.activation(out=Sb, in_=Sb, func=AF.Sqrt, bias=1.0, scale=-1.0), reads=[tS], writes=[tS])
                b.op("act", lambda e: e.activation(out=R, in_=R, func=AF.Exp, scale=NSP[:, c, d:d + 1]), reads=[tR, tLC], writes=[tR])
                b.op("dve", lambda e: e.tensor_tensor(out=GI, in0=GI, in1=Sb, op=ALU.mult), reads=[tGI, tS], writes=[tGI])
                b.op("dve", lambda e: e.tensor_tensor(out=GI, in0=GI, in1=XC, op=ALU.mult), reads=[tGI, tXC], writes=[tGI])
                for si, (s0, s1) in enumerate(seqs):
                    init = ST[:, c * 2 + d:c * 2 + d + 1] if lat else 0.0
                    if d == 0:
                        b.op("dve", lambda e: e.tensor_tensor_scan(out=Hd[:, s0:s1], data0=R[:, s0:s1], data1=GI[:, s0:s1],
                                                                   initial=init, op0=ALU.mult, op1=ALU.add),
                             reads=[tR, tGI, tLC], writes=[tHd])
                    else:
                        b.op("dve", lambda e: e.tensor_tensor_scan(out=rev(Hd[:, s0:s1]), data0=rev(R[:, s0:s1]), data1=rev(GI[:, s0:s1]),
                                                                   initial=init, op0=ALU.mult, op1=ALU.add),
                             reads=[tR, tGI, tLC], writes=[tHd])
                    if not lat:
                        src = Hd[:, s1 - 1:s1] if d == 0 else Hd[:, s0:s0 + 1]
                        k_ = (c * 2 + d) * 2 + si
                        b.op("act", lambda e: e.activation(out=SO[:, k_:k_ + 1], in_=src, func=AF.Identity), reads=[tHd], writes=[tSO])
            b.op("act", lambda e: e.activation(out=R, in_=GA, func=AF.Square), reads=[tGA, tGI], writes=[tR])
            b.op("dve", lambda e: e.tensor_scalar(out=R, in0=R, scalar1=0.044715, scalar2=1.0, op0=ALU.mult, op1=ALU.add), reads=[tR], writes=[tR])
            b.op("dve", lambda e: e.tensor_tensor(out=R, in0=R, in1=GA, op=ALU.mult), reads=[tR, tGA], writes=[tR])
            b.op("act", lambda e: e.activation(out=R, in_=R, func=AF.Sigmoid, scale=1.5957691216057308), reads=[tR], writes=[tR])
            b.op("dve", lambda e: e.tensor_tensor(out=R, in0=R, in1=GA, op=ALU.mult), reads=[tR, tGA], writes=[tR])
            b.op("dve", lambda e: e.tensor_tensor(out=HF, in0=HF, in1=HB, op=ALU.add), reads=[tHF, tHB], writes=[tHF])
            b.op("dve", lambda e: e.tensor_tensor(out=Y[:, c, :], in0=HF, in1=R, op=ALU.mult), reads=[tHF, tR], writes=[tY])
        if not lat:
            b.dma("sp", [(O[f"ost{l}"][ui], SO)], [tSO], [], sOut[0])
        b.barrier()
        cv.off = mark
        QT = cv.take(Tn * 2)
        KT = cv.take((Tn + 512) * 2)
        QM = cv.take(Tn * 2)
        KF = cv.take(Tn * 4, F32) if not lat else None
        VF = cv.take(Tn * 4, F32) if not lat else None
        nvb = (Tn + (512 if lat else 0)) // 128
        VT = cv.take(nvb * 128 * 2, BF16, [nvb, 128])
        PT = [cv.take(1024) for _ in range(2)]
        RD = cv.take(2048, F32)
        tQT, tKT, tQM, tKF, tVF, tVT, tRD = [T() for _ in range(7)]
        tPT = [T(), T()]
        if lat:
            TAB = [cv.take(2048, BF16) for _ in range(8)]
            tTAB = [T() for _ in range(8)]
            for kb in range(8):
                b.op("pool", lambda e, kb=kb: e.memset(TAB[kb], NEG), writes=[tTAB[kb]])
        for c in range(8 if not DBG.get('skip_qkv') else 0):
            proj(WP, tWP, wab[16 + c], nt, lambda pb, t: b.op("act", lambda e: e.activation(out=QT[:, TS(t)], in_=PS[pb][:], func=AF.Identity),
                                                            reads=[tPS[pb]], writes=[tQT]))

            def evk(pb, t):
                if not lat:
                    b.op("act", lambda e: e.activation(out=KF[:, TS(t)], in_=PS[pb][:], func=AF.Identity), reads=[tPS[pb]], writes=[tKF])
                    b.op("dve", lambda e: e.tensor_copy(out=KT[:, TS(t)], in_=KF[:, TS(t)]), reads=[tKF], writes=[tKT])
                else:
                    b.op("dve", lambda e: e.tensor_copy(out=KT[:, TS(t)], in_=PS[pb][:]), reads=[tPS[pb]], writes=[tKT])
            proj(WP, tWP, wab[24 + c], nt, evk)
            if not lat:
                proj(WP, tWP, wab[32 + c], nt, lambda pb, t: b.op("act", lambda e: e.activation(out=VF[:, TS(t)], in_=PS[pb][:], func=AF.Identity),
                                                                reads=[tPS[pb]], writes=[tVF]))
                b.dma("sp", [(O[f"onak{l}"][ui, c], KF), (O[f"onav{l}"][ui, c], VF)], [tKF, tVF], [], sOut[1])
            sl = wpk[0] % 3
            wpk[0] += 1
            b.dma("pool", [(WP[sl], wab[32 + c])], [], [tWP[sl]], sWP[sl], max_dma_last_dim=8192)
            for tb in range(Tn // 128 if not DBG.get('skip_vt') else 0):
                kk = pcount[0]
                pcount[0] += 1
                pb = kk % 2
                t = tb // 4
                b.group("pe", [lambda e, kc=kc: e.matmul(PS[pb][:, 0:128], H[:, kc, tb * 128:(tb + 1) * 128], WP[sl][:, kc * 128:(kc + 1) * 128],
                                                          start=(kc == 0), stop=(kc == 15)) for kc in range(16)],
                        reads=[tWP[sl]] + [tH[m][t] for m in range(NCH)], writes=[tPS[pb]])
                b.op("act", lambda e: e.activation(out=VT[:, tb, :], in_=PS[pb][:, 0:128], func=AF.Identity), reads=[tPS[pb]], writes=[tVT])
            if lat:
                b.dma("pool", [(KT[:, Tn:Tn + 512], W[f"nakc{l}"][ui, c]),
                               (VT[:, 8:12, :], W[f"navc{l}"][ui][:, :, c * 128:(c + 1) * 128].rearrange("k p n -> p k n"))],
                      [], [tKT, tVT], sIO[2], max_dma_last_dim=8192)
            for hh in range(2 if not DBG.get('skip_att') else 0):
                rows = slice(hh * 64, (hh + 1) * 64)
                b.op("dve", lambda e: e.tensor_scalar(out=QM, in0=QT, scalar1=HM[:, hh:hh + 1], scalar2=None, op0=ALU.mult),
                     reads=[tQT, tC], writes=[tQM])
                if lat:
                    hd = 2 * c + hh
                    bz = W[f"bz{l}"]
                    for i in range(16):
                        r_lo = 0 if i <= 7 else i - 3
                        r_hi = 15 if i >= 8 else i + 4
                        nr = r_hi - r_lo + 1
                        rr0 = r_lo - i + 7
                        kb = i // 2
                        src = bass.AP(bz.tensor, bz.offset + ((hd * 15 + rr0) * 64) * 64, [[64, 64], [4096, nr], [1, 64]])
                        dst = TAB[kb][(i % 2) * 64:(i % 2) * 64 + 64, r_lo * 64:(r_hi + 1) * 64].rearrange("p (r q) -> p r q", q=64)
                        b.dma("pool", [(dst, src)], [], [tTAB[kb]], sTAB[kb])
                    attention([lambda q0, n: QM[:, q0:q0 + n]], [lambda kb: KT[:, kb * 128:(kb + 1) * 128]],
                              lambda kb: VT[:, kb, :], 12, [(0, 512), (512, 512)], NA_SCALE, rows,
                              lambda q0, n: Y[rows, 8 + c, q0:q0 + n], [tQM, tKT, tVT], tY, PT, tPT, RD, tRD,
                              tab=lambda kb, q0, n: (TAB[kb][:, q0:q0 + n], tTAB[kb]) if kb < 8 else None)
                else:
                    for (s0, s1) in seqs:
                        kb0 = s0 // 128
                        attention([lambda q0, n: QM[:, q0:q0 + n]], [lambda kb: KT[:, (kb0 + kb) * 128:(kb0 + kb + 1) * 128]],
                                  lambda kb: VT[:, kb0 + kb, :], 2, [(s0, 256)], NA_SCALE, rows,
                                  lambda q0, n: Y[rows, 8 + c, q0:q0 + n], [tQM, tKT, tVT], tY, PT, tPT, RD, tRD)
        if not DBG.get('skip_outproj'):
            outproj(li, col, nt, Y, tY, WP, tWP, W[f"wabo{l}"])

    sOut = [b.slot("out") for _ in range(6)]
    outslots.extend(sOut)
    sTAB = [b.slot("tab") for _ in range(8)]

    def mixer_cd(li, l, kind, ui, col, nt):
        Tn = nt * 512
        lat = kind == "lat"
        Tk = Tn + (512 if lat else 0)
        seqs = [(0, 1024)] if lat else [(0, 256), (256, 512)]
        wcd = W[f"wcd{l}"]
        b.barrier()
        adaln(li, 1, col, nt)
        cv = Carver(BASE)
        Y8 = cv.take(8 * Tn * 2, BF16, [8, Tn])
        tY = T()
        WP = [cv.take(4096) for _ in range(3)]
        tWP = [T() for _ in range(3)]
        CDC = cv.take(40, F32)
        tCDC = T()
        b.dma("sp", [(CDC, W[f"cdc{l}"][:, :])], [], [tCDC], sIO[0])
        mark = cv.off

        def rope_from(src, t0, n, dst, rd, wr):
            b.op("act", lambda e: e.activation(out=SG[0][:, 0:n], in_=src, func=AF.Identity), reads=rd, writes=[tSG[0]])
            b.group("pe", [lambda e: e.matmul(PS[5][:, 0:n], perm, SG[0][:, 0:n], start=True, stop=True)], reads=[tSG[0], tMat], writes=[tPS[5]])
            b.op("dve", lambda e: e.tensor_tensor(out=TMP[0][:, 0:n], in0=src, in1=COS[:, t0:t0 + n], op=ALU.mult), reads=rd + [tC], writes=[tTMP[0]])
            b.op("dve", lambda e: e.tensor_tensor(out=TMP[1][:, 0:n], in0=PS[5][:, 0:n], in1=SIN[:, t0:t0 + n], op=ALU.mult), reads=[tPS[5], tC], writes=[tTMP[1]])
            b.op("dve", lambda e: e.tensor_tensor(out=dst, in0=TMP[0][:, 0:n], in1=TMP[1][:, 0:n], op=ALU.add), reads=[tTMP[0], tTMP[1]], writes=wr)

        K2 = [cv.take(Tk * 2) for _ in range(4)]
        tK2 = [T() for _ in range(4)]
        nkbT = Tk // 128
        VD = cv.take(nkbT * 512 * 2, BF16, [nkbT, 512])
        tVD = T()
        WV = cv.take(16384)
        tWV = T()
        QT = cv.take(Tn * 2)
        QM = cv.take(Tn * 2)
        STG = cv.take(2048, F32)
        RSQ = cv.take(2048, F32)
        PT = [cv.take(1024) for _ in range(2)]
        RD = cv.take(2048, F32)
        tQT, tQM, tSTG, tRSQ, tRD = [T() for _ in range(5)]
        tPT = [T(), T()]

        def headnorm(pb, t, gcol, dst, wr, rope, f32dma=None):
            b.op("act", lambda e: e.activation(out=STG, in_=PS[pb][:], func=AF.Identity), reads=[tPS[pb]], writes=[tSTG])
            b.op("act", lambda e: e.activation(out=SG[1], in_=PS[pb][:], func=AF.Square), reads=[tPS[pb]], writes=[tSG[1]])
            b.group("pe", [lambda e: e.matmul(PS[6][:], bones, SG[1], start=True, stop=True)], reads=[tSG[1], tMat], writes=[tPS[6]])
            b.op("act", lambda e: e.activation(out=RSQ, in_=PS[6][:], func=AF.Sqrt, bias=EPS, scale=1.0 / 64), reads=[tPS[6]], writes=[tRSQ])
            b.op("dve", lambda e: e.reciprocal(out=RSQ, in_=RSQ), reads=[tRSQ], writes=[tRSQ])
            if rope or f32dma is not None:
                b.op("dve", lambda e: e.scalar_tensor_tensor(out=STG, in0=STG, scalar=CDC[:, gcol:gcol + 1], in1=RSQ, op0=ALU.mult, op1=ALU.mult),
                     reads=[tSTG, tRSQ, tCDC], writes=[tSTG])
                if f32dma is not None:
                    f32dma()
                if rope:
                    rope_from(STG, t * 512, 512, dst, [tSTG], wr)
                else:
                    b.op("act", lambda e: e.activation(out=dst, in_=STG, func=AF.Identity), reads=[tSTG], writes=wr)
            else:
                b.op("dve", lambda e: e.scalar_tensor_tensor(out=dst, in0=STG, scalar=CDC[:, gcol:gcol + 1], in1=RSQ, op0=ALU.mult, op1=ALU.mult),
                     reads=[tSTG, tRSQ, tCDC], writes=wr)

        for kvh in range(4):
            def evk(pb, t, kvh=kvh):
                f = None
                if not lat:
                    f = lambda: b.dma("sp", [(O[f"ogk{l}"][ui, kvh], STG[0:64, :])], [tSTG], [], sOut[2])
                headnorm(pb, t, 1, K2[kvh][:, TS(t)], [tK2[kvh]], lat, f)
            proj(WP, tWP, wcd[8 + kvh], nt, evk)
        if lat:
            b.dma("pool", [(K2[kvh][:, Tn:Tn + 512], W[f"gqk{l}"][ui, kvh]) for kvh in range(4)]
                  + [(VD[:, 8:12, :], W[f"gqv{l}"][ui].rearrange("k p n -> p k n"))], [], tK2 + [tVD], sIO[2], max_dma_last_dim=8192)
        else:
            for i in range(2):
                def evv(pb, t, i=i):
                    b.op("act", lambda e: e.activation(out=STG, in_=PS[pb][:], func=AF.Identity), reads=[tPS[pb]], writes=[tSTG])
                    b.dma("sp", [(O[f"ogv{l}"][ui, i], STG)], [tSTG], [], sOut[2])
                proj(WP, tWP, wcd[12 + i], nt, evv)
        b.dma("pool", [(WV, W[f"wvd{l}"][:, :])], [], [tWV], sIO[3], max_dma_last_dim=8192)
        for tb in range(Tn // 128):
            kk = pcount[0]
            pcount[0] += 1
            pb = kk % 2
            b.group("pe", [lambda e, kc=kc: e.matmul(PS[pb][:], H[:, kc, tb * 128:(tb + 1) * 128], WV[:, kc * 512:(kc + 1) * 512],
                                                      start=(kc == 0), stop=(kc == 15)) for kc in range(16)],
                    reads=[tWV] + [tH[m][tb // 4] for m in range(NCH)], writes=[tPS[pb]])
            b.op("act", lambda e: e.activation(out=VD[:, tb, :], in_=PS[pb][:], func=AF.Identity), reads=[tPS[pb]], writes=[tVD])
        for c in range(8):
            kvh = c // 2
            proj(WP, tWP, wcd[c], nt, lambda pb, t: headnorm(pb, t, 0, QT[:, TS(t)], [tQT], lat))
            for hh in range(2):
                rows = slice(hh * 64, (hh + 1) * 64)
                b.op("dve", lambda e: e.tensor_scalar(out=QM, in0=QT, scalar1=HM[:, hh:hh + 1], scalar2=None, op0=ALU.mult),
                     reads=[tQT, tC], writes=[tQM])
                if lat:
                    attention([lambda q0, n: QM[:, q0:q0 + n]], [lambda kb: K2[kvh][:, kb * 128:(kb + 1) * 128]],
                              lambda kb: VD[:, kb, kvh * 128:(kvh + 1) * 128], 12, [(0, 512), (512, 512)], NA_SCALE, rows,
                              lambda q0, n: Y8[rows, c, q0:q0 + n], [tQM, tK2[kvh], tVD], tY, PT, tPT, RD, tRD)
                else:
                    for (s0, s1) in seqs:
                        kb0 = s0 // 128
                        attention([lambda q0, n: QM[:, q0:q0 + n]], [lambda kb: K2[kvh][:, (kb0 + kb) * 128:(kb0 + kb + 1) * 128]],
                                  lambda kb: VD[:, kb0 + kb, kvh * 128:(kvh + 1) * 128], 2, [(s0, 256)], NA_SCALE, rows,
                                  lambda q0, n: Y8[rows, c, q0:q0 + n], [tQM, tK2[kvh], tVD], tY, PT, tPT, RD, tRD)
        outproj(li, col, nt, Y8, tY, WP, tWP, W[f"wcdo{l}"], kcs=range(0, 8))

        b.barrier()
        cv.off = mark
        STG4 = cv.take(8192, F32, [4, 512])
        QAN = cv.take(4 * Tn * 2, BF16, [4, Tn])
        CKB = cv.take(4 * Tk * 2, BF16, [4, Tk])
        KR2 = cv.take(Tk * 2)
        RS2 = cv.take(2048, F32)
        WH = cv.take(4096, BF16, [4, 4, 128])
        QN = cv.take(Tn * 2)
        QRh = cv.take(Tn * 2)
        KN = cv.take(Tk * 2)
        VM = cv.take(nkbT * 128 * 2, BF16, [nkbT, 128])
        PT = [cv.take(1024) for _ in range(2)]
        RD = cv.take(2048, F32)
        tSTG4, tQAN, tCKB, tKR2, tRS2, tWH, tQN, tQRh, tKN, tVM, tRD = [T() for _ in range(11)]
        tPT = [T(), T()]

        def norm512(wbase, gcol0, t, after):
            for i in range(4):
                def ev(pb, t_, i=i):
                    b.op("act", lambda e: e.activation(out=STG4[:, i, :], in_=PS[pb][:], func=AF.Identity), reads=[tPS[pb]], writes=[tSTG4])
                    b.op("act", lambda e: e.activation(out=SG[i % 2], in_=PS[pb][:], func=AF.Square), reads=[tPS[pb]], writes=[tSG[i % 2]])
                    b.group("pe", [lambda e: e.matmul(PS[6][:], ones[:], SG[i % 2], start=(i == 0), stop=(i == 3))],
                            reads=[tSG[i % 2], tC], writes=[tPS[6]])
                proj(WP, tWP, wcd[wbase + i], nt, ev, tiles=[t])
            b.op("act", lambda e: e.activation(out=RS2, in_=PS[6][:], func=AF.Sqrt, bias=EPS, scale=1.0 / 512), reads=[tPS[6]], writes=[tRS2])
            b.op("dve", lambda e: e.reciprocal(out=RS2, in_=RS2), reads=[tRS2], writes=[tRS2])
            for i in range(4):
                b.op("dve", lambda e: e.scalar_tensor_tensor(out=STG4[:, i, :], in0=STG4[:, i, :], scalar=CDC[:, gcol0 + i:gcol0 + i + 1], in1=RS2,
                                                             op0=ALU.mult, op1=ALU.mult), reads=[tSTG4, tRS2, tCDC], writes=[tSTG4])
                after(i)

        for t in range(nt):
            norm512(14, 2, t, lambda i: b.op("act", lambda e: e.activation(out=QAN[:, i, TS(t)], in_=STG4[:, i, :], func=AF.Identity),
                                             reads=[tSTG4], writes=[tQAN]))

            def after_ckv(i):
                if not lat:
                    b.dma("sp", [(O[f"omc{l}"][ui, i], STG4[:, i, :])], [tSTG4], [], sOut[3])
                b.op("act", lambda e: e.activation(out=CKB[:, i, TS(t)], in_=STG4[:, i, :], func=AF.Identity), reads=[tSTG4], writes=[tCKB])
            norm512(18, 6, t, after_ckv)

        def evkr(pb, t):
            b.op("act", lambda e: e.activation(out=RS2, in_=PS[pb][:], func=AF.Identity), reads=[tPS[pb]], writes=[tRS2])
            if lat:
                rope_from(RS2, t * 512, 512, KR2[:, TS(t)], [tRS2], [tKR2])
            else:
                b.dma("sp", [(O[f"omr{l}"][ui], RS2[0:64, :])], [tRS2], [], sOut[4])
                b.op("act", lambda e: e.activation(out=KR2[:, TS(t)], in_=RS2, func=AF.Identity), reads=[tRS2], writes=[tKR2])
        proj(WP, tWP, wcd[22], nt, evkr)
        if lat:
            b.dma("pool", [(CKB[:, :, Tn:Tn + 512], W[f"mlc{l}"][ui].rearrange("k p n -> p k n")), (KR2[:, Tn:Tn + 512], W[f"mlr{l}"][ui])],
                  [], [tCKB, tKR2], sIO[4], max_dma_last_dim=8192)
        wuqn = W[f"wuqn{l}"][:, :].rearrange("p (k n) -> p k n", n=1024)
        wuk = W[f"wuk{l}"][:, :].rearrange("p (k n) -> p k n", n=1024)
        wuv = W[f"wuv{l}"][:, :].rearrange("p (k n) -> p k n", n=1024)
        wuqr = W[f"wuqr{l}"][:, :].rearrange("p (k n) -> p k n", n=512)
        for h in range(8):
            hs = slice(h * 128, (h + 1) * 128)
            b.op("pool", lambda e: e.memset(WH[:, 3, :, :], 0.0), writes=[tWH])
            b.dma("pool", [(WH[:, 0, :, :], wuqn[:, :, hs]), (WH[:, 1, :, :], wuk[:, :, hs]), (WH[:, 2, :, :], wuv[:, :, hs]),
                           (WH[:, 3, :, (h % 2) * 64:(h % 2) * 64 + 64], wuqr[:, :, h * 64:(h + 1) * 64])], [], [tWH], sIO[5])
            for t in range(nt):
                for (wi, dst, tdst, rp) in ((0, QN, tQN, False), (3, QRh, tQRh, lat)):
                    kk = pcount[0]
                    pcount[0] += 1
                    pb = kk % 2
                    b.group("pe", [lambda e, kc=kc: e.matmul(PS[pb][:], WH[:, wi, kc, :], QAN[:, kc, TS(t)], start=(kc == 0), stop=(kc == 3)) for kc in range(4)],
                            reads=[tWH, tQAN], writes=[tPS[pb]])
                    if rp:
                        b.op("act", lambda e: e.activation(out=RS2, in_=PS[pb][:], func=AF.Identity), reads=[tPS[pb]], writes=[tRS2])
                        rope_from(RS2, t * 512, 512, dst[:, TS(t)], [tRS2], [tdst])
                    else:
                        b.op("act", lambda e: e.activation(out=dst[:, TS(t)], in_=PS[pb][:], func=AF.Identity), reads=[tPS[pb]], writes=[tdst])
            for kt in range(Tk // 512):
                kk = pcount[0]
                pcount[0] += 1
                pb = kk % 2
                b.group("pe", [lambda e, kc=kc: e.matmul(PS[pb][:], WH[:, 1, kc, :], CKB[:, kc, TS(kt)], start=(kc == 0), stop=(kc == 3)) for kc in range(4)],
                        reads=[tWH, tCKB], writes=[tPS[pb]])
                b.op("act", lambda e: e.activation(out=KN[:, TS(kt)], in_=PS[pb][:], func=AF.Identity), reads=[tPS[pb]], writes=[tKN])
            for kb in range(nkbT):
                kk = pcount[0]
                pcount[0] += 1
                pb = kk % 2
                b.group("pe", [lambda e, kc=kc: e.matmul(PS[pb][:, 0:128], CKB[:, kc, kb * 128:(kb + 1) * 128], WH[:, 2, kc, :], start=(kc == 0), stop=(kc == 3)) for kc in range(4)],
                        reads=[tWH, tCKB], writes=[tPS[pb]])
                b.op("act", lambda e: e.activation(out=VM[:, kb, :], in_=PS[pb][:, 0:128], func=AF.Identity), reads=[tPS[pb]], writes=[tVM])
            rows = slice(0, 128)
            rt = [tQN, tQRh, tKN, tKR2, tVM]
            if lat:
                attention([lambda q0, n: QN[:, q0:q0 + n], lambda q0, n: QRh[:, q0:q0 + n]],
                          [lambda kb: KN[:, kb * 128:(kb + 1) * 128], lambda kb: KR2[:, kb * 128:(kb + 1) * 128]],
                          lambda kb: VM[:, kb, :], 12, [(0, 512), (512, 512)], MLA_SCALE, rows,
                          lambda q0, n: Y8[:, h, q0:q0 + n], rt, tY, PT, tPT, RD, tRD)
            else:
                for (s0, s1) in seqs:
                    kb0 = s0 // 128
                    attention([lambda q0, n: QN[:, q0:q0 + n], lambda q0, n: QRh[:, q0:q0 + n]],
                              [lambda kb: KN[:, (kb0 + kb) * 128:(kb0 + kb + 1) * 128], lambda kb: KR2[:, (kb0 + kb) * 128:(kb0 + kb + 1) * 128]],
                              lambda kb: VM[:, kb0 + kb, :], 2, [(s0, 256)], MLA_SCALE, rows,
                              lambda q0, n: Y8[:, h, q0:q0 + n], rt, tY, PT, tPT, RD, tRD)
        outproj(li, col, nt, Y8, tY, WP, tWP, W[f"wcdo{l}"], kcs=range(8, 16))

    for ui in range(nu):
        for kind in ("ctx", "lat"):
            if DBG.get('only_' + ('lat' if kind == 'ctx' else 'ctx')):
                continue
            nt = 1 if kind == "ctx" else 2
            Tn = nt * 512
            col = 0 if kind == "ctx" else 1 + ui
            src = xc if kind == "ctx" else xl
            dst = yc if kind == "ctx" else yl
            b.dma("sp", [(X[:, m, 0:Tn], src[ui, m]) for m in range(NCH)], [], allX, sX)
            for li, l in enumerate(layers):
                ffn(li, l, 0, col, nt)
                if mixers:
                    if l % 2 == 0:
                        mixer_ab(li, l, kind, ui, col, nt)
                    else:
                        mixer_cd(li, l, kind, ui, col, nt)
                    ffn(li, l, 1, col, nt)
            norm_rstd(nt)
            for t in range(nt):
                for m in range(NCH):
                    b.op("dve", lambda e, m=m: e.scalar_tensor_tensor(
                        out=X[:, m, TS(t)], in0=X[:, m, TS(t)], scalar=FG[:, m:m + 1], in1=RS[:, TS(t)], op0=ALU.mult, op1=ALU.mult),
                        reads=[tX[m][t], tRS[t], tC], writes=[tX[m][t]])
            b.dma("sp", [(dst[ui, m], X[:, m, 0:Tn]) for m in range(NCH)], allX, [], sY)
    b.finish(outslots)
    return b


def _chunks(w, nchunk):
    return np.ascontiguousarray(w.reshape(16, 128, nchunk, 128).transpose(2, 1, 0, 3)).reshape(nchunk, 128, 2048)


def _rows4(w, nk, ncol):
    return np.ascontiguousarray(w.reshape(nk, 128, ncol).transpose(1, 0, 2)).reshape(128, nk * ncol)


def _const_tables():
    t = np.arange(1024)
    nf = 16
    inv = 1.0 / (10000.0 ** (np.arange(nf, dtype=np.float32) / nf))
    d = np.arange(128) % 64
    half = d // 32
    i = d % 16
    pos = np.where(half[:, None] == 0, (t // 64)[None, :], (t % 64)[None, :]).astype(np.float32)
    ang = pos * inv[i][:, None]
    cos = np.cos(ang).astype(np.float32)
    sin = np.sin(ang).astype(np.float32)
    perm = np.zeros((128, 128), np.float32)
    for m in range(128):
        j = (m % 64) % 32
        if j < 16:
            perm[m + 16, m] = -1.0
        else:
            perm[m - 16, m] = 1.0
    bones = np.zeros((128, 128), np.float32)
    bones[:64, :64] = 1.0
    bones[64:, 64:] = 1.0
    matd = np.concatenate([perm, bones, np.zeros((128, 128), np.float32)], axis=1)
    hm = np.zeros((128, 2), np.float32)
    hm[:64, 0] = 1.0
    hm[64:, 1] = 1.0
    return cos, sin, matd, hm


def _layout_common(inp, layers, mixers=True):
    out = {}
    cos, sin, matd, hm = _const_tables()
    out["cosd"], out["sind"], out["matd"] = cos, sin, matd
    fg = np.asarray(inp["final_gain"]).reshape(16, 128).T
    out["cst"] = np.ascontiguousarray(np.concatenate([fg, hm], axis=1).astype(np.float32))
    for l in layers:
        j = l // 2
        wm = np.asarray(inp["w_mod"][l])
        out[f"wmod{l}"] = np.ascontiguousarray(wm.reshape(16, 128, 36, 512).transpose(2, 1, 0, 3)).reshape(36, 128, 8192)
        out[f"bmod{l}"] = np.ascontiguousarray(np.asarray(inp["b_mod"][l]).reshape(144, 128).T)
        out[f"gain{l}"] = np.ascontiguousarray(np.asarray(inp["norm_gain"][l]).reshape(3, 16, 128).transpose(2, 0, 1)).reshape(128, 48)
        for f in range(2):
            w = np.asarray(inp["w_ffn_in"][l][f])
            g = w[:, :DFF].reshape(16, 128, NJ, 128)
            u = w[:, DFF:].reshape(16, 128, NJ, 128)
            cat = np.concatenate([g, u], axis=-1)
            out[f"win{l}_{f}"] = np.ascontiguousarray(cat.transpose(2, 1, 0, 3)).reshape(NJ, 128, 4096)
            wo = np.asarray(inp["w_ffn_out"][l][f])
            out[f"wout{l}_{f}"] = np.ascontiguousarray(wo.reshape(11, 4, 128, 2048).transpose(0, 2, 1, 3)).reshape(11, 128, 8192)
        if not mixers:
            continue
        if l % 2 == 0:
            out[f"wab{l}"] = _chunks(np.asarray(inp["w_in_ab"][j]), 40)
            out[f"wabo{l}"] = _chunks(np.asarray(inp["w_out_ab"][j]), 16)
            wa = np.asarray(inp["lru_wa"][j])
            wx = np.asarray(inp["lru_wx"][j])
            bd = np.zeros((8, 128, 4, 128), np.float32)
            for c in range(8):
                for d in range(2):
                    for a, ww in enumerate((wa, wx)):
                        bd[c, :64, d * 2 + a, :64] = ww[d, 2 * c]
                        bd[c, 64:, d * 2 + a, 64:] = ww[d, 2 * c + 1]
            out[f"bd{l}"] = bd.reshape(8, 128, 512)
            cols = [np.asarray(inp["conv_w"][j])[k] for k in range(4)] + [np.asarray(inp["conv_b"][j])]
            cols += [np.asarray(inp["lru_ba"][j])[0], np.asarray(inp["lru_ba"][j])[1],
                     np.asarray(inp["lru_bx"][j])[0], np.asarray(inp["lru_bx"][j])[1],
                     np.asarray(inp["lru_lambda"][j])[0], np.asarray(inp["lru_lambda"][j])[1]]
            lc = np.stack(cols, axis=-1)
            out[f"lc{l}"] = np.ascontiguousarray(lc.reshape(8, 128, 11).transpose(1, 0, 2)).reshape(128, 88)
            nb = np.asarray(inp["na_bias"][j])
            kc = np.arange(64)[:, None]
            qc = np.arange(64)[None, :]
            relc = np.clip(kc - qc + 15, 0, 30)
            cstart = np.clip(qc - 8, 0, 48)
            ok = (kc >= cstart) & (kc < cstart + 16)
            bz = np.empty((16, 15, 64, 64), np.float32)
            for rr in range(15):
                g = nb[:, 14 - rr][:, relc]
                bz[:, rr] = np.where(ok[None], g, np.float32(NEG))
            out[f"bz{l}"] = bz
        else:
            w = np.asarray(inp["w_in_cd"][j])
            cols = [w[:, c * 128:(c + 1) * 128] for c in range(8)]
            for kvh in range(4):
                kk = w[:, 1024 + kvh * 64:1024 + (kvh + 1) * 64]
                cols.append(np.concatenate([kk, kk], axis=1))
            cols += [w[:, 1280:1408], w[:, 1408:1536]]
            cols += [w[:, 1536 + i * 128:1536 + (i + 1) * 128] for i in range(4)]
            cols += [w[:, 2048 + i * 128:2048 + (i + 1) * 128] for i in range(4)]
            cols.append(np.concatenate([w[:, 2560:2624], w[:, 2560:2624]], axis=1))
            out[f"wcd{l}"] = _chunks(np.concatenate(cols, axis=1), 23)
            vd = np.concatenate([np.concatenate([w[:, 1280 + k * 64:1280 + (k + 1) * 64]] * 2, axis=1) for k in range(4)], axis=1)
            out[f"wvd{l}"] = _rows4(vd, 16, 512)
            out[f"wcdo{l}"] = _chunks(np.asarray(inp["w_out_cd"][j]), 16)
            gq = np.tile(np.asarray(inp["gqa_q_gain"][j]), 2)[:, None]
            gk = np.tile(np.asarray(inp["gqa_k_gain"][j]), 2)[:, None]
            mq = np.asarray(inp["mla_q_gain"][j]).reshape(4, 128).T
            mk = np.asarray(inp["mla_kv_gain"][j]).reshape(4, 128).T
            out[f"cdc{l}"] = np.ascontiguousarray(np.concatenate([gq, gk, mq, mk], axis=1).astype(np.float32))
            uq = np.asarray(inp["mla_w_uq"][j]).reshape(512, 8, 192)
            out[f"wuqn{l}"] = _rows4(np.ascontiguousarray(uq[:, :, :128]).reshape(512, 1024), 4, 1024)
            out[f"wuqr{l}"] = _rows4(np.ascontiguousarray(uq[:, :, 128:]).reshape(512, 512), 4, 512)
            out[f"wuk{l}"] = _rows4(np.asarray(inp["mla_w_uk"][j]), 4, 1024)
            out[f"wuv{l}"] = _rows4(np.asarray(inp["mla_w_uv"][j]), 4, 1024)
    return out


def _layout_unit(inp, units, layers, mixers=True):
    nu = len(units)
    xp = np.asarray(inp["x_prompt"])
    xs = np.asarray(inp["x_sample"])
    c = np.asarray(inp["c"])
    cond = np.empty((1 + nu, D), np.float32)
    cond[0] = np.asarray(inp["c_ctx"])
    xc = np.empty((nu, NCH, 128, 512), np.float32)
    xl = np.empty((nu, NCH, 128, 1024), np.float32)
    for k, u in enumerate(units):
        tok = np.concatenate([xp[2 * u], xp[2 * u + 1]], axis=0)
        xc[k] = tok.T.reshape(NCH, 128, 512)
        xl[k] = xs[u].T.reshape(NCH, 128, 1024)
        cond[1 + k] = c[u]
    ncols = 1 + nu
    out = {"xc": xc, "xl": xl,
           "condT": np.ascontiguousarray(cond.T.reshape(NCH, 128, ncols).transpose(1, 0, 2)).reshape(128, NCH * ncols)}
    if not mixers:
        return out
    for l in layers:
        j = l // 2
        if l % 2 == 0:
            sf = np.asarray(inp["state_lru_fwd"])[units, j]
            sb = np.asarray(inp["state_lru_bwd"])[units, j]
            st = np.stack([sf, sb], axis=-1).reshape(nu, 8, 128, 2).transpose(0, 2, 1, 3)
            out[f"st{l}"] = np.ascontiguousarray(st).reshape(nu, 128, 16)
            kk = np.asarray(inp["cache_na_k"])[units, j].reshape(nu, 512, 1024)
            out[f"nakc{l}"] = np.ascontiguousarray(kk.transpose(0, 2, 1)).reshape(nu, 8, 128, 512)
            out[f"navc{l}"] = np.ascontiguousarray(np.asarray(inp["cache_na_v"])[units, j].reshape(nu, 4, 128, 1024))
        else:
            gk = np.asarray(inp["cache_gqa_k"])[units, j]
            kt = gk.transpose(0, 2, 3, 1)
            out[f"gqk{l}"] = np.ascontiguousarray(np.concatenate([kt, kt], axis=2))
            gv = np.asarray(inp["cache_gqa_v"])[units, j]
            out[f"gqv{l}"] = np.ascontiguousarray(np.concatenate([gv, gv], axis=3)).reshape(nu, 4, 128, 512)
            mc = np.asarray(inp["cache_mla_ckv"])[units, j]
            out[f"mlc{l}"] = np.ascontiguousarray(mc.transpose(0, 2, 1)).reshape(nu, 4, 128, 512)
            mr = np.asarray(inp["cache_mla_krope"])[units, j].transpose(0, 2, 1)
            out[f"mlr{l}"] = np.ascontiguousarray(np.concatenate([mr, mr], axis=1))
    return out


NCORES = 8


def kernel(**inp):
    ncores = NCORES
    nu = 8 // ncores
    layers = [0, 1, 2, 3]
    bld = build(nu, layers, True)
    common = _layout_common(inp, layers, True)
    in_maps = []
    for core in range(ncores):
        d = dict(common)
        d.update(_layout_unit(inp, [core * nu + k for k in range(nu)], layers, True))
        in_maps.append(d)
    res = run_bass_kernel_spmd(bld.nc, in_maps, core_ids=list(range(ncores)))
    y_prompt = np.empty((16, 256, D), np.float32)
    y_sample = np.empty((8, 1024, D), np.float32)
    st_f = np.empty((16, 2, 1024), np.float32)
    st_b = np.empty((16, 2, 1024), np.float32)
    na_k = np.empty((16, 2, 256, 16, 64), np.float32)
    na_v = np.empty((16, 2, 256, 16, 64), np.float32)
    gq_k = np.empty((16, 2, 256, 4, 64), np.float32)
    gq_v = np.empty((16, 2, 256, 4, 64), np.float32)
    ml_c = np.empty((16, 2, 256, 512), np.float32)
    ml_r = np.empty((16, 2, 256, 64), np.float32)
    for core in range(ncores):
        r = res.results[core]
        for k in range(nu):
            u = core * nu + k
            yp = r["yc"][k].reshape(D, 512).T
            y_prompt[2 * u] = yp[:256]
            y_prompt[2 * u + 1] = yp[256:]
            y_sample[u] = r["yl"][k].reshape(D, 1024).T
            for l in layers:
                j = l // 2
                if l % 2 == 0:
                    so = r[f"ost{l}"][k].reshape(128, 8, 2, 2).transpose(3, 2, 1, 0).reshape(2, 2, 1024)
                    kk = r[f"onak{l}"][k].reshape(1024, 512).T.reshape(2, 256, 16, 64)
                    vv = r[f"onav{l}"][k].reshape(1024, 512).T.reshape(2, 256, 16, 64)
                    for s in range(2):
                        st_f[2 * u + s, j] = so[s, 0]
                        st_b[2 * u + s, j] = so[s, 1]
                        na_k[2 * u + s, j] = kk[s]
                        na_v[2 * u + s, j] = vv[s]
                else:
                    kk = r[f"ogk{l}"][k].reshape(256, 512).T.reshape(2, 256, 4, 64)
                    vv = r[f"ogv{l}"][k].reshape(256, 512).T.reshape(2, 256, 4, 64)
                    cc = r[f"omc{l}"][k].reshape(512, 512).T.reshape(2, 256, 512)
                    rr = r[f"omr{l}"][k].T.reshape(2, 256, 64)
                    for s in range(2):
                        gq_k[2 * u + s, j] = kk[s]
                        gq_v[2 * u + s, j] = vv[s]
                        ml_c[2 * u + s, j] = cc[s]
                        ml_r[2 * u + s, j] = rr[s]
    return (y_prompt, y_sample, st_f, st_b, na_k, na_v, gq_k, gq_v, ml_c, ml_r)
```

```python
import numpy as np
from contextlib import ExitStack
import concourse.bass as bass
import concourse.mybir as mybir
from concourse.bass_utils import run_bass_kernel_spmd

F32 = mybir.dt.float32
BF16 = mybir.dt.bfloat16
AF = mybir.ActivationFunctionType
ALU = mybir.AluOpType

D = 2048
NCH = 16
TOK = 1536
NT = 3
DFF = 5632
NJ = 44
EPS = 1e-6
SEMCH = 30000


class T:
    __slots__ = ("w", "r", "x")

    def __init__(self, x=False):
        self.w = None
        self.r = []
        self.x = x


class Slot:
    def __init__(self, sem):
        self.sem = sem
        self.count = 0


class Builder:
    def __init__(self):
        self.nc = bass.Bass("TRN2", target_bir_lowering=False)
        self.es = ExitStack()
        nc = self.nc
        self.eng = {"pe": nc.tensor, "act": nc.scalar, "dve": nc.vector, "pool": nc.gpsimd, "sp": nc.sync}
        self.cnt = {e: 0 for e in self.eng}
        self.sems = {e: [] for e in self.eng}
        self.seen = {e: {} for e in self.eng}
        self.nsem = 0
        self.slots = []

    def new_sem(self, name):
        self.nsem += 1
        return self.es.enter_context(self.nc.semaphore(f"{name}_{self.nsem}"))

    def slot(self, name="s"):
        sl = Slot(self.new_sem(name))
        self.slots.append(sl)
        return sl

    def barrier(self):
        for e in self.eng:
            for e2 in self.eng:
                if e2 != e and self.cnt[e2] > 0:
                    self._wait(e, ("e", e2, self.cnt[e2]))
            for sl in self.slots:
                if sl.count:
                    self._wait(e, ("d", sl, sl.count))

    def sbuf(self, name, shape, dt):
        return self.es.enter_context(self.nc.sbuf_tensor(name, shape, dt))

    def psum(self, name, shape, dt):
        return self.es.enter_context(self.nc.psum_tensor(name, shape, dt))

    def _engsem(self, e, idx):
        ch = (idx - 1) // SEMCH
        while len(self.sems[e]) <= ch:
            self.sems[e].append(self.new_sem(f"e_{e}"))
        return self.sems[e][ch], idx - ch * SEMCH

    def _wait(self, e, ev):
        if ev[0] == "e":
            _, e2, idx = ev
            if e2 == e and e == "pe":
                return
            key = ("e", e2)
            if self.seen[e].get(key, 0) >= idx:
                return
            self.seen[e][key] = idx
            sem, val = self._engsem(e2, idx)
        else:
            _, sl, val = ev
            key = ("d", id(sl))
            if self.seen[e].get(key, 0) >= val:
                return
            self.seen[e][key] = val
            sem = sl.sem
        self.eng[e].wait_ge(sem, val)

    def _deps(self, e, reads, writes):
        for t in reads:
            if t.w is not None:
                self._wait(e, t.w)
            if t.x:
                for ev in t.r:
                    if not (ev[0] == "e" and ev[1] == e):
                        self._wait(e, ev)
        for t in writes:
            if t.w is not None:
                self._wait(e, t.w)
            for ev in t.r:
                self._wait(e, ev)

    def _commit(self, ev, reads, writes):
        for t in reads:
            t.r.append(ev)
            if len(t.r) > 24:
                t.r = t.r[-24:] if False else t.r
        for t in writes:
            t.w = ev
            t.r = []

    def op(self, e, fn, reads=(), writes=()):
        self._deps(e, reads, writes)
        ins = fn(self.eng[e])
        self.cnt[e] += 1
        idx = self.cnt[e]
        sem, _ = self._engsem(e, idx)
        ins.then_inc(sem, 1)
        self._commit(("e", e, idx), reads, writes)

    def group(self, e, fns, reads=(), writes=()):
        self._deps(e, reads, writes)
        ins = None
        for fn in fns:
            ins = fn(self.eng[e])
        self.cnt[e] += 1
        idx = self.cnt[e]
        sem, _ = self._engsem(e, idx)
        ins.then_inc(sem, 1)
        self._commit(("e", e, idx), reads, writes)

    def dma(self, e, pairs, reads, writes, slot, **kw):
        self._deps(e, reads, writes)
        for (o, i) in pairs:
            self.eng[e].dma_start(out=o, in_=i, **kw).then_inc(slot.sem, 16)
            slot.count += 16
        self._commit(("d", slot, slot.count), reads, writes)

    def finish(self, slots):
        for sl in slots:
            if sl.count:
                self.eng["sp"].wait_ge(sl.sem, sl.count)


def rev(ap2d):
    (ps, pn), (fs, fn) = ap2d.ap
    return bass.AP(ap2d.tensor, ap2d.offset + (fn - 1) * fs, [[ps, pn], [-fs, fn]])


NA_SCALE = 0.125
MLA_SCALE = 192 ** -0.5
NEG = -30000.0
DBG = {}


def build(nu, layers, mixers=True):
    b = Builder()
    nc = b.nc
    ncols = 1 + nu
    L = len(layers)
    TM = 1024

    def din(name, shape, dt=F32):
        return nc.dram_tensor(name, list(shape), dt, kind="ExternalInput").ap()

    def dout(name, shape, dt=F32):
        return nc.dram_tensor(name, list(shape), dt, kind="ExternalOutput").ap()

    xc = din("xc", [nu, NCH, 128, 512])
    xl = din("xl", [nu, NCH, 128, 1024])
    yc = dout("yc", [nu, NCH, 128, 512])
    yl = dout("yl", [nu, NCH, 128, 1024])
    condT = din("condT", [128, NCH * ncols])
    cst = din("cst", [128, 18])
    cosd = din("cosd", [128, 1024])
    sind = din("sind", [128, 1024])
    matd = din("matd", [128, 3 * 128])
    W = {}
    O = {}
    for l in layers:
        W[f"wmod{l}"] = din(f"wmod{l}", [36, 128, 16 * 512])
        W[f"bmod{l}"] = din(f"bmod{l}", [128, 144])
        W[f"gain{l}"] = din(f"gain{l}", [128, 48])
        for f in range(2):
            W[f"win{l}_{f}"] = din(f"win{l}_{f}", [NJ, 128, 16 * 256])
            W[f"wout{l}_{f}"] = din(f"wout{l}_{f}", [11, 128, 4 * 2048])
        if not mixers:
            continue
        if l % 2 == 0:
            W[f"wab{l}"] = din(f"wab{l}", [40, 128, 2048])
            W[f"wabo{l}"] = din(f"wabo{l}", [16, 128, 2048])
            W[f"bd{l}"] = din(f"bd{l}", [8, 128, 512])
            W[f"lc{l}"] = din(f"lc{l}", [128, 8 * 11])
            W[f"bz{l}"] = din(f"bz{l}", [16, 15, 64, 64])
            W[f"st{l}"] = din(f"st{l}", [nu, 128, 16])
            W[f"nakc{l}"] = din(f"nakc{l}", [nu, 8, 128, 512])
            W[f"navc{l}"] = din(f"navc{l}", [nu, 4, 128, 1024])
            O[f"ost{l}"] = dout(f"ost{l}", [nu, 128, 32])
            O[f"onak{l}"] = dout(f"onak{l}", [nu, 8, 128, 512])
            O[f"onav{l}"] = dout(f"onav{l}", [nu, 8, 128, 512])
        else:
            W[f"wcd{l}"] = din(f"wcd{l}", [23, 128, 2048])
            W[f"wvd{l}"] = din(f"wvd{l}", [128, 16 * 512])
            W[f"wcdo{l}"] = din(f"wcdo{l}", [16, 128, 2048])
            W[f"cdc{l}"] = din(f"cdc{l}", [128, 10])
            W[f"wuqn{l}"] = din(f"wuqn{l}", [128, 4 * 1024])
            W[f"wuqr{l}"] = din(f"wuqr{l}", [128, 4 * 512])
            W[f"wuk{l}"] = din(f"wuk{l}", [128, 4 * 1024])
            W[f"wuv{l}"] = din(f"wuv{l}", [128, 4 * 1024])
            W[f"gqk{l}"] = din(f"gqk{l}", [nu, 4, 128, 512])
            W[f"gqv{l}"] = din(f"gqv{l}", [nu, 4, 128, 512])
            W[f"mlc{l}"] = din(f"mlc{l}", [nu, 4, 128, 512])
            W[f"mlr{l}"] = din(f"mlr{l}", [nu, 128, 512])
            O[f"ogk{l}"] = dout(f"ogk{l}", [nu, 4, 64, 512])
            O[f"ogv{l}"] = dout(f"ogv{l}", [nu, 2, 128, 512])
            O[f"omc{l}"] = dout(f"omc{l}", [nu, 4, 128, 512])
            O[f"omr{l}"] = dout(f"omr{l}", [nu, 64, 512])

    X = b.sbuf("X", [128, NCH, TM], F32)
    H = b.sbuf("H", [128, NCH, TM], BF16)
    RS = b.sbuf("RS", [128, TM], F32)
    ones = b.sbuf("ones", [128, 128], BF16)
    MATS = b.sbuf("MATS", [128, 384], BF16)
    perm = MATS[:, 0:128]
    bones = MATS[:, 128:256]
    HM = b.sbuf("HM", [128, 2], F32)
    COS = b.sbuf("COS", [128, 1024], F32)
    SIN = b.sbuf("SIN", [128, 1024], F32)
    DER = b.sbuf("DER", [128, L * 3 * ncols * 3 * NCH], F32)
    GAIN = b.sbuf("GAIN", [128, L, 48], F32)
    FG = b.sbuf("FG", [128, NCH], F32)
    CT = b.sbuf("CT", [128, NCH * ncols], F32)
    CB = b.sbuf("CB", [128, NCH * ncols], BF16)
    ABYTES = 92160
    AR = b.sbuf("ARENA", [128, ABYTES // 2], BF16)
    PS = [b.psum(f"ps{i}", [128, 512], F32) for i in range(8)]

    class Carver:
        def __init__(self, base=0):
            self.off = base

        def take(self, nbytes, dt=BF16, shape=None):
            assert self.off % 4 == 0
            a = AR[:, self.off // 2:(self.off + nbytes) // 2]
            self.off += nbytes
            assert self.off <= ABYTES, self.off
            if dt == F32:
                a = a.bitcast(F32)
            if shape is not None:
                names = " ".join(f"d{i}" for i in range(len(shape)))
                a = a.rearrange(f"p ({names}) -> p {names}", **{f"d{i}": s for i, s in enumerate(shape)})
            return a

    tX = [[T() for _ in range(2)] for _ in range(NCH)]
    tH = [[T() for _ in range(2)] for _ in range(NCH)]
    tRS = [T() for _ in range(2)]
    tPS = [T(True) for _ in range(8)]
    tC = T()
    tDER = T()
    allX = [tX[m][t] for m in range(NCH) for t in range(2)]
    allH = [tH[m][t] for m in range(NCH) for t in range(2)]

    def der(li, n, col, k, m):
        off = ((((li * 3 + n) * ncols + col) * 3 + k) * NCH) + m
        return DER[:, off:off + 1]

    cv0 = Carver()
    SG = [cv0.take(1024) for _ in range(2)]
    TMP = [cv0.take(2048, F32) for _ in range(2)]
    tSG = [T(), T()]
    tTMP = [T(), T()]
    BASE = cv0.off
    sMisc = b.slot("misc")
    sX = b.slot("x")
    sY = b.slot("y")
    outslots = [sY]

    b.op("dve", lambda e: e.memset(ones[:], 1.0), writes=[tC])
    pairs = [(CT[:], condT[:, :]), (FG[:], cst[:, 0:16]), (HM[:], cst[:, 16:18]), (COS[:], cosd[:, :]), (SIN[:], sind[:, :])]
    for li, l in enumerate(layers):
        pairs.append((GAIN[:, li, :], W[f"gain{l}"][:, :]))
    b.dma("sp", pairs, [], [tC], sMisc)
    sMat = b.slot("mat")
    tMat = T()
    b.dma("pool", [(MATS[:], matd[:, :])], [], [tMat], sMat)
    b.op("act", lambda e: e.activation(out=CB[:], in_=CT[:], func=AF.Silu), reads=[tC], writes=[tC])

    cv = Carver(BASE)
    WM = [cv.take(16384) for _ in range(2)]
    MODT = cv.take(144 * ncols * 4, F32)
    BM = cv.take(144 * 4, F32)
    tWM = [T(), T()]
    tMODT = T()
    tBM = T()
    sWM = [b.slot("wm"), b.slot("wm")]
    sBM = b.slot("bm")
    k = 0
    for li, l in enumerate(layers):
        pm = PS[7]
        b.dma("sp", [(BM, W[f"bmod{l}"][:, :])], [], [tBM], sBM)
        for s in range(36):
            sl = k % 2
            k += 1
            b.dma("pool", [(WM[sl], W[f"wmod{l}"][s])], [], [tWM[sl]], sWM[sl], max_dma_last_dim=8192)
            fns = []
            for q in range(4):
                oc = 4 * s + q
                for kc in range(16):
                    fns.append(lambda e, q=q, kc=kc, oc=oc, sl=sl: e.matmul(
                        pm[:, oc * ncols:(oc + 1) * ncols],
                        WM[sl][:, kc * 512 + q * 128: kc * 512 + (q + 1) * 128],
                        CB[:, kc * ncols:(kc + 1) * ncols], start=(kc == 0), stop=(kc == 15)))
            b.group("pe", fns, reads=[tWM[sl], tC], writes=[tPS[7]])
        for col in range(ncols):
            b.op("dve", lambda e, col=col: e.tensor_tensor(
                out=MODT[:, col:144 * ncols:ncols], in0=pm[:, col:144 * ncols:ncols], in1=BM, op=ALU.add),
                reads=[tPS[7], tBM], writes=[tMODT])
        for n in range(3):
            for col in range(ncols):
                sh0 = ((3 * n + 0) * 16) * ncols + col
                sc0 = ((3 * n + 1) * 16) * ncols + col
                g0 = ((3 * n + 2) * 16) * ncols + col
                o = ((li * 3 + n) * ncols + col) * 3 * NCH
                b.op("dve", lambda e, sc0=sc0, o=o, n=n, li=li: e.scalar_tensor_tensor(
                    out=DER[:, o:o + NCH], in0=MODT[:, sc0:sc0 + 15 * ncols + 1:ncols], scalar=1.0,
                    in1=GAIN[:, li, n * 16:(n + 1) * 16], op0=ALU.add, op1=ALU.mult),
                    reads=[tMODT, tC], writes=[tDER])
                b.op("dve", lambda e, g0=g0, o=o, n=n: e.tensor_scalar(
                    out=DER[:, o + NCH:o + 2 * NCH], in0=MODT[:, g0:g0 + 15 * ncols + 1:ncols],
                    scalar1=(1.0 if n == 1 else 0.5), scalar2=None, op0=ALU.mult),
                    reads=[tMODT], writes=[tDER])
                b.op("dve", lambda e, sh0=sh0, o=o: e.tensor_copy(
                    out=DER[:, o + 2 * NCH:o + 3 * NCH], in_=MODT[:, sh0:sh0 + 15 * ncols + 1:ncols]),
                    reads=[tMODT], writes=[tDER])

    def TS(t):
        return slice(t * 512, (t + 1) * 512)

    def norm_rstd(nt):
        for t in range(nt):
            pb = 6
            for m in range(NCH):
                b.op("act", lambda e, m=m: e.activation(out=SG[m % 2], in_=X[:, m, TS(t)], func=AF.Square),
                     reads=[tX[m][t]], writes=[tSG[m % 2]])
                b.group("pe", [lambda e, m=m: e.matmul(PS[pb][:], ones[:], SG[m % 2], start=(m == 0), stop=(m == 15))],
                        reads=[tSG[m % 2], tC], writes=[tPS[pb]])
            b.op("act", lambda e: e.activation(out=RS[:, TS(t)], in_=PS[pb][:], func=AF.Sqrt, bias=EPS, scale=1.0 / D),
                 reads=[tPS[pb]], writes=[tRS[t]])
            b.op("dve", lambda e: e.reciprocal(out=RS[:, TS(t)], in_=RS[:, TS(t)]), reads=[tRS[t]], writes=[tRS[t]])

    def adaln(li, n, col, nt):
        norm_rstd(nt)
        for t in range(nt):
            for m in range(NCH):
                b.op("dve", lambda e, m=m: e.tensor_tensor(out=TMP[m % 2], in0=X[:, m, TS(t)], in1=RS[:, TS(t)], op=ALU.mult),
                     reads=[tX[m][t], tRS[t]], writes=[tTMP[m % 2]])
                b.op("act", lambda e, m=m: e.activation(
                    out=H[:, m, TS(t)], in_=TMP[m % 2], func=AF.Identity, bias=der(li, n, col, 2, m), scale=der(li, n, col, 0, m)),
                    reads=[tTMP[m % 2], tDER], writes=[tH[m][t]])

    pcount = [0]

    def ffn(li, l, f, col, nt):
        n = 0 if f == 0 else 2
        Tn = nt * 512
        b.barrier()
        adaln(li, n, col, nt)
        cv = Carver(BASE)
        WIN = [cv.take(8192) for _ in range(2)]
        WOUT = [cv.take(16384) for _ in range(2)]
        HID = cv.take(8192)
        tWIN = [T(), T()]
        tWOUT = [T(), T()]
        tHID = [T() for _ in range(2)]
        win = W[f"win{l}_{f}"]
        wout = W[f"wout{l}_{f}"]
        for g in range(11):
            gs = g % 2
            b.dma("pool", [(WOUT[gs], wout[g])], [], [tWOUT[gs]], sFW[2 + gs], max_dma_last_dim=8192)
            for c in range(4):
                j = 4 * g + c
                ws = j % 2
                b.dma("pool", [(WIN[ws], win[j])], [], [tWIN[ws]], sFW[ws], max_dma_last_dim=8192)
                for t in range(nt):
                    kk = pcount[0]
                    pcount[0] += 1
                    pg = kk % 2
                    pu = 2 + kk % 2
                    rd = [tWIN[ws]] + [tH[m][t] for m in range(NCH)]
                    b.group("pe", [lambda e, kc=kc: e.matmul(PS[pg][:], WIN[ws][:, kc * 256: kc * 256 + 128], H[:, kc, TS(t)],
                                                              start=(kc == 0), stop=(kc == 15)) for kc in range(16)],
                            reads=rd, writes=[tPS[pg]])
                    b.group("pe", [lambda e, kc=kc: e.matmul(PS[pu][:], WIN[ws][:, kc * 256 + 128: kc * 256 + 256], H[:, kc, TS(t)],
                                                              start=(kc == 0), stop=(kc == 15)) for kc in range(16)],
                            reads=rd, writes=[tPS[pu]])
                    b.op("act", lambda e: e.activation(out=SG[kk % 2], in_=PS[pg][:], func=AF.Silu),
                         reads=[tPS[pg]], writes=[tSG[kk % 2]])
                    b.op("dve", lambda e: e.tensor_tensor(
                        out=HID[:, c * Tn + t * 512: c * Tn + (t + 1) * 512], in0=SG[kk % 2], in1=PS[pu][:], op=ALU.mult),
                        reads=[tSG[kk % 2], tPS[pu]], writes=[tHID[t]])
            for m in range(NCH):
                for t in range(nt):
                    kk = pcount[0]
                    pcount[0] += 1
                    po = 4 + kk % 2
                    b.group("pe", [lambda e, c=c: e.matmul(PS[po][:], WOUT[gs][:, c * 2048 + m * 128: c * 2048 + (m + 1) * 128],
                                                            HID[:, c * Tn + t * 512: c * Tn + (t + 1) * 512],
                                                            start=(c == 0), stop=(c == 3)) for c in range(4)],
                            reads=[tWOUT[gs], tHID[t]], writes=[tPS[po]])
                    b.op("dve", lambda e: e.scalar_tensor_tensor(
                        out=X[:, m, TS(t)], in0=PS[po][:], scalar=der(li, n, col, 1, m), in1=X[:, m, TS(t)],
                        op0=ALU.mult, op1=ALU.add), reads=[tPS[po], tDER, tX[m][t]], writes=[tX[m][t]])

    sFW = [b.slot("fw") for _ in range(4)]
    sWP = [b.slot("wp") for _ in range(3)]
    sIO = [b.slot("io") for _ in range(6)]
    wpk = [0]

    def proj(WP, tWP, wchunk, nt, evac, tiles=None):
        sl = wpk[0] % len(WP)
        wpk[0] += 1
        b.dma("pool", [(WP[sl], wchunk)], [], [tWP[sl]], sWP[sl], max_dma_last_dim=8192)
        for t in (range(nt) if tiles is None else tiles):
            kk = pcount[0]
            pcount[0] += 1
            pb = kk % 2
            b.group("pe", [lambda e, kc=kc: e.matmul(PS[pb][:], WP[sl][:, kc * 128:(kc + 1) * 128], H[:, kc, TS(t)],
                                                      start=(kc == 0), stop=(kc == 15)) for kc in range(16)],
                    reads=[tWP[sl]] + [tH[m][t] for m in range(NCH)], writes=[tPS[pb]])
            evac(pb, t)

    def outproj(li, col, nt, Y, tY, WP, tWP, wo, kcs=range(16)):
        for m in range(NCH):
            sl = wpk[0] % len(WP)
            wpk[0] += 1
            b.dma("pool", [(WP[sl], wo[m])], [], [tWP[sl]], sWP[sl], max_dma_last_dim=8192)
            for t in range(nt):
                kk = pcount[0]
                pcount[0] += 1
                po = 4 + kk % 2
                k0 = kcs[0]
                b.group("pe", [lambda e, kc=kc: e.matmul(PS[po][:], WP[sl][:, kc * 128:(kc + 1) * 128], Y[:, kc - k0, TS(t)],
                                                          start=(kc == kcs[0]), stop=(kc == kcs[-1])) for kc in kcs],
                        reads=[tWP[sl], tY], writes=[tPS[po]])
                b.op("dve", lambda e: e.scalar_tensor_tensor(
                    out=X[:, m, TS(t)], in0=PS[po][:], scalar=der(li, 1, col, 1, m), in1=X[:, m, TS(t)],
                    op0=ALU.mult, op1=ALU.add), reads=[tPS[po], tDER, tX[m][t]], writes=[tX[m][t]])

    def attention(qparts, kparts, vfn, nkb, q0s, scale, rows, ydst, rtiles, wtile, PT, tPT, RD, tRD, tab=None):
        its = [(q0, n, kb) for (q0, n) in q0s for kb in range(nkb)]
        kks = []

        def qk(idx):
            q0, n, kb = its[idx]
            kk = pcount[0]
            pcount[0] += 1
            kks.append(kk)
            sb = kk % 2
            b.group("pe", [lambda e, i=i: e.matmul(PS[sb][:, 0:n], kparts[i](kb), qparts[i](q0, n),
                                                    start=(i == 0), stop=(i == len(qparts) - 1)) for i in range(len(qparts))],
                    reads=rtiles, writes=[tPS[sb]])

        qk(0)
        for idx, (q0, n, kb) in enumerate(its):
            if idx + 1 < len(its):
                qk(idx + 1)
            kk = kks[idx]
            sb = kk % 2
            pt = PT[kk % 2]
            tb_ = tab(kb, q0, n) if tab is not None else None
            if tb_ is not None:
                tap, ttile = tb_
                b.op("dve", lambda e: e.scalar_tensor_tensor(out=TMP[kk % 2][:, 0:n], in0=PS[sb][:, 0:n], scalar=scale,
                                                             in1=tap, op0=ALU.mult, op1=ALU.add),
                     reads=[tPS[sb], ttile], writes=[tTMP[kk % 2]])
                b.op("act", lambda e: e.activation(out=pt[:, 0:n], in_=TMP[kk % 2][:, 0:n], func=AF.Exp),
                     reads=[tTMP[kk % 2]], writes=[tPT[kk % 2]])
            else:
                b.op("act", lambda e: e.activation(out=pt[:, 0:n], in_=PS[sb][:, 0:n], func=AF.Exp, scale=scale),
                     reads=[tPS[sb]], writes=[tPT[kk % 2]])
            b.group("pe", [lambda e: e.matmul(PS[2][:, 0:n], vfn(kb), pt[:, 0:n], start=(kb == 0), stop=(kb == nkb - 1)),
                           lambda e: e.matmul(PS[3][:, 0:n], ones[:], pt[:, 0:n], start=(kb == 0), stop=(kb == nkb - 1))],
                    reads=[tPT[kk % 2], tC] + rtiles, writes=[tPS[2], tPS[3]])
            if kb == nkb - 1:
                b.op("dve", lambda e: e.reciprocal(out=RD[rows, 0:n], in_=PS[3][rows, 0:n]), reads=[tPS[3]], writes=[tRD])
                b.op("dve", lambda e: e.tensor_tensor(out=ydst(q0, n), in0=PS[2][rows, 0:n], in1=RD[rows, 0:n], op=ALU.mult),
                     reads=[tPS[2], tRD], writes=[wtile])

    def mixer_ab(li, l, kind, ui, col, nt):
        Tn = nt * 512
        lat = kind == "lat"
        seqs = [(0, 1024)] if lat else [(0, 256), (256, 512)]
        b.barrier()
        adaln(li, 1, col, nt)
        cv = Carver(BASE)
        Y = cv.take(16 * Tn * 2, BF16, [16, Tn])
        tY = T()
        WP = [cv.take(4096) for _ in range(3)]
        tWP = [T() for _ in range(3)]
        LC = cv.take(8 * 11 * 4, F32, [8, 11])
        NSP = cv.take(8 * 4 * 4, F32, [8, 4])
        ST = cv.take(64, F32)
        SO = cv.take(128, F32)
        tLC = T()
        tSO = T()
        wab = W[f"wab{l}"]
        prs = [(LC, W[f"lc{l}"][:, :].rearrange("p (c k) -> p c k", k=11))]
        if lat:
            prs.append((ST, W[f"st{l}"][ui]))
        b.dma("sp", prs, [], [tLC], sIO[0])
        if DBG.get('skip_nsp'):
            return
        b.op("act", lambda e: e.activation(out=NSP[:, :, 0:2], in_=LC[:, :, 9:11], func=AF.Exp, scale=-1.0), reads=[tLC], writes=[tLC])
        b.op("act", lambda e: e.activation(out=NSP[:, :, 0:2], in_=NSP[:, :, 0:2], func=AF.Ln, bias=1.0), reads=[tLC], writes=[tLC])
        b.op("dve", lambda e: e.tensor_scalar(out=NSP[:, :, 2:4], in0=NSP[:, :, 0:2], scalar1=-16.0, scalar2=None, op0=ALU.mult), reads=[tLC], writes=[tLC])
        b.op("dve", lambda e: e.tensor_scalar(out=NSP[:, :, 0:2], in0=NSP[:, :, 0:2], scalar1=-8.0, scalar2=None, op0=ALU.mult), reads=[tLC], writes=[tLC])
        mark = cv.off
        XA2 = [cv.take(Tn * 4, F32) for _ in range(2)]
        GA2 = [cv.take(Tn * 4, F32) for _ in range(2)]
        XC = cv.take(Tn * 4, F32)
        R = cv.take(Tn * 4, F32)
        GI = cv.take(Tn * 4, F32)
        HF = cv.take(Tn * 4, F32)
        HB = cv.take(Tn * 4, F32)
        XCB = cv.take(Tn * 2)
        BD = cv.take(1024, BF16, [4, 128])
        tXA2 = [T(), T()]
        tGA2 = [T(), T()]
        tXC, tR, tGI, tHF, tHB, tXCB, tBD = [T() for _ in range(7)]
        for c in range(8 if not DBG.get('skip_lru') else 0):
            XA, tXA, GA, tGA = XA2[c % 2], tXA2[c % 2], GA2[c % 2], tGA2[c % 2]
            Sb, tS = XA, tXA
            b.dma("pool", [(BD, W[f"bd{l}"][c].rearrange("p (k n) -> p k n", n=128))], [], [tBD], sIO[1], max_dma_last_dim=8192)
            proj(WP, tWP, wab[c], nt, lambda pb, t: b.op("act", lambda e: e.activation(out=XA[:, TS(t)], in_=PS[pb][:], func=AF.Identity),
                                                       reads=[tPS[pb]], writes=[tXA]))
            proj(WP, tWP, wab[8 + c], nt, lambda pb, t: b.op("act", lambda e: e.activation(out=GA[:, TS(t)], in_=PS[pb][:], func=AF.Identity),
                                                           reads=[tPS[pb]], writes=[tGA]))
            for (s0, s1) in seqs:
                b.op("act", lambda e: e.activation(out=XC[:, s0:s1], in_=XA[:, s0:s1], func=AF.Identity,
                                                   bias=LC[:, c, 4:5], scale=LC[:, c, 2:3]), reads=[tXA, tLC], writes=[tXC])
                for (jj, sh) in ((0, -2), (1, -1), (3, 1)):
                    if sh < 0:
                        o_ = XC[:, s0 - sh:s1]
                        i_ = XA[:, s0:s1 + sh]
                    else:
                        o_ = XC[:, s0:s1 - sh]
                        i_ = XA[:, s0 + sh:s1]
                    b.op("dve", lambda e, o_=o_, i_=i_, jj=jj: e.scalar_tensor_tensor(
                        out=o_, in0=i_, scalar=LC[:, c, jj:jj + 1], in1=o_, op0=ALU.mult, op1=ALU.add),
                        reads=[tXA, tXC, tLC], writes=[tXC])
            b.op("dve", lambda e: e.tensor_copy(out=XCB, in_=XC), reads=[tXC], writes=[tXCB])
            for d in range(2):
                Hd, tHd = (HF, tHF) if d == 0 else (HB, tHB)
                for t in range(nt):
                    for (a, dst, tdst, bcol) in ((0, R, tR, 5 + d), (1, GI, tGI, 7 + d)):
                        kk = pcount[0]
                        pcount[0] += 1
                        pb = kk % 2
                        b.group("pe", [lambda e: e.matmul(PS[pb][:], BD[:, d * 2 + a, :], XCB[:, TS(t)], start=True, stop=True)],
                                reads=[tBD, tXCB], writes=[tPS[pb]])
                        b.op("act", lambda e: e.activation(out=dst[:, TS(t)], in_=PS[pb][:], func=AF.Sigmoid, bias=LC[:, c, bcol:bcol + 1]),
                             reads=[tPS[pb], tLC], writes=[tdst])
                b.op("act", lambda e: e.activation(out=Sb, in_=R, func=AF.Exp, scale=NSP[:, c, 2 + d:3 + d]), reads=[tR, tLC, tXC], writes=[tS])
                b.op("act", lambda e: e.activation(out=Sb, in_=Sb, func=AF.Sqrt, bias=1.0, scale=-1.0), reads=[tS], writes=[tS])
                b.op("act", lambda e: e.activation(out=R, in_=R, func=AF.Exp, scale=NSP[:, c, d:d + 1]), reads=[tR, tLC], writes=[tR])
                b.op("dve", lambda e: e.tensor_tensor(out=GI, in0=GI, in1=Sb, op=ALU.mult), reads=[tGI, tS], writes=[tGI])
                b.op("dve", lambda e: e.tensor_tensor(out=GI, in0=GI, in1=XC, op=ALU.mult), reads=[tGI, tXC], writes=[tGI])
                for si, (s0, s1) in enumerate(seqs):
                    init = ST[:, c * 2 + d:c * 2 + d + 1] if lat else 0.0
                    if d == 0:
                        b.op("dve", lambda e: e.tensor_tensor_scan(out=Hd[:, s0:s1], data0=R[:, s0:s1], data1=GI[:, s0:s1],
                                                                   initial=init, op0=ALU.mult, op1=ALU.add),
                             reads=[tR, tGI, tLC], writes=[tHd])
                    else:
                        b.op("dve", lambda e: e.tensor_tensor_scan(out=rev(Hd[:, s0:s1]), data0=rev(R[:, s0:s1]), data1=rev(GI[:, s0:s1]),
                                                                   initial=init, op0=ALU.mult, op1=ALU.add),
                             reads=[tR, tGI, tLC], writes=[tHd])
                    if not lat:
                        src = Hd[:, s1 - 1:s1] if d == 0 else Hd[:, s0:s0 + 1]
                        k_ = (c * 2 + d) * 2 + si
                        b.op("act", lambda e: e.activation(out=SO[:, k_:k_ + 1], in_=src, func=AF.Identity), reads=[tHd], writes=[tSO])
            b.op("act", lambda e: e.activation(out=R, in_=GA, func=AF.Square), reads=[tGA, tGI], writes=[tR])
            b.op("dve", lambda e: e.tensor_scalar(out=R, in0=R, scalar1=0.044715, scalar2=1.0, op0=ALU.mult, op1=ALU.add), reads=[tR], writes=[tR])
            b.op("dve", lambda e: e.tensor_tensor(out=R, in0=R, in1=GA, op=ALU.mult), reads=[tR, tGA], writes=[tR])
            b.op("act", lambda e: e.activation(out=R, in_=R, func=AF.Sigmoid, scale=1.5957691216057308), reads=[tR], writes=[tR])
            b.op("dve", lambda e: e.tensor_tensor(out=R, in0=R, in1=GA, op=ALU.mult), reads=[tR, tGA], writes=[tR])
            b.op("dve", lambda e: e.tensor_tensor(out=HF, in0=HF, in1=HB, op=ALU.add), reads=[tHF, tHB], writes=[tHF])
            b.op("dve", lambda e: e.tensor_tensor(out=Y[:, c, :], in0=HF, in1=R, op=ALU.mult), reads=[tHF, tR], writes=[tY])
        if not lat:
            b.dma("sp", [(O[f"ost{l}"][ui], SO)], [tSO], [], sOut[0])
        b.barrier()
        cv.off = mark
        QT = cv.take(Tn * 2)
        KT = cv.take((Tn + 512) * 2)
        QM = cv.take(Tn * 2)
        KF = cv.take(Tn * 4, F32) if not lat else None
        VF = cv.take(Tn * 4, F32) if not lat else None
        nvb = (Tn + (512 if lat else 0)) // 128
        VT = cv.take(nvb * 128 * 2, BF16, [nvb, 128])
        PT = [cv.take(1024) for _ in range(2)]
        RD = cv.take(2048, F32)
        tQT, tKT, tQM, tKF, tVF, tVT, tRD = [T() for _ in range(7)]
        tPT = [T(), T()]
        if lat:
            TAB = [cv.take(2048, BF16) for _ in range(8)]
            tTAB = [T() for _ in range(8)]
            for kb in range(8):
                b.op("pool", lambda e, kb=kb: e.memset(TAB[kb], NEG), writes=[tTAB[kb]])
        for c in range(8 if not DBG.get('skip_qkv') else 0):
            proj(WP, tWP, wab[16 + c], nt, lambda pb, t: b.op("act", lambda e: e.activation(out=QT[:, TS(t)], in_=PS[pb][:], func=AF.Identity),
                                                            reads=[tPS[pb]], writes=[tQT]))

            def evk(pb, t):
                if not lat:
                    b.op("act", lambda e: e.activation(out=KF[:, TS(t)], in_=PS[pb][:], func=AF.Identity), reads=[tPS[pb]], writes=[tKF])
                    b.op("dve", lambda e: e.tensor_copy(out=KT[:, TS(t)], in_=KF[:, TS(t)]), reads=[tKF], writes=[tKT])
                else:
                    b.op("dve", lambda e: e.tensor_copy(out=KT[:, TS(t)], in_=PS[pb][:]), reads=[tPS[pb]], writes=[tKT])
            proj(WP, tWP, wab[24 + c], nt, evk)
            if not lat:
                proj(WP, tWP, wab[32 + c], nt, lambda pb, t: b.op("act", lambda e: e.activation(out=VF[:, TS(t)], in_=PS[pb][:], func=AF.Identity),
                                                                reads=[tPS[pb]], writes=[tVF]))
                b.dma("sp", [(O[f"onak{l}"][ui, c], KF), (O[f"onav{l}"][ui, c], VF)], [tKF, tVF], [], sOut[1])
            sl = wpk[0] % 3
            wpk[0] += 1
            b.dma("pool", [(WP[sl], wab[32 + c])], [], [tWP[sl]], sWP[sl], max_dma_last_dim=8192)
            for tb in range(Tn // 128 if not DBG.get('skip_vt') else 0):
                kk = pcount[0]
                pcount[0] += 1
                pb = kk % 2
                t = tb // 4
                b.group("pe", [lambda e, kc=kc: e.matmul(PS[pb][:, 0:128], H[:, kc, tb * 128:(tb + 1) * 128], WP[sl][:, kc * 128:(kc + 1) * 128],
                                                          start=(kc == 0), stop=(kc == 15)) for kc in range(16)],
                        reads=[tWP[sl]] + [tH[m][t] for m in range(NCH)], writes=[tPS[pb]])
                b.op("act", lambda e: e.activation(out=VT[:, tb, :], in_=PS[pb][:, 0:128], func=AF.Identity), reads=[tPS[pb]], writes=[tVT])
            if lat:
                b.dma("pool", [(KT[:, Tn:Tn + 512], W[f"nakc{l}"][ui, c]),
                               (VT[:, 8:12, :], W[f"navc{l}"][ui][:, :, c * 128:(c + 1) * 128].rearrange("k p n -> p k n"))],
                      [], [tKT, tVT], sIO[2], max_dma_last_dim=8192)
            for hh in range(2 if not DBG.get('skip_att') else 0):
                rows = slice(hh * 64, (hh + 1) * 64)
                b.op("dve", lambda e: e.tensor_scalar(out=QM, in0=QT, scalar1=HM[:, hh:hh + 1], scalar2=None, op0=ALU.mult),
                     reads=[tQT, tC], writes=[tQM])
                if lat:
                    hd = 2 * c + hh
                    bz = W[f"bz{l}"]
                    for i in range(16):
                        r_lo = 0 if i <= 7 else i - 3
                        r_hi = 15 if i >= 8 else i + 4
                        nr = r_hi - r_lo + 1
                        rr0 = r_lo - i + 7
                        kb = i // 2
                        src = bass.AP(bz.tensor, bz.offset + ((hd * 15 + rr0) * 64) * 64, [[64, 64], [4096, nr], [1, 64]])
                        dst = TAB[kb][(i % 2) * 64:(i % 2) * 64 + 64, r_lo * 64:(r_hi + 1) * 64].rearrange("p (r q) -> p r q", q=64)
                        b.dma("pool", [(dst, src)], [], [tTAB[kb]], sTAB[kb])
                    attention([lambda q0, n: QM[:, q0:q0 + n]], [lambda kb: KT[:, kb * 128:(kb + 1) * 128]],
                              lambda kb: VT[:, kb, :], 12, [(0, 512), (512, 512)], NA_SCALE, rows,
                              lambda q0, n: Y[rows, 8 + c, q0:q0 + n], [tQM, tKT, tVT], tY, PT, tPT, RD, tRD,
                              tab=lambda kb, q0, n: (TAB[kb][:, q0:q0 + n], tTAB[kb]) if kb < 8 else None)
                else:
                    for (s0, s1) in seqs:
                        kb0 = s0 // 128
                        attention([lambda q0, n: QM[:, q0:q0 + n]], [lambda kb: KT[:, (kb0 + kb) * 128:(kb0 + kb + 1) * 128]],
                                  lambda kb: VT[:, kb0 + kb, :], 2, [(s0, 256)], NA_SCALE, rows,
                                  lambda q0, n: Y[rows, 8 + c, q0:q0 + n], [tQM, tKT, tVT], tY, PT, tPT, RD, tRD)
        if not DBG.get('skip_outproj'):
            outproj(li, col, nt, Y, tY, WP, tWP, W[f"wabo{l}"])

    sOut = [b.slot("out") for _ in range(6)]
    outslots.extend(sOut)
    sTAB = [b.slot("tab") for _ in range(8)]

    def mixer_cd(li, l, kind, ui, col, nt):
        Tn = nt * 512
        lat = kind == "lat"
        Tk = Tn + (512 if lat else 0)
        seqs = [(0, 1024)] if lat else [(0, 256), (256, 512)]
        wcd = W[f"wcd{l}"]
        b.barrier()
        adaln(li, 1, col, nt)
        cv = Carver(BASE)
        Y8 = cv.take(8 * Tn * 2, BF16, [8, Tn])
        tY = T()
        WP = [cv.take(4096) for _ in range(3)]
        tWP = [T() for _ in range(3)]
        CDC = cv.take(40, F32)
        tCDC = T()
        b.dma("sp", [(CDC, W[f"cdc{l}"][:, :])], [], [tCDC], sIO[0])
        mark = cv.off

        def rope_from(src, t0, n, dst, rd, wr):
            b.op("act", lambda e: e.activation(out=SG[0][:, 0:n], in_=src, func=AF.Identity), reads=rd, writes=[tSG[0]])
            b.group("pe", [lambda e: e.matmul(PS[5][:, 0:n], perm, SG[0][:, 0:n], start=True, stop=True)], reads=[tSG[0], tMat], writes=[tPS[5]])
            b.op("dve", lambda e: e.tensor_tensor(out=TMP[0][:, 0:n], in0=src, in1=COS[:, t0:t0 + n], op=ALU.mult), reads=rd + [tC], writes=[tTMP[0]])
            b.op("dve", lambda e: e.tensor_tensor(out=TMP[1][:, 0:n], in0=PS[5][:, 0:n], in1=SIN[:, t0:t0 + n], op=ALU.mult), reads=[tPS[5], tC], writes=[tTMP[1]])
            b.op("dve", lambda e: e.tensor_tensor(out=dst, in0=TMP[0][:, 0:n], in1=TMP[1][:, 0:n], op=ALU.add), reads=[tTMP[0], tTMP[1]], writes=wr)

        K2 = [cv.take(Tk * 2) for _ in range(4)]
        tK2 = [T() for _ in range(4)]
        nkbT = Tk // 128
        VD = cv.take(nkbT * 512 * 2, BF16, [nkbT, 512])
        tVD = T()
        WV = cv.take(16384)
        tWV = T()
        QT = cv.take(Tn * 2)
        QM = cv.take(Tn * 2)
        STG = cv.take(2048, F32)
        RSQ = cv.take(2048, F32)
        PT = [cv.take(1024) for _ in range(2)]
        RD = cv.take(2048, F32)
        tQT, tQM, tSTG, tRSQ, tRD = [T() for _ in range(5)]
        tPT = [T(), T()]

        def headnorm(pb, t, gcol, dst, wr, rope, f32dma=None):
            b.op("act", lambda e: e.activation(out=STG, in_=PS[pb][:], func=AF.Identity), reads=[tPS[pb]], writes=[tSTG])
            b.op("act", lambda e: e.activation(out=SG[1], in_=PS[pb][:], func=AF.Square), reads=[tPS[pb]], writes=[tSG[1]])
            b.group("pe", [lambda e: e.matmul(PS[6][:], bones, SG[1], start=True, stop=True)], reads=[tSG[1], tMat], writes=[tPS[6]])
            b.op("act", lambda e: e.activation(out=RSQ, in_=PS[6][:], func=AF.Sqrt, bias=EPS, scale=1.0 / 64), reads=[tPS[6]], writes=[tRSQ])
            b.op("dve", lambda e: e.reciprocal(out=RSQ, in_=RSQ), reads=[tRSQ], writes=[tRSQ])
            if rope or f32dma is not None:
                b.op("dve", lambda e: e.scalar_tensor_tensor(out=STG, in0=STG, scalar=CDC[:, gcol:gcol + 1], in1=RSQ, op0=ALU.mult, op1=ALU.mult),
                     reads=[tSTG, tRSQ, tCDC], writes=[tSTG])
                if f32dma is not None:
                    f32dma()
                if rope:
                    rope_from(STG, t * 512, 512, dst, [tSTG], wr)
                else:
                    b.op("act", lambda e: e.activation(out=dst, in_=STG, func=AF.Identity), reads=[tSTG], writes=wr)
            else:
                b.op("dve", lambda e: e.scalar_tensor_tensor(out=dst, in0=STG, scalar=CDC[:, gcol:gcol + 1], in1=RSQ, op0=ALU.mult, op1=ALU.mult),
                     reads=[tSTG, tRSQ, tCDC], writes=wr)

        for kvh in range(4):
            def evk(pb, t, kvh=kvh):
                f = None
                if not lat:
                    f = lambda: b.dma("sp", [(O[f"ogk{l}"][ui, kvh], STG[0:64, :])], [tSTG], [], sOut[2])
                headnorm(pb, t, 1, K2[kvh][:, TS(t)], [tK2[kvh]], lat, f)
            proj(WP, tWP, wcd[8 + kvh], nt, evk)
        if lat:
            b.dma("pool", [(K2[kvh][:, Tn:Tn + 512], W[f"gqk{l}"][ui, kvh]) for kvh in range(4)]
                  + [(VD[:, 8:12, :], W[f"gqv{l}"][ui].rearrange("k p n -> p k n"))], [], tK2 + [tVD], sIO[2], max_dma_last_dim=8192)
        else:
            for i in range(2):
                def evv(pb, t, i=i):
                    b.op("act", lambda e: e.activation(out=STG, in_=PS[pb][:], func=AF.Identity), reads=[tPS[pb]], writes=[tSTG])
                    b.dma("sp", [(O[f"ogv{l}"][ui, i], STG)], [tSTG], [], sOut[2])
                proj(WP, tWP, wcd[12 + i], nt, evv)
        b.dma("pool", [(WV, W[f"wvd{l}"][:, :])], [], [tWV], sIO[3], max_dma_last_dim=8192)
        for tb in range(Tn // 128):
            kk = pcount[0]
            pcount[0] += 1
            pb = kk % 2
            b.group("pe", [lambda e, kc=kc: e.matmul(PS[pb][:], H[:, kc, tb * 128:(tb + 1) * 128], WV[:, kc * 512:(kc + 1) * 512],
                                                      start=(kc == 0), stop=(kc == 15)) for kc in range(16)],
                    reads=[tWV] + [tH[m][tb // 4] for m in range(NCH)], writes=[tPS[pb]])
            b.op("act", lambda e: e.activation(out=VD[:, tb, :], in_=PS[pb][:], func=AF.Identity), reads=[tPS[pb]], writes=[tVD])
        for c in range(8):
            kvh = c // 2
            proj(WP, tWP, wcd[c], nt, lambda pb, t: headnorm(pb, t, 0, QT[:, TS(t)], [tQT], lat))
            for hh in range(2):
                rows = slice(hh * 64, (hh + 1) * 64)
                b.op("dve", lambda e: e.tensor_scalar(out=QM, in0=QT, scalar1=HM[:, hh:hh + 1], scalar2=None, op0=ALU.mult),
                     reads=[tQT, tC], writes=[tQM])
                if lat:
                    attention([lambda q0, n: QM[:, q0:q0 + n]], [lambda kb: K2[kvh][:, kb * 128:(kb + 1) * 128]],
                              lambda kb: VD[:, kb, kvh * 128:(kvh + 1) * 128], 12, [(0, 512), (512, 512)], NA_SCALE, rows,
                              lambda q0, n: Y8[rows, c, q0:q0 + n], [tQM, tK2[kvh], tVD], tY, PT, tPT, RD, tRD)
                else:
                    for (s0, s1) in seqs:
                        kb0 = s0 // 128
                        attention([lambda q0, n: QM[:, q0:q0 + n]], [lambda kb: K2[kvh][:, (kb0 + kb) * 128:(kb0 + kb + 1) * 128]],
                                  lambda kb: VD[:, kb0 + kb, kvh * 128:(kvh + 1) * 128], 2, [(s0, 256)], NA_SCALE, rows,
                                  lambda q0, n: Y8[rows, c, q0:q0 + n], [tQM, tK2[kvh], tVD], tY, PT, tPT, RD, tRD)
        outproj(li, col, nt, Y8, tY, WP, tWP, W[f"wcdo{l}"], kcs=range(0, 8))

        b.barrier()
        cv.off = mark
        STG4 = cv.take(8192, F32, [4, 512])
        QAN = cv.take(4 * Tn * 2, BF16, [4, Tn])
        CKB = cv.take(4 * Tk * 2, BF16, [4, Tk])
        KR2 = cv.take(Tk * 2)
        RS2 = cv.take(2048, F32)
        WH = cv.take(4096, BF16, [4, 4, 128])
        QN = cv.take(Tn * 2)
        QRh = cv.take(Tn * 2)
        KN = cv.take(Tk * 2)
        VM = cv.take(nkbT * 128 * 2, BF16, [nkbT, 128])
        PT = [cv.take(1024) for _ in range(2)]
        RD = cv.take(2048, F32)
        tSTG4, tQAN, tCKB, tKR2, tRS2, tWH, tQN, tQRh, tKN, tVM, tRD = [T() for _ in range(11)]
        tPT = [T(), T()]

        def norm512(wbase, gcol0, t, after):
            for i in range(4):
                def ev(pb, t_, i=i):
                    b.op("act", lambda e: e.activation(out=STG4[:, i, :], in_=PS[pb][:], func=AF.Identity), reads=[tPS[pb]], writes=[tSTG4])
                    b.op("act", lambda e: e.activation(out=SG[i % 2], in_=PS[pb][:], func=AF.Square), reads=[tPS[pb]], writes=[tSG[i % 2]])
                    b.group("pe", [lambda e: e.matmul(PS[6][:], ones[:], SG[i % 2], start=(i == 0), stop=(i == 3))],
                            reads=[tSG[i % 2], tC], writes=[tPS[6]])
                proj(WP, tWP, wcd[wbase + i], nt, ev, tiles=[t])
            b.op("act", lambda e: e.activation(out=RS2, in_=PS[6][:], func=AF.Sqrt, bias=EPS, scale=1.0 / 512), reads=[tPS[6]], writes=[tRS2])
            b.op("dve", lambda e: e.reciprocal(out=RS2, in_=RS2), reads=[tRS2], writes=[tRS2])
            for i in range(4):
                b.op("dve", lambda e: e.scalar_tensor_tensor(out=STG4[:, i, :], in0=STG4[:, i, :], scalar=CDC[:, gcol0 + i:gcol0 + i + 1], in1=RS2,
                                                             op0=ALU.mult, op1=ALU.mult), reads=[tSTG4, tRS2, tCDC], writes=[tSTG4])
                after(i)

        for t in range(nt):
            norm512(14, 2, t, lambda i: b.op("act", lambda e: e.activation(out=QAN[:, i, TS(t)], in_=STG4[:, i, :], func=AF.Identity),
                                             reads=[tSTG4], writes=[tQAN]))

            def after_ckv(i):
                if not lat:
                    b.dma("sp", [(O[f"omc{l}"][ui, i], STG4[:, i, :])], [tSTG4], [], sOut[3])
                b.op("act", lambda e: e.activation(out=CKB[:, i, TS(t)], in_=STG4[:, i, :], func=AF.Identity), reads=[tSTG4], writes=[tCKB])
            norm512(18, 6, t, after_ckv)

        def evkr(pb, t):
            b.op("act", lambda e: e.activation(out=RS2, in_=PS[pb][:], func=AF.Identity), reads=[tPS[pb]], writes=[tRS2])
            if lat:
                rope_from(RS2, t * 512, 512, KR2[:, TS(t)], [tRS2], [tKR2])
            else:
                b.dma("sp", [(O[f"omr{l}"][ui], RS2[0:64, :])], [tRS2], [], sOut[4])
                b.op("act", lambda e: e.activation(out=KR2[:, TS(t)], in_=RS2, func=AF.Identity), reads=[tRS2], writes=[tKR2])
        proj(WP, tWP, wcd[22], nt, evkr)
        if lat:
            b.dma("pool", [(CKB[:, :, Tn:Tn + 512], W[f"mlc{l}"][ui].rearrange("k p n -> p k n")), (KR2[:, Tn:Tn + 512], W[f"mlr{l}"][ui])],
                  [], [tCKB, tKR2], sIO[4], max_dma_last_dim=8192)
        wuqn = W[f"wuqn{l}"][:, :].rearrange("p (k n) -> p k n", n=1024)
        wuk = W[f"wuk{l}"][:, :].rearrange("p (k n) -> p k n", n=1024)
        wuv = W[f"wuv{l}"][:, :].rearrange("p (k n) -> p k n", n=1024)
        wuqr = W[f"wuqr{l}"][:, :].rearrange("p (k n) -> p k n", n=512)
        for h in range(8):
            hs = slice(h * 128, (h + 1) * 128)
            b.op("pool", lambda e: e.memset(WH[:, 3, :, :], 0.0), writes=[tWH])
            b.dma("pool", [(WH[:, 0, :, :], wuqn[:, :, hs]), (WH[:, 1, :, :], wuk[:, :, hs]), (WH[:, 2, :, :], wuv[:, :, hs]),
                           (WH[:, 3, :, (h % 2) * 64:(h % 2) * 64 + 64], wuqr[:, :, h * 64:(h + 1) * 64])], [], [tWH], sIO[5])
            for t in range(nt):
                for (wi, dst, tdst, rp) in ((0, QN, tQN, False), (3, QRh, tQRh, lat)):
                    kk = pcount[0]
                    pcount[0] += 1
                    pb = kk % 2
                    b.group("pe", [lambda e, kc=kc: e.matmul(PS[pb][:], WH[:, wi, kc, :], QAN[:, kc, TS(t)], start=(kc == 0), stop=(kc == 3)) for kc in range(4)],
                            reads=[tWH, tQAN], writes=[tPS[pb]])
                    if rp:
                        b.op("act", lambda e: e.activation(out=RS2, in_=PS[pb][:], func=AF.Identity), reads=[tPS[pb]], writes=[tRS2])
                        rope_from(RS2, t * 512, 512, dst[:, TS(t)], [tRS2], [tdst])
                    else:
                        b.op("act", lambda e: e.activation(out=dst[:, TS(t)], in_=PS[pb][:], func=AF.Identity), reads=[tPS[pb]], writes=[tdst])
            for kt in range(Tk // 512):
                kk = pcount[0]
                pcount[0] += 1
                pb = kk % 2
                b.group("pe", [lambda e, kc=kc: e.matmul(PS[pb][:], WH[:, 1, kc, :], CKB[:, kc, TS(kt)], start=(kc == 0), stop=(kc == 3)) for kc in range(4)],
                        reads=[tWH, tCKB], writes=[tPS[pb]])
                b.op("act", lambda e: e.activation(out=KN[:, TS(kt)], in_=PS[pb][:], func=AF.Identity), reads=[tPS[pb]], writes=[tKN])
            for kb in range(nkbT):
                kk = pcount[0]
                pcount[0] += 1
                pb = kk % 2
                b.group("pe", [lambda e, kc=kc: e.matmul(PS[pb][:, 0:128], CKB[:, kc, kb * 128:(kb + 1) * 128], WH[:, 2, kc, :], start=(kc == 0), stop=(kc == 3)) for kc in range(4)],
                        reads=[tWH, tCKB], writes=[tPS[pb]])
                b.op("act", lambda e: e.activation(out=VM[:, kb, :], in_=PS[pb][:, 0:128], func=AF.Identity), reads=[tPS[pb]], writes=[tVM])
            rows = slice(0, 128)
            rt = [tQN, tQRh, tKN, tKR2, tVM]
            if lat:
                attention([lambda q0, n: QN[:, q0:q0 + n], lambda q0, n: QRh[:, q0:q0 + n]],
                          [lambda kb: KN[:, kb * 128:(kb + 1) * 128], lambda kb: KR2[:, kb * 128:(kb + 1) * 128]],
                          lambda kb: VM[:, kb, :], 12, [(0, 512), (512, 512)], MLA_SCALE, rows,
                          lambda q0, n: Y8[:, h, q0:q0 + n], rt, tY, PT, tPT, RD, tRD)
            else:
                for (s0, s1) in seqs:
                    kb0 = s0 // 128
                    attention([lambda q0, n: QN[:, q0:q0 + n], lambda q0, n: QRh[:, q0:q0 + n]],
                              [lambda kb: KN[:, (kb0 + kb) * 128:(kb0 + kb + 1) * 128], lambda kb: KR2[:, (kb0 + kb) * 128:(kb0 + kb + 1) * 128]],
                              lambda kb: VM[:, kb0 + kb, :], 2, [(s0, 256)], MLA_SCALE, rows,
                              lambda q0, n: Y8[:, h, q0:q0 + n], rt, tY, PT, tPT, RD, tRD)
        outproj(li, col, nt, Y8, tY, WP, tWP, W[f"wcdo{l}"], kcs=range(8, 16))

    for ui in range(nu):
        for kind in ("ctx", "lat"):
            if DBG.get('only_' + ('lat' if kind == 'ctx' else 'ctx')):
                continue
            nt = 1 if kind == "ctx" else 2
            Tn = nt * 512
            col = 0 if kind == "ctx" else 1 + ui
            src = xc if kind == "ctx" else xl
            dst = yc if kind == "ctx" else yl
            b.dma("sp", [(X[:, m, 0:Tn], src[ui, m]) for m in range(NCH)], [], allX, sX)
            for li, l in enumerate(layers):
                ffn(li, l, 0, col, nt)
                if mixers:
                    if l % 2 == 0:
                        mixer_ab(li, l, kind, ui, col, nt)
                    else:
                        mixer_cd(li, l, kind, ui, col, nt)
                    ffn(li, l, 1, col, nt)
            norm_rstd(nt)
            for t in range(nt):
                for m in range(NCH):
                    b.op("dve", lambda e, m=m: e.scalar_tensor_tensor(
                        out=X[:, m, TS(t)], in0=X[:, m, TS(t)], scalar=FG[:, m:m + 1], in1=RS[:, TS(t)], op0=ALU.mult, op1=ALU.mult),
                        reads=[tX[m][t], tRS[t], tC], writes=[tX[m][t]])
            b.dma("sp", [(dst[ui, m], X[:, m, 0:Tn]) for m in range(NCH)], allX, [], sY)
    b.finish(outslots)
    return b


def _chunks(w, nchunk):
    return np.ascontiguousarray(w.reshape(16, 128, nchunk, 128).transpose(2, 1, 0, 3)).reshape(nchunk, 128, 2048)


def _rows4(w, nk, ncol):
    return np.ascontiguousarray(w.reshape(nk, 128, ncol).transpose(1, 0, 2)).reshape(128, nk * ncol)


def _const_tables():
    t = np.arange(1024)
    nf = 16
    inv = 1.0 / (10000.0 ** (np.arange(nf, dtype=np.float32) / nf))
    d = np.arange(128) % 64
    half = d // 32
    i = d % 16
    pos = np.where(half[:, None] == 0, (t // 64)[None, :], (t % 64)[None, :]).astype(np.float32)
    ang = pos * inv[i][:, None]
    cos = np.cos(ang).astype(np.float32)
    sin = np.sin(ang).astype(np.float32)
    perm = np.zeros((128, 128), np.float32)
    for m in range(128):
        j = (m % 64) % 32
        if j < 16:
            perm[m + 16, m] = -1.0
        else:
            perm[m - 16, m] = 1.0
    bones = np.zeros((128, 128), np.float32)
    bones[:64, :64] = 1.0
    bones[64:, 64:] = 1.0
    matd = np.concatenate([perm, bones, np.zeros((128, 128), np.float32)], axis=1)
    hm = np.zeros((128, 2), np.float32)
    hm[:64, 0] = 1.0
    hm[64:, 1] = 1.0
    return cos, sin, matd, hm


def _layout_common(inp, layers, mixers=True):
    out = {}
    cos, sin, matd, hm = _const_tables()
    out["cosd"], out["sind"], out["matd"] = cos, sin, matd
    fg = np.asarray(inp["final_gain"]).reshape(16, 128).T
    out["cst"] = np.ascontiguousarray(np.concatenate([fg, hm], axis=1).astype(np.float32))
    for l in layers:
        j = l // 2
        wm = np.asarray(inp["w_mod"][l])
        out[f"wmod{l}"] = np.ascontiguousarray(wm.reshape(16, 128, 36, 512).transpose(2, 1, 0, 3)).reshape(36, 128, 8192)
        out[f"bmod{l}"] = np.ascontiguousarray(np.asarray(inp["b_mod"][l]).reshape(144, 128).T)
        out[f"gain{l}"] = np.ascontiguousarray(np.asarray(inp["norm_gain"][l]).reshape(3, 16, 128).transpose(2, 0, 1)).reshape(128, 48)
        for f in range(2):
            w = np.asarray(inp["w_ffn_in"][l][f])
            g = w[:, :DFF].reshape(16, 128, NJ, 128)
            u = w[:, DFF:].reshape(16, 128, NJ, 128)
            cat = np.concatenate([g, u], axis=-1)
            out[f"win{l}_{f}"] = np.ascontiguousarray(cat.transpose(2, 1, 0, 3)).reshape(NJ, 128, 4096)
            wo = np.asarray(inp["w_ffn_out"][l][f])
            out[f"wout{l}_{f}"] = np.ascontiguousarray(wo.reshape(11, 4, 128, 2048).transpose(0, 2, 1, 3)).reshape(11, 128, 8192)
        if not mixers:
            continue
        if l % 2 == 0:
            out[f"wab{l}"] = _chunks(np.asarray(inp["w_in_ab"][j]), 40)
            out[f"wabo{l}"] = _chunks(np.asarray(inp["w_out_ab"][j]), 16)
            wa = np.asarray(inp["lru_wa"][j])
            wx = np.asarray(inp["lru_wx"][j])
            bd = np.zeros((8, 128, 4, 128), np.float32)
            for c in range(8):
                for d in range(2):
                    for a, ww in enumerate((wa, wx)):
                        bd[c, :64, d * 2 + a, :64] = ww[d, 2 * c]
                        bd[c, 64:, d * 2 + a, 64:] = ww[d, 2 * c + 1]
            out[f"bd{l}"] = bd.reshape(8, 128, 512)
            cols = [np.asarray(inp["conv_w"][j])[k] for k in range(4)] + [np.asarray(inp["conv_b"][j])]
            cols += [np.asarray(inp["lru_ba"][j])[0], np.asarray(inp["lru_ba"][j])[1],
                     np.asarray(inp["lru_bx"][j])[0], np.asarray(inp["lru_bx"][j])[1],
                     np.asarray(inp["lru_lambda"][j])[0], np.asarray(inp["lru_lambda"][j])[1]]
            lc = np.stack(cols, axis=-1)
            out[f"lc{l}"] = np.ascontiguousarray(lc.reshape(8, 128, 11).transpose(1, 0, 2)).reshape(128, 88)
            nb = np.asarray(inp["na_bias"][j])
            kc = np.arange(64)[:, None]
            qc = np.arange(64)[None, :]
            relc = np.clip(kc - qc + 15, 0, 30)
            cstart = np.clip(qc - 8, 0, 48)
            ok = (kc >= cstart) & (kc < cstart + 16)
            bz = np.empty((16, 15, 64, 64), np.float32)
            for rr in range(15):
                g = nb[:, 14 - rr][:, relc]
                bz[:, rr] = np.where(ok[None], g, np.float32(NEG))
            out[f"bz{l}"] = bz
        else:
            w = np.asarray(inp["w_in_cd"][j])
            cols = [w[:, c * 128:(c + 1) * 128] for c in range(8)]
            for kvh in range(4):
                kk = w[:, 1024 + kvh * 64:1024 + (kvh + 1) * 64]
                cols.append(np.concatenate([kk, kk], axis=1))
            cols += [w[:, 1280:1408], w[:, 1408:1536]]
            cols += [w[:, 1536 + i * 128:1536 + (i + 1) * 128] for i in range(4)]
            cols += [w[:, 2048 + i * 128:2048 + (i + 1) * 128] for i in range(4)]
            cols.append(np.concatenate([w[:, 2560:2624], w[:, 2560:2624]], axis=1))
            out[f"wcd{l}"] = _chunks(np.concatenate(cols, axis=1), 23)
            vd = np.concatenate([np.concatenate([w[:, 1280 + k * 64:1280 + (k + 1) * 64]] * 2, axis=1) for k in range(4)], axis=1)
            out[f"wvd{l}"] = _rows4(vd, 16, 512)
            out[f"wcdo{l}"] = _chunks(np.asarray(inp["w_out_cd"][j]), 16)
            gq = np.tile(np.asarray(inp["gqa_q_gain"][j]), 2)[:, None]
            gk = np.tile(np.asarray(inp["gqa_k_gain"][j]), 2)[:, None]
            mq = np.asarray(inp["mla_q_gain"][j]).reshape(4, 128).T
            mk = np.asarray(inp["mla_kv_gain"][j]).reshape(4, 128).T
            out[f"cdc{l}"] = np.ascontiguousarray(np.concatenate([gq, gk, mq, mk], axis=1).astype(np.float32))
            uq = np.asarray(inp["mla_w_uq"][j]).reshape(512, 8, 192)
            out[f"wuqn{l}"] = _rows4(np.ascontiguousarray(uq[:, :, :128]).reshape(512, 1024), 4, 1024)
            out[f"wuqr{l}"] = _rows4(np.ascontiguousarray(uq[:, :, 128:]).reshape(512, 512), 4, 512)
            out[f"wuk{l}"] = _rows4(np.asarray(inp["mla_w_uk"][j]), 4, 1024)
            out[f"wuv{l}"] = _rows4(np.asarray(inp["mla_w_uv"][j]), 4, 1024)
    return out


def _layout_unit(inp, units, layers, mixers=True):
    nu = len(units)
    xp = np.asarray(inp["x_prompt"])
    xs = np.asarray(inp["x_sample"])
    c = np.asarray(inp["c"])
    cond = np.empty((1 + nu, D), np.float32)
    cond[0] = np.asarray(inp["c_ctx"])
    xc = np.empty((nu, NCH, 128, 512), np.float32)
    xl = np.empty((nu, NCH, 128, 1024), np.float32)
    for k, u in enumerate(units):
        tok = np.concatenate([xp[2 * u], xp[2 * u + 1]], axis=0)
        xc[k] = tok.T.reshape(NCH, 128, 512)
        xl[k] = xs[u].T.reshape(NCH, 128, 1024)
        cond[1 + k] = c[u]
    ncols = 1 + nu
    out = {"xc": xc, "xl": xl,
           "condT": np.ascontiguousarray(cond.T.reshape(NCH, 128, ncols).transpose(1, 0, 2)).reshape(128, NCH * ncols)}
    if not mixers:
        return out
    for l in layers:
        j = l // 2
        if l % 2 == 0:
            sf = np.asarray(inp["state_lru_fwd"])[units, j]
            sb = np.asarray(inp["state_lru_bwd"])[units, j]
            st = np.stack([sf, sb], axis=-1).reshape(nu, 8, 128, 2).transpose(0, 2, 1, 3)
            out[f"st{l}"] = np.ascontiguousarray(st).reshape(nu, 128, 16)
            kk = np.asarray(inp["cache_na_k"])[units, j].reshape(nu, 512, 1024)
            out[f"nakc{l}"] = np.ascontiguousarray(kk.transpose(0, 2, 1)).reshape(nu, 8, 128, 512)
            out[f"navc{l}"] = np.ascontiguousarray(np.asarray(inp["cache_na_v"])[units, j].reshape(nu, 4, 128, 1024))
        else:
            gk = np.asarray(inp["cache_gqa_k"])[units, j]
            kt = gk.transpose(0, 2, 3, 1)
            out[f"gqk{l}"] = np.ascontiguousarray(np.concatenate([kt, kt], axis=2))
            gv = np.asarray(inp["cache_gqa_v"])[units, j]
            out[f"gqv{l}"] = np.ascontiguousarray(np.concatenate([gv, gv], axis=3)).reshape(nu, 4, 128, 512)
            mc = np.asarray(inp["cache_mla_ckv"])[units, j]
            out[f"mlc{l}"] = np.ascontiguousarray(mc.transpose(0, 2, 1)).reshape(nu, 4, 128, 512)
            mr = np.asarray(inp["cache_mla_krope"])[units, j].transpose(0, 2, 1)
            out[f"mlr{l}"] = np.ascontiguousarray(np.concatenate([mr, mr], axis=1))
    return out


NCORES = 8


def kernel(**inp):
    ncores = NCORES
    nu = 8 // ncores
    layers = [0, 1, 2, 3]
    bld = build(nu, layers, True)
    common = _layout_common(inp, layers, True)
    in_maps = []
    for core in range(ncores):
        d = dict(common)
        d.update(_layout_unit(inp, [core * nu + k for k in range(nu)], layers, True))
        in_maps.append(d)
    res = run_bass_kernel_spmd(bld.nc, in_maps, core_ids=list(range(ncores)))
    y_prompt = np.empty((16, 256, D), np.float32)
    y_sample = np.empty((8, 1024, D), np.float32)
    st_f = np.empty((16, 2, 1024), np.float32)
    st_b = np.empty((16, 2, 1024), np.float32)
    na_k = np.empty((16, 2, 256, 16, 64), np.float32)
    na_v = np.empty((16, 2, 256, 16, 64), np.float32)
    gq_k = np.empty((16, 2, 256, 4, 64), np.float32)
    gq_v = np.empty((16, 2, 256, 4, 64), np.float32)
    ml_c = np.empty((16, 2, 256, 512), np.float32)
    ml_r = np.empty((16, 2, 256, 64), np.float32)
    for core in range(ncores):
        r = res.results[core]
        for k in range(nu):
            u = core * nu + k
            yp = r["yc"][k].reshape(D, 512).T
            y_prompt[2 * u] = yp[:256]
            y_prompt[2 * u + 1] = yp[256:]
            y_sample[u] = r["yl"][k].reshape(D, 1024).T
            for l in layers:
                j = l // 2
                if l % 2 == 0:
                    so = r[f"ost{l}"][k].reshape(128, 8, 2, 2).transpose(3, 2, 1, 0).reshape(2, 2, 1024)
                    kk = r[f"onak{l}"][k].reshape(1024, 512).T.reshape(2, 256, 16, 64)
                    vv = r[f"onav{l}"][k].reshape(1024, 512).T.reshape(2, 256, 16, 64)
                    for s in range(2):
                        st_f[2 * u + s, j] = so[s, 0]
                        st_b[2 * u + s, j] = so[s, 1]
                        na_k[2 * u + s, j] = kk[s]
                        na_v[2 * u + s, j] = vv[s]
                else:
                    kk = r[f"ogk{l}"][k].reshape(256, 512).T.reshape(2, 256, 4, 64)
                    vv = r[f"ogv{l}"][k].reshape(256, 512).T.reshape(2, 256, 4, 64)
                    cc = r[f"omc{l}"][k].reshape(512, 512).T.reshape(2, 256, 512)
                    rr = r[f"omr{l}"][k].T.reshape(2, 256, 64)
                    for s in range(2):
                        gq_k[2 * u + s, j] = kk[s]
                        gq_v[2 * u + s, j] = vv[s]
                        ml_c[2 * u + s, j] = cc[s]
                        ml_r[2 * u + s, j] = rr[s]
    return (y_prompt, y_sample, st_f, st_b, na_k, na_v, gq_k, gq_v, ml_c, ml_r)
```

```python
import numpy as np
from contextlib import ExitStack
import concourse.bass as bass
import concourse.mybir as mybir
from concourse.bass_utils import run_bass_kernel_spmd

F32 = mybir.dt.float32
BF16 = mybir.dt.bfloat16
AF = mybir.ActivationFunctionType
ALU = mybir.AluOpType

D = 2048
NCH = 16
TOK = 1536
NT = 3
DFF = 5632
NJ = 44
EPS = 1e-6
SEMCH = 30000


class T:
    __slots__ = ("w", "r", "x")

    def __init__(self, x=False):
        self.w = None
        self.r = []
        self.x = x


class Slot:
    def __init__(self, sem):
        self.sem = sem
        self.count = 0


class Builder:
    def __init__(self):
        self.nc = bass.Bass("TRN2", target_bir_lowering=False)
        self.es = ExitStack()
        nc = self.nc
        self.eng = {"pe": nc.tensor, "act": nc.scalar, "dve": nc.vector, "pool": nc.gpsimd, "sp": nc.sync}
        self.cnt = {e: 0 for e in self.eng}
        self.sems = {e: [] for e in self.eng}
        self.seen = {e: {} for e in self.eng}
        self.nsem = 0
        self.slots = []

    def new_sem(self, name):
        self.nsem += 1
        return self.es.enter_context(self.nc.semaphore(f"{name}_{self.nsem}"))

    def slot(self, name="s"):
        sl = Slot(self.new_sem(name))
        self.slots.append(sl)
        return sl

    def barrier(self):
        for e in self.eng:
            for e2 in self.eng:
                if e2 != e and self.cnt[e2] > 0:
                    self._wait(e, ("e", e2, self.cnt[e2]))
            for sl in self.slots:
                if sl.count:
                    self._wait(e, ("d", sl, sl.count))

    def sbuf(self, name, shape, dt):
        return self.es.enter_context(self.nc.sbuf_tensor(name, shape, dt))

    def psum(self, name, shape, dt):
        return self.es.enter_context(self.nc.psum_tensor(name, shape, dt))

    def _engsem(self, e, idx):
        ch = (idx - 1) // SEMCH
        while len(self.sems[e]) <= ch:
            self.sems[e].append(self.new_sem(f"e_{e}"))
        return self.sems[e][ch], idx - ch * SEMCH

    def _wait(self, e, ev):
        if ev[0] == "e":
            _, e2, idx = ev
            if e2 == e and e == "pe":
                return
            key = ("e", e2)
            if self.seen[e].get(key, 0) >= idx:
                return
            self.seen[e][key] = idx
            sem, val = self._engsem(e2, idx)
        else:
            _, sl, val = ev
            key = ("d", id(sl))
            if self.seen[e].get(key, 0) >= val:
                return
            self.seen[e][key] = val
            sem = sl.sem
        self.eng[e].wait_ge(sem, val)

    def _deps(self, e, reads, writes):
        for t in reads:
            if t.w is not None:
                self._wait(e, t.w)
            if t.x:
                for ev in t.r:
                    if not (ev[0] == "e" and ev[1] == e):
                        self._wait(e, ev)
        for t in writes:
            if t.w is not None:
                self._wait(e, t.w)
            for ev in t.r:
                self._wait(e, ev)

    def _commit(self, ev, reads, writes):
        for t in reads:
            t.r.append(ev)
            if len(t.r) > 24:
                t.r = t.r[-24:] if False else t.r
        for t in writes:
            t.w = ev
            t.r = []

    def op(self, e, fn, reads=(), writes=()):
        self._deps(e, reads, writes)
        ins = fn(self.eng[e])
        self.cnt[e] += 1
        idx = self.cnt[e]
        sem, _ = self._engsem(e, idx)
        ins.then_inc(sem, 1)
        self._commit(("e", e, idx), reads, writes)

    def group(self, e, fns, reads=(), writes=()):
        self._deps(e, reads, writes)
        ins = None
        for fn in fns:
            ins = fn(self.eng[e])
        self.cnt[e] += 1
        idx = self.cnt[e]
        sem, _ = self._engsem(e, idx)
        ins.then_inc(sem, 1)
        self._commit(("e", e, idx), reads, writes)

    def dma(self, e, pairs, reads, writes, slot, **kw):
        self._deps(e, reads, writes)
        for (o, i) in pairs:
            self.eng[e].dma_start(out=o, in_=i, **kw).then_inc(slot.sem, 16)
            slot.count += 16
        self._commit(("d", slot, slot.count), reads, writes)

    def finish(self, slots):
        for sl in slots:
            if sl.count:
                self.eng["sp"].wait_ge(sl.sem, sl.count)


def rev(ap2d):
    (ps, pn), (fs, fn) = ap2d.ap
    return bass.AP(ap2d.tensor, ap2d.offset + (fn - 1) * fs, [[ps, pn], [-fs, fn]])


NA_SCALE = 0.125
MLA_SCALE = 192 ** -0.5
NEG = -30000.0
DBG = {}


def build(nu, layers, mixers=True):
    b = Builder()
    nc = b.nc
    ncols = 1 + nu
    L = len(layers)
    TM = 1024

    def din(name, shape, dt=F32):
        return nc.dram_tensor(name, list(shape), dt, kind="ExternalInput").ap()

    def dout(name, shape, dt=F32):
        return nc.dram_tensor(name, list(shape), dt, kind="ExternalOutput").ap()

    xc = din("xc", [nu, NCH, 128, 512])
    xl = din("xl", [nu, NCH, 128, 1024])
    yc = dout("yc", [nu, NCH, 128, 512])
    yl = dout("yl", [nu, NCH, 128, 1024])
    condT = din("condT", [128, NCH * ncols])
    cst = din("cst", [128, 18])
    cosd = din("cosd", [128, 1024])
    sind = din("sind", [128, 1024])
    matd = din("matd", [128, 3 * 128])
    W = {}
    O = {}
    for l in layers:
        W[f"wmod{l}"] = din(f"wmod{l}", [36, 128, 16 * 512])
        W[f"bmod{l}"] = din(f"bmod{l}", [128, 144])
        W[f"gain{l}"] = din(f"gain{l}", [128, 48])
        for f in range(2):
            W[f"win{l}_{f}"] = din(f"win{l}_{f}", [NJ, 128, 16 * 256])
            W[f"wout{l}_{f}"] = din(f"wout{l}_{f}", [11, 128, 4 * 2048])
        if not mixers:
            continue
        if l % 2 == 0:
            W[f"wab{l}"] = din(f"wab{l}", [40, 128, 2048])
            W[f"wabo{l}"] = din(f"wabo{l}", [16, 128, 2048])
            W[f"bd{l}"] = din(f"bd{l}", [8, 128, 512])
            W[f"lc{l}"] = din(f"lc{l}", [128, 8 * 11])
            W[f"bz{l}"] = din(f"bz{l}", [16, 15, 64, 64])
            W[f"st{l}"] = din(f"st{l}", [nu, 128, 16])
            W[f"nakc{l}"] = din(f"nakc{l}", [nu, 8, 128, 512])
            W[f"navc{l}"] = din(f"navc{l}", [nu, 4, 128, 1024])
            O[f"ost{l}"] = dout(f"ost{l}", [nu, 128, 32])
            O[f"onak{l}"] = dout(f"onak{l}", [nu, 8, 128, 512])
            O[f"onav{l}"] = dout(f"onav{l}", [nu, 8, 128, 512])
        else:
            W[f"wcd{l}"] = din(f"wcd{l}", [23, 128, 2048])
            W[f"wvd{l}"] = din(f"wvd{l}", [128, 16 * 512])
            W[f"wcdo{l}"] = din(f"wcdo{l}", [16, 128, 2048])
            W[f"cdc{l}"] = din(f"cdc{l}", [128, 10])
            W[f"wuqn{l}"] = din(f"wuqn{l}", [128, 4 * 1024])
            W[f"wuqr{l}"] = din(f"wuqr{l}", [128, 4 * 512])
            W[f"wuk{l}"] = din(f"wuk{l}", [128, 4 * 1024])
            W[f"wuv{l}"] = din(f"wuv{l}", [128, 4 * 1024])
            W[f"gqk{l}"] = din(f"gqk{l}", [nu, 4, 128, 512])
            W[f"gqv{l}"] = din(f"gqv{l}", [nu, 4, 128, 512])
            W[f"mlc{l}"] = din(f"mlc{l}", [nu, 4, 128, 512])
            W[f"mlr{l}"] = din(f"mlr{l}", [nu, 128, 512])
            O[f"ogk{l}"] = dout(f"ogk{l}", [nu, 4, 64, 512])
            O[f"ogv{l}"] = dout(f"ogv{l}", [nu, 2, 128, 512])
            O[f"omc{l}"] = dout(f"omc{l}", [nu, 4, 128, 512])
            O[f"omr{l}"] = dout(f"omr{l}", [nu, 64, 512])

    X = b.sbuf("X", [128, NCH, TM], F32)
    H = b.sbuf("H", [128, NCH, TM], BF16)
    RS = b.sbuf("RS", [128, TM], F32)
    ones = b.sbuf("ones", [128, 128], BF16)
    MATS = b.sbuf("MATS", [128, 384], BF16)
    perm = MATS[:, 0:128]
    bones = MATS[:, 128:256]
    HM = b.sbuf("HM", [128, 2], F32)
    COS = b.sbuf("COS", [128, 1024], F32)
    SIN = b.sbuf("SIN", [128, 1024], F32)
    DER = b.sbuf("DER", [128, L * 3 * ncols * 3 * NCH], F32)
    GAIN = b.sbuf("GAIN", [128, L, 48], F32)
    FG = b.sbuf("FG", [128, NCH], F32)
    CT = b.sbuf("CT", [128, NCH * ncols], F32)
    CB = b.sbuf("CB", [128, NCH * ncols], BF16)
    ABYTES = 92160
    AR = b.sbuf("ARENA", [128, ABYTES // 2], BF16)
    PS = [b.psum(f"ps{i}", [128, 512], F32) for i in range(8)]

    class Carver:
        def __init__(self, base=0):
            self.off = base

        def take(self, nbytes, dt=BF16, shape=None):
            assert self.off % 4 == 0
            a = AR[:, self.off // 2:(self.off + nbytes) // 2]
            self.off += nbytes
            assert self.off <= ABYTES, self.off
            if dt == F32:
                a = a.bitcast(F32)
            if shape is not None:
                names = " ".join(f"d{i}" for i in range(len(shape)))
                a = a.rearrange(f"p ({names}) -> p {names}", **{f"d{i}": s for i, s in enumerate(shape)})
            return a

    tX = [[T() for _ in range(2)] for _ in range(NCH)]
    tH = [[T() for _ in range(2)] for _ in range(NCH)]
    tRS = [T() for _ in range(2)]
    tPS = [T(True) for _ in range(8)]
    tC = T()
    tDER = T()
    allX = [tX[m][t] for m in range(NCH) for t in range(2)]
    allH = [tH[m][t] for m in range(NCH) for t in range(2)]

    def der(li, n, col, k, m):
        off = ((((li * 3 + n) * ncols + col) * 3 + k) * NCH) + m
        return DER[:, off:off + 1]

    cv0 = Carver()
    SG = [cv0.take(1024) for _ in range(2)]
    TMP = [cv0.take(2048, F32) for _ in range(2)]
    tSG = [T(), T()]
    tTMP = [T(), T()]
    BASE = cv0.off
    sMisc = b.slot("misc")
    sX = b.slot("x")
    sY = b.slot("y")
    outslots = [sY]

    b.op("dve", lambda e: e.memset(ones[:], 1.0), writes=[tC])
    pairs = [(CT[:], condT[:, :]), (FG[:], cst[:, 0:16]), (HM[:], cst[:, 16:18]), (COS[:], cosd[:, :]), (SIN[:], sind[:, :])]
    for li, l in enumerate(layers):
        pairs.append((GAIN[:, li, :], W[f"gain{l}"][:, :]))
    b.dma("sp", pairs, [], [tC], sMisc)
    sMat = b.slot("mat")
    tMat = T()
    b.dma("pool", [(MATS[:], matd[:, :])], [], [tMat], sMat)
    b.op("act", lambda e: e.activation(out=CB[:], in_=CT[:], func=AF.Silu), reads=[tC], writes=[tC])

    cv = Carver(BASE)
    WM = [cv.take(16384) for _ in range(2)]
    MODT = cv.take(144 * ncols * 4, F32)
    BM = cv.take(144 * 4, F32)
    tWM = [T(), T()]
    tMODT = T()
    tBM = T()
    sWM = [b.slot("wm"), b.slot("wm")]
    sBM = b.slot("bm")
    k = 0
    for li, l in enumerate(layers):
        pm = PS[7]
        b.dma("sp", [(BM, W[f"bmod{l}"][:, :])], [], [tBM], sBM)
        for s in range(36):
            sl = k % 2
            k += 1
            b.dma("pool", [(WM[sl], W[f"wmod{l}"][s])], [], [tWM[sl]], sWM[sl], max_dma_last_dim=8192)
            fns = []
            for q in range(4):
                oc = 4 * s + q
                for kc in range(16):
                    fns.append(lambda e, q=q, kc=kc, oc=oc, sl=sl: e.matmul(
                        pm[:, oc * ncols:(oc + 1) * ncols],
                        WM[sl][:, kc * 512 + q * 128: kc * 512 + (q + 1) * 128],
                        CB[:, kc * ncols:(kc + 1) * ncols], start=(kc == 0), stop=(kc == 15)))
            b.group("pe", fns, reads=[tWM[sl], tC], writes=[tPS[7]])
        for col in range(ncols):
            b.op("dve", lambda e, col=col: e.tensor_tensor(
                out=MODT[:, col:144 * ncols:ncols], in0=pm[:, col:144 * ncols:ncols], in1=BM, op=ALU.add),
                reads=[tPS[7], tBM], writes=[tMODT])
        for n in range(3):
            for col in range(ncols):
                sh0 = ((3 * n + 0) * 16) * ncols + col
                sc0 = ((3 * n + 1) * 16) * ncols + col
                g0 = ((3 * n + 2) * 16) * ncols + col
                o = ((li * 3 + n) * ncols + col) * 3 * NCH
                b.op("dve", lambda e, sc0=sc0, o=o, n=n, li=li: e.scalar_tensor_tensor(
                    out=DER[:, o:o + NCH], in0=MODT[:, sc0:sc0 + 15 * ncols + 1:ncols], scalar=1.0,
                    in1=GAIN[:, li, n * 16:(n + 1) * 16], op0=ALU.add, op1=ALU.mult),
                    reads=[tMODT, tC], writes=[tDER])
                b.op("dve", lambda e, g0=g0, o=o, n=n: e.tensor_scalar(
                    out=DER[:, o + NCH:o + 2 * NCH], in0=MODT[:, g0:g0 + 15 * ncols + 1:ncols],
                    scalar1=(1.0 if n == 1 else 0.5), scalar2=None, op0=ALU.mult),
                    reads=[tMODT], writes=[tDER])
                b.op("dve", lambda e, sh0=sh0, o=o: e.tensor_copy(
                    out=DER[:, o + 2 * NCH:o + 3 * NCH], in_=MODT[:, sh0:sh0 + 15 * ncols + 1:ncols]),
                    reads=[tMODT], writes=[tDER])

    def TS(t):
        return slice(t * 512, (t + 1) * 512)

    def norm_rstd(nt):
        for t in range(nt):
            pb = 6
            for m in range(NCH):
                b.op("act", lambda e, m=m: e.activation(out=SG[m % 2], in_=X[:, m, TS(t)], func=AF.Square),
                     reads=[tX[m][t]], writes=[tSG[m % 2]])
                b.group("pe", [lambda e, m=m: e.matmul(PS[pb][:], ones[:], SG[m % 2], start=(m == 0), stop=(m == 15))],
                        reads=[tSG[m % 2], tC], writes=[tPS[pb]])
            b.op("act", lambda e: e.activation(out=RS[:, TS(t)], in_=PS[pb][:], func=AF.Sqrt, bias=EPS, scale=1.0 / D),
                 reads=[tPS[pb]], writes=[tRS[t]])
            b.op("dve", lambda e: e.reciprocal(out=RS[:, TS(t)], in_=RS[:, TS(t)]), reads=[tRS[t]], writes=[tRS[t]])

    def adaln(li, n, col, nt):
        norm_rstd(nt)
        for t in range(nt):
            for m in range(NCH):
                b.op("dve", lambda e, m=m: e.tensor_tensor(out=TMP[m % 2], in0=X[:, m, TS(t)], in1=RS[:, TS(t)], op=ALU.mult),
                     reads=[tX[m][t], tRS[t]], writes=[tTMP[m % 2]])
                b.op("act", lambda e, m=m: e.activation(
                    out=H[:, m, TS(t)], in_=TMP[m % 2], func=AF.Identity, bias=der(li, n, col, 2, m), scale=der(li, n, col, 0, m)),
                    reads=[tTMP[m % 2], tDER], writes=[tH[m][t]])

    pcount = [0]

    def ffn(li, l, f, col, nt):
        n = 0 if f == 0 else 2
        Tn = nt * 512
        b.barrier()
        adaln(li, n, col, nt)
        cv = Carver(BASE)
        WIN = [cv.take(8192) for _ in range(4)]
        WOUT = [cv.take(16384) for _ in range(2)]
        HID = cv.take(8192)
        tWIN = [T() for _ in range(4)]
        tWOUT = [T(), T()]
        tHID = [T() for _ in range(2)]
        win = W[f"win{l}_{f}"]
        wout = W[f"wout{l}_{f}"]
        for g in range(11):
            gs = g % 2
            b.dma("pool", [(WOUT[gs], wout[g])], [], [tWOUT[gs]], sFW[2 + gs], max_dma_last_dim=8192)
            for c in range(4):
                j = 4 * g + c
                ws = j % 4
                b.dma("pool", [(WIN[ws], win[j])], [], [tWIN[ws]], sFWI[ws], max_dma_last_dim=8192)
                for t in range(nt):
                    kk = pcount[0]
                    pcount[0] += 1
                    pg = kk % 2
                    pu = 2 + kk % 2
                    rd = [tWIN[ws]] + [tH[m][t] for m in range(NCH)]
                    b.group("pe", [lambda e, kc=kc: e.matmul(PS[pg][:], WIN[ws][:, kc * 256: kc * 256 + 128], H[:, kc, TS(t)],
                                                              start=(kc == 0), stop=(kc == 15)) for kc in range(16)],
                            reads=rd, writes=[tPS[pg]])
                    b.group("pe", [lambda e, kc=kc: e.matmul(PS[pu][:], WIN[ws][:, kc * 256 + 128: kc * 256 + 256], H[:, kc, TS(t)],
                                                              start=(kc == 0), stop=(kc == 15)) for kc in range(16)],
                            reads=rd, writes=[tPS[pu]])
                    b.op("act", lambda e: e.activation(out=SG[kk % 2], in_=PS[pg][:], func=AF.Silu),
                         reads=[tPS[pg]], writes=[tSG[kk % 2]])
                    b.op("dve", lambda e: e.tensor_tensor(
                        out=HID[:, c * Tn + t * 512: c * Tn + (t + 1) * 512], in0=SG[kk % 2], in1=PS[pu][:], op=ALU.mult),
                        reads=[tSG[kk % 2], tPS[pu]], writes=[tHID[t]])
            for m in range(NCH):
                for t in range(nt):
                    kk = pcount[0]
                    pcount[0] += 1
                    po = 4 + kk % 2
                    b.group("pe", [lambda e, c=c: e.matmul(PS[po][:], WOUT[gs][:, c * 2048 + m * 128: c * 2048 + (m + 1) * 128],
                                                            HID[:, c * Tn + t * 512: c * Tn + (t + 1) * 512],
                                                            start=(c == 0), stop=(c == 3)) for c in range(4)],
                            reads=[tWOUT[gs], tHID[t]], writes=[tPS[po]])
                    b.op("dve", lambda e: e.scalar_tensor_tensor(
                        out=X[:, m, TS(t)], in0=PS[po][:], scalar=der(li, n, col, 1, m), in1=X[:, m, TS(t)],
                        op0=ALU.mult, op1=ALU.add), reads=[tPS[po], tDER, tX[m][t]], writes=[tX[m][t]])

    sFW = [b.slot("fw") for _ in range(4)]
    sFWI = [b.slot("fwi") for _ in range(4)]
    sWP = [b.slot("wp") for _ in range(3)]
    sIO = [b.slot("io") for _ in range(6)]
    wpk = [0]

    def proj(WP, tWP, wchunk, nt, evac, tiles=None):
        sl = wpk[0] % len(WP)
        wpk[0] += 1
        b.dma("pool", [(WP[sl], wchunk)], [], [tWP[sl]], sWP[sl], max_dma_last_dim=8192)
        for t in (range(nt) if tiles is None else tiles):
            kk = pcount[0]
            pcount[0] += 1
            pb = kk % 2
            b.group("pe", [lambda e, kc=kc: e.matmul(PS[pb][:], WP[sl][:, kc * 128:(kc + 1) * 128], H[:, kc, TS(t)],
                                                      start=(kc == 0), stop=(kc == 15)) for kc in range(16)],
                    reads=[tWP[sl]] + [tH[m][t] for m in range(NCH)], writes=[tPS[pb]])
            evac(pb, t)

    def outproj(li, col, nt, Y, tY, WP, tWP, wo, kcs=range(16)):
        for m in range(NCH):
            sl = wpk[0] % len(WP)
            wpk[0] += 1
            b.dma("pool", [(WP[sl], wo[m])], [], [tWP[sl]], sWP[sl], max_dma_last_dim=8192)
            for t in range(nt):
                kk = pcount[0]
                pcount[0] += 1
                po = 4 + kk % 2
                k0 = kcs[0]
                b.group("pe", [lambda e, kc=kc: e.matmul(PS[po][:], WP[sl][:, kc * 128:(kc + 1) * 128], Y[:, kc - k0, TS(t)],
                                                          start=(kc == kcs[0]), stop=(kc == kcs[-1])) for kc in kcs],
                        reads=[tWP[sl], tY], writes=[tPS[po]])
                b.op("dve", lambda e: e.scalar_tensor_tensor(
                    out=X[:, m, TS(t)], in0=PS[po][:], scalar=der(li, 1, col, 1, m), in1=X[:, m, TS(t)],
                    op0=ALU.mult, op1=ALU.add), reads=[tPS[po], tDER, tX[m][t]], writes=[tX[m][t]])

    def attention(qparts, kparts, vfn, nkb, q0s, scale, rows, ydst, rtiles, wtile, PT, tPT, RD, tRD, tab=None):
        its = [(q0, n, kb) for (q0, n) in q0s for kb in range(nkb)]
        SBK = [0, 1, 4, 7]
        LA = 3

        def qk(idx):
            q0, n, kb = its[idx]
            sb = SBK[idx % 4]
            b.group("pe", [lambda e, i=i: e.matmul(PS[sb][:, 0:n], kparts[i](kb), qparts[i](q0, n),
                                                    start=(i == 0), stop=(i == len(qparts) - 1)) for i in range(len(qparts))],
                    reads=rtiles, writes=[tPS[sb]])

        for j in range(min(LA, len(its))):
            qk(j)
        for idx, (q0, n, kb) in enumerate(its):
            if idx + LA < len(its):
                qk(idx + LA)
            sb = SBK[idx % 4]
            kk = idx
            pt = PT[kk % 2]
            tb_ = tab(kb, q0, n) if tab is not None else None
            if tb_ is not None:
                tap, ttile = tb_
                b.op("dve", lambda e: e.scalar_tensor_tensor(out=TMP[kk % 2][:, 0:n], in0=PS[sb][:, 0:n], scalar=scale,
                                                             in1=tap, op0=ALU.mult, op1=ALU.add),
                     reads=[tPS[sb], ttile], writes=[tTMP[kk % 2]])
                b.op("act", lambda e: e.activation(out=pt[:, 0:n], in_=TMP[kk % 2][:, 0:n], func=AF.Exp),
                     reads=[tTMP[kk % 2]], writes=[tPT[kk % 2]])
            else:
                b.op("act", lambda e: e.activation(out=pt[:, 0:n], in_=PS[sb][:, 0:n], func=AF.Exp, scale=scale),
                     reads=[tPS[sb]], writes=[tPT[kk % 2]])
            b.group("pe", [lambda e: e.matmul(PS[2][:, 0:n], vfn(kb), pt[:, 0:n], start=(kb == 0), stop=(kb == nkb - 1)),
                           lambda e: e.matmul(PS[3][:, 0:n], ones[:], pt[:, 0:n], start=(kb == 0), stop=(kb == nkb - 1))],
                    reads=[tPT[kk % 2], tC] + rtiles, writes=[tPS[2], tPS[3]])
            if kb == nkb - 1:
                b.op("dve", lambda e: e.reciprocal(out=RD[rows, 0:n], in_=PS[3][rows, 0:n]), reads=[tPS[3]], writes=[tRD])
                b.op("dve", lambda e: e.tensor_tensor(out=ydst(q0, n), in0=PS[2][rows, 0:n], in1=RD[rows, 0:n], op=ALU.mult),
                     reads=[tPS[2], tRD], writes=[wtile])

    def mixer_ab(li, l, kind, ui, col, nt):
        Tn = nt * 512
        lat = kind == "lat"
        seqs = [(0, 1024)] if lat else [(0, 256), (256, 512)]
        b.barrier()
        adaln(li, 1, col, nt)
        cv = Carver(BASE)
        Y = cv.take(16 * Tn * 2, BF16, [16, Tn])
        tY = T()
        WP = [cv.take(4096) for _ in range(3)]
        tWP = [T() for _ in range(3)]
        LC = cv.take(8 * 11 * 4, F32, [8, 11])
        NSP = cv.take(8 * 4 * 4, F32, [8, 4])
        ST = cv.take(64, F32)
        SO = cv.take(128, F32)
        tLC = T()
        tSO = T()
        wab = W[f"wab{l}"]
        prs = [(LC, W[f"lc{l}"][:, :].rearrange("p (c k) -> p c k", k=11))]
        if lat:
            prs.append((ST, W[f"st{l}"][ui]))
        b.dma("sp", prs, [], [tLC], sIO[0])
        if DBG.get('skip_nsp'):
            return
        b.op("act", lambda e: e.activation(out=NSP[:, :, 0:2], in_=LC[:, :, 9:11], func=AF.Exp, scale=-1.0), reads=[tLC], writes=[tLC])
        b.op("act", lambda e: e.activation(out=NSP[:, :, 0:2], in_=NSP[:, :, 0:2], func=AF.Ln, bias=1.0), reads=[tLC], writes=[tLC])
        b.op("dve", lambda e: e.tensor_scalar(out=NSP[:, :, 2:4], in0=NSP[:, :, 0:2], scalar1=-16.0, scalar2=None, op0=ALU.mult), reads=[tLC], writes=[tLC])
        b.op("dve", lambda e: e.tensor_scalar(out=NSP[:, :, 0:2], in0=NSP[:, :, 0:2], scalar1=-8.0, scalar2=None, op0=ALU.mult), reads=[tLC], writes=[tLC])
        mark = cv.off
        XA2 = [cv.take(Tn * 4, F32) for _ in range(2)]
        GA2 = [cv.take(Tn * 4, F32) for _ in range(2)]
        XC = cv.take(Tn * 4, F32)
        R = cv.take(Tn * 4, F32)
        GI = cv.take(Tn * 4, F32)
        HF = cv.take(Tn * 4, F32)
        HB = cv.take(Tn * 4, F32)
        XCB = cv.take(Tn * 2)
        BD = cv.take(1024, BF16, [4, 128])
        tXA2 = [T(), T()]
        tGA2 = [T(), T()]
        tXC, tR, tGI, tHF, tHB, tXCB, tBD = [T() for _ in range(7)]
        for c in range(8 if not DBG.get('skip_lru') else 0):
            XA, tXA, GA, tGA = XA2[c % 2], tXA2[c % 2], GA2[c % 2], tGA2[c % 2]
            Sb, tS = XA, tXA
            b.dma("pool", [(BD, W[f"bd{l}"][c].rearrange("p (k n) -> p k n", n=128))], [], [tBD], sIO[1], max_dma_last_dim=8192)
            proj(WP, tWP, wab[c], nt, lambda pb, t: b.op("act", lambda e: e.activation(out=XA[:, TS(t)], in_=PS[pb][:], func=AF.Identity),
                                                       reads=[tPS[pb]], writes=[tXA]))
            proj(WP, tWP, wab[8 + c], nt, lambda pb, t: b.op("act", lambda e: e.activation(out=GA[:, TS(t)], in_=PS[pb][:], func=AF.Identity),
                                                           reads=[tPS[pb]], writes=[tGA]))
            for (s0, s1) in seqs:
                b.op("act", lambda e: e.activation(out=XC[:, s0:s1], in_=XA[:, s0:s1], func=AF.Identity,
                                                   bias=LC[:, c, 4:5], scale=LC[:, c, 2:3]), reads=[tXA, tLC], writes=[tXC])
                for (jj, sh) in ((0, -2), (1, -1), (3, 1)):
                    if sh < 0:
                        o_ = XC[:, s0 - sh:s1]
                        i_ = XA[:, s0:s1 + sh]
                    else:
                        o_ = XC[:, s0:s1 - sh]
                        i_ = XA[:, s0 + sh:s1]
                    b.op("dve", lambda e, o_=o_, i_=i_, jj=jj: e.scalar_tensor_tensor(
                        out=o_, in0=i_, scalar=LC[:, c, jj:jj + 1], in1=o_, op0=ALU.mult, op1=ALU.add),
                        reads=[tXA, tXC, tLC], writes=[tXC])
            b.op("dve", lambda e: e.tensor_copy(out=XCB, in_=XC), reads=[tXC], writes=[tXCB])
            for d in range(2):
                Hd, tHd = (HF, tHF) if d == 0 else (HB, tHB)
                for t in range(nt):
                    for (a, dst, tdst, bcol) in ((0, R, tR, 5 + d), (1, GI, tGI, 7 + d)):
                        kk = pcount[0]
                        pcount[0] += 1
                        pb = kk % 2
                        b.group("pe", [lambda e: e.matmul(PS[pb][:], BD[:, d * 2 + a, :], XCB[:, TS(t)], start=True, stop=True)],
                                reads=[tBD, tXCB], writes=[tPS[pb]])
                        b.op("act", lambda e: e.activation(out=dst[:, TS(t)], in_=PS[pb][:], func=AF.Sigmoid, bias=LC[:, c, bcol:bcol + 1]),
                             reads=[tPS[pb], tLC], writes=[tdst])
                b.op("act", lambda e: e.activation(out=Sb, in_=R, func=AF.Exp, scale=NSP[:, c, 2 + d:3 + d]), reads=[tR, tLC, tXC], writes=[tS])
                b.op("act", lambda e: e.activation(out=Sb, in_=Sb, func=AF.Sqrt, bias=1.0, scale=-1.0), reads=[tS], writes=[tS])
                b.op("act", lambda e: e.activation(out=R, in_=R, func=AF.Exp, scale=NSP[:, c, d:d + 1]), reads=[tR, tLC], writes=[tR])
                b.op("dve", lambda e: e.tensor_tensor(out=GI, in0=GI, in1=Sb, op=ALU.mult), reads=[tGI, tS], writes=[tGI])
                b.op("dve", lambda e: e.tensor_tensor(out=GI, in0=GI, in1=XC, op=ALU.mult), reads=[tGI, tXC], writes=[tGI])
                for si, (s0, s1) in enumerate(seqs):
                    init = ST[:, c * 2 + d:c * 2 + d + 1] if lat else 0.0
                    if d == 0:
                        b.op("dve", lambda e: e.tensor_tensor_scan(out=Hd[:, s0:s1], data0=R[:, s0:s1], data1=GI[:, s0:s1],
                                                                   initial=init, op0=ALU.mult, op1=ALU.add),
                             reads=[tR, tGI, tLC], writes=[tHd])
                    else:
                        b.op("dve", lambda e: e.tensor_tensor_scan(out=rev(Hd[:, s0:s1]), data0=rev(R[:, s0:s1]), data1=rev(GI[:, s0:s1]),
                                                                   initial=init, op0=ALU.mult, op1=ALU.add),
                             reads=[tR, tGI, tLC], writes=[tHd])
                    if not lat:
                        src = Hd[:, s1 - 1:s1] if d == 0 else Hd[:, s0:s0 + 1]
                        k_ = (c * 2 + d) * 2 + si
                        b.op("act", lambda e: e.activation(out=SO[:, k_:k_ + 1], in_=src, func=AF.Identity), reads=[tHd], writes=[tSO])
            b.op("act", lambda e: e.activation(out=R, in_=GA, func=AF.Square), reads=[tGA, tGI], writes=[tR])
            b.op("dve", lambda e: e.tensor_scalar(out=R, in0=R, scalar1=0.044715, scalar2=1.0, op0=ALU.mult, op1=ALU.add), reads=[tR], writes=[tR])
            b.op("dve", lambda e: e.tensor_tensor(out=R, in0=R, in1=GA, op=ALU.mult), reads=[tR, tGA], writes=[tR])
            b.op("act", lambda e: e.activation(out=R, in_=R, func=AF.Sigmoid, scale=1.5957691216057308), reads=[tR], writes=[tR])
            b.op("dve", lambda e: e.tensor_tensor(out=R, in0=R, in1=GA, op=ALU.mult), reads=[tR, tGA], writes=[tR])
            b.op("dve", lambda e: e.tensor_tensor(out=HF, in0=HF, in1=HB, op=ALU.add), reads=[tHF, tHB], writes=[tHF])
            b.op("dve", lambda e: e.tensor_tensor(out=Y[:, c, :], in0=HF, in1=R, op=ALU.mult), reads=[tHF, tR], writes=[tY])
        if not lat:
            b.dma("sp", [(O[f"ost{l}"][ui], SO)], [tSO], [], sOut[0])
        b.barrier()
        cv.off = mark
        QT = cv.take(Tn * 2)
        KT = cv.take((Tn + 512) * 2)
        QM = cv.take(Tn * 2)
        KF = cv.take(Tn * 4, F32) if not lat else None
        VF = cv.take(Tn * 4, F32) if not lat else None
        nvb = (Tn + (512 if lat else 0)) // 128
        VT = cv.take(nvb * 128 * 2, BF16, [nvb, 128])
        PT = [cv.take(1024) for _ in range(2)]
        RD = cv.take(2048, F32)
        tQT, tKT, tQM, tKF, tVF, tVT, tRD = [T() for _ in range(7)]
        tPT = [T(), T()]
        if lat:
            KC32 = cv.take(2048, F32)
            VC32 = cv.take(2048, F32, [4, 128])
            tKC32 = T()
            TAB = [cv.take(2048, BF16) for _ in range(8)]
            tTAB = [T() for _ in range(8)]
            for kb in range(8):
                b.op("pool", lambda e, kb=kb: e.memset(TAB[kb], NEG), writes=[tTAB[kb]])
        for c in range(8 if not DBG.get('skip_qkv') else 0):
            proj(WP, tWP, wab[16 + c], nt, lambda pb, t: b.op("act", lambda e: e.activation(out=QT[:, TS(t)], in_=PS[pb][:], func=AF.Identity),
                                                            reads=[tPS[pb]], writes=[tQT]))

            def evk(pb, t):
                if not lat:
                    b.op("act", lambda e: e.activation(out=KF[:, TS(t)], in_=PS[pb][:], func=AF.Identity), reads=[tPS[pb]], writes=[tKF])
                    b.op("dve", lambda e: e.tensor_copy(out=KT[:, TS(t)], in_=KF[:, TS(t)]), reads=[tKF], writes=[tKT])
                else:
                    b.op("dve", lambda e: e.tensor_copy(out=KT[:, TS(t)], in_=PS[pb][:]), reads=[tPS[pb]], writes=[tKT])
            proj(WP, tWP, wab[24 + c], nt, evk)
            if not lat:
                proj(WP, tWP, wab[32 + c], nt, lambda pb, t: b.op("act", lambda e: e.activation(out=VF[:, TS(t)], in_=PS[pb][:], func=AF.Identity),
                                                                reads=[tPS[pb]], writes=[tVF]))
                b.dma("sp", [(O[f"onak{l}"][ui, c], KF), (O[f"onav{l}"][ui, c], VF)], [tKF, tVF], [], sOut[1])
            sl = wpk[0] % 3
            wpk[0] += 1
            b.dma("pool", [(WP[sl], wab[32 + c])], [], [tWP[sl]], sWP[sl], max_dma_last_dim=8192)
            for tb in range(Tn // 128 if not DBG.get('skip_vt') else 0):
                kk = pcount[0]
                pcount[0] += 1
                pb = kk % 2
                t = tb // 4
                b.group("pe", [lambda e, kc=kc: e.matmul(PS[pb][:, 0:128], H[:, kc, tb * 128:(tb + 1) * 128], WP[sl][:, kc * 128:(kc + 1) * 128],
                                                          start=(kc == 0), stop=(kc == 15)) for kc in range(16)],
                        reads=[tWP[sl]] + [tH[m][t] for m in range(NCH)], writes=[tPS[pb]])
                b.op("act", lambda e: e.activation(out=VT[:, tb, :], in_=PS[pb][:, 0:128], func=AF.Identity), reads=[tPS[pb]], writes=[tVT])
            if lat:
                b.dma("sp", [(KC32, W[f"nakc{l}"][ui, c]),
                             (VC32, W[f"navc{l}"][ui][:, :, c * 128:(c + 1) * 128].rearrange("k p n -> p k n"))],
                      [], [tKC32], sIO[2])
                b.op("dve", lambda e: e.tensor_copy(out=KT[:, Tn:Tn + 512], in_=KC32), reads=[tKC32], writes=[tKT])
                b.op("dve", lambda e: e.tensor_copy(out=VT[:, 8:12, :], in_=VC32), reads=[tKC32], writes=[tVT])
            for hh in range(2 if not DBG.get('skip_att') else 0):
                rows = slice(hh * 64, (hh + 1) * 64)
                b.op("dve", lambda e: e.tensor_scalar(out=QM, in0=QT, scalar1=HM[:, hh:hh + 1], scalar2=None, op0=ALU.mult),
                     reads=[tQT, tC], writes=[tQM])
                if lat:
                    hd = 2 * c + hh
                    bz = W[f"bz{l}"]
                    for i in range(16):
                        r_lo = 0 if i <= 7 else i - 3
                        r_hi = 15 if i >= 8 else i + 4
                        nr = r_hi - r_lo + 1
                        rr0 = r_lo - i + 7
                        kb = i // 2
                        src = bass.AP(bz.tensor, bz.offset + ((hd * 15 + rr0) * 64) * 64, [[64, 64], [4096, nr], [1, 64]])
                        dst = TAB[kb][(i % 2) * 64:(i % 2) * 64 + 64, r_lo * 64:(r_hi + 1) * 64].rearrange("p (r q) -> p r q", q=64)
                        b.dma("pool", [(dst, src)], [], [tTAB[kb]], sTAB[kb])
                    attention([lambda q0, n: QM[:, q0:q0 + n]], [lambda kb: KT[:, kb * 128:(kb + 1) * 128]],
                              lambda kb: VT[:, kb, :], 12, [(0, 512), (512, 512)], NA_SCALE, rows,
                              lambda q0, n: Y[rows, 8 + c, q0:q0 + n], [tQM, tKT, tVT], tY, PT, tPT, RD, tRD,
                              tab=lambda kb, q0, n: (TAB[kb][:, q0:q0 + n], tTAB[kb]) if kb < 8 else None)
                else:
                    for (s0, s1) in seqs:
                        kb0 = s0 // 128
                        attention([lambda q0, n: QM[:, q0:q0 + n]], [lambda kb: KT[:, (kb0 + kb) * 128:(kb0 + kb + 1) * 128]],
                                  lambda kb: VT[:, kb0 + kb, :], 2, [(s0, 256)], NA_SCALE, rows,
                                  lambda q0, n: Y[rows, 8 + c, q0:q0 + n], [tQM, tKT, tVT], tY, PT, tPT, RD, tRD)
        if not DBG.get('skip_outproj'):
            outproj(li, col, nt, Y, tY, WP, tWP, W[f"wabo{l}"])

    sOut = [b.slot("out") for _ in range(6)]
    outslots.extend(sOut)
    sTAB = [b.slot("tab") for _ in range(8)]

    def mixer_cd(li, l, kind, ui, col, nt):
        Tn = nt * 512
        lat = kind == "lat"
        Tk = Tn + (512 if lat else 0)
        seqs = [(0, 1024)] if lat else [(0, 256), (256, 512)]
        wcd = W[f"wcd{l}"]
        b.barrier()
        adaln(li, 1, col, nt)
        cv = Carver(BASE)
        Y8 = cv.take(8 * Tn * 2, BF16, [8, Tn])
        tY = T()
        WP = [cv.take(4096) for _ in range(3)]
        tWP = [T() for _ in range(3)]
        CDC = cv.take(40, F32)
        tCDC = T()
        b.dma("sp", [(CDC, W[f"cdc{l}"][:, :])], [], [tCDC], sIO[0])
        mark = cv.off

        def rope_from(src, t0, n, dst, rd, wr):
            b.op("act", lambda e: e.activation(out=SG[0][:, 0:n], in_=src, func=AF.Identity), reads=rd, writes=[tSG[0]])
            b.group("pe", [lambda e: e.matmul(PS[5][:, 0:n], perm, SG[0][:, 0:n], start=True, stop=True)], reads=[tSG[0], tMat], writes=[tPS[5]])
            b.op("dve", lambda e: e.tensor_tensor(out=TMP[0][:, 0:n], in0=src, in1=COS[:, t0:t0 + n], op=ALU.mult), reads=rd + [tC], writes=[tTMP[0]])
            b.op("dve", lambda e: e.tensor_tensor(out=TMP[1][:, 0:n], in0=PS[5][:, 0:n], in1=SIN[:, t0:t0 + n], op=ALU.mult), reads=[tPS[5], tC], writes=[tTMP[1]])
            b.op("dve", lambda e: e.tensor_tensor(out=dst, in0=TMP[0][:, 0:n], in1=TMP[1][:, 0:n], op=ALU.add), reads=[tTMP[0], tTMP[1]], writes=wr)

        K2 = [cv.take(Tk * 2) for _ in range(4)]
        tK2 = [T() for _ in range(4)]
        nkbT = Tk // 128
        VD = cv.take(nkbT * 512 * 2, BF16, [nkbT, 512])
        tVD = T()
        WV = cv.take(16384)
        tWV = T()
        QT = cv.take(Tn * 2)
        QM = cv.take(Tn * 2)
        STG = cv.take(2048, F32)
        RSQ = cv.take(2048, F32)
        PT = [cv.take(1024) for _ in range(2)]
        RD = cv.take(2048, F32)
        tQT, tQM, tSTG, tRSQ, tRD = [T() for _ in range(5)]
        tPT = [T(), T()]

        def headnorm(pb, t, gcol, dst, wr, rope, f32dma=None):
            b.op("act", lambda e: e.activation(out=STG, in_=PS[pb][:], func=AF.Identity), reads=[tPS[pb]], writes=[tSTG])
            b.op("act", lambda e: e.activation(out=SG[1], in_=PS[pb][:], func=AF.Square), reads=[tPS[pb]], writes=[tSG[1]])
            b.group("pe", [lambda e: e.matmul(PS[6][:], bones, SG[1], start=True, stop=True)], reads=[tSG[1], tMat], writes=[tPS[6]])
            b.op("act", lambda e: e.activation(out=RSQ, in_=PS[6][:], func=AF.Sqrt, bias=EPS, scale=1.0 / 64), reads=[tPS[6]], writes=[tRSQ])
            b.op("dve", lambda e: e.reciprocal(out=RSQ, in_=RSQ), reads=[tRSQ], writes=[tRSQ])
            if rope or f32dma is not None:
                b.op("dve", lambda e: e.scalar_tensor_tensor(out=STG, in0=STG, scalar=CDC[:, gcol:gcol + 1], in1=RSQ, op0=ALU.mult, op1=ALU.mult),
                     reads=[tSTG, tRSQ, tCDC], writes=[tSTG])
                if f32dma is not None:
                    f32dma()
                if rope:
                    rope_from(STG, t * 512, 512, dst, [tSTG], wr)
                else:
                    b.op("act", lambda e: e.activation(out=dst, in_=STG, func=AF.Identity), reads=[tSTG], writes=wr)
            else:
                b.op("dve", lambda e: e.scalar_tensor_tensor(out=dst, in0=STG, scalar=CDC[:, gcol:gcol + 1], in1=RSQ, op0=ALU.mult, op1=ALU.mult),
                     reads=[tSTG, tRSQ, tCDC], writes=wr)

        for kvh in range(4):
            def evk(pb, t, kvh=kvh):
                f = None
                if not lat:
                    f = lambda: b.dma("sp", [(O[f"ogk{l}"][ui, kvh], STG[0:64, :])], [tSTG], [], sOut[2])
                headnorm(pb, t, 1, K2[kvh][:, TS(t)], [tK2[kvh]], lat, f)
            proj(WP, tWP, wcd[8 + kvh], nt, evk)
        if lat:
            b.dma("pool", [(K2[kvh][:, Tn:Tn + 512], W[f"gqk{l}"][ui, kvh]) for kvh in range(4)]
                  + [(VD[:, 8:12, :], W[f"gqv{l}"][ui].rearrange("k p n -> p k n"))], [], tK2 + [tVD], sIO[2], max_dma_last_dim=8192)
        else:
            for i in range(2):
                def evv(pb, t, i=i):
                    b.op("act", lambda e: e.activation(out=STG, in_=PS[pb][:], func=AF.Identity), reads=[tPS[pb]], writes=[tSTG])
                    b.dma("sp", [(O[f"ogv{l}"][ui, i], STG)], [tSTG], [], sOut[2])
                proj(WP, tWP, wcd[12 + i], nt, evv)
        b.dma("pool", [(WV, W[f"wvd{l}"][:, :])], [], [tWV], sIO[3], max_dma_last_dim=8192)
        for tb in range(Tn // 128):
            kk = pcount[0]
            pcount[0] += 1
            pb = kk % 2
            b.group("pe", [lambda e, kc=kc: e.matmul(PS[pb][:], H[:, kc, tb * 128:(tb + 1) * 128], WV[:, kc * 512:(kc + 1) * 512],
                                                      start=(kc == 0), stop=(kc == 15)) for kc in range(16)],
                    reads=[tWV] + [tH[m][tb // 4] for m in range(NCH)], writes=[tPS[pb]])
            b.op("act", lambda e: e.activation(out=VD[:, tb, :], in_=PS[pb][:], func=AF.Identity), reads=[tPS[pb]], writes=[tVD])
        for c in range(8):
            kvh = c // 2
            proj(WP, tWP, wcd[c], nt, lambda pb, t: headnorm(pb, t, 0, QT[:, TS(t)], [tQT], lat))
            for hh in range(2):
                rows = slice(hh * 64, (hh + 1) * 64)
                b.op("dve", lambda e: e.tensor_scalar(out=QM, in0=QT, scalar1=HM[:, hh:hh + 1], scalar2=None, op0=ALU.mult),
                     reads=[tQT, tC], writes=[tQM])
                if lat:
                    attention([lambda q0, n: QM[:, q0:q0 + n]], [lambda kb: K2[kvh][:, kb * 128:(kb + 1) * 128]],
                              lambda kb: VD[:, kb, kvh * 128:(kvh + 1) * 128], 12, [(0, 512), (512, 512)], NA_SCALE, rows,
                              lambda q0, n: Y8[rows, c, q0:q0 + n], [tQM, tK2[kvh], tVD], tY, PT, tPT, RD, tRD)
                else:
                    for (s0, s1) in seqs:
                        kb0 = s0 // 128
                        attention([lambda q0, n: QM[:, q0:q0 + n]], [lambda kb: K2[kvh][:, (kb0 + kb) * 128:(kb0 + kb + 1) * 128]],
                                  lambda kb: VD[:, kb0 + kb, kvh * 128:(kvh + 1) * 128], 2, [(s0, 256)], NA_SCALE, rows,
                                  lambda q0, n: Y8[rows, c, q0:q0 + n], [tQM, tK2[kvh], tVD], tY, PT, tPT, RD, tRD)
        outproj(li, col, nt, Y8, tY, WP, tWP, W[f"wcdo{l}"], kcs=range(0, 8))

        b.barrier()
        cv.off = mark
        STG4 = cv.take(8192, F32, [4, 512])
        QAN = cv.take(4 * Tn * 2, BF16, [4, Tn])
        CKB = cv.take(4 * Tk * 2, BF16, [4, Tk])
        KR2 = cv.take(Tk * 2)
        RS2 = cv.take(2048, F32)
        WH = cv.take(4096, BF16, [4, 4, 128])
        QN = cv.take(Tn * 2)
        QRh = cv.take(Tn * 2)
        KN = cv.take(Tk * 2)
        VM = cv.take(nkbT * 128 * 2, BF16, [nkbT, 128])
        PT = [cv.take(1024) for _ in range(2)]
        RD = cv.take(2048, F32)
        tSTG4, tQAN, tCKB, tKR2, tRS2, tWH, tQN, tQRh, tKN, tVM, tRD = [T() for _ in range(11)]
        tPT = [T(), T()]

        def norm512(wbase, gcol0, t, after):
            for i in range(4):
                def ev(pb, t_, i=i):
                    b.op("act", lambda e: e.activation(out=STG4[:, i, :], in_=PS[pb][:], func=AF.Identity), reads=[tPS[pb]], writes=[tSTG4])
                    b.op("act", lambda e: e.activation(out=SG[i % 2], in_=PS[pb][:], func=AF.Square), reads=[tPS[pb]], writes=[tSG[i % 2]])
                    b.group("pe", [lambda e: e.matmul(PS[6][:], ones[:], SG[i % 2], start=(i == 0), stop=(i == 3))],
                            reads=[tSG[i % 2], tC], writes=[tPS[6]])
                proj(WP, tWP, wcd[wbase + i], nt, ev, tiles=[t])
            b.op("act", lambda e: e.activation(out=RS2, in_=PS[6][:], func=AF.Sqrt, bias=EPS, scale=1.0 / 512), reads=[tPS[6]], writes=[tRS2])
            b.op("dve", lambda e: e.reciprocal(out=RS2, in_=RS2), reads=[tRS2], writes=[tRS2])
            for i in range(4):
                b.op("dve", lambda e: e.scalar_tensor_tensor(out=STG4[:, i, :], in0=STG4[:, i, :], scalar=CDC[:, gcol0 + i:gcol0 + i + 1], in1=RS2,
                                                             op0=ALU.mult, op1=ALU.mult), reads=[tSTG4, tRS2, tCDC], writes=[tSTG4])
                after(i)

        for t in range(nt):
            norm512(14, 2, t, lambda i: b.op("act", lambda e: e.activation(out=QAN[:, i, TS(t)], in_=STG4[:, i, :], func=AF.Identity),
                                             reads=[tSTG4], writes=[tQAN]))

            def after_ckv(i):
                if not lat:
                    b.dma("sp", [(O[f"omc{l}"][ui, i], STG4[:, i, :])], [tSTG4], [], sOut[3])
                b.op("act", lambda e: e.activation(out=CKB[:, i, TS(t)], in_=STG4[:, i, :], func=AF.Identity), reads=[tSTG4], writes=[tCKB])
            norm512(18, 6, t, after_ckv)

        def evkr(pb, t):
            b.op("act", lambda e: e.activation(out=RS2, in_=PS[pb][:], func=AF.Identity), reads=[tPS[pb]], writes=[tRS2])
            if lat:
                rope_from(RS2, t * 512, 512, KR2[:, TS(t)], [tRS2], [tKR2])
            else:
                b.dma("sp", [(O[f"omr{l}"][ui], RS2[0:64, :])], [tRS2], [], sOut[4])
                b.op("act", lambda e: e.activation(out=KR2[:, TS(t)], in_=RS2, func=AF.Identity), reads=[tRS2], writes=[tKR2])
        proj(WP, tWP, wcd[22], nt, evkr)
        if lat:
            b.dma("pool", [(CKB[:, :, Tn:Tn + 512], W[f"mlc{l}"][ui].rearrange("k p n -> p k n")), (KR2[:, Tn:Tn + 512], W[f"mlr{l}"][ui])],
                  [], [tCKB, tKR2], sIO[4], max_dma_last_dim=8192)
        wuqn = W[f"wuqn{l}"][:, :].rearrange("p (k n) -> p k n", n=1024)
        wuk = W[f"wuk{l}"][:, :].rearrange("p (k n) -> p k n", n=1024)
        wuv = W[f"wuv{l}"][:, :].rearrange("p (k n) -> p k n", n=1024)
        wuqr = W[f"wuqr{l}"][:, :].rearrange("p (k n) -> p k n", n=512)
        for h in range(8):
            hs = slice(h * 128, (h + 1) * 128)
            b.op("pool", lambda e: e.memset(WH[:, 3, :, :], 0.0), writes=[tWH])
            b.dma("pool", [(WH[:, 0, :, :], wuqn[:, :, hs]), (WH[:, 1, :, :], wuk[:, :, hs]), (WH[:, 2, :, :], wuv[:, :, hs]),
                           (WH[:, 3, :, (h % 2) * 64:(h % 2) * 64 + 64], wuqr[:, :, h * 64:(h + 1) * 64])], [], [tWH], sIO[5])
            for t in range(nt):
                for (wi, dst, tdst, rp) in ((0, QN, tQN, False), (3, QRh, tQRh, lat)):
                    kk = pcount[0]
                    pcount[0] += 1
                    pb = kk % 2
                    b.group("pe", [lambda e, kc=kc: e.matmul(PS[pb][:], WH[:, wi, kc, :], QAN[:, kc, TS(t)], start=(kc == 0), stop=(kc == 3)) for kc in range(4)],
                            reads=[tWH, tQAN], writes=[tPS[pb]])
                    if rp:
                        b.op("act", lambda e: e.activation(out=RS2, in_=PS[pb][:], func=AF.Identity), reads=[tPS[pb]], writes=[tRS2])
                        rope_from(RS2, t * 512, 512, dst[:, TS(t)], [tRS2], [tdst])
                    else:
                        b.op("act", lambda e: e.activation(out=dst[:, TS(t)], in_=PS[pb][:], func=AF.Identity), reads=[tPS[pb]], writes=[tdst])
            for kt in range(Tk // 512):
                kk = pcount[0]
                pcount[0] += 1
                pb = kk % 2
                b.group("pe", [lambda e, kc=kc: e.matmul(PS[pb][:], WH[:, 1, kc, :], CKB[:, kc, TS(kt)], start=(kc == 0), stop=(kc == 3)) for kc in range(4)],
                        reads=[tWH, tCKB], writes=[tPS[pb]])
                b.op("act", lambda e: e.activation(out=KN[:, TS(kt)], in_=PS[pb][:], func=AF.Identity), reads=[tPS[pb]], writes=[tKN])
            for kb in range(nkbT):
                kk = pcount[0]
                pcount[0] += 1
                pb = kk % 2
                b.group("pe", [lambda e, kc=kc: e.matmul(PS[pb][:, 0:128], CKB[:, kc, kb * 128:(kb + 1) * 128], WH[:, 2, kc, :], start=(kc == 0), stop=(kc == 3)) for kc in range(4)],
                        reads=[tWH, tCKB], writes=[tPS[pb]])
                b.op("act", lambda e: e.activation(out=VM[:, kb, :], in_=PS[pb][:, 0:128], func=AF.Identity), reads=[tPS[pb]], writes=[tVM])
            rows = slice(0, 128)
            rt = [tQN, tQRh, tKN, tKR2, tVM]
            if lat:
                attention([lambda q0, n: QN[:, q0:q0 + n], lambda q0, n: QRh[:, q0:q0 + n]],
                          [lambda kb: KN[:, kb * 128:(kb + 1) * 128], lambda kb: KR2[:, kb * 128:(kb + 1) * 128]],
                          lambda kb: VM[:, kb, :], 12, [(0, 512), (512, 512)], MLA_SCALE, rows,
                          lambda q0, n: Y8[:, h, q0:q0 + n], rt, tY, PT, tPT, RD, tRD)
            else:
                for (s0, s1) in seqs:
                    kb0 = s0 // 128
                    attention([lambda q0, n: QN[:, q0:q0 + n], lambda q0, n: QRh[:, q0:q0 + n]],
                              [lambda kb: KN[:, (kb0 + kb) * 128:(kb0 + kb + 1) * 128], lambda kb: KR2[:, (kb0 + kb) * 128:(kb0 + kb + 1) * 128]],
                              lambda kb: VM[:, kb0 + kb, :], 2, [(s0, 256)], MLA_SCALE, rows,
                              lambda q0, n: Y8[:, h, q0:q0 + n], rt, tY, PT, tPT, RD, tRD)
        outproj(li, col, nt, Y8, tY, WP, tWP, W[f"wcdo{l}"], kcs=range(8, 16))

    for ui in range(nu):
        for kind in ("ctx", "lat"):
            if DBG.get('only_' + ('lat' if kind == 'ctx' else 'ctx')):
                continue
            nt = 1 if kind == "ctx" else 2
            Tn = nt * 512
            col = 0 if kind == "ctx" else 1 + ui
            src = xc if kind == "ctx" else xl
            dst = yc if kind == "ctx" else yl
            b.dma("sp", [(X[:, m, 0:Tn], src[ui, m]) for m in range(NCH)], [], allX, sX)
            for li, l in enumerate(layers):
                ffn(li, l, 0, col, nt)
                if mixers:
                    if l % 2 == 0:
                        mixer_ab(li, l, kind, ui, col, nt)
                    else:
                        mixer_cd(li, l, kind, ui, col, nt)
                    ffn(li, l, 1, col, nt)
            norm_rstd(nt)
            for t in range(nt):
                for m in range(NCH):
                    b.op("dve", lambda e, m=m: e.scalar_tensor_tensor(
                        out=X[:, m, TS(t)], in0=X[:, m, TS(t)], scalar=FG[:, m:m + 1], in1=RS[:, TS(t)], op0=ALU.mult, op1=ALU.mult),
                        reads=[tX[m][t], tRS[t], tC], writes=[tX[m][t]])
            b.dma("sp", [(dst[ui, m], X[:, m, 0:Tn]) for m in range(NCH)], allX, [], sY)
    b.finish(outslots)
    return b


def _chunks(w, nchunk):
    return np.ascontiguousarray(w.reshape(16, 128, nchunk, 128).transpose(2, 1, 0, 3)).reshape(nchunk, 128, 2048)


def _rows4(w, nk, ncol):
    return np.ascontiguousarray(w.reshape(nk, 128, ncol).transpose(1, 0, 2)).reshape(128, nk * ncol)


def _const_tables():
    t = np.arange(1024)
    nf = 16
    inv = 1.0 / (10000.0 ** (np.arange(nf, dtype=np.float32) / nf))
    d = np.arange(128) % 64
    half = d // 32
    i = d % 16
    pos = np.where(half[:, None] == 0, (t // 64)[None, :], (t % 64)[None, :]).astype(np.float32)
    ang = pos * inv[i][:, None]
    cos = np.cos(ang).astype(np.float32)
    sin = np.sin(ang).astype(np.float32)
    perm = np.zeros((128, 128), np.float32)
    for m in range(128):
        j = (m % 64) % 32
        if j < 16:
            perm[m + 16, m] = -1.0
        else:
            perm[m - 16, m] = 1.0
    bones = np.zeros((128, 128), np.float32)
    bones[:64, :64] = 1.0
    bones[64:, 64:] = 1.0
    matd = np.concatenate([perm, bones, np.zeros((128, 128), np.float32)], axis=1)
    hm = np.zeros((128, 2), np.float32)
    hm[:64, 0] = 1.0
    hm[64:, 1] = 1.0
    return cos, sin, matd, hm


def _layout_common(inp, layers, mixers=True):
    out = {}
    cos, sin, matd, hm = _const_tables()
    out["cosd"], out["sind"], out["matd"] = cos, sin, matd
    fg = np.asarray(inp["final_gain"]).reshape(16, 128).T
    out["cst"] = np.ascontiguousarray(np.concatenate([fg, hm], axis=1).astype(np.float32))
    for l in layers:
        j = l // 2
        wm = np.asarray(inp["w_mod"][l])
        out[f"wmod{l}"] = np.ascontiguousarray(wm.reshape(16, 128, 36, 512).transpose(2, 1, 0, 3)).reshape(36, 128, 8192)
        out[f"bmod{l}"] = np.ascontiguousarray(np.asarray(inp["b_mod"][l]).reshape(144, 128).T)
        out[f"gain{l}"] = np.ascontiguousarray(np.asarray(inp["norm_gain"][l]).reshape(3, 16, 128).transpose(2, 0, 1)).reshape(128, 48)
        for f in range(2):
            w = np.asarray(inp["w_ffn_in"][l][f])
            g = w[:, :DFF].reshape(16, 128, NJ, 128)
            u = w[:, DFF:].reshape(16, 128, NJ, 128)
            cat = np.concatenate([g, u], axis=-1)
            out[f"win{l}_{f}"] = np.ascontiguousarray(cat.transpose(2, 1, 0, 3)).reshape(NJ, 128, 4096)
            wo = np.asarray(inp["w_ffn_out"][l][f])
            out[f"wout{l}_{f}"] = np.ascontiguousarray(wo.reshape(11, 4, 128, 2048).transpose(0, 2, 1, 3)).reshape(11, 128, 8192)
        if not mixers:
            continue
        if l % 2 == 0:
            out[f"wab{l}"] = _chunks(np.asarray(inp["w_in_ab"][j]), 40)
            out[f"wabo{l}"] = _chunks(np.asarray(inp["w_out_ab"][j]), 16)
            wa = np.asarray(inp["lru_wa"][j])
            wx = np.asarray(inp["lru_wx"][j])
            bd = np.zeros((8, 128, 4, 128), np.float32)
            for c in range(8):
                for d in range(2):
                    for a, ww in enumerate((wa, wx)):
                        bd[c, :64, d * 2 + a, :64] = ww[d, 2 * c]
                        bd[c, 64:, d * 2 + a, 64:] = ww[d, 2 * c + 1]
            out[f"bd{l}"] = bd.reshape(8, 128, 512)
            cols = [np.asarray(inp["conv_w"][j])[k] for k in range(4)] + [np.asarray(inp["conv_b"][j])]
            cols += [np.asarray(inp["lru_ba"][j])[0], np.asarray(inp["lru_ba"][j])[1],
                     np.asarray(inp["lru_bx"][j])[0], np.asarray(inp["lru_bx"][j])[1],
                     np.asarray(inp["lru_lambda"][j])[0], np.asarray(inp["lru_lambda"][j])[1]]
            lc = np.stack(cols, axis=-1)
            out[f"lc{l}"] = np.ascontiguousarray(lc.reshape(8, 128, 11).transpose(1, 0, 2)).reshape(128, 88)
            nb = np.asarray(inp["na_bias"][j])
            kc = np.arange(64)[:, None]
            qc = np.arange(64)[None, :]
            relc = np.clip(kc - qc + 15, 0, 30)
            cstart = np.clip(qc - 8, 0, 48)
            ok = (kc >= cstart) & (kc < cstart + 16)
            bz = np.empty((16, 15, 64, 64), np.float32)
            for rr in range(15):
                g = nb[:, 14 - rr][:, relc]
                bz[:, rr] = np.where(ok[None], g, np.float32(NEG))
            out[f"bz{l}"] = bz
        else:
            w = np.asarray(inp["w_in_cd"][j])
            cols = [w[:, c * 128:(c + 1) * 128] for c in range(8)]
            for kvh in range(4):
                kk = w[:, 1024 + kvh * 64:1024 + (kvh + 1) * 64]
                cols.append(np.concatenate([kk, kk], axis=1))
            cols += [w[:, 1280:1408], w[:, 1408:1536]]
            cols += [w[:, 1536 + i * 128:1536 + (i + 1) * 128] for i in range(4)]
            cols += [w[:, 2048 + i * 128:2048 + (i + 1) * 128] for i in range(4)]
            cols.append(np.concatenate([w[:, 2560:2624], w[:, 2560:2624]], axis=1))
            out[f"wcd{l}"] = _chunks(np.concatenate(cols, axis=1), 23)
            vd = np.concatenate([np.concatenate([w[:, 1280 + k * 64:1280 + (k + 1) * 64]] * 2, axis=1) for k in range(4)], axis=1)
            out[f"wvd{l}"] = _rows4(vd, 16, 512)
            out[f"wcdo{l}"] = _chunks(np.asarray(inp["w_out_cd"][j]), 16)
            gq = np.tile(np.asarray(inp["gqa_q_gain"][j]), 2)[:, None]
            gk = np.tile(np.asarray(inp["gqa_k_gain"][j]), 2)[:, None]
            mq = np.asarray(inp["mla_q_gain"][j]).reshape(4, 128).T
            mk = np.asarray(inp["mla_kv_gain"][j]).reshape(4, 128).T
            out[f"cdc{l}"] = np.ascontiguousarray(np.concatenate([gq, gk, mq, mk], axis=1).astype(np.float32))
            uq = np.asarray(inp["mla_w_uq"][j]).reshape(512, 8, 192)
            out[f"wuqn{l}"] = _rows4(np.ascontiguousarray(uq[:, :, :128]).reshape(512, 1024), 4, 1024)
            out[f"wuqr{l}"] = _rows4(np.ascontiguousarray(uq[:, :, 128:]).reshape(512, 512), 4, 512)
            out[f"wuk{l}"] = _rows4(np.asarray(inp["mla_w_uk"][j]), 4, 1024)
            out[f"wuv{l}"] = _rows4(np.asarray(inp["mla_w_uv"][j]), 4, 1024)
    return out


def _layout_unit(inp, units, layers, mixers=True):
    nu = len(units)
    xp = np.asarray(inp["x_prompt"])
    xs = np.asarray(inp["x_sample"])
    c = np.asarray(inp["c"])
    cond = np.empty((1 + nu, D), np.float32)
    cond[0] = np.asarray(inp["c_ctx"])
    xc = np.empty((nu, NCH, 128, 512), np.float32)
    xl = np.empty((nu, NCH, 128, 1024), np.float32)
    for k, u in enumerate(units):
        tok = np.concatenate([xp[2 * u], xp[2 * u + 1]], axis=0)
        xc[k] = tok.T.reshape(NCH, 128, 512)
        xl[k] = xs[u].T.reshape(NCH, 128, 1024)
        cond[1 + k] = c[u]
    ncols = 1 + nu
    out = {"xc": xc, "xl": xl,
           "condT": np.ascontiguousarray(cond.T.reshape(NCH, 128, ncols).transpose(1, 0, 2)).reshape(128, NCH * ncols)}
    if not mixers:
        return out
    for l in layers:
        j = l // 2
        if l % 2 == 0:
            sf = np.asarray(inp["state_lru_fwd"])[units, j]
            sb = np.asarray(inp["state_lru_bwd"])[units, j]
            st = np.stack([sf, sb], axis=-1).reshape(nu, 8, 128, 2).transpose(0, 2, 1, 3)
            out[f"st{l}"] = np.ascontiguousarray(st).reshape(nu, 128, 16)
            kk = np.asarray(inp["cache_na_k"])[units, j].reshape(nu, 512, 1024)
            out[f"nakc{l}"] = np.ascontiguousarray(kk.transpose(0, 2, 1)).reshape(nu, 8, 128, 512)
            out[f"navc{l}"] = np.ascontiguousarray(np.asarray(inp["cache_na_v"])[units, j].reshape(nu, 4, 128, 1024))
        else:
            gk = np.asarray(inp["cache_gqa_k"])[units, j]
            kt = gk.transpose(0, 2, 3, 1)
            out[f"gqk{l}"] = np.ascontiguousarray(np.concatenate([kt, kt], axis=2))
            gv = np.asarray(inp["cache_gqa_v"])[units, j]
            out[f"gqv{l}"] = np.ascontiguousarray(np.concatenate([gv, gv], axis=3)).reshape(nu, 4, 128, 512)
            mc = np.asarray(inp["cache_mla_ckv"])[units, j]
            out[f"mlc{l}"] = np.ascontiguousarray(mc.transpose(0, 2, 1)).reshape(nu, 4, 128, 512)
            mr = np.asarray(inp["cache_mla_krope"])[units, j].transpose(0, 2, 1)
            out[f"mlr{l}"] = np.ascontiguousarray(np.concatenate([mr, mr], axis=1))
    return out


NCORES = 8


def kernel(**inp):
    ncores = NCORES
    nu = 8 // ncores
    layers = [0, 1, 2, 3]
    bld = build(nu, layers, True)
    common = _layout_common(inp, layers, True)
    in_maps = []
    for core in range(ncores):
        d = dict(common)
        d.update(_layout_unit(inp, [core * nu + k for k in range(nu)], layers, True))
        in_maps.append(d)
    res = run_bass_kernel_spmd(bld.nc, in_maps, core_ids=list(range(ncores)))
    y_prompt = np.empty((16, 256, D), np.float32)
    y_sample = np.empty((8, 1024, D), np.float32)
    st_f = np.empty((16, 2, 1024), np.float32)
    st_b = np.empty((16, 2, 1024), np.float32)
    na_k = np.empty((16, 2, 256, 16, 64), np.float32)
    na_v = np.empty((16, 2, 256, 16, 64), np.float32)
    gq_k = np.empty((16, 2, 256, 4, 64), np.float32)
    gq_v = np.empty((16, 2, 256, 4, 64), np.float32)
    ml_c = np.empty((16, 2, 256, 512), np.float32)
    ml_r = np.empty((16, 2, 256, 64), np.float32)
    for core in range(ncores):
        r = res.results[core]
        for k in range(nu):
            u = core * nu + k
            yp = r["yc"][k].reshape(D, 512).T
            y_prompt[2 * u] = yp[:256]
            y_prompt[2 * u + 1] = yp[256:]
            y_sample[u] = r["yl"][k].reshape(D, 1024).T
            for l in layers:
                j = l // 2
                if l % 2 == 0:
                    so = r[f"ost{l}"][k].reshape(128, 8, 2, 2).transpose(3, 2, 1, 0).reshape(2, 2, 1024)
                    kk = r[f"onak{l}"][k].reshape(1024, 512).T.reshape(2, 256, 16, 64)
                    vv = r[f"onav{l}"][k].reshape(1024, 512).T.reshape(2, 256, 16, 64)
                    for s in range(2):
                        st_f[2 * u + s, j] = so[s, 0]
                        st_b[2 * u + s, j] = so[s, 1]
                        na_k[2 * u + s, j] = kk[s]
                        na_v[2 * u + s, j] = vv[s]
                else:
                    kk = r[f"ogk{l}"][k].reshape(256, 512).T.reshape(2, 256, 4, 64)
                    vv = r[f"ogv{l}"][k].reshape(256, 512).T.reshape(2, 256, 4, 64)
                    cc = r[f"omc{l}"][k].reshape(512, 512).T.reshape(2, 256, 512)
                    rr = r[f"omr{l}"][k].T.reshape(2, 256, 64)
                    for s in range(2):
                        gq_k[2 * u + s, j] = kk[s]
                        gq_v[2 * u + s, j] = vv[s]
                        ml_c[2 * u + s, j] = cc[s]
                        ml_r[2 * u + s, j] = rr[s]
    return (y_prompt, y_sample, st_f, st_b, na_k, na_v, gq_k, gq_v, ml_c, ml_r)
```

```python
import numpy as np
from contextlib import ExitStack
import concourse.bass as bass
import concourse.mybir as mybir
from concourse.bass_utils import run_bass_kernel_spmd

F32 = mybir.dt.float32
BF16 = mybir.dt.bfloat16
AF = mybir.ActivationFunctionType
ALU = mybir.AluOpType

D = 2048
NCH = 16
TOK = 1536
NT = 3
DFF = 5632
NJ = 44
EPS = 1e-6
SEMCH = 30000


class T:
    __slots__ = ("w", "r", "x")

    def __init__(self, x=False):
        self.w = None
        self.r = []
        self.x = x


class Slot:
    def __init__(self, sem):
        self.sem = sem
        self.count = 0


class Builder:
    def __init__(self):
        self.nc = bass.Bass("TRN2", target_bir_lowering=False)
        self.es = ExitStack()
        nc = self.nc
        self.eng = {"pe": nc.tensor, "act": nc.scalar, "dve": nc.vector, "pool": nc.gpsimd, "sp": nc.sync}
        self.cnt = {e: 0 for e in self.eng}
        self.sems = {e: [] for e in self.eng}
        self.seen = {e: {} for e in self.eng}
        self.nsem = 0
        self.slots = []

    def new_sem(self, name):
        self.nsem += 1
        return self.es.enter_context(self.nc.semaphore(f"{name}_{self.nsem}"))

    def slot(self, name="s"):
        sl = Slot(self.new_sem(name))
        self.slots.append(sl)
        return sl

    def barrier(self):
        for e in self.eng:
            for e2 in self.eng:
                if e2 != e and self.cnt[e2] > 0:
                    self._wait(e, ("e", e2, self.cnt[e2]))
            for sl in self.slots:
                if sl.count:
                    self._wait(e, ("d", sl, sl.count))

    def sbuf(self, name, shape, dt):
        return self.es.enter_context(self.nc.sbuf_tensor(name, shape, dt))

    def psum(self, name, shape, dt):
        return self.es.enter_context(self.nc.psum_tensor(name, shape, dt))

    def _engsem(self, e, idx):
        ch = (idx - 1) // SEMCH
        while len(self.sems[e]) <= ch:
            self.sems[e].append(self.new_sem(f"e_{e}"))
        return self.sems[e][ch], idx - ch * SEMCH

    def _wait(self, e, ev):
        if ev[0] == "e":
            _, e2, idx = ev
            if e2 == e and e == "pe":
                return
            key = ("e", e2)
            if self.seen[e].get(key, 0) >= idx:
                return
            self.seen[e][key] = idx
            sem, val = self._engsem(e2, idx)
        else:
            _, sl, val = ev
            key = ("d", id(sl))
            if self.seen[e].get(key, 0) >= val:
                return
            self.seen[e][key] = val
            sem = sl.sem
        self.eng[e].wait_ge(sem, val)

    def _deps(self, e, reads, writes):
        for t in reads:
            if t.w is not None:
                self._wait(e, t.w)
            if t.x:
                for ev in t.r:
                    if not (ev[0] == "e" and ev[1] == e):
                        self._wait(e, ev)
        for t in writes:
            if t.w is not None:
                self._wait(e, t.w)
            for ev in t.r:
                self._wait(e, ev)

    def _commit(self, ev, reads, writes):
        for t in reads:
            t.r.append(ev)
            if len(t.r) > 24:
                t.r = t.r[-24:] if False else t.r
        for t in writes:
            t.w = ev
            t.r = []

    def op(self, e, fn, reads=(), writes=()):
        self._deps(e, reads, writes)
        ins = fn(self.eng[e])
        self.cnt[e] += 1
        idx = self.cnt[e]
        sem, _ = self._engsem(e, idx)
        ins.then_inc(sem, 1)
        self._commit(("e", e, idx), reads, writes)

    def group(self, e, fns, reads=(), writes=()):
        self._deps(e, reads, writes)
        ins = None
        for fn in fns:
            ins = fn(self.eng[e])
        self.cnt[e] += 1
        idx = self.cnt[e]
        sem, _ = self._engsem(e, idx)
        ins.then_inc(sem, 1)
        self._commit(("e", e, idx), reads, writes)

    def dma(self, e, pairs, reads, writes, slot, **kw):
        self._deps(e, reads, writes)
        for (o, i) in pairs:
            self.eng[e].dma_start(out=o, in_=i, **kw).then_inc(slot.sem, 16)
            slot.count += 16
        self._commit(("d", slot, slot.count), reads, writes)

    def finish(self, slots):
        for sl in slots:
            if sl.count:
                self.eng["sp"].wait_ge(sl.sem, sl.count)


def rev(ap2d):
    (ps, pn), (fs, fn) = ap2d.ap
    return bass.AP(ap2d.tensor, ap2d.offset + (fn - 1) * fs, [[ps, pn], [-fs, fn]])


NA_SCALE = 0.125
MLA_SCALE = 192 ** -0.5
NEG = -30000.0
DBG = {}


def build(nu, layers, mixers=True):
    b = Builder()
    nc = b.nc
    ncols = 1 + nu
    L = len(layers)
    TM = 1024

    def din(name, shape, dt=F32):
        return nc.dram_tensor(name, list(shape), dt, kind="ExternalInput").ap()

    def dout(name, shape, dt=F32):
        return nc.dram_tensor(name, list(shape), dt, kind="ExternalOutput").ap()

    xc = din("xc", [nu, NCH, 128, 512])
    xl = din("xl", [nu, NCH, 128, 1024])
    yc = dout("yc", [nu, NCH, 128, 512])
    yl = dout("yl", [nu, NCH, 128, 1024])
    condT = din("condT", [128, NCH * ncols])
    cst = din("cst", [128, 18])
    cosd = din("cosd", [128, 1024])
    sind = din("sind", [128, 1024])
    matd = din("matd", [128, 3 * 128])
    W = {}
    O = {}
    for l in layers:
        W[f"wmod{l}"] = din(f"wmod{l}", [72, 128, 16 * 256])
        W[f"bmod{l}"] = din(f"bmod{l}", [128, 144])
        W[f"gain{l}"] = din(f"gain{l}", [128, 48])
        for f in range(2):
            W[f"win{l}_{f}"] = din(f"win{l}_{f}", [NJ, 128, 16 * 256])
            W[f"wout{l}_{f}"] = din(f"wout{l}_{f}", [11, 128, 4 * 2048])
        if not mixers:
            continue
        if l % 2 == 0:
            W[f"wab{l}"] = din(f"wab{l}", [40, 128, 2048])
            W[f"wabo{l}"] = din(f"wabo{l}", [16, 128, 2048])
            W[f"bd{l}"] = din(f"bd{l}", [8, 128, 512])
            W[f"lc{l}"] = din(f"lc{l}", [128, 8 * 11])
            W[f"bz{l}"] = din(f"bz{l}", [16, 15, 64, 64])
            W[f"st{l}"] = din(f"st{l}", [nu, 128, 16])
            W[f"nakc{l}"] = din(f"nakc{l}", [nu, 8, 128, 512])
            W[f"navc{l}"] = din(f"navc{l}", [nu, 4, 128, 1024])
            O[f"ost{l}"] = dout(f"ost{l}", [nu, 128, 32])
            O[f"onak{l}"] = dout(f"onak{l}", [nu, 8, 128, 512])
            O[f"onav{l}"] = dout(f"onav{l}", [nu, 8, 128, 512])
        else:
            W[f"wcd{l}"] = din(f"wcd{l}", [23, 128, 2048])
            W[f"wvd{l}"] = din(f"wvd{l}", [128, 16 * 512])
            W[f"wcdo{l}"] = din(f"wcdo{l}", [16, 128, 2048])
            W[f"cdc{l}"] = din(f"cdc{l}", [128, 10])
            W[f"wuqn{l}"] = din(f"wuqn{l}", [128, 4 * 1024])
            W[f"wuqr{l}"] = din(f"wuqr{l}", [128, 4 * 512])
            W[f"wuk{l}"] = din(f"wuk{l}", [128, 4 * 1024])
            W[f"wuv{l}"] = din(f"wuv{l}", [128, 4 * 1024])
            W[f"gqk{l}"] = din(f"gqk{l}", [nu, 4, 128, 512])
            W[f"gqv{l}"] = din(f"gqv{l}", [nu, 4, 128, 512])
            W[f"mlc{l}"] = din(f"mlc{l}", [nu, 4, 128, 512])
            W[f"mlr{l}"] = din(f"mlr{l}", [nu, 128, 512])
            O[f"ogk{l}"] = dout(f"ogk{l}", [nu, 4, 64, 512])
            O[f"ogv{l}"] = dout(f"ogv{l}", [nu, 2, 128, 512])
            O[f"omc{l}"] = dout(f"omc{l}", [nu, 4, 128, 512])
            O[f"omr{l}"] = dout(f"omr{l}", [nu, 64, 512])

    X = b.sbuf("X", [128, NCH, TM], F32)
    H = b.sbuf("H", [128, NCH, TM], BF16)
    RS = b.sbuf("RS", [128, TM], F32)
    ones = b.sbuf("ones", [128, 128], BF16)
    MATS = b.sbuf("MATS", [128, 384], BF16)
    perm = MATS[:, 0:128]
    bones = MATS[:, 128:256]
    HM = b.sbuf("HM", [128, 2], F32)
    COS = b.sbuf("COS", [128, 1024], F32)
    SIN = b.sbuf("SIN", [128, 1024], F32)
    DER = b.sbuf("DER", [128, L * 3 * ncols * 3 * NCH], F32)
    GAIN = b.sbuf("GAIN", [128, L, 48], F32)
    FG = b.sbuf("FG", [128, NCH], F32)
    CT = b.sbuf("CT", [128, NCH * ncols], F32)
    CB = b.sbuf("CB", [128, NCH * ncols], BF16)
    ABYTES = 92160
    AR = b.sbuf("ARENA", [128, ABYTES // 2], BF16)
    PS = [b.psum(f"ps{i}", [128, 512], F32) for i in range(8)]

    class Carver:
        def __init__(self, base=0):
            self.off = base

        def take(self, nbytes, dt=BF16, shape=None):
            assert self.off % 4 == 0
            a = AR[:, self.off // 2:(self.off + nbytes) // 2]
            self.off += nbytes
            assert self.off <= ABYTES, self.off
            if dt == F32:
                a = a.bitcast(F32)
            if shape is not None:
                names = " ".join(f"d{i}" for i in range(len(shape)))
                a = a.rearrange(f"p ({names}) -> p {names}", **{f"d{i}": s for i, s in enumerate(shape)})
            return a

    tX = [[T() for _ in range(2)] for _ in range(NCH)]
    tH = [[T() for _ in range(2)] for _ in range(NCH)]
    tRS = [T() for _ in range(2)]
    tPS = [T(True) for _ in range(8)]
    tC = T()
    tDER = T()
    allX = [tX[m][t] for m in range(NCH) for t in range(2)]
    allH = [tH[m][t] for m in range(NCH) for t in range(2)]

    def der(li, n, col, k, m):
        off = ((((li * 3 + n) * ncols + col) * 3 + k) * NCH) + m
        return DER[:, off:off + 1]

    cv0 = Carver()
    SG = [cv0.take(1024) for _ in range(2)]
    TMP = [cv0.take(2048, F32) for _ in range(2)]
    tSG = [T(), T()]
    tTMP = [T(), T()]
    BASE = cv0.off
    sMisc = b.slot("misc")
    sX = b.slot("x")
    sY = b.slot("y")
    outslots = [sY]

    b.op("dve", lambda e: e.memset(ones[:], 1.0), writes=[tC])
    pairs = [(CT[:], condT[:, :]), (FG[:], cst[:, 0:16]), (HM[:], cst[:, 16:18]), (COS[:], cosd[:, :]), (SIN[:], sind[:, :])]
    for li, l in enumerate(layers):
        pairs.append((GAIN[:, li, :], W[f"gain{l}"][:, :]))
    b.dma("sp", pairs, [], [tC], sMisc)
    sMat = b.slot("mat")
    tMat = T()
    b.dma("pool", [(MATS[:], matd[:, :])], [], [tMat], sMat)
    b.op("act", lambda e: e.activation(out=CB[:], in_=CT[:], func=AF.Silu), reads=[tC], writes=[tC])

    MODB = BASE + 65536
    WM = [AR[:, (MODB + i * 8192) // 2:(MODB + (i + 1) * 8192) // 2] for i in range(2)]
    MODT = AR[:, (MODB + 16384) // 2:(MODB + 16384 + 144 * ncols * 4) // 2].bitcast(F32)
    BM = AR[:, (MODB + 16384 + 1728) // 2:(MODB + 16384 + 1728 + 576) // 2].bitcast(F32)
    assert MODB + 16384 + 1728 + 576 <= ABYTES
    tWM = [T(), T()]
    tMODT = T()
    tBM = T()
    sWM = [b.slot("wm"), b.slot("wm")]
    sBM = b.slot("bm")
    wmk = [0]

    def mod_gen(li, l):
        pm = PS[7]
        for s_ in range(72):
            sl = wmk[0] % 2
            wmk[0] += 1
            b.dma("pool", [(WM[sl], W[f"wmod{l}"][s_])], [], [tWM[sl]], sWM[sl], max_dma_last_dim=8192)
            fns = []
            for q in range(2):
                oc = 2 * s_ + q
                for kc in range(16):
                    fns.append(lambda e, q=q, kc=kc, oc=oc, sl=sl: e.matmul(
                        pm[:, oc * ncols:(oc + 1) * ncols],
                        WM[sl][:, kc * 256 + q * 128: kc * 256 + (q + 1) * 128],
                        CB[:, kc * ncols:(kc + 1) * ncols], start=(kc == 0), stop=(kc == 15)))
            b.group("pe", fns, reads=[tWM[sl], tC], writes=[tPS[7]])
            yield
        b.dma("sp", [(BM, W[f"bmod{l}"][:, :])], [], [tBM], sBM)
        for col in range(ncols):
            b.op("dve", lambda e, col=col: e.tensor_tensor(
                out=MODT[:, col:144 * ncols:ncols], in0=pm[:, col:144 * ncols:ncols], in1=BM, op=ALU.add),
                reads=[tPS[7], tBM], writes=[tMODT])
        for n in range(3):
            for col in range(ncols):
                sh0 = ((3 * n + 0) * 16) * ncols + col
                sc0 = ((3 * n + 1) * 16) * ncols + col
                g0 = ((3 * n + 2) * 16) * ncols + col
                o = ((li * 3 + n) * ncols + col) * 3 * NCH
                b.op("dve", lambda e, sc0=sc0, o=o, n=n, li=li: e.scalar_tensor_tensor(
                    out=DER[:, o:o + NCH], in0=MODT[:, sc0:sc0 + 15 * ncols + 1:ncols], scalar=1.0,
                    in1=GAIN[:, li, n * 16:(n + 1) * 16], op0=ALU.add, op1=ALU.mult),
                    reads=[tMODT, tC], writes=[tDER])
                b.op("dve", lambda e, g0=g0, o=o, n=n: e.tensor_scalar(
                    out=DER[:, o + NCH:o + 2 * NCH], in0=MODT[:, g0:g0 + 15 * ncols + 1:ncols],
                    scalar1=(1.0 if n == 1 else 0.5), scalar2=None, op0=ALU.mult),
                    reads=[tMODT], writes=[tDER])
                b.op("dve", lambda e, sh0=sh0, o=o: e.tensor_copy(
                    out=DER[:, o + 2 * NCH:o + 3 * NCH], in_=MODT[:, sh0:sh0 + 15 * ncols + 1:ncols]),
                    reads=[tMODT], writes=[tDER])
        yield

    ticker = [None]

    def tick():
        if ticker[0] is not None:
            try:
                next(ticker[0])
            except StopIteration:
                ticker[0] = None

    def drain():
        while ticker[0] is not None:
            tick()

    ticker[0] = mod_gen(0, layers[0])
    drain()

    def TS(t):
        return slice(t * 512, (t + 1) * 512)

    def norm_rstd(nt):
        for t in range(nt):
            pb = 6
            for m in range(NCH):
                b.op("act", lambda e, m=m: e.activation(out=SG[m % 2], in_=X[:, m, TS(t)], func=AF.Square),
                     reads=[tX[m][t]], writes=[tSG[m % 2]])
                b.group("pe", [lambda e, m=m: e.matmul(PS[pb][:], ones[:], SG[m % 2], start=(m == 0), stop=(m == 15))],
                        reads=[tSG[m % 2], tC], writes=[tPS[pb]])
            b.op("act", lambda e: e.activation(out=RS[:, TS(t)], in_=PS[pb][:], func=AF.Sqrt, bias=EPS, scale=1.0 / D),
                 reads=[tPS[pb]], writes=[tRS[t]])
            b.op("dve", lambda e: e.reciprocal(out=RS[:, TS(t)], in_=RS[:, TS(t)]), reads=[tRS[t]], writes=[tRS[t]])

    def adaln(li, n, col, nt):
        norm_rstd(nt)
        for t in range(nt):
            for m in range(NCH):
                b.op("dve", lambda e, m=m: e.tensor_tensor(out=TMP[m % 2], in0=X[:, m, TS(t)], in1=RS[:, TS(t)], op=ALU.mult),
                     reads=[tX[m][t], tRS[t]], writes=[tTMP[m % 2]])
                b.op("act", lambda e, m=m: e.activation(
                    out=H[:, m, TS(t)], in_=TMP[m % 2], func=AF.Identity, bias=der(li, n, col, 2, m), scale=der(li, n, col, 0, m)),
                    reads=[tTMP[m % 2], tDER], writes=[tH[m][t]])

    pcount = [0]

    def ffn(li, l, f, col, nt):
        n = 0 if f == 0 else 2
        Tn = nt * 512
        b.barrier()
        adaln(li, n, col, nt)
        cv = Carver(BASE)
        WIN = [cv.take(8192) for _ in range(3)]
        WOUT = [cv.take(16384) for _ in range(2)]
        HID = cv.take(8192)
        tWIN = [T() for _ in range(3)]
        tWOUT = [T(), T()]
        tHID = [T() for _ in range(2)]
        win = W[f"win{l}_{f}"]
        wout = W[f"wout{l}_{f}"]
        for g in range(11):
            gs = g % 2
            b.dma("pool", [(WOUT[gs], wout[g])], [], [tWOUT[gs]], sFW[2 + gs], max_dma_last_dim=8192)
            for c in range(4):
                j = 4 * g + c
                ws = j % 3
                b.dma("pool", [(WIN[ws], win[j])], [], [tWIN[ws]], sFWI[ws], max_dma_last_dim=8192)
                for t in range(nt):
                    kk = pcount[0]
                    pcount[0] += 1
                    pg = kk % 2
                    pu = 2 + kk % 2
                    rd = [tWIN[ws]] + [tH[m][t] for m in range(NCH)]
                    b.group("pe", [lambda e, kc=kc: e.matmul(PS[pg][:], WIN[ws][:, kc * 256: kc * 256 + 128], H[:, kc, TS(t)],
                                                              start=(kc == 0), stop=(kc == 15)) for kc in range(16)],
                            reads=rd, writes=[tPS[pg]])
                    b.group("pe", [lambda e, kc=kc: e.matmul(PS[pu][:], WIN[ws][:, kc * 256 + 128: kc * 256 + 256], H[:, kc, TS(t)],
                                                              start=(kc == 0), stop=(kc == 15)) for kc in range(16)],
                            reads=rd, writes=[tPS[pu]])
                    b.op("act", lambda e: e.activation(out=SG[kk % 2], in_=PS[pg][:], func=AF.Silu),
                         reads=[tPS[pg]], writes=[tSG[kk % 2]])
                    b.op("dve", lambda e: e.tensor_tensor(
                        out=HID[:, c * Tn + t * 512: c * Tn + (t + 1) * 512], in0=SG[kk % 2], in1=PS[pu][:], op=ALU.mult),
                        reads=[tSG[kk % 2], tPS[pu]], writes=[tHID[t]])
                tick()
            for m in range(NCH):
                for t in range(nt):
                    kk = pcount[0]
                    pcount[0] += 1
                    po = 4 + kk % 2
                    b.group("pe", [lambda e, c=c: e.matmul(PS[po][:], WOUT[gs][:, c * 2048 + m * 128: c * 2048 + (m + 1) * 128],
                                                            HID[:, c * Tn + t * 512: c * Tn + (t + 1) * 512],
                                                            start=(c == 0), stop=(c == 3)) for c in range(4)],
                            reads=[tWOUT[gs], tHID[t]], writes=[tPS[po]])
                    b.op("dve", lambda e: e.scalar_tensor_tensor(
                        out=X[:, m, TS(t)], in0=PS[po][:], scalar=der(li, n, col, 1, m), in1=X[:, m, TS(t)],
                        op0=ALU.mult, op1=ALU.add), reads=[tPS[po], tDER, tX[m][t]], writes=[tX[m][t]])

    sFW = [b.slot("fw") for _ in range(4)]
    sFWI = [b.slot("fwi") for _ in range(4)]
    sWP = [b.slot("wp") for _ in range(3)]
    sIO = [b.slot("io") for _ in range(6)]
    wpk = [0]

    def proj(WP, tWP, wchunk, nt, evac, tiles=None):
        sl = wpk[0] % len(WP)
        wpk[0] += 1
        b.dma("pool", [(WP[sl], wchunk)], [], [tWP[sl]], sWP[sl], max_dma_last_dim=8192)
        for t in (range(nt) if tiles is None else tiles):
            kk = pcount[0]
            pcount[0] += 1
            pb = kk % 2
            b.group("pe", [lambda e, kc=kc: e.matmul(PS[pb][:], WP[sl][:, kc * 128:(kc + 1) * 128], H[:, kc, TS(t)],
                                                      start=(kc == 0), stop=(kc == 15)) for kc in range(16)],
                    reads=[tWP[sl]] + [tH[m][t] for m in range(NCH)], writes=[tPS[pb]])
            evac(pb, t)

    def outproj(li, col, nt, Y, tY, WP, tWP, wo, kcs=range(16)):
        for m in range(NCH):
            sl = wpk[0] % len(WP)
            wpk[0] += 1
            b.dma("pool", [(WP[sl], wo[m])], [], [tWP[sl]], sWP[sl], max_dma_last_dim=8192)
            for t in range(nt):
                kk = pcount[0]
                pcount[0] += 1
                po = 4 + kk % 2
                k0 = kcs[0]
                b.group("pe", [lambda e, kc=kc: e.matmul(PS[po][:], WP[sl][:, kc * 128:(kc + 1) * 128], Y[:, kc - k0, TS(t)],
                                                          start=(kc == kcs[0]), stop=(kc == kcs[-1])) for kc in kcs],
                        reads=[tWP[sl], tY], writes=[tPS[po]])
                b.op("dve", lambda e: e.scalar_tensor_tensor(
                    out=X[:, m, TS(t)], in0=PS[po][:], scalar=der(li, 1, col, 1, m), in1=X[:, m, TS(t)],
                    op0=ALU.mult, op1=ALU.add), reads=[tPS[po], tDER, tX[m][t]], writes=[tX[m][t]])

    def attention(qparts, kparts, vfn, nkb, q0s, scale, rows, ydst, rtiles, wtile, PT, tPT, RD, tRD, tab=None):
        its = [(q0, n, kb) for (q0, n) in q0s for kb in range(nkb)]
        SBK = [0, 1, 4, 5]
        LA = 3

        def qk(idx):
            q0, n, kb = its[idx]
            sb = SBK[idx % 4]
            b.group("pe", [lambda e, i=i: e.matmul(PS[sb][:, 0:n], kparts[i](kb), qparts[i](q0, n),
                                                    start=(i == 0), stop=(i == len(qparts) - 1)) for i in range(len(qparts))],
                    reads=rtiles, writes=[tPS[sb]])

        for j in range(min(LA, len(its))):
            qk(j)
        for idx, (q0, n, kb) in enumerate(its):
            if idx + LA < len(its):
                qk(idx + LA)
            sb = SBK[idx % 4]
            kk = idx
            pt = PT[kk % len(PT)]
            tb_ = tab(kb, q0, n) if tab is not None else None
            if tb_ is not None:
                tap, ttile = tb_
                b.op("dve", lambda e: e.scalar_tensor_tensor(out=TMP[kk % 2][:, 0:n], in0=PS[sb][:, 0:n], scalar=scale,
                                                             in1=tap, op0=ALU.mult, op1=ALU.add),
                     reads=[tPS[sb], ttile], writes=[tTMP[kk % 2]])
                b.op("act", lambda e: e.activation(out=pt[:, 0:n], in_=TMP[kk % 2][:, 0:n], func=AF.Exp),
                     reads=[tTMP[kk % 2]], writes=[tPT[kk % len(PT)]])
            else:
                b.op("act", lambda e: e.activation(out=pt[:, 0:n], in_=PS[sb][:, 0:n], func=AF.Exp, scale=scale),
                     reads=[tPS[sb]], writes=[tPT[kk % len(PT)]])
            b.group("pe", [lambda e: e.matmul(PS[2][:, 0:n], vfn(kb), pt[:, 0:n], start=(kb == 0), stop=(kb == nkb - 1)),
                           lambda e: e.matmul(PS[3][:, 0:n], ones[:], pt[:, 0:n], start=(kb == 0), stop=(kb == nkb - 1))],
                    reads=[tPT[kk % len(PT)], tC] + rtiles, writes=[tPS[2], tPS[3]])
            if kb == nkb - 1:
                b.op("dve", lambda e: e.reciprocal(out=RD[rows, 0:n], in_=PS[3][rows, 0:n]), reads=[tPS[3]], writes=[tRD])
                b.op("dve", lambda e: e.tensor_tensor(out=ydst(q0, n), in0=PS[2][rows, 0:n], in1=RD[rows, 0:n], op=ALU.mult),
                     reads=[tPS[2], tRD], writes=[wtile])

    def mixer_ab(li, l, kind, ui, col, nt):
        Tn = nt * 512
        lat = kind == "lat"
        seqs = [(0, 1024)] if lat else [(0, 256), (256, 512)]
        b.barrier()
        adaln(li, 1, col, nt)
        cv = Carver(BASE)
        Y = cv.take(16 * Tn * 2, BF16, [16, Tn])
        tY = T()
        WP = [cv.take(4096) for _ in range(3)]
        tWP = [T() for _ in range(3)]
        LC = cv.take(8 * 11 * 4, F32, [8, 11])
        NSP = cv.take(8 * 4 * 4, F32, [8, 4])
        ST = cv.take(64, F32)
        SO = cv.take(128, F32)
        tLC = T()
        tSO = T()
        wab = W[f"wab{l}"]
        prs = [(LC, W[f"lc{l}"][:, :].rearrange("p (c k) -> p c k", k=11))]
        if lat:
            prs.append((ST, W[f"st{l}"][ui]))
        b.dma("sp", prs, [], [tLC], sIO[0])
        if DBG.get('skip_nsp'):
            return
        b.op("act", lambda e: e.activation(out=NSP[:, :, 0:2], in_=LC[:, :, 9:11], func=AF.Exp, scale=-1.0), reads=[tLC], writes=[tLC])
        b.op("act", lambda e: e.activation(out=NSP[:, :, 0:2], in_=NSP[:, :, 0:2], func=AF.Ln, bias=1.0), reads=[tLC], writes=[tLC])
        b.op("dve", lambda e: e.tensor_scalar(out=NSP[:, :, 2:4], in0=NSP[:, :, 0:2], scalar1=-16.0, scalar2=None, op0=ALU.mult), reads=[tLC], writes=[tLC])
        b.op("dve", lambda e: e.tensor_scalar(out=NSP[:, :, 0:2], in0=NSP[:, :, 0:2], scalar1=-8.0, scalar2=None, op0=ALU.mult), reads=[tLC], writes=[tLC])
        mark = cv.off
        XA2 = [cv.take(Tn * 4, F32) for _ in range(2)]
        GA2 = [cv.take(Tn * 4, F32) for _ in range(2)]
        XC = cv.take(Tn * 4, F32)
        R = cv.take(Tn * 4, F32)
        GI = cv.take(Tn * 4, F32)
        HF = cv.take(Tn * 4, F32)
        HB = cv.take(Tn * 4, F32)
        XCB = cv.take(Tn * 2)
        BD = cv.take(1024, BF16, [4, 128])
        tXA2 = [T(), T()]
        tGA2 = [T(), T()]
        tXC, tR, tGI, tHF, tHB, tXCB, tBD = [T() for _ in range(7)]
        for c in range(8 if not DBG.get('skip_lru') else 0):
            XA, tXA, GA, tGA = XA2[c % 2], tXA2[c % 2], GA2[c % 2], tGA2[c % 2]
            Sb, tS = XA, tXA
            b.dma("pool", [(BD, W[f"bd{l}"][c].rearrange("p (k n) -> p k n", n=128))], [], [tBD], sIO[1], max_dma_last_dim=8192)
            proj(WP, tWP, wab[c], nt, lambda pb, t: b.op("dve", lambda e: e.tensor_copy(out=XA[:, TS(t)], in_=PS[pb][:]),
                                                       reads=[tPS[pb]], writes=[tXA]))
            proj(WP, tWP, wab[8 + c], nt, lambda pb, t: b.op("dve", lambda e: e.tensor_copy(out=GA[:, TS(t)], in_=PS[pb][:]),
                                                           reads=[tPS[pb]], writes=[tGA]))
            for (s0, s1) in seqs:
                b.op("act", lambda e: e.activation(out=XC[:, s0:s1], in_=XA[:, s0:s1], func=AF.Identity,
                                                   bias=LC[:, c, 4:5], scale=LC[:, c, 2:3]), reads=[tXA, tLC], writes=[tXC])
                for (jj, sh) in ((0, -2), (1, -1), (3, 1)):
                    if sh < 0:
                        o_ = XC[:, s0 - sh:s1]
                        i_ = XA[:, s0:s1 + sh]
                    else:
                        o_ = XC[:, s0:s1 - sh]
                        i_ = XA[:, s0 + sh:s1]
                    b.op("dve", lambda e, o_=o_, i_=i_, jj=jj: e.scalar_tensor_tensor(
                        out=o_, in0=i_, scalar=LC[:, c, jj:jj + 1], in1=o_, op0=ALU.mult, op1=ALU.add),
                        reads=[tXA, tXC, tLC], writes=[tXC])
            b.op("dve", lambda e: e.tensor_copy(out=XCB, in_=XC), reads=[tXC], writes=[tXCB])
            for d in range(2):
                Hd, tHd = (HF, tHF) if d == 0 else (HB, tHB)
                for t in range(nt):
                    for (a, dst, tdst, bcol) in ((0, R, tR, 5 + d), (1, GI, tGI, 7 + d)):
                        kk = pcount[0]
                        pcount[0] += 1
                        pb = kk % 2
                        b.group("pe", [lambda e: e.matmul(PS[pb][:], BD[:, d * 2 + a, :], XCB[:, TS(t)], start=True, stop=True)],
                                reads=[tBD, tXCB], writes=[tPS[pb]])
                        b.op("act", lambda e: e.activation(out=dst[:, TS(t)], in_=PS[pb][:], func=AF.Sigmoid, bias=LC[:, c, bcol:bcol + 1]),
                             reads=[tPS[pb], tLC], writes=[tdst])
                b.op("act", lambda e: e.activation(out=Sb, in_=R, func=AF.Exp, scale=NSP[:, c, 2 + d:3 + d]), reads=[tR, tLC, tXC], writes=[tS])
                b.op("act", lambda e: e.activation(out=Sb, in_=Sb, func=AF.Sqrt, bias=1.0, scale=-1.0), reads=[tS], writes=[tS])
                b.op("act", lambda e: e.activation(out=R, in_=R, func=AF.Exp, scale=NSP[:, c, d:d + 1]), reads=[tR, tLC], writes=[tR])
                b.op("dve", lambda e: e.tensor_tensor(out=GI, in0=GI, in1=Sb, op=ALU.mult), reads=[tGI, tS], writes=[tGI])
                b.op("dve", lambda e: e.tensor_tensor(out=GI, in0=GI, in1=XC, op=ALU.mult), reads=[tGI, tXC], writes=[tGI])
                for si, (s0, s1) in enumerate(seqs):
                    init = ST[:, c * 2 + d:c * 2 + d + 1] if lat else 0.0
                    if d == 0:
                        b.op("dve", lambda e: e.tensor_tensor_scan(out=Hd[:, s0:s1], data0=R[:, s0:s1], data1=GI[:, s0:s1],
                                                                   initial=init, op0=ALU.mult, op1=ALU.add),
                             reads=[tR, tGI, tLC], writes=[tHd])
                    else:
                        b.op("dve", lambda e: e.tensor_tensor_scan(out=rev(Hd[:, s0:s1]), data0=rev(R[:, s0:s1]), data1=rev(GI[:, s0:s1]),
                                                                   initial=init, op0=ALU.mult, op1=ALU.add),
                             reads=[tR, tGI, tLC], writes=[tHd])
                    if not lat:
                        src = Hd[:, s1 - 1:s1] if d == 0 else Hd[:, s0:s0 + 1]
                        k_ = (c * 2 + d) * 2 + si
                        b.op("act", lambda e: e.activation(out=SO[:, k_:k_ + 1], in_=src, func=AF.Identity), reads=[tHd], writes=[tSO])
            b.op("dve", lambda e: e.tensor_tensor(out=R, in0=GA, in1=GA, op=ALU.mult), reads=[tGA, tGI], writes=[tR])
            b.op("dve", lambda e: e.tensor_scalar(out=R, in0=R, scalar1=0.044715, scalar2=1.0, op0=ALU.mult, op1=ALU.add), reads=[tR], writes=[tR])
            b.op("dve", lambda e: e.tensor_tensor(out=R, in0=R, in1=GA, op=ALU.mult), reads=[tR, tGA], writes=[tR])
            b.op("act", lambda e: e.activation(out=R, in_=R, func=AF.Sigmoid, scale=1.5957691216057308), reads=[tR], writes=[tR])
            b.op("dve", lambda e: e.tensor_tensor(out=R, in0=R, in1=GA, op=ALU.mult), reads=[tR, tGA], writes=[tR])
            b.op("dve", lambda e: e.tensor_tensor(out=HF, in0=HF, in1=HB, op=ALU.add), reads=[tHF, tHB], writes=[tHF])
            b.op("dve", lambda e: e.tensor_tensor(out=Y[:, c, :], in0=HF, in1=R, op=ALU.mult), reads=[tHF, tR], writes=[tY])
        if not lat:
            b.dma("sp", [(O[f"ost{l}"][ui], SO)], [tSO], [], sOut[0])
        b.barrier()
        cv.off = mark
        QT = cv.take(Tn * 2)
        KT = cv.take((Tn + 512) * 2)
        QM = cv.take(Tn * 2)
        KF = cv.take(Tn * 4, F32) if not lat else None
        VF = cv.take(Tn * 4, F32) if not lat else None
        nvb = (Tn + (512 if lat else 0)) // 128
        VT = cv.take(nvb * 128 * 2, BF16, [nvb, 128])
        PT = [cv.take(1024) for _ in range(4)]
        RD = cv.take(2048, F32)
        tQT, tKT, tQM, tKF, tVF, tVT, tRD = [T() for _ in range(7)]
        tPT = [T() for _ in range(4)]
        if lat:
            KC32 = cv.take(2048, F32)
            VC32 = cv.take(2048, F32, [4, 128])
            tKC32 = T()
            TAB = [cv.take(2048, BF16) for _ in range(8)]
            tTAB = [T() for _ in range(8)]
            for kb in range(8):
                b.op("pool", lambda e, kb=kb: e.memset(TAB[kb], NEG), writes=[tTAB[kb]])
        for c in range(8 if not DBG.get('skip_qkv') else 0):
            proj(WP, tWP, wab[16 + c], nt, lambda pb, t: b.op("act", lambda e: e.activation(out=QT[:, TS(t)], in_=PS[pb][:], func=AF.Identity),
                                                            reads=[tPS[pb]], writes=[tQT]))

            def evk(pb, t):
                if not lat:
                    b.op("act", lambda e: e.activation(out=KF[:, TS(t)], in_=PS[pb][:], func=AF.Identity), reads=[tPS[pb]], writes=[tKF])
                    b.op("dve", lambda e: e.tensor_copy(out=KT[:, TS(t)], in_=KF[:, TS(t)]), reads=[tKF], writes=[tKT])
                else:
                    b.op("dve", lambda e: e.tensor_copy(out=KT[:, TS(t)], in_=PS[pb][:]), reads=[tPS[pb]], writes=[tKT])
            proj(WP, tWP, wab[24 + c], nt, evk)
            if not lat:
                proj(WP, tWP, wab[32 + c], nt, lambda pb, t: b.op("act", lambda e: e.activation(out=VF[:, TS(t)], in_=PS[pb][:], func=AF.Identity),
                                                                reads=[tPS[pb]], writes=[tVF]))
                b.dma("sp", [(O[f"onak{l}"][ui, c], KF), (O[f"onav{l}"][ui, c], VF)], [tKF, tVF], [], sOut[1])
            sl = wpk[0] % 3
            wpk[0] += 1
            b.dma("pool", [(WP[sl], wab[32 + c])], [], [tWP[sl]], sWP[sl], max_dma_last_dim=8192)
            for tb in range(Tn // 128 if not DBG.get('skip_vt') else 0):
                kk = pcount[0]
                pcount[0] += 1
                pb = kk % 2
                t = tb // 4
                b.group("pe", [lambda e, kc=kc: e.matmul(PS[pb][:, 0:128], H[:, kc, tb * 128:(tb + 1) * 128], WP[sl][:, kc * 128:(kc + 1) * 128],
                                                          start=(kc == 0), stop=(kc == 15)) for kc in range(16)],
                        reads=[tWP[sl]] + [tH[m][t] for m in range(NCH)], writes=[tPS[pb]])
                b.op("act", lambda e: e.activation(out=VT[:, tb, :], in_=PS[pb][:, 0:128], func=AF.Identity), reads=[tPS[pb]], writes=[tVT])
            if lat:
                b.dma("sp", [(KC32, W[f"nakc{l}"][ui, c]),
                             (VC32, W[f"navc{l}"][ui][:, :, c * 128:(c + 1) * 128].rearrange("k p n -> p k n"))],
                      [], [tKC32], sIO[2])
                b.op("dve", lambda e: e.tensor_copy(out=KT[:, Tn:Tn + 512], in_=KC32), reads=[tKC32], writes=[tKT])
                b.op("dve", lambda e: e.tensor_copy(out=VT[:, 8:12, :], in_=VC32), reads=[tKC32], writes=[tVT])
            for hh in range(2 if not DBG.get('skip_att') else 0):
                rows = slice(hh * 64, (hh + 1) * 64)
                b.op("dve", lambda e: e.tensor_scalar(out=QM, in0=QT, scalar1=HM[:, hh:hh + 1], scalar2=None, op0=ALU.mult),
                     reads=[tQT, tC], writes=[tQM])
                if lat:
                    hd = 2 * c + hh
                    bz = W[f"bz{l}"]
                    for i in range(16):
                        r_lo = 0 if i <= 7 else i - 3
                        r_hi = 15 if i >= 8 else i + 4
                        nr = r_hi - r_lo + 1
                        rr0 = r_lo - i + 7
                        kb = i // 2
                        src = bass.AP(bz.tensor, bz.offset + ((hd * 15 + rr0) * 64) * 64, [[64, 64], [4096, nr], [1, 64]])
                        dst = TAB[kb][(i % 2) * 64:(i % 2) * 64 + 64, r_lo * 64:(r_hi + 1) * 64].rearrange("p (r q) -> p r q", q=64)
                        b.dma("pool", [(dst, src)], [], [tTAB[kb]], sTAB[kb])
                    attention([lambda q0, n: QM[:, q0:q0 + n]], [lambda kb: KT[:, kb * 128:(kb + 1) * 128]],
                              lambda kb: VT[:, kb, :], 12, [(0, 512), (512, 512)], NA_SCALE, rows,
                              lambda q0, n: Y[rows, 8 + c, q0:q0 + n], [tQM, tKT, tVT], tY, PT, tPT, RD, tRD,
                              tab=lambda kb, q0, n: (TAB[kb][:, q0:q0 + n], tTAB[kb]) if kb < 8 else None)
                else:
                    for (s0, s1) in seqs:
                        kb0 = s0 // 128
                        attention([lambda q0, n: QM[:, q0:q0 + n]], [lambda kb: KT[:, (kb0 + kb) * 128:(kb0 + kb + 1) * 128]],
                                  lambda kb: VT[:, kb0 + kb, :], 2, [(s0, 256)], NA_SCALE, rows,
                                  lambda q0, n: Y[rows, 8 + c, q0:q0 + n], [tQM, tKT, tVT], tY, PT, tPT, RD, tRD)
        if not DBG.get('skip_outproj'):
            outproj(li, col, nt, Y, tY, WP, tWP, W[f"wabo{l}"])

    sOut = [b.slot("out") for _ in range(6)]
    outslots.extend(sOut)
    sTAB = [b.slot("tab") for _ in range(8)]

    def mixer_cd(li, l, kind, ui, col, nt):
        Tn = nt * 512
        lat = kind == "lat"
        Tk = Tn + (512 if lat else 0)
        seqs = [(0, 1024)] if lat else [(0, 256), (256, 512)]
        wcd = W[f"wcd{l}"]
        b.barrier()
        adaln(li, 1, col, nt)
        cv = Carver(BASE)
        Y8 = cv.take(8 * Tn * 2, BF16, [8, Tn])
        tY = T()
        WP = [cv.take(4096) for _ in range(3)]
        tWP = [T() for _ in range(3)]
        CDC = cv.take(40, F32)
        tCDC = T()
        b.dma("sp", [(CDC, W[f"cdc{l}"][:, :])], [], [tCDC], sIO[0])
        mark = cv.off

        def rope_from(src, t0, n, dst, rd, wr):
            b.op("act", lambda e: e.activation(out=SG[0][:, 0:n], in_=src, func=AF.Identity), reads=rd, writes=[tSG[0]])
            b.group("pe", [lambda e: e.matmul(PS[5][:, 0:n], perm, SG[0][:, 0:n], start=True, stop=True)], reads=[tSG[0], tMat], writes=[tPS[5]])
            b.op("dve", lambda e: e.tensor_tensor(out=TMP[0][:, 0:n], in0=src, in1=COS[:, t0:t0 + n], op=ALU.mult), reads=rd + [tC], writes=[tTMP[0]])
            b.op("dve", lambda e: e.tensor_tensor(out=TMP[1][:, 0:n], in0=PS[5][:, 0:n], in1=SIN[:, t0:t0 + n], op=ALU.mult), reads=[tPS[5], tC], writes=[tTMP[1]])
            b.op("dve", lambda e: e.tensor_tensor(out=dst, in0=TMP[0][:, 0:n], in1=TMP[1][:, 0:n], op=ALU.add), reads=[tTMP[0], tTMP[1]], writes=wr)

        K2 = [cv.take(Tk * 2) for _ in range(4)]
        tK2 = [T() for _ in range(4)]
        nkbT = Tk // 128
        VD = cv.take(nkbT * 512 * 2, BF16, [nkbT, 512])
        tVD = T()
        WV = cv.take(16384)
        tWV = T()
        QT = cv.take(Tn * 2)
        QM = cv.take(Tn * 2)
        STG = cv.take(2048, F32)
        RSQ = cv.take(2048, F32)
        PT = [cv.take(1024) for _ in range(4)]
        RD = cv.take(2048, F32)
        tQT, tQM, tSTG, tRSQ, tRD = [T() for _ in range(5)]
        tPT = [T() for _ in range(4)]

        def headnorm(pb, t, gcol, dst, wr, rope, f32dma=None):
            b.op("act", lambda e: e.activation(out=STG, in_=PS[pb][:], func=AF.Identity), reads=[tPS[pb]], writes=[tSTG])
            b.op("act", lambda e: e.activation(out=SG[1], in_=PS[pb][:], func=AF.Square), reads=[tPS[pb]], writes=[tSG[1]])
            b.group("pe", [lambda e: e.matmul(PS[6][:], bones, SG[1], start=True, stop=True)], reads=[tSG[1], tMat], writes=[tPS[6]])
            b.op("act", lambda e: e.activation(out=RSQ, in_=PS[6][:], func=AF.Sqrt, bias=EPS, scale=1.0 / 64), reads=[tPS[6]], writes=[tRSQ])
            b.op("dve", lambda e: e.reciprocal(out=RSQ, in_=RSQ), reads=[tRSQ], writes=[tRSQ])
            if rope or f32dma is not None:
                b.op("dve", lambda e: e.scalar_tensor_tensor(out=STG, in0=STG, scalar=CDC[:, gcol:gcol + 1], in1=RSQ, op0=ALU.mult, op1=ALU.mult),
                     reads=[tSTG, tRSQ, tCDC], writes=[tSTG])
                if f32dma is not None:
                    f32dma()
                if rope:
                    rope_from(STG, t * 512, 512, dst, [tSTG], wr)
                else:
                    b.op("act", lambda e: e.activation(out=dst, in_=STG, func=AF.Identity), reads=[tSTG], writes=wr)
            else:
                b.op("dve", lambda e: e.scalar_tensor_tensor(out=dst, in0=STG, scalar=CDC[:, gcol:gcol + 1], in1=RSQ, op0=ALU.mult, op1=ALU.mult),
                     reads=[tSTG, tRSQ, tCDC], writes=wr)

        for kvh in range(4):
            def evk(pb, t, kvh=kvh):
                f = None
                if not lat:
                    f = lambda: b.dma("sp", [(O[f"ogk{l}"][ui, kvh], STG[0:64, :])], [tSTG], [], sOut[2])
                headnorm(pb, t, 1, K2[kvh][:, TS(t)], [tK2[kvh]], lat, f)
            proj(WP, tWP, wcd[8 + kvh], nt, evk)
        if lat:
            b.dma("pool", [(K2[kvh][:, Tn:Tn + 512], W[f"gqk{l}"][ui, kvh]) for kvh in range(4)]
                  + [(VD[:, 8:12, :], W[f"gqv{l}"][ui].rearrange("k p n -> p k n"))], [], tK2 + [tVD], sIO[2], max_dma_last_dim=8192)
        else:
            for i in range(2):
                def evv(pb, t, i=i):
                    b.op("act", lambda e: e.activation(out=STG, in_=PS[pb][:], func=AF.Identity), reads=[tPS[pb]], writes=[tSTG])
                    b.dma("sp", [(O[f"ogv{l}"][ui, i], STG)], [tSTG], [], sOut[2])
                proj(WP, tWP, wcd[12 + i], nt, evv)
        b.dma("pool", [(WV, W[f"wvd{l}"][:, :])], [], [tWV], sIO[3], max_dma_last_dim=8192)
        for tb in range(Tn // 128):
            kk = pcount[0]
            pcount[0] += 1
            pb = kk % 2
            b.group("pe", [lambda e, kc=kc: e.matmul(PS[pb][:], H[:, kc, tb * 128:(tb + 1) * 128], WV[:, kc * 512:(kc + 1) * 512],
                                                      start=(kc == 0), stop=(kc == 15)) for kc in range(16)],
                    reads=[tWV] + [tH[m][tb // 4] for m in range(NCH)], writes=[tPS[pb]])
            b.op("act", lambda e: e.activation(out=VD[:, tb, :], in_=PS[pb][:], func=AF.Identity), reads=[tPS[pb]], writes=[tVD])
        for c in range(8):
            kvh = c // 2
            proj(WP, tWP, wcd[c], nt, lambda pb, t: headnorm(pb, t, 0, QT[:, TS(t)], [tQT], lat))
            for hh in range(2):
                rows = slice(hh * 64, (hh + 1) * 64)
                b.op("dve", lambda e: e.tensor_scalar(out=QM, in0=QT, scalar1=HM[:, hh:hh + 1], scalar2=None, op0=ALU.mult),
                     reads=[tQT, tC], writes=[tQM])
                if lat:
                    attention([lambda q0, n: QM[:, q0:q0 + n]], [lambda kb: K2[kvh][:, kb * 128:(kb + 1) * 128]],
                              lambda kb: VD[:, kb, kvh * 128:(kvh + 1) * 128], 12, [(0, 512), (512, 512)], NA_SCALE, rows,
                              lambda q0, n: Y8[rows, c, q0:q0 + n], [tQM, tK2[kvh], tVD], tY, PT, tPT, RD, tRD)
                else:
                    for (s0, s1) in seqs:
                        kb0 = s0 // 128
                        attention([lambda q0, n: QM[:, q0:q0 + n]], [lambda kb: K2[kvh][:, (kb0 + kb) * 128:(kb0 + kb + 1) * 128]],
                                  lambda kb: VD[:, kb0 + kb, kvh * 128:(kvh + 1) * 128], 2, [(s0, 256)], NA_SCALE, rows,
                                  lambda q0, n: Y8[rows, c, q0:q0 + n], [tQM, tK2[kvh], tVD], tY, PT, tPT, RD, tRD)
        outproj(li, col, nt, Y8, tY, WP, tWP, W[f"wcdo{l}"], kcs=range(0, 8))

        b.barrier()
        cv.off = mark
        STG4 = cv.take(8192, F32, [4, 512])
        QAN = cv.take(4 * Tn * 2, BF16, [4, Tn])
        CKB = cv.take(4 * Tk * 2, BF16, [4, Tk])
        KR2 = cv.take(Tk * 2)
        RS2 = cv.take(2048, F32)
        WH = cv.take(4096, BF16, [4, 4, 128])
        QN = cv.take(Tn * 2)
        QRh = cv.take(Tn * 2)
        KN = cv.take(Tk * 2)
        VM = cv.take(nkbT * 128 * 2, BF16, [nkbT, 128])
        PT = [cv.take(1024) for _ in range(4)]
        RD = cv.take(2048, F32)
        tSTG4, tQAN, tCKB, tKR2, tRS2, tWH, tQN, tQRh, tKN, tVM, tRD = [T() for _ in range(11)]
        tPT = [T() for _ in range(4)]

        def norm512(wbase, gcol0, t, after):
            for i in range(4):
                def ev(pb, t_, i=i):
                    b.op("act", lambda e: e.activation(out=STG4[:, i, :], in_=PS[pb][:], func=AF.Identity), reads=[tPS[pb]], writes=[tSTG4])
                    b.op("act", lambda e: e.activation(out=SG[i % 2], in_=PS[pb][:], func=AF.Square), reads=[tPS[pb]], writes=[tSG[i % 2]])
                    b.group("pe", [lambda e: e.matmul(PS[6][:], ones[:], SG[i % 2], start=(i == 0), stop=(i == 3))],
                            reads=[tSG[i % 2], tC], writes=[tPS[6]])
                proj(WP, tWP, wcd[wbase + i], nt, ev, tiles=[t])
            b.op("act", lambda e: e.activation(out=RS2, in_=PS[6][:], func=AF.Sqrt, bias=EPS, scale=1.0 / 512), reads=[tPS[6]], writes=[tRS2])
            b.op("dve", lambda e: e.reciprocal(out=RS2, in_=RS2), reads=[tRS2], writes=[tRS2])
            for i in range(4):
                b.op("dve", lambda e: e.scalar_tensor_tensor(out=STG4[:, i, :], in0=STG4[:, i, :], scalar=CDC[:, gcol0 + i:gcol0 + i + 1], in1=RS2,
                                                             op0=ALU.mult, op1=ALU.mult), reads=[tSTG4, tRS2, tCDC], writes=[tSTG4])
                after(i)

        for t in range(nt):
            norm512(14, 2, t, lambda i: b.op("act", lambda e: e.activation(out=QAN[:, i, TS(t)], in_=STG4[:, i, :], func=AF.Identity),
                                             reads=[tSTG4], writes=[tQAN]))

            def after_ckv(i):
                if not lat:
                    b.dma("sp", [(O[f"omc{l}"][ui, i], STG4[:, i, :])], [tSTG4], [], sOut[3])
                b.op("act", lambda e: e.activation(out=CKB[:, i, TS(t)], in_=STG4[:, i, :], func=AF.Identity), reads=[tSTG4], writes=[tCKB])
            norm512(18, 6, t, after_ckv)

        def evkr(pb, t):
            b.op("act", lambda e: e.activation(out=RS2, in_=PS[pb][:], func=AF.Identity), reads=[tPS[pb]], writes=[tRS2])
            if lat:
                rope_from(RS2, t * 512, 512, KR2[:, TS(t)], [tRS2], [tKR2])
            else:
                b.dma("sp", [(O[f"omr{l}"][ui], RS2[0:64, :])], [tRS2], [], sOut[4])
                b.op("act", lambda e: e.activation(out=KR2[:, TS(t)], in_=RS2, func=AF.Identity), reads=[tRS2], writes=[tKR2])
        proj(WP, tWP, wcd[22], nt, evkr)
        if lat:
            b.dma("pool", [(CKB[:, :, Tn:Tn + 512], W[f"mlc{l}"][ui].rearrange("k p n -> p k n")), (KR2[:, Tn:Tn + 512], W[f"mlr{l}"][ui])],
                  [], [tCKB, tKR2], sIO[4], max_dma_last_dim=8192)
        wuqn = W[f"wuqn{l}"][:, :].rearrange("p (k n) -> p k n", n=1024)
        wuk = W[f"wuk{l}"][:, :].rearrange("p (k n) -> p k n", n=1024)
        wuv = W[f"wuv{l}"][:, :].rearrange("p (k n) -> p k n", n=1024)
        wuqr = W[f"wuqr{l}"][:, :].rearrange("p (k n) -> p k n", n=512)
        for h in range(8):
            hs = slice(h * 128, (h + 1) * 128)
            b.op("pool", lambda e: e.memset(WH[:, 3, :, :], 0.0), writes=[tWH])
            b.dma("pool", [(WH[:, 0, :, :], wuqn[:, :, hs]), (WH[:, 1, :, :], wuk[:, :, hs]), (WH[:, 2, :, :], wuv[:, :, hs]),
                           (WH[:, 3, :, (h % 2) * 64:(h % 2) * 64 + 64], wuqr[:, :, h * 64:(h + 1) * 64])], [], [tWH], sIO[5])
            for t in range(nt):
                for (wi, dst, tdst, rp) in ((0, QN, tQN, False), (3, QRh, tQRh, lat)):
                    kk = pcount[0]
                    pcount[0] += 1
                    pb = kk % 2
                    b.group("pe", [lambda e, kc=kc: e.matmul(PS[pb][:], WH[:, wi, kc, :], QAN[:, kc, TS(t)], start=(kc == 0), stop=(kc == 3)) for kc in range(4)],
                            reads=[tWH, tQAN], writes=[tPS[pb]])
                    if rp:
                        b.op("act", lambda e: e.activation(out=RS2, in_=PS[pb][:], func=AF.Identity), reads=[tPS[pb]], writes=[tRS2])
                        rope_from(RS2, t * 512, 512, dst[:, TS(t)], [tRS2], [tdst])
                    else:
                        b.op("act", lambda e: e.activation(out=dst[:, TS(t)], in_=PS[pb][:], func=AF.Identity), reads=[tPS[pb]], writes=[tdst])
            for kt in range(Tk // 512):
                kk = pcount[0]
                pcount[0] += 1
                pb = kk % 2
                b.group("pe", [lambda e, kc=kc: e.matmul(PS[pb][:], WH[:, 1, kc, :], CKB[:, kc, TS(kt)], start=(kc == 0), stop=(kc == 3)) for kc in range(4)],
                        reads=[tWH, tCKB], writes=[tPS[pb]])
                b.op("act", lambda e: e.activation(out=KN[:, TS(kt)], in_=PS[pb][:], func=AF.Identity), reads=[tPS[pb]], writes=[tKN])
            for kb in range(nkbT):
                kk = pcount[0]
                pcount[0] += 1
                pb = kk % 2
                b.group("pe", [lambda e, kc=kc: e.matmul(PS[pb][:, 0:128], CKB[:, kc, kb * 128:(kb + 1) * 128], WH[:, 2, kc, :], start=(kc == 0), stop=(kc == 3)) for kc in range(4)],
                        reads=[tWH, tCKB], writes=[tPS[pb]])
                b.op("act", lambda e: e.activation(out=VM[:, kb, :], in_=PS[pb][:, 0:128], func=AF.Identity), reads=[tPS[pb]], writes=[tVM])
            rows = slice(0, 128)
            rt = [tQN, tQRh, tKN, tKR2, tVM]
            if lat:
                attention([lambda q0, n: QN[:, q0:q0 + n], lambda q0, n: QRh[:, q0:q0 + n]],
                          [lambda kb: KN[:, kb * 128:(kb + 1) * 128], lambda kb: KR2[:, kb * 128:(kb + 1) * 128]],
                          lambda kb: VM[:, kb, :], 12, [(0, 512), (512, 512)], MLA_SCALE, rows,
                          lambda q0, n: Y8[:, h, q0:q0 + n], rt, tY, PT, tPT, RD, tRD)
            else:
                for (s0, s1) in seqs:
                    kb0 = s0 // 128
                    attention([lambda q0, n: QN[:, q0:q0 + n], lambda q0, n: QRh[:, q0:q0 + n]],
                              [lambda kb: KN[:, (kb0 + kb) * 128:(kb0 + kb + 1) * 128], lambda kb: KR2[:, (kb0 + kb) * 128:(kb0 + kb + 1) * 128]],
                              lambda kb: VM[:, kb0 + kb, :], 2, [(s0, 256)], MLA_SCALE, rows,
                              lambda q0, n: Y8[:, h, q0:q0 + n], rt, tY, PT, tPT, RD, tRD)
        outproj(li, col, nt, Y8, tY, WP, tWP, W[f"wcdo{l}"], kcs=range(8, 16))

    for ui in range(nu):
        for kind in ("ctx", "lat"):
            if DBG.get('only_' + ('lat' if kind == 'ctx' else 'ctx')):
                continue
            nt = 1 if kind == "ctx" else 2
            Tn = nt * 512
            col = 0 if kind == "ctx" else 1 + ui
            src = xc if kind == "ctx" else xl
            dst = yc if kind == "ctx" else yl
            b.dma("sp", [(X[:, m, 0:Tn], src[ui, m]) for m in range(NCH)], [], allX, sX)
            for li, l in enumerate(layers):
                if ui == 0 and kind == "ctx" and li + 1 < len(layers) and not DBG.get('only_lat'):
                    ticker[0] = mod_gen(li + 1, layers[li + 1])
                ffn(li, l, 0, col, nt)
                if mixers:
                    if l % 2 == 0:
                        mixer_ab(li, l, kind, ui, col, nt)
                    else:
                        mixer_cd(li, l, kind, ui, col, nt)
                    ffn(li, l, 1, col, nt)
                if ticker[0] is not None:
                    b.barrier()
                    drain()
            norm_rstd(nt)
            for t in range(nt):
                for m in range(NCH):
                    b.op("dve", lambda e, m=m: e.scalar_tensor_tensor(
                        out=X[:, m, TS(t)], in0=X[:, m, TS(t)], scalar=FG[:, m:m + 1], in1=RS[:, TS(t)], op0=ALU.mult, op1=ALU.mult),
                        reads=[tX[m][t], tRS[t], tC], writes=[tX[m][t]])
            b.dma("sp", [(dst[ui, m], X[:, m, 0:Tn]) for m in range(NCH)], allX, [], sY)
    b.finish(outslots)
    return b


def _chunks(w, nchunk):
    return np.ascontiguousarray(w.reshape(16, 128, nchunk, 128).transpose(2, 1, 0, 3)).reshape(nchunk, 128, 2048)


def _rows4(w, nk, ncol):
    return np.ascontiguousarray(w.reshape(nk, 128, ncol).transpose(1, 0, 2)).reshape(128, nk * ncol)


def _const_tables():
    t = np.arange(1024)
    nf = 16
    inv = 1.0 / (10000.0 ** (np.arange(nf, dtype=np.float32) / nf))
    d = np.arange(128) % 64
    half = d // 32
    i = d % 16
    pos = np.where(half[:, None] == 0, (t // 64)[None, :], (t % 64)[None, :]).astype(np.float32)
    ang = pos * inv[i][:, None]
    cos = np.cos(ang).astype(np.float32)
    sin = np.sin(ang).astype(np.float32)
    perm = np.zeros((128, 128), np.float32)
    for m in range(128):
        j = (m % 64) % 32
        if j < 16:
            perm[m + 16, m] = -1.0
        else:
            perm[m - 16, m] = 1.0
    bones = np.zeros((128, 128), np.float32)
    bones[:64, :64] = 1.0
    bones[64:, 64:] = 1.0
    matd = np.concatenate([perm, bones, np.zeros((128, 128), np.float32)], axis=1)
    hm = np.zeros((128, 2), np.float32)
    hm[:64, 0] = 1.0
    hm[64:, 1] = 1.0
    return cos, sin, matd, hm


def _layout_common(inp, layers, mixers=True):
    out = {}
    cos, sin, matd, hm = _const_tables()
    out["cosd"], out["sind"], out["matd"] = cos, sin, matd
    fg = np.asarray(inp["final_gain"]).reshape(16, 128).T
    out["cst"] = np.ascontiguousarray(np.concatenate([fg, hm], axis=1).astype(np.float32))
    for l in layers:
        j = l // 2
        wm = np.asarray(inp["w_mod"][l])
        out[f"wmod{l}"] = np.ascontiguousarray(wm.reshape(16, 128, 72, 256).transpose(2, 1, 0, 3)).reshape(72, 128, 4096)
        out[f"bmod{l}"] = np.ascontiguousarray(np.asarray(inp["b_mod"][l]).reshape(144, 128).T)
        out[f"gain{l}"] = np.ascontiguousarray(np.asarray(inp["norm_gain"][l]).reshape(3, 16, 128).transpose(2, 0, 1)).reshape(128, 48)
        for f in range(2):
            w = np.asarray(inp["w_ffn_in"][l][f])
            g = w[:, :DFF].reshape(16, 128, NJ, 128)
            u = w[:, DFF:].reshape(16, 128, NJ, 128)
            cat = np.concatenate([g, u], axis=-1)
            out[f"win{l}_{f}"] = np.ascontiguousarray(cat.transpose(2, 1, 0, 3)).reshape(NJ, 128, 4096)
            wo = np.asarray(inp["w_ffn_out"][l][f])
            out[f"wout{l}_{f}"] = np.ascontiguousarray(wo.reshape(11, 4, 128, 2048).transpose(0, 2, 1, 3)).reshape(11, 128, 8192)
        if not mixers:
            continue
        if l % 2 == 0:
            out[f"wab{l}"] = _chunks(np.asarray(inp["w_in_ab"][j]), 40)
            out[f"wabo{l}"] = _chunks(np.asarray(inp["w_out_ab"][j]), 16)
            wa = np.asarray(inp["lru_wa"][j])
            wx = np.asarray(inp["lru_wx"][j])
            bd = np.zeros((8, 128, 4, 128), np.float32)
            for c in range(8):
                for d in range(2):
                    for a, ww in enumerate((wa, wx)):
                        bd[c, :64, d * 2 + a, :64] = ww[d, 2 * c]
                        bd[c, 64:, d * 2 + a, 64:] = ww[d, 2 * c + 1]
            out[f"bd{l}"] = bd.reshape(8, 128, 512)
            cols = [np.asarray(inp["conv_w"][j])[k] for k in range(4)] + [np.asarray(inp["conv_b"][j])]
            cols += [np.asarray(inp["lru_ba"][j])[0], np.asarray(inp["lru_ba"][j])[1],
                     np.asarray(inp["lru_bx"][j])[0], np.asarray(inp["lru_bx"][j])[1],
                     np.asarray(inp["lru_lambda"][j])[0], np.asarray(inp["lru_lambda"][j])[1]]
            lc = np.stack(cols, axis=-1)
            out[f"lc{l}"] = np.ascontiguousarray(lc.reshape(8, 128, 11).transpose(1, 0, 2)).reshape(128, 88)
            nb = np.asarray(inp["na_bias"][j])
            kc = np.arange(64)[:, None]
            qc = np.arange(64)[None, :]
            relc = np.clip(kc - qc + 15, 0, 30)
            cstart = np.clip(qc - 8, 0, 48)
            ok = (kc >= cstart) & (kc < cstart + 16)
            bz = np.empty((16, 15, 64, 64), np.float32)
            for rr in range(15):
                g = nb[:, 14 - rr][:, relc]
                bz[:, rr] = np.where(ok[None], g, np.float32(NEG))
            out[f"bz{l}"] = bz
        else:
            w = np.asarray(inp["w_in_cd"][j])
            cols = [w[:, c * 128:(c + 1) * 128] for c in range(8)]
            for kvh in range(4):
                kk = w[:, 1024 + kvh * 64:1024 + (kvh + 1) * 64]
                cols.append(np.concatenate([kk, kk], axis=1))
            cols += [w[:, 1280:1408], w[:, 1408:1536]]
            cols += [w[:, 1536 + i * 128:1536 + (i + 1) * 128] for i in range(4)]
            cols += [w[:, 2048 + i * 128:2048 + (i + 1) * 128] for i in range(4)]
            cols.append(np.concatenate([w[:, 2560:2624], w[:, 2560:2624]], axis=1))
            out[f"wcd{l}"] = _chunks(np.concatenate(cols, axis=1), 23)
            vd = np.concatenate([np.concatenate([w[:, 1280 + k * 64:1280 + (k + 1) * 64]] * 2, axis=1) for k in range(4)], axis=1)
            out[f"wvd{l}"] = _rows4(vd, 16, 512)
            out[f"wcdo{l}"] = _chunks(np.asarray(inp["w_out_cd"][j]), 16)
            gq = np.tile(np.asarray(inp["gqa_q_gain"][j]), 2)[:, None]
            gk = np.tile(np.asarray(inp["gqa_k_gain"][j]), 2)[:, None]
            mq = np.asarray(inp["mla_q_gain"][j]).reshape(4, 128).T
            mk = np.asarray(inp["mla_kv_gain"][j]).reshape(4, 128).T
            out[f"cdc{l}"] = np.ascontiguousarray(np.concatenate([gq, gk, mq, mk], axis=1).astype(np.float32))
            uq = np.asarray(inp["mla_w_uq"][j]).reshape(512, 8, 192)
            out[f"wuqn{l}"] = _rows4(np.ascontiguousarray(uq[:, :, :128]).reshape(512, 1024), 4, 1024)
            out[f"wuqr{l}"] = _rows4(np.ascontiguousarray(uq[:, :, 128:]).reshape(512, 512), 4, 512)
            out[f"wuk{l}"] = _rows4(np.asarray(inp["mla_w_uk"][j]), 4, 1024)
            out[f"wuv{l}"] = _rows4(np.asarray(inp["mla_w_uv"][j]), 4, 1024)
    return out


def _layout_unit(inp, units, layers, mixers=True):
    nu = len(units)
    xp = np.asarray(inp["x_prompt"])
    xs = np.asarray(inp["x_sample"])
    c = np.asarray(inp["c"])
    cond = np.empty((1 + nu, D), np.float32)
    cond[0] = np.asarray(inp["c_ctx"])
    xc = np.empty((nu, NCH, 128, 512), np.float32)
    xl = np.empty((nu, NCH, 128, 1024), np.float32)
    for k, u in enumerate(units):
        tok = np.concatenate([xp[2 * u], xp[2 * u + 1]], axis=0)
        xc[k] = tok.T.reshape(NCH, 128, 512)
        xl[k] = xs[u].T.reshape(NCH, 128, 1024)
        cond[1 + k] = c[u]
    ncols = 1 + nu
    out = {"xc": xc, "xl": xl,
           "condT": np.ascontiguousarray(cond.T.reshape(NCH, 128, ncols).transpose(1, 0, 2)).reshape(128, NCH * ncols)}
    if not mixers:
        return out
    for l in layers:
        j = l // 2
        if l % 2 == 0:
            sf = np.asarray(inp["state_lru_fwd"])[units, j]
            sb = np.asarray(inp["state_lru_bwd"])[units, j]
            st = np.stack([sf, sb], axis=-1).reshape(nu, 8, 128, 2).transpose(0, 2, 1, 3)
            out[f"st{l}"] = np.ascontiguousarray(st).reshape(nu, 128, 16)
            kk = np.asarray(inp["cache_na_k"])[units, j].reshape(nu, 512, 1024)
            out[f"nakc{l}"] = np.ascontiguousarray(kk.transpose(0, 2, 1)).reshape(nu, 8, 128, 512)
            out[f"navc{l}"] = np.ascontiguousarray(np.asarray(inp["cache_na_v"])[units, j].reshape(nu, 4, 128, 1024))
        else:
            gk = np.asarray(inp["cache_gqa_k"])[units, j]
            kt = gk.transpose(0, 2, 3, 1)
            out[f"gqk{l}"] = np.ascontiguousarray(np.concatenate([kt, kt], axis=2))
            gv = np.asarray(inp["cache_gqa_v"])[units, j]
            out[f"gqv{l}"] = np.ascontiguousarray(np.concatenate([gv, gv], axis=3)).reshape(nu, 4, 128, 512)
            mc = np.asarray(inp["cache_mla_ckv"])[units, j]
            out[f"mlc{l}"] = np.ascontiguousarray(mc.transpose(0, 2, 1)).reshape(nu, 4, 128, 512)
            mr = np.asarray(inp["cache_mla_krope"])[units, j].transpose(0, 2, 1)
            out[f"mlr{l}"] = np.ascontiguousarray(np.concatenate([mr, mr], axis=1))
    return out


NCORES = 8


def kernel(**inp):
    ncores = NCORES
    nu = 8 // ncores
    layers = [0, 1, 2, 3]
    bld = build(nu, layers, True)
    common = _layout_common(inp, layers, True)
    in_maps = []
    for core in range(ncores):
        d = dict(common)
        d.update(_layout_unit(inp, [core * nu + k for k in range(nu)], layers, True))
        in_maps.append(d)
    res = run_bass_kernel_spmd(bld.nc, in_maps, core_ids=list(range(ncores)))
    y_prompt = np.empty((16, 256, D), np.float32)
    y_sample = np.empty((8, 1024, D), np.float32)
    st_f = np.empty((16, 2, 1024), np.float32)
    st_b = np.empty((16, 2, 1024), np.float32)
    na_k = np.empty((16, 2, 256, 16, 64), np.float32)
    na_v = np.empty((16, 2, 256, 16, 64), np.float32)
    gq_k = np.empty((16, 2, 256, 4, 64), np.float32)
    gq_v = np.empty((16, 2, 256, 4, 64), np.float32)
    ml_c = np.empty((16, 2, 256, 512), np.float32)
    ml_r = np.empty((16, 2, 256, 64), np.float32)
    for core in range(ncores):
        r = res.results[core]
        for k in range(nu):
            u = core * nu + k
            yp = r["yc"][k].reshape(D, 512).T
            y_prompt[2 * u] = yp[:256]
            y_prompt[2 * u + 1] = yp[256:]
            y_sample[u] = r["yl"][k].reshape(D, 1024).T
            for l in layers:
                j = l // 2
                if l % 2 == 0:
                    so = r[f"ost{l}"][k].reshape(128, 8, 2, 2).transpose(3, 2, 1, 0).reshape(2, 2, 1024)
                    kk = r[f"onak{l}"][k].reshape(1024, 512).T.reshape(2, 256, 16, 64)
                    vv = r[f"onav{l}"][k].reshape(1024, 512).T.reshape(2, 256, 16, 64)
                    for s in range(2):
                        st_f[2 * u + s, j] = so[s, 0]
                        st_b[2 * u + s, j] = so[s, 1]
                        na_k[2 * u + s, j] = kk[s]
                        na_v[2 * u + s, j] = vv[s]
                else:
                    kk = r[f"ogk{l}"][k].reshape(256, 512).T.reshape(2, 256, 4, 64)
                    vv = r[f"ogv{l}"][k].reshape(256, 512).T.reshape(2, 256, 4, 64)
                    cc = r[f"omc{l}"][k].reshape(512, 512).T.reshape(2, 256, 512)
                    rr = r[f"omr{l}"][k].T.reshape(2, 256, 64)
                    for s in range(2):
                        gq_k[2 * u + s, j] = kk[s]
                        gq_v[2 * u + s, j] = vv[s]
                        ml_c[2 * u + s, j] = cc[s]
                        ml_r[2 * u + s, j] = rr[s]
    return (y_prompt, y_sample, st_f, st_b, na_k, na_v, gq_k, gq_v, ml_c, ml_r)
```

```python
import numpy as np
from contextlib import ExitStack
import concourse.bass as bass
import concourse.mybir as mybir
from concourse.bass_utils import run_bass_kernel_spmd

F32 = mybir.dt.float32
BF16 = mybir.dt.bfloat16
AF = mybir.ActivationFunctionType
ALU = mybir.AluOpType

D = 2048
NCH = 16
TOK = 1536
NT = 3
DFF = 5632
NJ = 44
EPS = 1e-6
SEMCH = 30000


class T:
    __slots__ = ("w", "r", "x")

    def __init__(self, x=False):
        self.w = None
        self.r = []
        self.x = x


class Slot:
    def __init__(self, sem):
        self.sem = sem
        self.count = 0


class Builder:
    def __init__(self):
        self.nc = bass.Bass("TRN2", target_bir_lowering=False)
        self.es = ExitStack()
        nc = self.nc
        self.eng = {"pe": nc.tensor, "act": nc.scalar, "dve": nc.vector, "pool": nc.gpsimd, "sp": nc.sync}
        self.cnt = {e: 0 for e in self.eng}
        self.sems = {e: [] for e in self.eng}
        self.seen = {e: {} for e in self.eng}
        self.nsem = 0
        self.slots = []

    def new_sem(self, name):
        self.nsem += 1
        return self.es.enter_context(self.nc.semaphore(f"{name}_{self.nsem}"))

    def slot(self, name="s"):
        sl = Slot(self.new_sem(name))
        self.slots.append(sl)
        return sl

    def barrier(self):
        for e in self.eng:
            for e2 in self.eng:
                if e2 != e and self.cnt[e2] > 0:
                    self._wait(e, ("e", e2, self.cnt[e2]))
            for sl in self.slots:
                if sl.count:
                    self._wait(e, ("d", sl, sl.count))

    def sbuf(self, name, shape, dt):
        return self.es.enter_context(self.nc.sbuf_tensor(name, shape, dt))

    def psum(self, name, shape, dt):
        return self.es.enter_context(self.nc.psum_tensor(name, shape, dt))

    def _engsem(self, e, idx):
        ch = (idx - 1) // SEMCH
        while len(self.sems[e]) <= ch:
            self.sems[e].append(self.new_sem(f"e_{e}"))
        return self.sems[e][ch], idx - ch * SEMCH

    def _wait(self, e, ev):
        if ev[0] == "e":
            _, e2, idx = ev
            if e2 == e and e == "pe":
                return
            key = ("e", e2)
            if self.seen[e].get(key, 0) >= idx:
                return
            self.seen[e][key] = idx
            sem, val = self._engsem(e2, idx)
        else:
            _, sl, val = ev
            key = ("d", id(sl))
            if self.seen[e].get(key, 0) >= val:
                return
            self.seen[e][key] = val
            sem = sl.sem
        self.eng[e].wait_ge(sem, val)

    def _deps(self, e, reads, writes):
        for t in reads:
            if t.w is not None:
                self._wait(e, t.w)
            if t.x:
                for ev in t.r:
                    if not (ev[0] == "e" and ev[1] == e):
                        self._wait(e, ev)
        for t in writes:
            if t.w is not None:
                self._wait(e, t.w)
            for ev in t.r:
                self._wait(e, ev)

    def _commit(self, ev, reads, writes):
        for t in reads:
            t.r.append(ev)
            if len(t.r) > 24:
                t.r = t.r[-24:] if False else t.r
        for t in writes:
            t.w = ev
            t.r = []

    def op(self, e, fn, reads=(), writes=()):
        self._deps(e, reads, writes)
        ins = fn(self.eng[e])
        self.cnt[e] += 1
        idx = self.cnt[e]
        sem, _ = self._engsem(e, idx)
        ins.then_inc(sem, 1)
        self._commit(("e", e, idx), reads, writes)

    def group(self, e, fns, reads=(), writes=()):
        self._deps(e, reads, writes)
        ins = None
        for fn in fns:
            ins = fn(self.eng[e])
        self.cnt[e] += 1
        idx = self.cnt[e]
        sem, _ = self._engsem(e, idx)
        ins.then_inc(sem, 1)
        self._commit(("e", e, idx), reads, writes)

    def dma(self, e, pairs, reads, writes, slot, **kw):
        self._deps(e, reads, writes)
        for (o, i) in pairs:
            self.eng[e].dma_start(out=o, in_=i, **kw).then_inc(slot.sem, 16)
            slot.count += 16
        self._commit(("d", slot, slot.count), reads, writes)

    def finish(self, slots):
        for sl in slots:
            if sl.count:
                self.eng["sp"].wait_ge(sl.sem, sl.count)


def rev(ap2d):
    (ps, pn), (fs, fn) = ap2d.ap
    return bass.AP(ap2d.tensor, ap2d.offset + (fn - 1) * fs, [[ps, pn], [-fs, fn]])


NA_SCALE = 0.125
MLA_SCALE = 192 ** -0.5
NEG = -30000.0
DBG = {}


def build(nu, layers, mixers=True):
    b = Builder()
    nc = b.nc
    ncols = 1 + nu
    L = len(layers)
    TM = 1024

    def din(name, shape, dt=F32):
        return nc.dram_tensor(name, list(shape), dt, kind="ExternalInput").ap()

    def dout(name, shape, dt=F32):
        return nc.dram_tensor(name, list(shape), dt, kind="ExternalOutput").ap()

    xc = din("xc", [nu, NCH, 128, 512])
    xl = din("xl", [nu, NCH, 128, 1024])
    yc = dout("yc", [nu, NCH, 128, 512])
    yl = dout("yl", [nu, NCH, 128, 1024])
    condT = din("condT", [128, NCH * ncols])
    cst = din("cst", [128, 18])
    cosd = din("cosd", [128, 1024])
    sind = din("sind", [128, 1024])
    matd = din("matd", [128, 3 * 128])
    W = {}
    O = {}
    for l in layers:
        W[f"wmod{l}"] = din(f"wmod{l}", [72, 128, 16 * 256])
        W[f"bmod{l}"] = din(f"bmod{l}", [128, 144])
        W[f"gain{l}"] = din(f"gain{l}", [128, 48])
        for f in range(2):
            W[f"win{l}_{f}"] = din(f"win{l}_{f}", [NJ, 128, 16 * 256])
            W[f"wout{l}_{f}"] = din(f"wout{l}_{f}", [11, 128, 4 * 2048])
        if not mixers:
            continue
        if l % 2 == 0:
            W[f"wab{l}"] = din(f"wab{l}", [40, 128, 2048])
            W[f"wabo{l}"] = din(f"wabo{l}", [16, 128, 2048])
            W[f"bd{l}"] = din(f"bd{l}", [8, 128, 512])
            W[f"lc{l}"] = din(f"lc{l}", [128, 8 * 11])
            W[f"bz{l}"] = din(f"bz{l}", [16, 15, 64, 64])
            W[f"st{l}"] = din(f"st{l}", [nu, 128, 16])
            W[f"nakc{l}"] = din(f"nakc{l}", [nu, 8, 128, 512])
            W[f"navc{l}"] = din(f"navc{l}", [nu, 4, 128, 1024])
            O[f"ost{l}"] = dout(f"ost{l}", [nu, 128, 32])
            O[f"onak{l}"] = dout(f"onak{l}", [nu, 8, 128, 512])
            O[f"onav{l}"] = dout(f"onav{l}", [nu, 8, 128, 512])
        else:
            W[f"wcd{l}"] = din(f"wcd{l}", [23, 128, 2048])
            W[f"wvd{l}"] = din(f"wvd{l}", [128, 16 * 512])
            W[f"wcdo{l}"] = din(f"wcdo{l}", [16, 128, 2048])
            W[f"cdc{l}"] = din(f"cdc{l}", [128, 10])
            W[f"wuqn{l}"] = din(f"wuqn{l}", [128, 4 * 1024])
            W[f"wuqr{l}"] = din(f"wuqr{l}", [128, 4 * 512])
            W[f"wuk{l}"] = din(f"wuk{l}", [128, 4 * 1024])
            W[f"wuv{l}"] = din(f"wuv{l}", [128, 4 * 1024])
            W[f"gqk{l}"] = din(f"gqk{l}", [nu, 4, 128, 512])
            W[f"gqv{l}"] = din(f"gqv{l}", [nu, 4, 128, 512])
            W[f"mlc{l}"] = din(f"mlc{l}", [nu, 4, 128, 512])
            W[f"mlr{l}"] = din(f"mlr{l}", [nu, 128, 512])
            O[f"ogk{l}"] = dout(f"ogk{l}", [nu, 4, 64, 512])
            O[f"ogv{l}"] = dout(f"ogv{l}", [nu, 2, 128, 512])
            O[f"omc{l}"] = dout(f"omc{l}", [nu, 4, 128, 512])
            O[f"omr{l}"] = dout(f"omr{l}", [nu, 64, 512])

    X = b.sbuf("X", [128, NCH, TM], F32)
    H = b.sbuf("H", [128, NCH, TM], BF16)
    RS = b.sbuf("RS", [128, TM], F32)
    ones = b.sbuf("ones", [128, 128], BF16)
    MATS = b.sbuf("MATS", [128, 384], BF16)
    perm = MATS[:, 0:128]
    bones = MATS[:, 128:256]
    HM = b.sbuf("HM", [128, 2], F32)
    COS = b.sbuf("COS", [128, 1024], F32)
    SIN = b.sbuf("SIN", [128, 1024], F32)
    DER = b.sbuf("DER", [128, L * 3 * ncols * 3 * NCH], F32)
    GAIN = b.sbuf("GAIN", [128, L, 48], F32)
    FG = b.sbuf("FG", [128, NCH], F32)
    CT = b.sbuf("CT", [128, NCH * ncols], F32)
    CB = b.sbuf("CB", [128, NCH * ncols], BF16)
    ABYTES = 92160
    AR = b.sbuf("ARENA", [128, ABYTES // 2], BF16)
    PS = [b.psum(f"ps{i}", [128, 512], F32) for i in range(8)]

    class Carver:
        def __init__(self, base=0):
            self.off = base

        def take(self, nbytes, dt=BF16, shape=None):
            assert self.off % 4 == 0
            a = AR[:, self.off // 2:(self.off + nbytes) // 2]
            self.off += nbytes
            assert self.off <= ABYTES, self.off
            if dt == F32:
                a = a.bitcast(F32)
            if shape is not None:
                names = " ".join(f"d{i}" for i in range(len(shape)))
                a = a.rearrange(f"p ({names}) -> p {names}", **{f"d{i}": s for i, s in enumerate(shape)})
            return a

    tX = [[T() for _ in range(2)] for _ in range(NCH)]
    tH = [[T() for _ in range(2)] for _ in range(NCH)]
    tRS = [T() for _ in range(2)]
    tPS = [T(True) for _ in range(8)]
    tC = T()
    tDER = T()
    allX = [tX[m][t] for m in range(NCH) for t in range(2)]
    allH = [tH[m][t] for m in range(NCH) for t in range(2)]

    def der(li, n, col, k, m):
        off = ((((li * 3 + n) * ncols + col) * 3 + k) * NCH) + m
        return DER[:, off:off + 1]

    cv0 = Carver()
    SG = [cv0.take(1024) for _ in range(2)]
    TMP = [cv0.take(2048, F32) for _ in range(2)]
    tSG = [T(), T()]
    tTMP = [T(), T()]
    BASE = cv0.off
    sMisc = b.slot("misc")
    sX = b.slot("x")
    sY = b.slot("y")
    outslots = [sY]

    b.op("dve", lambda e: e.memset(ones[:], 1.0), writes=[tC])
    pairs = [(CT[:], condT[:, :]), (FG[:], cst[:, 0:16]), (HM[:], cst[:, 16:18]), (COS[:], cosd[:, :]), (SIN[:], sind[:, :])]
    for li, l in enumerate(layers):
        pairs.append((GAIN[:, li, :], W[f"gain{l}"][:, :]))
    b.dma("sp", pairs, [], [tC], sMisc)
    sMat = b.slot("mat")
    tMat = T()
    b.dma("pool", [(MATS[:], matd[:, :])], [], [tMat], sMat)
    b.op("act", lambda e: e.activation(out=CB[:], in_=CT[:], func=AF.Silu), reads=[tC], writes=[tC])

    MODB = BASE + 65536
    WM = [AR[:, (MODB + i * 8192) // 2:(MODB + (i + 1) * 8192) // 2] for i in range(2)]
    MODT = AR[:, (MODB + 16384) // 2:(MODB + 16384 + 144 * ncols * 4) // 2].bitcast(F32)
    BM = AR[:, (MODB + 16384 + 1728) // 2:(MODB + 16384 + 1728 + 576) // 2].bitcast(F32)
    assert MODB + 16384 + 1728 + 576 <= ABYTES
    tWM = [T(), T()]
    tMODT = T()
    tBM = T()
    sWM = [b.slot("wm"), b.slot("wm")]
    sBM = b.slot("bm")
    wmk = [0]

    def mod_gen(li, l):
        pm = PS[7]
        for s_ in range(72):
            sl = wmk[0] % 2
            wmk[0] += 1
            b.dma("pool", [(WM[sl], W[f"wmod{l}"][s_])], [], [tWM[sl]], sWM[sl], max_dma_last_dim=8192)
            fns = []
            for q in range(2):
                oc = 2 * s_ + q
                for kc in range(16):
                    fns.append(lambda e, q=q, kc=kc, oc=oc, sl=sl: e.matmul(
                        pm[:, oc * ncols:(oc + 1) * ncols],
                        WM[sl][:, kc * 256 + q * 128: kc * 256 + (q + 1) * 128],
                        CB[:, kc * ncols:(kc + 1) * ncols], start=(kc == 0), stop=(kc == 15)))
            b.group("pe", fns, reads=[tWM[sl], tC], writes=[tPS[7]])
            yield
        b.dma("sp", [(BM, W[f"bmod{l}"][:, :])], [], [tBM], sBM)
        for col in range(ncols):
            b.op("dve", lambda e, col=col: e.tensor_tensor(
                out=MODT[:, col:144 * ncols:ncols], in0=pm[:, col:144 * ncols:ncols], in1=BM, op=ALU.add),
                reads=[tPS[7], tBM], writes=[tMODT])
        for n in range(3):
            for col in range(ncols):
                sh0 = ((3 * n + 0) * 16) * ncols + col
                sc0 = ((3 * n + 1) * 16) * ncols + col
                g0 = ((3 * n + 2) * 16) * ncols + col
                o = ((li * 3 + n) * ncols + col) * 3 * NCH
                b.op("dve", lambda e, sc0=sc0, o=o, n=n, li=li: e.scalar_tensor_tensor(
                    out=DER[:, o:o + NCH], in0=MODT[:, sc0:sc0 + 15 * ncols + 1:ncols], scalar=1.0,
                    in1=GAIN[:, li, n * 16:(n + 1) * 16], op0=ALU.add, op1=ALU.mult),
                    reads=[tMODT, tC], writes=[tDER])
                b.op("dve", lambda e, g0=g0, o=o, n=n: e.tensor_scalar(
                    out=DER[:, o + NCH:o + 2 * NCH], in0=MODT[:, g0:g0 + 15 * ncols + 1:ncols],
                    scalar1=(1.0 if n == 1 else 0.5), scalar2=None, op0=ALU.mult),
                    reads=[tMODT], writes=[tDER])
                b.op("dve", lambda e, sh0=sh0, o=o: e.tensor_copy(
                    out=DER[:, o + 2 * NCH:o + 3 * NCH], in_=MODT[:, sh0:sh0 + 15 * ncols + 1:ncols]),
                    reads=[tMODT], writes=[tDER])
        yield

    ticker = [None]

    def tick():
        if ticker[0] is not None:
            try:
                next(ticker[0])
            except StopIteration:
                ticker[0] = None

    def drain():
        while ticker[0] is not None:
            tick()

    ticker[0] = mod_gen(0, layers[0])
    drain()

    def TS(t):
        return slice(t * 512, (t + 1) * 512)

    def norm_rstd(nt):
        for t in range(nt):
            pb = 6
            for m in range(NCH):
                b.op("act", lambda e, m=m: e.activation(out=SG[m % 2], in_=X[:, m, TS(t)], func=AF.Square),
                     reads=[tX[m][t]], writes=[tSG[m % 2]])
                b.group("pe", [lambda e, m=m: e.matmul(PS[pb][:], ones[:], SG[m % 2], start=(m == 0), stop=(m == 15))],
                        reads=[tSG[m % 2], tC], writes=[tPS[pb]])
            b.op("act", lambda e: e.activation(out=RS[:, TS(t)], in_=PS[pb][:], func=AF.Sqrt, bias=EPS, scale=1.0 / D),
                 reads=[tPS[pb]], writes=[tRS[t]])
            b.op("dve", lambda e: e.reciprocal(out=RS[:, TS(t)], in_=RS[:, TS(t)]), reads=[tRS[t]], writes=[tRS[t]])

    def adaln(li, n, col, nt):
        norm_rstd(nt)
        for t in range(nt):
            for m in range(NCH):
                b.op("dve", lambda e, m=m: e.tensor_tensor(out=TMP[m % 2], in0=X[:, m, TS(t)], in1=RS[:, TS(t)], op=ALU.mult),
                     reads=[tX[m][t], tRS[t]], writes=[tTMP[m % 2]])
                b.op("act", lambda e, m=m: e.activation(
                    out=H[:, m, TS(t)], in_=TMP[m % 2], func=AF.Identity, bias=der(li, n, col, 2, m), scale=der(li, n, col, 0, m)),
                    reads=[tTMP[m % 2], tDER], writes=[tH[m][t]])

    pcount = [0]

    def ffn(li, l, f, col, nt):
        n = 0 if f == 0 else 2
        Tn = nt * 512
        b.barrier()
        adaln(li, n, col, nt)
        cv = Carver(BASE)
        WIN = [cv.take(8192) for _ in range(3)]
        WOUT = [cv.take(16384) for _ in range(2)]
        HID = cv.take(8192)
        tWIN = [T() for _ in range(3)]
        tWOUT = [T(), T()]
        tHID = [T() for _ in range(2)]
        win = W[f"win{l}_{f}"]
        wout = W[f"wout{l}_{f}"]
        for g in range(11):
            gs = g % 2
            b.dma("pool", [(WOUT[gs], wout[g])], [], [tWOUT[gs]], sFW[2 + gs], max_dma_last_dim=8192)
            for c in range(4):
                j = 4 * g + c
                ws = j % 3
                b.dma("pool", [(WIN[ws], win[j])], [], [tWIN[ws]], sFWI[ws], max_dma_last_dim=8192)
                for t in range(nt):
                    kk = pcount[0]
                    pcount[0] += 1
                    pg = kk % 2
                    pu = 2 + kk % 2
                    rd = [tWIN[ws]] + [tH[m][t] for m in range(NCH)]
                    b.group("pe", [lambda e, kc=kc: e.matmul(PS[pg][:], WIN[ws][:, kc * 256: kc * 256 + 128], H[:, kc, TS(t)],
                                                              start=(kc == 0), stop=(kc == 15)) for kc in range(16)],
                            reads=rd, writes=[tPS[pg]])
                    b.group("pe", [lambda e, kc=kc: e.matmul(PS[pu][:], WIN[ws][:, kc * 256 + 128: kc * 256 + 256], H[:, kc, TS(t)],
                                                              start=(kc == 0), stop=(kc == 15)) for kc in range(16)],
                            reads=rd, writes=[tPS[pu]])
                    b.op("act", lambda e: e.activation(out=SG[kk % 2], in_=PS[pg][:], func=AF.Silu),
                         reads=[tPS[pg]], writes=[tSG[kk % 2]])
                    b.op("dve", lambda e: e.tensor_tensor(
                        out=HID[:, c * Tn + t * 512: c * Tn + (t + 1) * 512], in0=SG[kk % 2], in1=PS[pu][:], op=ALU.mult),
                        reads=[tSG[kk % 2], tPS[pu]], writes=[tHID[t]])
                tick()
            for m in range(NCH):
                for t in range(nt):
                    kk = pcount[0]
                    pcount[0] += 1
                    po = 4 + kk % 2
                    b.group("pe", [lambda e, c=c: e.matmul(PS[po][:], WOUT[gs][:, c * 2048 + m * 128: c * 2048 + (m + 1) * 128],
                                                            HID[:, c * Tn + t * 512: c * Tn + (t + 1) * 512],
                                                            start=(c == 0), stop=(c == 3)) for c in range(4)],
                            reads=[tWOUT[gs], tHID[t]], writes=[tPS[po]])
                    b.op("dve", lambda e: e.scalar_tensor_tensor(
                        out=X[:, m, TS(t)], in0=PS[po][:], scalar=der(li, n, col, 1, m), in1=X[:, m, TS(t)],
                        op0=ALU.mult, op1=ALU.add), reads=[tPS[po], tDER, tX[m][t]], writes=[tX[m][t]])

    sFW = [b.slot("fw") for _ in range(4)]
    sFWI = [b.slot("fwi") for _ in range(4)]
    sWP = [b.slot("wp") for _ in range(3)]
    sIO = [b.slot("io") for _ in range(6)]
    sWH = [b.slot("wh") for _ in range(2)]
    wpk = [0]

    def proj(WP, tWP, wchunk, nt, evac, tiles=None):
        sl = wpk[0] % len(WP)
        wpk[0] += 1
        b.dma("pool", [(WP[sl], wchunk)], [], [tWP[sl]], sWP[sl], max_dma_last_dim=8192)
        for t in (range(nt) if tiles is None else tiles):
            kk = pcount[0]
            pcount[0] += 1
            pb = kk % 2
            b.group("pe", [lambda e, kc=kc: e.matmul(PS[pb][:], WP[sl][:, kc * 128:(kc + 1) * 128], H[:, kc, TS(t)],
                                                      start=(kc == 0), stop=(kc == 15)) for kc in range(16)],
                    reads=[tWP[sl]] + [tH[m][t] for m in range(NCH)], writes=[tPS[pb]])
            evac(pb, t)

    def outproj(li, col, nt, Y, tY, WP, tWP, wo, kcs=range(16)):
        for m in range(NCH):
            sl = wpk[0] % len(WP)
            wpk[0] += 1
            b.dma("pool", [(WP[sl], wo[m])], [], [tWP[sl]], sWP[sl], max_dma_last_dim=8192)
            for t in range(nt):
                kk = pcount[0]
                pcount[0] += 1
                po = 4 + kk % 2
                k0 = kcs[0]
                b.group("pe", [lambda e, kc=kc: e.matmul(PS[po][:], WP[sl][:, kc * 128:(kc + 1) * 128], Y[:, kc - k0, TS(t)],
                                                          start=(kc == kcs[0]), stop=(kc == kcs[-1])) for kc in kcs],
                        reads=[tWP[sl], tY], writes=[tPS[po]])
                b.op("dve", lambda e: e.scalar_tensor_tensor(
                    out=X[:, m, TS(t)], in0=PS[po][:], scalar=der(li, 1, col, 1, m), in1=X[:, m, TS(t)],
                    op0=ALU.mult, op1=ALU.add), reads=[tPS[po], tDER, tX[m][t]], writes=[tX[m][t]])

    def attention(qparts, kparts, vfn, nkb, q0s, scale, rows, ydst, rtiles, wtile, PT, tPT, RD, tRD, tab=None):
        its = [(q0, n, kb) for (q0, n) in q0s for kb in range(nkb)]
        SBK = [0, 1, 4, 5]
        LA = 3

        def qk(idx):
            q0, n, kb = its[idx]
            sb = SBK[idx % 4]
            b.group("pe", [lambda e, i=i: e.matmul(PS[sb][:, 0:n], kparts[i](kb, q0), qparts[i](q0, n),
                                                    start=(i == 0), stop=(i == len(qparts) - 1)) for i in range(len(qparts))],
                    reads=rtiles, writes=[tPS[sb]])

        for j in range(min(LA, len(its))):
            qk(j)
        for idx, (q0, n, kb) in enumerate(its):
            if idx + LA < len(its):
                qk(idx + LA)
            sb = SBK[idx % 4]
            kk = idx
            pt = PT[kk % len(PT)]
            tb_ = tab(kb, q0, n) if tab is not None else None
            if tb_ is not None:
                tap, ttile = tb_
                b.op("dve", lambda e: e.scalar_tensor_tensor(out=TMP[kk % 2][:, 0:n], in0=PS[sb][:, 0:n], scalar=scale,
                                                             in1=tap, op0=ALU.mult, op1=ALU.add),
                     reads=[tPS[sb], ttile], writes=[tTMP[kk % 2]])
                b.op("act", lambda e: e.activation(out=pt[:, 0:n], in_=TMP[kk % 2][:, 0:n], func=AF.Exp),
                     reads=[tTMP[kk % 2]], writes=[tPT[kk % len(PT)]])
            else:
                b.op("act", lambda e: e.activation(out=pt[:, 0:n], in_=PS[sb][:, 0:n], func=AF.Exp, scale=scale),
                     reads=[tPS[sb]], writes=[tPT[kk % len(PT)]])
            b.group("pe", [lambda e: e.matmul(PS[2][:, 0:n], vfn(kb, q0), pt[:, 0:n], start=(kb == 0), stop=(kb == nkb - 1)),
                           lambda e: e.matmul(PS[3][:, 0:n], ones[:], pt[:, 0:n], start=(kb == 0), stop=(kb == nkb - 1))],
                    reads=[tPT[kk % len(PT)], tC] + rtiles, writes=[tPS[2], tPS[3]])
            if kb == nkb - 1:
                b.op("act", lambda e: e.activation(out=RD[rows, 0:n], in_=PS[3][rows, 0:n], func=AF.Ln), reads=[tPS[3]], writes=[tRD])
                b.op("act", lambda e: e.activation(out=RD[rows, 0:n], in_=RD[rows, 0:n], func=AF.Exp, scale=-1.0), reads=[tRD], writes=[tRD])
                b.op("dve", lambda e: e.tensor_tensor(out=ydst(q0, n), in0=PS[2][rows, 0:n], in1=RD[rows, 0:n], op=ALU.mult),
                     reads=[tPS[2], tRD], writes=[wtile])

    def mixer_ab(li, l, kind, ui, col, nt):
        Tn = nt * 512
        lat = kind == "lat"
        seqs = [(0, 1024)] if lat else [(0, 256), (256, 512)]
        b.barrier()
        adaln(li, 1, col, nt)
        cv = Carver(BASE)
        Y = cv.take(16 * Tn * 2, BF16, [16, Tn])
        tY = T()
        WP = [cv.take(4096) for _ in range(3)]
        tWP = [T() for _ in range(3)]
        LC = cv.take(8 * 11 * 4, F32, [8, 11])
        NSP = cv.take(8 * 4 * 4, F32, [8, 4])
        ST = cv.take(64, F32)
        SO = cv.take(128, F32)
        tLC = T()
        tSO = T()
        wab = W[f"wab{l}"]
        prs = [(LC, W[f"lc{l}"][:, :].rearrange("p (c k) -> p c k", k=11))]
        if lat:
            prs.append((ST, W[f"st{l}"][ui]))
        b.dma("sp", prs, [], [tLC], sIO[0])
        if DBG.get('skip_nsp'):
            return
        b.op("act", lambda e: e.activation(out=NSP[:, :, 0:2], in_=LC[:, :, 9:11], func=AF.Exp, scale=-1.0), reads=[tLC], writes=[tLC])
        b.op("act", lambda e: e.activation(out=NSP[:, :, 0:2], in_=NSP[:, :, 0:2], func=AF.Ln, bias=1.0), reads=[tLC], writes=[tLC])
        b.op("dve", lambda e: e.tensor_scalar(out=NSP[:, :, 2:4], in0=NSP[:, :, 0:2], scalar1=-16.0, scalar2=None, op0=ALU.mult), reads=[tLC], writes=[tLC])
        b.op("dve", lambda e: e.tensor_scalar(out=NSP[:, :, 0:2], in0=NSP[:, :, 0:2], scalar1=-8.0, scalar2=None, op0=ALU.mult), reads=[tLC], writes=[tLC])
        mark = cv.off
        XA2 = [cv.take(Tn * 4, F32) for _ in range(2)]
        GA2 = [cv.take(Tn * 4, F32) for _ in range(2)]
        XC = cv.take(Tn * 4, F32)
        R = cv.take(Tn * 4, F32)
        GI = cv.take(Tn * 4, F32)
        HF = cv.take(Tn * 4, F32)
        HB = cv.take(Tn * 4, F32)
        XCB = cv.take(Tn * 2)
        BD = cv.take(1024, BF16, [4, 128])
        tXA2 = [T(), T()]
        tGA2 = [T(), T()]
        tXC, tR, tGI, tHF, tHB, tXCB, tBD = [T() for _ in range(7)]
        for c in range(8 if not DBG.get('skip_lru') else 0):
            XA, tXA, GA, tGA = XA2[c % 2], tXA2[c % 2], GA2[c % 2], tGA2[c % 2]
            Sb, tS = XA, tXA
            b.dma("pool", [(BD, W[f"bd{l}"][c].rearrange("p (k n) -> p k n", n=128))], [], [tBD], sIO[1], max_dma_last_dim=8192)
            proj(WP, tWP, wab[c], nt, lambda pb, t: b.op("dve", lambda e: e.tensor_copy(out=XA[:, TS(t)], in_=PS[pb][:]),
                                                       reads=[tPS[pb]], writes=[tXA]))
            proj(WP, tWP, wab[8 + c], nt, lambda pb, t: b.op("dve", lambda e: e.tensor_copy(out=GA[:, TS(t)], in_=PS[pb][:]),
                                                           reads=[tPS[pb]], writes=[tGA]))
            for (s0, s1) in seqs:
                b.op("act", lambda e: e.activation(out=XC[:, s0:s1], in_=XA[:, s0:s1], func=AF.Identity,
                                                   bias=LC[:, c, 4:5], scale=LC[:, c, 2:3]), reads=[tXA, tLC], writes=[tXC])
                for (jj, sh) in ((0, -2), (1, -1), (3, 1)):
                    if sh < 0:
                        o_ = XC[:, s0 - sh:s1]
                        i_ = XA[:, s0:s1 + sh]
                    else:
                        o_ = XC[:, s0:s1 - sh]
                        i_ = XA[:, s0 + sh:s1]
                    b.op("dve", lambda e, o_=o_, i_=i_, jj=jj: e.scalar_tensor_tensor(
                        out=o_, in0=i_, scalar=LC[:, c, jj:jj + 1], in1=o_, op0=ALU.mult, op1=ALU.add),
                        reads=[tXA, tXC, tLC], writes=[tXC])
            b.op("dve", lambda e: e.tensor_copy(out=XCB, in_=XC), reads=[tXC], writes=[tXCB])
            for d in range(2):
                Hd, tHd = (HF, tHF) if d == 0 else (HB, tHB)
                for t in range(nt):
                    for (a, dst, tdst, bcol) in ((0, R, tR, 5 + d), (1, GI, tGI, 7 + d)):
                        kk = pcount[0]
                        pcount[0] += 1
                        pb = kk % 2
                        b.group("pe", [lambda e: e.matmul(PS[pb][:], BD[:, d * 2 + a, :], XCB[:, TS(t)], start=True, stop=True)],
                                reads=[tBD, tXCB], writes=[tPS[pb]])
                        b.op("act", lambda e: e.activation(out=dst[:, TS(t)], in_=PS[pb][:], func=AF.Sigmoid, bias=LC[:, c, bcol:bcol + 1]),
                             reads=[tPS[pb], tLC], writes=[tdst])
                b.op("act", lambda e: e.activation(out=Sb, in_=R, func=AF.Exp, scale=NSP[:, c, 2 + d:3 + d]), reads=[tR, tLC, tXC], writes=[tS])
                b.op("act", lambda e: e.activation(out=Sb, in_=Sb, func=AF.Sqrt, bias=1.0, scale=-1.0), reads=[tS], writes=[tS])
                b.op("act", lambda e: e.activation(out=R, in_=R, func=AF.Exp, scale=NSP[:, c, d:d + 1]), reads=[tR, tLC], writes=[tR])
                b.op("dve", lambda e: e.tensor_tensor(out=GI, in0=GI, in1=Sb, op=ALU.mult), reads=[tGI, tS], writes=[tGI])
                b.op("dve", lambda e: e.tensor_tensor(out=GI, in0=GI, in1=XC, op=ALU.mult), reads=[tGI, tXC], writes=[tGI])
                for si, (s0, s1) in enumerate(seqs):
                    init = ST[:, c * 2 + d:c * 2 + d + 1] if lat else 0.0
                    if d == 0:
                        b.op("dve", lambda e: e.tensor_tensor_scan(out=Hd[:, s0:s1], data0=R[:, s0:s1], data1=GI[:, s0:s1],
                                                                   initial=init, op0=ALU.mult, op1=ALU.add),
                             reads=[tR, tGI, tLC], writes=[tHd])
                    else:
                        b.op("dve", lambda e: e.tensor_tensor_scan(out=rev(Hd[:, s0:s1]), data0=rev(R[:, s0:s1]), data1=rev(GI[:, s0:s1]),
                                                                   initial=init, op0=ALU.mult, op1=ALU.add),
                             reads=[tR, tGI, tLC], writes=[tHd])
                    if not lat:
                        src = Hd[:, s1 - 1:s1] if d == 0 else Hd[:, s0:s0 + 1]
                        k_ = (c * 2 + d) * 2 + si
                        b.op("act", lambda e: e.activation(out=SO[:, k_:k_ + 1], in_=src, func=AF.Identity), reads=[tHd], writes=[tSO])
            b.op("dve", lambda e: e.tensor_tensor(out=R, in0=GA, in1=GA, op=ALU.mult), reads=[tGA, tGI], writes=[tR])
            b.op("dve", lambda e: e.tensor_scalar(out=R, in0=R, scalar1=0.044715, scalar2=1.0, op0=ALU.mult, op1=ALU.add), reads=[tR], writes=[tR])
            b.op("dve", lambda e: e.tensor_tensor(out=R, in0=R, in1=GA, op=ALU.mult), reads=[tR, tGA], writes=[tR])
            b.op("act", lambda e: e.activation(out=R, in_=R, func=AF.Sigmoid, scale=1.5957691216057308), reads=[tR], writes=[tR])
            b.op("dve", lambda e: e.tensor_tensor(out=R, in0=R, in1=GA, op=ALU.mult), reads=[tR, tGA], writes=[tR])
            b.op("dve", lambda e: e.tensor_tensor(out=HF, in0=HF, in1=HB, op=ALU.add), reads=[tHF, tHB], writes=[tHF])
            b.op("dve", lambda e: e.tensor_tensor(out=Y[:, c, :], in0=HF, in1=R, op=ALU.mult), reads=[tHF, tR], writes=[tY])
        if not lat:
            b.dma("sp", [(O[f"ost{l}"][ui], SO)], [tSO], [], sOut[0])
        b.barrier()
        cv.off = mark
        QT = cv.take(Tn * 2)
        KT = cv.take((Tn + 512) * 2)
        QM = cv.take(Tn * 2)
        KF = cv.take(Tn * 4, F32) if not lat else None
        VF = cv.take(Tn * 4, F32) if not lat else None
        nvb = (Tn + (512 if lat else 0)) // 128
        VT = cv.take(nvb * 128 * 2, BF16, [nvb, 128])
        PT = [cv.take(1024) for _ in range(4)]
        RD = cv.take(2048, F32)
        tQT, tKT, tQM, tKF, tVF, tVT, tRD = [T() for _ in range(7)]
        tPT = [T() for _ in range(4)]
        if lat:
            KC32 = cv.take(2048, F32)
            VC32 = cv.take(2048, F32, [4, 128])
            tKC32 = T()
            TAB = [cv.take(2048, BF16) for _ in range(8)]
            tTAB = [T() for _ in range(8)]
            for kb in range(8):
                b.op("pool", lambda e, kb=kb: e.memset(TAB[kb], NEG), writes=[tTAB[kb]])
        for c in range(8 if not DBG.get('skip_qkv') else 0):
            proj(WP, tWP, wab[16 + c], nt, lambda pb, t: b.op("act", lambda e: e.activation(out=QT[:, TS(t)], in_=PS[pb][:], func=AF.Identity),
                                                            reads=[tPS[pb]], writes=[tQT]))

            def evk(pb, t):
                if not lat:
                    b.op("act", lambda e: e.activation(out=KF[:, TS(t)], in_=PS[pb][:], func=AF.Identity), reads=[tPS[pb]], writes=[tKF])
                    b.op("dve", lambda e: e.tensor_copy(out=KT[:, TS(t)], in_=KF[:, TS(t)]), reads=[tKF], writes=[tKT])
                else:
                    b.op("dve", lambda e: e.tensor_copy(out=KT[:, TS(t)], in_=PS[pb][:]), reads=[tPS[pb]], writes=[tKT])
            proj(WP, tWP, wab[24 + c], nt, evk)
            if not lat:
                proj(WP, tWP, wab[32 + c], nt, lambda pb, t: b.op("act", lambda e: e.activation(out=VF[:, TS(t)], in_=PS[pb][:], func=AF.Identity),
                                                                reads=[tPS[pb]], writes=[tVF]))
                b.dma("sp", [(O[f"onak{l}"][ui, c], KF), (O[f"onav{l}"][ui, c], VF)], [tKF, tVF], [], sOut[1])
            sl = wpk[0] % 3
            wpk[0] += 1
            b.dma("pool", [(WP[sl], wab[32 + c])], [], [tWP[sl]], sWP[sl], max_dma_last_dim=8192)
            for tb in range(Tn // 128 if not DBG.get('skip_vt') else 0):
                kk = pcount[0]
                pcount[0] += 1
                pb = kk % 2
                t = tb // 4
                b.group("pe", [lambda e, kc=kc: e.matmul(PS[pb][:, 0:128], H[:, kc, tb * 128:(tb + 1) * 128], WP[sl][:, kc * 128:(kc + 1) * 128],
                                                          start=(kc == 0), stop=(kc == 15)) for kc in range(16)],
                        reads=[tWP[sl]] + [tH[m][t] for m in range(NCH)], writes=[tPS[pb]])
                b.op("act", lambda e: e.activation(out=VT[:, tb, :], in_=PS[pb][:, 0:128], func=AF.Identity), reads=[tPS[pb]], writes=[tVT])
            if lat:
                b.dma("sp", [(KC32, W[f"nakc{l}"][ui, c]),
                             (VC32, W[f"navc{l}"][ui][:, :, c * 128:(c + 1) * 128].rearrange("k p n -> p k n"))],
                      [], [tKC32], sIO[2])
                b.op("dve", lambda e: e.tensor_copy(out=KT[:, Tn:Tn + 512], in_=KC32), reads=[tKC32], writes=[tKT])
                b.op("dve", lambda e: e.tensor_copy(out=VT[:, 8:12, :], in_=VC32), reads=[tKC32], writes=[tVT])
            for hh in range(2 if not DBG.get('skip_att') else 0):
                rows = slice(hh * 64, (hh + 1) * 64)
                b.op("dve", lambda e: e.tensor_scalar(out=QM, in0=QT, scalar1=HM[:, hh:hh + 1], scalar2=None, op0=ALU.mult),
                     reads=[tQT, tC], writes=[tQM])
                if lat:
                    hd = 2 * c + hh
                    bz = W[f"bz{l}"]
                    for i in range(16):
                        r_lo = 0 if i <= 7 else i - 3
                        r_hi = 15 if i >= 8 else i + 4
                        nr = r_hi - r_lo + 1
                        rr0 = r_lo - i + 7
                        kb = i // 2
                        src = bass.AP(bz.tensor, bz.offset + ((hd * 15 + rr0) * 64) * 64, [[64, 64], [4096, nr], [1, 64]])
                        dst = TAB[kb][(i % 2) * 64:(i % 2) * 64 + 64, r_lo * 64:(r_hi + 1) * 64].rearrange("p (r q) -> p r q", q=64)
                        b.dma("pool", [(dst, src)], [], [tTAB[kb]], sTAB[kb])
                    attention([lambda q0, n: QM[:, q0:q0 + n]], [lambda kb, q0: KT[:, kb * 128:(kb + 1) * 128]],
                              lambda kb, q0: VT[:, kb, :], 12, [(0, 512), (512, 512)], NA_SCALE, rows,
                              lambda q0, n: Y[rows, 8 + c, q0:q0 + n], [tQM, tKT, tVT], tY, PT, tPT, RD, tRD,
                              tab=lambda kb, q0, n: (TAB[kb][:, q0:q0 + n], tTAB[kb]) if kb < 8 else None)
                else:
                    if True:
                        attention([lambda q0, n: QM[:, q0:q0 + n]], [lambda kb, q0: KT[:, (q0 // 128 + kb) * 128:(q0 // 128 + kb + 1) * 128]],
                                  lambda kb, q0: VT[:, q0 // 128 + kb, :], 2, [(0, 256), (256, 256)], NA_SCALE, rows,
                                  lambda q0, n: Y[rows, 8 + c, q0:q0 + n], [tQM, tKT, tVT], tY, PT, tPT, RD, tRD)
        if not DBG.get('skip_outproj'):
            outproj(li, col, nt, Y, tY, WP, tWP, W[f"wabo{l}"])

    sOut = [b.slot("out") for _ in range(6)]
    outslots.extend(sOut)
    sTAB = [b.slot("tab") for _ in range(8)]

    def mixer_cd(li, l, kind, ui, col, nt):
        Tn = nt * 512
        lat = kind == "lat"
        Tk = Tn + (512 if lat else 0)
        seqs = [(0, 1024)] if lat else [(0, 256), (256, 512)]
        wcd = W[f"wcd{l}"]
        b.barrier()
        adaln(li, 1, col, nt)
        cv = Carver(BASE)
        Y8 = cv.take(8 * Tn * 2, BF16, [8, Tn])
        tY = T()
        WP = [cv.take(4096) for _ in range(2)]
        tWP = [T() for _ in range(2)]
        CDC = cv.take(40, F32)
        tCDC = T()
        b.dma("sp", [(CDC, W[f"cdc{l}"][:, :])], [], [tCDC], sIO[0])
        mark = cv.off

        def rope_from(src, t0, n, dst, rd, wr):
            b.op("act", lambda e: e.activation(out=SG[0][:, 0:n], in_=src, func=AF.Identity), reads=rd, writes=[tSG[0]])
            b.group("pe", [lambda e: e.matmul(PS[5][:, 0:n], perm, SG[0][:, 0:n], start=True, stop=True)], reads=[tSG[0], tMat], writes=[tPS[5]])
            b.op("dve", lambda e: e.tensor_tensor(out=TMP[0][:, 0:n], in0=src, in1=COS[:, t0:t0 + n], op=ALU.mult), reads=rd + [tC], writes=[tTMP[0]])
            b.op("dve", lambda e: e.tensor_tensor(out=TMP[1][:, 0:n], in0=PS[5][:, 0:n], in1=SIN[:, t0:t0 + n], op=ALU.mult), reads=[tPS[5], tC], writes=[tTMP[1]])
            b.op("dve", lambda e: e.tensor_tensor(out=dst, in0=TMP[0][:, 0:n], in1=TMP[1][:, 0:n], op=ALU.add), reads=[tTMP[0], tTMP[1]], writes=wr)

        K2 = [cv.take(Tk * 2) for _ in range(4)]
        tK2 = [T() for _ in range(4)]
        nkbT = Tk // 128
        VD = cv.take(nkbT * 512 * 2, BF16, [nkbT, 512])
        tVD = T()
        WV = cv.take(16384)
        tWV = T()
        QT = cv.take(Tn * 2)
        QM = cv.take(Tn * 2)
        STG = cv.take(2048, F32)
        RSQ = cv.take(2048, F32)
        PT = [cv.take(1024) for _ in range(4)]
        RD = cv.take(2048, F32)
        tQT, tQM, tSTG, tRSQ, tRD = [T() for _ in range(5)]
        tPT = [T() for _ in range(4)]

        def headnorm(pb, t, gcol, dst, wr, rope, f32dma=None):
            b.op("act", lambda e: e.activation(out=STG, in_=PS[pb][:], func=AF.Identity), reads=[tPS[pb]], writes=[tSTG])
            b.op("act", lambda e: e.activation(out=SG[1], in_=PS[pb][:], func=AF.Square), reads=[tPS[pb]], writes=[tSG[1]])
            b.group("pe", [lambda e: e.matmul(PS[6][:], bones, SG[1], start=True, stop=True)], reads=[tSG[1], tMat], writes=[tPS[6]])
            b.op("act", lambda e: e.activation(out=RSQ, in_=PS[6][:], func=AF.Sqrt, bias=EPS, scale=1.0 / 64), reads=[tPS[6]], writes=[tRSQ])
            b.op("dve", lambda e: e.reciprocal(out=RSQ, in_=RSQ), reads=[tRSQ], writes=[tRSQ])
            if rope or f32dma is not None:
                b.op("dve", lambda e: e.scalar_tensor_tensor(out=STG, in0=STG, scalar=CDC[:, gcol:gcol + 1], in1=RSQ, op0=ALU.mult, op1=ALU.mult),
                     reads=[tSTG, tRSQ, tCDC], writes=[tSTG])
                if f32dma is not None:
                    f32dma()
                if rope:
                    rope_from(STG, t * 512, 512, dst, [tSTG], wr)
                else:
                    b.op("act", lambda e: e.activation(out=dst, in_=STG, func=AF.Identity), reads=[tSTG], writes=wr)
            else:
                b.op("dve", lambda e: e.scalar_tensor_tensor(out=dst, in0=STG, scalar=CDC[:, gcol:gcol + 1], in1=RSQ, op0=ALU.mult, op1=ALU.mult),
                     reads=[tSTG, tRSQ, tCDC], writes=wr)

        for kvh in range(4):
            def evk(pb, t, kvh=kvh):
                f = None
                if not lat:
                    f = lambda: b.dma("sp", [(O[f"ogk{l}"][ui, kvh], STG[0:64, :])], [tSTG], [], sOut[2])
                headnorm(pb, t, 1, K2[kvh][:, TS(t)], [tK2[kvh]], lat, f)
            proj(WP, tWP, wcd[8 + kvh], nt, evk)
        if lat:
            b.dma("pool", [(K2[kvh][:, Tn:Tn + 512], W[f"gqk{l}"][ui, kvh]) for kvh in range(4)]
                  + [(VD[:, 8:12, :], W[f"gqv{l}"][ui].rearrange("k p n -> p k n"))], [], tK2 + [tVD], sIO[2], max_dma_last_dim=8192)
        else:
            for i in range(2):
                def evv(pb, t, i=i):
                    b.op("act", lambda e: e.activation(out=STG, in_=PS[pb][:], func=AF.Identity), reads=[tPS[pb]], writes=[tSTG])
                    b.dma("sp", [(O[f"ogv{l}"][ui, i], STG)], [tSTG], [], sOut[2])
                proj(WP, tWP, wcd[12 + i], nt, evv)
        b.dma("pool", [(WV, W[f"wvd{l}"][:, :])], [], [tWV], sIO[3], max_dma_last_dim=8192)
        for tb in range(Tn // 128):
            kk = pcount[0]
            pcount[0] += 1
            pb = kk % 2
            b.group("pe", [lambda e, kc=kc: e.matmul(PS[pb][:], H[:, kc, tb * 128:(tb + 1) * 128], WV[:, kc * 512:(kc + 1) * 512],
                                                      start=(kc == 0), stop=(kc == 15)) for kc in range(16)],
                    reads=[tWV] + [tH[m][tb // 4] for m in range(NCH)], writes=[tPS[pb]])
            b.op("act", lambda e: e.activation(out=VD[:, tb, :], in_=PS[pb][:], func=AF.Identity), reads=[tPS[pb]], writes=[tVD])
        for c in range(8):
            kvh = c // 2
            proj(WP, tWP, wcd[c], nt, lambda pb, t: headnorm(pb, t, 0, QT[:, TS(t)], [tQT], lat))
            for hh in range(2):
                rows = slice(hh * 64, (hh + 1) * 64)
                b.op("dve", lambda e: e.tensor_scalar(out=QM, in0=QT, scalar1=HM[:, hh:hh + 1], scalar2=None, op0=ALU.mult),
                     reads=[tQT, tC], writes=[tQM])
                if lat:
                    attention([lambda q0, n: QM[:, q0:q0 + n]], [lambda kb, q0: K2[kvh][:, kb * 128:(kb + 1) * 128]],
                              lambda kb, q0: VD[:, kb, kvh * 128:(kvh + 1) * 128], 12, [(0, 512), (512, 512)], NA_SCALE, rows,
                              lambda q0, n: Y8[rows, c, q0:q0 + n], [tQM, tK2[kvh], tVD], tY, PT, tPT, RD, tRD)
                else:
                    if True:
                        attention([lambda q0, n: QM[:, q0:q0 + n]], [lambda kb, q0: K2[kvh][:, (q0 // 128 + kb) * 128:(q0 // 128 + kb + 1) * 128]],
                                  lambda kb, q0: VD[:, q0 // 128 + kb, kvh * 128:(kvh + 1) * 128], 2, [(0, 256), (256, 256)], NA_SCALE, rows,
                                  lambda q0, n: Y8[rows, c, q0:q0 + n], [tQM, tK2[kvh], tVD], tY, PT, tPT, RD, tRD)
        outproj(li, col, nt, Y8, tY, WP, tWP, W[f"wcdo{l}"], kcs=range(0, 8))

        b.barrier()
        cv.off = mark
        STG4 = cv.take(8192, F32, [4, 512])
        QAN = cv.take(4 * Tn * 2, BF16, [4, Tn])
        CKB = cv.take(4 * Tk * 2, BF16, [4, Tk])
        KR2 = cv.take(Tk * 2)
        RS2 = cv.take(2048, F32)
        WH2 = [cv.take(4096, BF16, [4, 4, 128]) for _ in range(2)]
        QN = cv.take(Tn * 2)
        QRh = cv.take(Tn * 2)
        KN = cv.take(Tk * 2)
        VM = cv.take(nkbT * 128 * 2, BF16, [nkbT, 128])
        PT = [cv.take(1024) for _ in range(4)]
        RD = cv.take(2048, F32)
        tSTG4, tQAN, tCKB, tKR2, tRS2, tQN, tQRh, tKN, tVM, tRD = [T() for _ in range(10)]
        tWH2 = [T(), T()]
        tPT = [T() for _ in range(4)]

        def norm512(wbase, gcol0, t, after):
            for i in range(4):
                def ev(pb, t_, i=i):
                    b.op("act", lambda e: e.activation(out=STG4[:, i, :], in_=PS[pb][:], func=AF.Identity), reads=[tPS[pb]], writes=[tSTG4])
                    b.op("act", lambda e: e.activation(out=SG[i % 2], in_=PS[pb][:], func=AF.Square), reads=[tPS[pb]], writes=[tSG[i % 2]])
                    b.group("pe", [lambda e: e.matmul(PS[6][:], ones[:], SG[i % 2], start=(i == 0), stop=(i == 3))],
                            reads=[tSG[i % 2], tC], writes=[tPS[6]])
                proj(WP, tWP, wcd[wbase + i], nt, ev, tiles=[t])
            b.op("act", lambda e: e.activation(out=RS2, in_=PS[6][:], func=AF.Sqrt, bias=EPS, scale=1.0 / 512), reads=[tPS[6]], writes=[tRS2])
            b.op("dve", lambda e: e.reciprocal(out=RS2, in_=RS2), reads=[tRS2], writes=[tRS2])
            for i in range(4):
                b.op("dve", lambda e: e.scalar_tensor_tensor(out=STG4[:, i, :], in0=STG4[:, i, :], scalar=CDC[:, gcol0 + i:gcol0 + i + 1], in1=RS2,
                                                             op0=ALU.mult, op1=ALU.mult), reads=[tSTG4, tRS2, tCDC], writes=[tSTG4])
                after(i)

        for t in range(nt):
            norm512(14, 2, t, lambda i: b.op("act", lambda e: e.activation(out=QAN[:, i, TS(t)], in_=STG4[:, i, :], func=AF.Identity),
                                             reads=[tSTG4], writes=[tQAN]))

            def after_ckv(i):
                if not lat:
                    b.dma("sp", [(O[f"omc{l}"][ui, i], STG4[:, i, :])], [tSTG4], [], sOut[3])
                b.op("act", lambda e: e.activation(out=CKB[:, i, TS(t)], in_=STG4[:, i, :], func=AF.Identity), reads=[tSTG4], writes=[tCKB])
            norm512(18, 6, t, after_ckv)

        def evkr(pb, t):
            b.op("act", lambda e: e.activation(out=RS2, in_=PS[pb][:], func=AF.Identity), reads=[tPS[pb]], writes=[tRS2])
            if lat:
                rope_from(RS2, t * 512, 512, KR2[:, TS(t)], [tRS2], [tKR2])
            else:
                b.dma("sp", [(O[f"omr{l}"][ui], RS2[0:64, :])], [tRS2], [], sOut[4])
                b.op("act", lambda e: e.activation(out=KR2[:, TS(t)], in_=RS2, func=AF.Identity), reads=[tRS2], writes=[tKR2])
        proj(WP, tWP, wcd[22], nt, evkr)
        if lat:
            b.dma("pool", [(CKB[:, :, Tn:Tn + 512], W[f"mlc{l}"][ui].rearrange("k p n -> p k n")), (KR2[:, Tn:Tn + 512], W[f"mlr{l}"][ui])],
                  [], [tCKB, tKR2], sIO[4], max_dma_last_dim=8192)
        wuqn = W[f"wuqn{l}"][:, :].rearrange("p (k n) -> p k n", n=1024)
        wuk = W[f"wuk{l}"][:, :].rearrange("p (k n) -> p k n", n=1024)
        wuv = W[f"wuv{l}"][:, :].rearrange("p (k n) -> p k n", n=1024)
        wuqr = W[f"wuqr{l}"][:, :].rearrange("p (k n) -> p k n", n=512)
        for h in range(8):
            hs = slice(h * 128, (h + 1) * 128)
            WH, tWH = WH2[h % 2], tWH2[h % 2]
            b.op("pool", lambda e: e.memset(WH[:, 3, :, :], 0.0), writes=[tWH])
            b.dma("pool", [(WH[:, 0, :, :], wuqn[:, :, hs]), (WH[:, 1, :, :], wuk[:, :, hs]), (WH[:, 2, :, :], wuv[:, :, hs]),
                           (WH[:, 3, :, (h % 2) * 64:(h % 2) * 64 + 64], wuqr[:, :, h * 64:(h + 1) * 64])], [], [tWH], sWH[h % 2])
            for t in range(nt):
                for (wi, dst, tdst, rp) in ((0, QN, tQN, False), (3, QRh, tQRh, lat)):
                    kk = pcount[0]
                    pcount[0] += 1
                    pb = kk % 2
                    b.group("pe", [lambda e, kc=kc: e.matmul(PS[pb][:], WH[:, wi, kc, :], QAN[:, kc, TS(t)], start=(kc == 0), stop=(kc == 3)) for kc in range(4)],
                            reads=[tWH, tQAN], writes=[tPS[pb]])
                    if rp:
                        b.op("act", lambda e: e.activation(out=RS2, in_=PS[pb][:], func=AF.Identity), reads=[tPS[pb]], writes=[tRS2])
                        rope_from(RS2, t * 512, 512, dst[:, TS(t)], [tRS2], [tdst])
                    else:
                        b.op("act", lambda e: e.activation(out=dst[:, TS(t)], in_=PS[pb][:], func=AF.Identity), reads=[tPS[pb]], writes=[tdst])
            for kt in range(Tk // 512):
                kk = pcount[0]
                pcount[0] += 1
                pb = kk % 2
                b.group("pe", [lambda e, kc=kc: e.matmul(PS[pb][:], WH[:, 1, kc, :], CKB[:, kc, TS(kt)], start=(kc == 0), stop=(kc == 3)) for kc in range(4)],
                        reads=[tWH, tCKB], writes=[tPS[pb]])
                b.op("act", lambda e: e.activation(out=KN[:, TS(kt)], in_=PS[pb][:], func=AF.Identity), reads=[tPS[pb]], writes=[tKN])
            for kb in range(nkbT):
                kk = pcount[0]
                pcount[0] += 1
                pb = kk % 2
                b.group("pe", [lambda e, kc=kc: e.matmul(PS[pb][:, 0:128], CKB[:, kc, kb * 128:(kb + 1) * 128], WH[:, 2, kc, :], start=(kc == 0), stop=(kc == 3)) for kc in range(4)],
                        reads=[tWH, tCKB], writes=[tPS[pb]])
                b.op("act", lambda e: e.activation(out=VM[:, kb, :], in_=PS[pb][:, 0:128], func=AF.Identity), reads=[tPS[pb]], writes=[tVM])
            rows = slice(0, 128)
            rt = [tQN, tQRh, tKN, tKR2, tVM]
            if lat:
                attention([lambda q0, n: QN[:, q0:q0 + n], lambda q0, n: QRh[:, q0:q0 + n]],
                          [lambda kb, q0: KN[:, kb * 128:(kb + 1) * 128], lambda kb, q0: KR2[:, kb * 128:(kb + 1) * 128]],
                          lambda kb, q0: VM[:, kb, :], 12, [(0, 512), (512, 512)], MLA_SCALE, rows,
                          lambda q0, n: Y8[:, h, q0:q0 + n], rt, tY, PT, tPT, RD, tRD)
            else:
                if True:
                    attention([lambda q0, n: QN[:, q0:q0 + n], lambda q0, n: QRh[:, q0:q0 + n]],
                              [lambda kb, q0: KN[:, (q0 // 128 + kb) * 128:(q0 // 128 + kb + 1) * 128], lambda kb, q0: KR2[:, (q0 // 128 + kb) * 128:(q0 // 128 + kb + 1) * 128]],
                              lambda kb, q0: VM[:, q0 // 128 + kb, :], 2, [(0, 256), (256, 256)], MLA_SCALE, rows,
                              lambda q0, n: Y8[:, h, q0:q0 + n], rt, tY, PT, tPT, RD, tRD)
        outproj(li, col, nt, Y8, tY, WP, tWP, W[f"wcdo{l}"], kcs=range(8, 16))

    for ui in range(nu):
        for kind in ("ctx", "lat"):
            if DBG.get('only_' + ('lat' if kind == 'ctx' else 'ctx')):
                continue
            nt = 1 if kind == "ctx" else 2
            Tn = nt * 512
            col = 0 if kind == "ctx" else 1 + ui
            src = xc if kind == "ctx" else xl
            dst = yc if kind == "ctx" else yl
            b.dma("sp", [(X[:, m, 0:Tn], src[ui, m]) for m in range(NCH)], [], allX, sX)
            for li, l in enumerate(layers):
                if ui == 0 and kind == "ctx" and li + 1 < len(layers) and not DBG.get('only_lat'):
                    ticker[0] = mod_gen(li + 1, layers[li + 1])
                ffn(li, l, 0, col, nt)
                if mixers:
                    if l % 2 == 0:
                        mixer_ab(li, l, kind, ui, col, nt)
                    else:
                        mixer_cd(li, l, kind, ui, col, nt)
                    ffn(li, l, 1, col, nt)
                if ticker[0] is not None:
                    b.barrier()
                    drain()
            norm_rstd(nt)
            for t in range(nt):
                for m in range(NCH):
                    b.op("dve", lambda e, m=m: e.scalar_tensor_tensor(
                        out=X[:, m, TS(t)], in0=X[:, m, TS(t)], scalar=FG[:, m:m + 1], in1=RS[:, TS(t)], op0=ALU.mult, op1=ALU.mult),
                        reads=[tX[m][t], tRS[t], tC], writes=[tX[m][t]])
            b.dma("sp", [(dst[ui, m], X[:, m, 0:Tn]) for m in range(NCH)], allX, [], sY)
    b.finish(outslots)
    return b


def _chunks(w, nchunk):
    return np.ascontiguousarray(w.reshape(16, 128, nchunk, 128).transpose(2, 1, 0, 3)).reshape(nchunk, 128, 2048)


def _rows4(w, nk, ncol):
    return np.ascontiguousarray(w.reshape(nk, 128, ncol).transpose(1, 0, 2)).reshape(128, nk * ncol)


def _const_tables():
    t = np.arange(1024)
    nf = 16
    inv = 1.0 / (10000.0 ** (np.arange(nf, dtype=np.float32) / nf))
    d = np.arange(128) % 64
    half = d // 32
    i = d % 16
    pos = np.where(half[:, None] == 0, (t // 64)[None, :], (t % 64)[None, :]).astype(np.float32)
    ang = pos * inv[i][:, None]
    cos = np.cos(ang).astype(np.float32)
    sin = np.sin(ang).astype(np.float32)
    perm = np.zeros((128, 128), np.float32)
    for m in range(128):
        j = (m % 64) % 32
        if j < 16:
            perm[m + 16, m] = -1.0
        else:
            perm[m - 16, m] = 1.0
    bones = np.zeros((128, 128), np.float32)
    bones[:64, :64] = 1.0
    bones[64:, 64:] = 1.0
    matd = np.concatenate([perm, bones, np.zeros((128, 128), np.float32)], axis=1)
    hm = np.zeros((128, 2), np.float32)
    hm[:64, 0] = 1.0
    hm[64:, 1] = 1.0
    return cos, sin, matd, hm


def _layout_common(inp, layers, mixers=True):
    out = {}
    cos, sin, matd, hm = _const_tables()
    out["cosd"], out["sind"], out["matd"] = cos, sin, matd
    fg = np.asarray(inp["final_gain"]).reshape(16, 128).T
    out["cst"] = np.ascontiguousarray(np.concatenate([fg, hm], axis=1).astype(np.float32))
    for l in layers:
        j = l // 2
        wm = np.asarray(inp["w_mod"][l])
        out[f"wmod{l}"] = np.ascontiguousarray(wm.reshape(16, 128, 72, 256).transpose(2, 1, 0, 3)).reshape(72, 128, 4096)
        out[f"bmod{l}"] = np.ascontiguousarray(np.asarray(inp["b_mod"][l]).reshape(144, 128).T)
        out[f"gain{l}"] = np.ascontiguousarray(np.asarray(inp["norm_gain"][l]).reshape(3, 16, 128).transpose(2, 0, 1)).reshape(128, 48)
        for f in range(2):
            w = np.asarray(inp["w_ffn_in"][l][f])
            g = w[:, :DFF].reshape(16, 128, NJ, 128)
            u = w[:, DFF:].reshape(16, 128, NJ, 128)
            cat = np.concatenate([g, u], axis=-1)
            out[f"win{l}_{f}"] = np.ascontiguousarray(cat.transpose(2, 1, 0, 3)).reshape(NJ, 128, 4096)
            wo = np.asarray(inp["w_ffn_out"][l][f])
            out[f"wout{l}_{f}"] = np.ascontiguousarray(wo.reshape(11, 4, 128, 2048).transpose(0, 2, 1, 3)).reshape(11, 128, 8192)
        if not mixers:
            continue
        if l % 2 == 0:
            out[f"wab{l}"] = _chunks(np.asarray(inp["w_in_ab"][j]), 40)
            out[f"wabo{l}"] = _chunks(np.asarray(inp["w_out_ab"][j]), 16)
            wa = np.asarray(inp["lru_wa"][j])
            wx = np.asarray(inp["lru_wx"][j])
            bd = np.zeros((8, 128, 4, 128), np.float32)
            for c in range(8):
                for d in range(2):
                    for a, ww in enumerate((wa, wx)):
                        bd[c, :64, d * 2 + a, :64] = ww[d, 2 * c]
                        bd[c, 64:, d * 2 + a, 64:] = ww[d, 2 * c + 1]
            out[f"bd{l}"] = bd.reshape(8, 128, 512)
            cols = [np.asarray(inp["conv_w"][j])[k] for k in range(4)] + [np.asarray(inp["conv_b"][j])]
            cols += [np.asarray(inp["lru_ba"][j])[0], np.asarray(inp["lru_ba"][j])[1],
                     np.asarray(inp["lru_bx"][j])[0], np.asarray(inp["lru_bx"][j])[1],
                     np.asarray(inp["lru_lambda"][j])[0], np.asarray(inp["lru_lambda"][j])[1]]
            lc = np.stack(cols, axis=-1)
            out[f"lc{l}"] = np.ascontiguousarray(lc.reshape(8, 128, 11).transpose(1, 0, 2)).reshape(128, 88)
            nb = np.asarray(inp["na_bias"][j])
            kc = np.arange(64)[:, None]
            qc = np.arange(64)[None, :]
            relc = np.clip(kc - qc + 15, 0, 30)
            cstart = np.clip(qc - 8, 0, 48)
            ok = (kc >= cstart) & (kc < cstart + 16)
            bz = np.empty((16, 15, 64, 64), np.float32)
            for rr in range(15):
                g = nb[:, 14 - rr][:, relc]
                bz[:, rr] = np.where(ok[None], g, np.float32(NEG))
            out[f"bz{l}"] = bz
        else:
            w = np.asarray(inp["w_in_cd"][j])
            cols = [w[:, c * 128:(c + 1) * 128] for c in range(8)]
            for kvh in range(4):
                kk = w[:, 1024 + kvh * 64:1024 + (kvh + 1) * 64]
                cols.append(np.concatenate([kk, kk], axis=1))
            cols += [w[:, 1280:1408], w[:, 1408:1536]]
            cols += [w[:, 1536 + i * 128:1536 + (i + 1) * 128] for i in range(4)]
            cols += [w[:, 2048 + i * 128:2048 + (i + 1) * 128] for i in range(4)]
            cols.append(np.concatenate([w[:, 2560:2624], w[:, 2560:2624]], axis=1))
            out[f"wcd{l}"] = _chunks(np.concatenate(cols, axis=1), 23)
            vd = np.concatenate([np.concatenate([w[:, 1280 + k * 64:1280 + (k + 1) * 64]] * 2, axis=1) for k in range(4)], axis=1)
            out[f"wvd{l}"] = _rows4(vd, 16, 512)
            out[f"wcdo{l}"] = _chunks(np.asarray(inp["w_out_cd"][j]), 16)
            gq = np.tile(np.asarray(inp["gqa_q_gain"][j]), 2)[:, None]
            gk = np.tile(np.asarray(inp["gqa_k_gain"][j]), 2)[:, None]
            mq = np.asarray(inp["mla_q_gain"][j]).reshape(4, 128).T
            mk = np.asarray(inp["mla_kv_gain"][j]).reshape(4, 128).T
            out[f"cdc{l}"] = np.ascontiguousarray(np.concatenate([gq, gk, mq, mk], axis=1).astype(np.float32))
            uq = np.asarray(inp["mla_w_uq"][j]).reshape(512, 8, 192)
            out[f"wuqn{l}"] = _rows4(np.ascontiguousarray(uq[:, :, :128]).reshape(512, 1024), 4, 1024)
            out[f"wuqr{l}"] = _rows4(np.ascontiguousarray(uq[:, :, 128:]).reshape(512, 512), 4, 512)
            out[f"wuk{l}"] = _rows4(np.asarray(inp["mla_w_uk"][j]), 4, 1024)
            out[f"wuv{l}"] = _rows4(np.asarray(inp["mla_w_uv"][j]), 4, 1024)
    return out


def _layout_unit(inp, units, layers, mixers=True):
    nu = len(units)
    xp = np.asarray(inp["x_prompt"])
    xs = np.asarray(inp["x_sample"])
    c = np.asarray(inp["c"])
    cond = np.empty((1 + nu, D), np.float32)
    cond[0] = np.asarray(inp["c_ctx"])
    xc = np.empty((nu, NCH, 128, 512), np.float32)
    xl = np.empty((nu, NCH, 128, 1024), np.float32)
    for k, u in enumerate(units):
        tok = np.concatenate([xp[2 * u], xp[2 * u + 1]], axis=0)
        xc[k] = tok.T.reshape(NCH, 128, 512)
        xl[k] = xs[u].T.reshape(NCH, 128, 1024)
        cond[1 + k] = c[u]
    ncols = 1 + nu
    out = {"xc": xc, "xl": xl,
           "condT": np.ascontiguousarray(cond.T.reshape(NCH, 128, ncols).transpose(1, 0, 2)).reshape(128, NCH * ncols)}
    if not mixers:
        return out
    for l in layers:
        j = l // 2
        if l % 2 == 0:
            sf = np.asarray(inp["state_lru_fwd"])[units, j]
            sb = np.asarray(inp["state_lru_bwd"])[units, j]
            st = np.stack([sf, sb], axis=-1).reshape(nu, 8, 128, 2).transpose(0, 2, 1, 3)
            out[f"st{l}"] = np.ascontiguousarray(st).reshape(nu, 128, 16)
            kk = np.asarray(inp["cache_na_k"])[units, j].reshape(nu, 512, 1024)
            out[f"nakc{l}"] = np.ascontiguousarray(kk.transpose(0, 2, 1)).reshape(nu, 8, 128, 512)
            out[f"navc{l}"] = np.ascontiguousarray(np.asarray(inp["cache_na_v"])[units, j].reshape(nu, 4, 128, 1024))
        else:
            gk = np.asarray(inp["cache_gqa_k"])[units, j]
            kt = gk.transpose(0, 2, 3, 1)
            out[f"gqk{l}"] = np.ascontiguousarray(np.concatenate([kt, kt], axis=2))
            gv = np.asarray(inp["cache_gqa_v"])[units, j]
            out[f"gqv{l}"] = np.ascontiguousarray(np.concatenate([gv, gv], axis=3)).reshape(nu, 4, 128, 512)
            mc = np.asarray(inp["cache_mla_ckv"])[units, j]
            out[f"mlc{l}"] = np.ascontiguousarray(mc.transpose(0, 2, 1)).reshape(nu, 4, 128, 512)
            mr = np.asarray(inp["cache_mla_krope"])[units, j].transpose(0, 2, 1)
            out[f"mlr{l}"] = np.ascontiguousarray(np.concatenate([mr, mr], axis=1))
    return out


NCORES = 8


def kernel(**inp):
    ncores = NCORES
    nu = 8 // ncores
    layers = [0, 1, 2, 3]
    bld = build(nu, layers, True)
    common = _layout_common(inp, layers, True)
    in_maps = []
    for core in range(ncores):
        d = dict(common)
        d.update(_layout_unit(inp, [core * nu + k for k in range(nu)], layers, True))
        in_maps.append(d)
    res = run_bass_kernel_spmd(bld.nc, in_maps, core_ids=list(range(ncores)))
    y_prompt = np.empty((16, 256, D), np.float32)
    y_sample = np.empty((8, 1024, D), np.float32)
    st_f = np.empty((16, 2, 1024), np.float32)
    st_b = np.empty((16, 2, 1024), np.float32)
    na_k = np.empty((16, 2, 256, 16, 64), np.float32)
    na_v = np.empty((16, 2, 256, 16, 64), np.float32)
    gq_k = np.empty((16, 2, 256, 4, 64), np.float32)
    gq_v = np.empty((16, 2, 256, 4, 64), np.float32)
    ml_c = np.empty((16, 2, 256, 512), np.float32)
    ml_r = np.empty((16, 2, 256, 64), np.float32)
    for core in range(ncores):
        r = res.results[core]
        for k in range(nu):
            u = core * nu + k
            yp = r["yc"][k].reshape(D, 512).T
            y_prompt[2 * u] = yp[:256]
            y_prompt[2 * u + 1] = yp[256:]
            y_sample[u] = r["yl"][k].reshape(D, 1024).T
            for l in layers:
                j = l // 2
                if l % 2 == 0:
                    so = r[f"ost{l}"][k].reshape(128, 8, 2, 2).transpose(3, 2, 1, 0).reshape(2, 2, 1024)
                    kk = r[f"onak{l}"][k].reshape(1024, 512).T.reshape(2, 256, 16, 64)
                    vv = r[f"onav{l}"][k].reshape(1024, 512).T.reshape(2, 256, 16, 64)
                    for s in range(2):
                        st_f[2 * u + s, j] = so[s, 0]
                        st_b[2 * u + s, j] = so[s, 1]
                        na_k[2 * u + s, j] = kk[s]
                        na_v[2 * u + s, j] = vv[s]
                else:
                    kk = r[f"ogk{l}"][k].reshape(256, 512).T.reshape(2, 256, 4, 64)
                    vv = r[f"ogv{l}"][k].reshape(256, 512).T.reshape(2, 256, 4, 64)
                    cc = r[f"omc{l}"][k].reshape(512, 512).T.reshape(2, 256, 512)
                    rr = r[f"omr{l}"][k].T.reshape(2, 256, 64)
                    for s in range(2):
                        gq_k[2 * u + s, j] = kk[s]
                        gq_v[2 * u + s, j] = vv[s]
                        ml_c[2 * u + s, j] = cc[s]
                        ml_r[2 * u + s, j] = rr[s]
    return (y_prompt, y_sample, st_f, st_b, na_k, na_v, gq_k, gq_v, ml_c, ml_r)
```

```python
import numpy as np
from contextlib import ExitStack
import concourse.bass as bass
import concourse.mybir as mybir
from concourse.bass_utils import run_bass_kernel_spmd

F32 = mybir.dt.float32
BF16 = mybir.dt.bfloat16
AF = mybir.ActivationFunctionType
ALU = mybir.AluOpType

D = 2048
NCH = 16
TOK = 1536
NT = 3
DFF = 5632
NJ = 44
EPS = 1e-6
SEMCH = 30000


class T:
    __slots__ = ("w", "r", "x")

    def __init__(self, x=False):
        self.w = None
        self.r = []
        self.x = x


class Slot:
    def __init__(self, sem):
        self.sem = sem
        self.count = 0


class Builder:
    def __init__(self):
        self.nc = bass.Bass("TRN2", target_bir_lowering=False)
        self.es = ExitStack()
        nc = self.nc
        self.eng = {"pe": nc.tensor, "act": nc.scalar, "dve": nc.vector, "pool": nc.gpsimd, "sp": nc.sync}
        self.cnt = {e: 0 for e in self.eng}
        self.sems = {e: [] for e in self.eng}
        self.seen = {e: {} for e in self.eng}
        self.nsem = 0
        self.slots = []

    def new_sem(self, name):
        self.nsem += 1
        return self.es.enter_context(self.nc.semaphore(f"{name}_{self.nsem}"))

    def slot(self, name="s"):
        sl = Slot(self.new_sem(name))
        self.slots.append(sl)
        return sl

    def barrier(self):
        for e in self.eng:
            for e2 in self.eng:
                if e2 != e and self.cnt[e2] > 0:
                    self._wait(e, ("e", e2, self.cnt[e2]))
            for sl in self.slots:
                if sl.count:
                    self._wait(e, ("d", sl, sl.count))

    def sbuf(self, name, shape, dt):
        return self.es.enter_context(self.nc.sbuf_tensor(name, shape, dt))

    def psum(self, name, shape, dt):
        return self.es.enter_context(self.nc.psum_tensor(name, shape, dt))

    def _engsem(self, e, idx):
        ch = (idx - 1) // SEMCH
        while len(self.sems[e]) <= ch:
            self.sems[e].append(self.new_sem(f"e_{e}"))
        return self.sems[e][ch], idx - ch * SEMCH

    def _wait(self, e, ev):
        if ev[0] == "e":
            _, e2, idx = ev
            if e2 == e and e == "pe":
                return
            key = ("e", e2)
            if self.seen[e].get(key, 0) >= idx:
                return
            self.seen[e][key] = idx
            sem, val = self._engsem(e2, idx)
        else:
            _, sl, val = ev
            key = ("d", id(sl))
            if self.seen[e].get(key, 0) >= val:
                return
            self.seen[e][key] = val
            sem = sl.sem
        self.eng[e].wait_ge(sem, val)

    def _deps(self, e, reads, writes):
        for t in reads:
            if t.w is not None:
                self._wait(e, t.w)
            if t.x:
                for ev in t.r:
                    if not (ev[0] == "e" and ev[1] == e):
                        self._wait(e, ev)
        for t in writes:
            if t.w is not None:
                self._wait(e, t.w)
            for ev in t.r:
                self._wait(e, ev)

    def _commit(self, ev, reads, writes):
        for t in reads:
            t.r.append(ev)
            if len(t.r) > 24:
                t.r = t.r[-24:] if False else t.r
        for t in writes:
            t.w = ev
            t.r = []

    def op(self, e, fn, reads=(), writes=()):
        self._deps(e, reads, writes)
        ins = fn(self.eng[e])
        self.cnt[e] += 1
        idx = self.cnt[e]
        sem, _ = self._engsem(e, idx)
        ins.then_inc(sem, 1)
        self._commit(("e", e, idx), reads, writes)

    def group(self, e, fns, reads=(), writes=()):
        self._deps(e, reads, writes)
        ins = None
        for fn in fns:
            ins = fn(self.eng[e])
        self.cnt[e] += 1
        idx = self.cnt[e]
        sem, _ = self._engsem(e, idx)
        ins.then_inc(sem, 1)
        self._commit(("e", e, idx), reads, writes)

    def dma(self, e, pairs, reads, writes, slot, **kw):
        self._deps(e, reads, writes)
        for (o, i) in pairs:
            self.eng[e].dma_start(out=o, in_=i, **kw).then_inc(slot.sem, 16)
            slot.count += 16
        self._commit(("d", slot, slot.count), reads, writes)

    def finish(self, slots):
        for sl in slots:
            if sl.count:
                self.eng["sp"].wait_ge(sl.sem, sl.count)


def rev(ap2d):
    (ps, pn), (fs, fn) = ap2d.ap
    return bass.AP(ap2d.tensor, ap2d.offset + (fn - 1) * fs, [[ps, pn], [-fs, fn]])


NA_SCALE = 0.125
MLA_SCALE = 192 ** -0.5
NEG = -30000.0
DBG = {}


def build(nu, layers, mixers=True):
    b = Builder()
    nc = b.nc
    ncols = 1 + nu
    L = len(layers)
    TM = 1024

    def din(name, shape, dt=F32):
        return nc.dram_tensor(name, list(shape), dt, kind="ExternalInput").ap()

    def dout(name, shape, dt=F32):
        return nc.dram_tensor(name, list(shape), dt, kind="ExternalOutput").ap()

    xc = din("xc", [nu, NCH, 128, 512])
    xl = din("xl", [nu, NCH, 128, 1024])
    yc = dout("yc", [nu, NCH, 128, 512])
    yl = dout("yl", [nu, NCH, 128, 1024])
    condT = din("condT", [128, NCH * ncols])
    cst = din("cst", [128, 18])
    cosd = din("cosd", [128, 1024])
    sind = din("sind", [128, 1024])
    matd = din("matd", [128, 3 * 128])
    W = {}
    O = {}
    for l in layers:
        W[f"wmod{l}"] = din(f"wmod{l}", [72, 128, 16 * 256])
        W[f"bmod{l}"] = din(f"bmod{l}", [128, 144])
        W[f"gain{l}"] = din(f"gain{l}", [128, 48])
        for f in range(2):
            W[f"win{l}_{f}"] = din(f"win{l}_{f}", [NJ, 128, 16 * 256])
            W[f"wout{l}_{f}"] = din(f"wout{l}_{f}", [11, 128, 4 * 2048])
        if not mixers:
            continue
        if l % 2 == 0:
            W[f"wab{l}"] = din(f"wab{l}", [40, 128, 2048])
            W[f"wabo{l}"] = din(f"wabo{l}", [16, 128, 2048])
            W[f"bd{l}"] = din(f"bd{l}", [8, 128, 512])
            W[f"lc{l}"] = din(f"lc{l}", [128, 8 * 11])
            W[f"bz{l}"] = din(f"bz{l}", [16, 15, 64, 64])
            W[f"st{l}"] = din(f"st{l}", [nu, 128, 16])
            W[f"nakc{l}"] = din(f"nakc{l}", [nu, 8, 128, 512])
            W[f"navc{l}"] = din(f"navc{l}", [nu, 4, 128, 1024])
            O[f"ost{l}"] = dout(f"ost{l}", [nu, 128, 32])
            O[f"onak{l}"] = dout(f"onak{l}", [nu, 8, 128, 512])
            O[f"onav{l}"] = dout(f"onav{l}", [nu, 8, 128, 512])
        else:
            W[f"wcd{l}"] = din(f"wcd{l}", [23, 128, 2048])
            W[f"wvd{l}"] = din(f"wvd{l}", [128, 16 * 512])
            W[f"wcdo{l}"] = din(f"wcdo{l}", [16, 128, 2048])
            W[f"cdc{l}"] = din(f"cdc{l}", [128, 10])
            W[f"wuqn{l}"] = din(f"wuqn{l}", [128, 4 * 1024])
            W[f"wuqr{l}"] = din(f"wuqr{l}", [128, 4 * 512])
            W[f"wuk{l}"] = din(f"wuk{l}", [128, 4 * 1024])
            W[f"wuv{l}"] = din(f"wuv{l}", [128, 4 * 1024])
            W[f"gqk{l}"] = din(f"gqk{l}", [nu, 4, 128, 512])
            W[f"gqv{l}"] = din(f"gqv{l}", [nu, 4, 128, 512])
            W[f"mlc{l}"] = din(f"mlc{l}", [nu, 4, 128, 512])
            W[f"mlr{l}"] = din(f"mlr{l}", [nu, 128, 512])
            O[f"ogk{l}"] = dout(f"ogk{l}", [nu, 4, 64, 512])
            O[f"ogv{l}"] = dout(f"ogv{l}", [nu, 2, 128, 512])
            O[f"omc{l}"] = dout(f"omc{l}", [nu, 4, 128, 512])
            O[f"omr{l}"] = dout(f"omr{l}", [nu, 64, 512])

    X = b.sbuf("X", [128, NCH, TM], F32)
    H = b.sbuf("H", [128, NCH, TM], BF16)
    RS = b.sbuf("RS", [128, TM], F32)
    ones = b.sbuf("ones", [128, 128], BF16)
    MATS = b.sbuf("MATS", [128, 384], BF16)
    perm = MATS[:, 0:128]
    bones = MATS[:, 128:256]
    HM = b.sbuf("HM", [128, 2], F32)
    COS = b.sbuf("COS", [128, 1024], F32)
    SIN = b.sbuf("SIN", [128, 1024], F32)
    DER = b.sbuf("DER", [128, L * 3 * ncols * 3 * NCH], F32)
    GAIN = b.sbuf("GAIN", [128, L, 48], F32)
    FG = b.sbuf("FG", [128, NCH], F32)
    CT = b.sbuf("CT", [128, NCH * ncols], F32)
    CB = b.sbuf("CB", [128, NCH * ncols], BF16)
    ABYTES = 92160
    AR = b.sbuf("ARENA", [128, ABYTES // 2], BF16)
    PS = [b.psum(f"ps{i}", [128, 512], F32) for i in range(8)]

    class Carver:
        def __init__(self, base=0):
            self.off = base

        def take(self, nbytes, dt=BF16, shape=None):
            assert self.off % 4 == 0
            a = AR[:, self.off // 2:(self.off + nbytes) // 2]
            self.off += nbytes
            assert self.off <= ABYTES, self.off
            if dt == F32:
                a = a.bitcast(F32)
            if shape is not None:
                names = " ".join(f"d{i}" for i in range(len(shape)))
                a = a.rearrange(f"p ({names}) -> p {names}", **{f"d{i}": s for i, s in enumerate(shape)})
            return a

    tX = [[T() for _ in range(2)] for _ in range(NCH)]
    tH = [[T() for _ in range(2)] for _ in range(NCH)]
    tRS = [T() for _ in range(2)]
    tPS = [T(True) for _ in range(8)]
    tC = T()
    tDER = T()
    allX = [tX[m][t] for m in range(NCH) for t in range(2)]
    allH = [tH[m][t] for m in range(NCH) for t in range(2)]

    def der(li, n, col, k, m):
        off = ((((li * 3 + n) * ncols + col) * 3 + k) * NCH) + m
        return DER[:, off:off + 1]

    cv0 = Carver()
    SG = [cv0.take(1024) for _ in range(2)]
    TMP = [cv0.take(2048, F32) for _ in range(2)]
    tSG = [T(), T()]
    tTMP = [T(), T()]
    BASE = cv0.off
    sMisc = b.slot("misc")
    sX = b.slot("x")
    sY = b.slot("y")
    outslots = [sY]

    b.op("dve", lambda e: e.memset(ones[:], 1.0), writes=[tC])
    pairs = [(CT[:], condT[:, :]), (FG[:], cst[:, 0:16]), (HM[:], cst[:, 16:18]), (COS[:], cosd[:, :]), (SIN[:], sind[:, :])]
    for li, l in enumerate(layers):
        pairs.append((GAIN[:, li, :], W[f"gain{l}"][:, :]))
    b.dma("sp", pairs, [], [tC], sMisc)
    sMat = b.slot("mat")
    tMat = T()
    b.dma("pool", [(MATS[:], matd[:, :])], [], [tMat], sMat)
    b.op("act", lambda e: e.activation(out=CB[:], in_=CT[:], func=AF.Silu), reads=[tC], writes=[tC])

    MODB = BASE + 65536
    WM = [AR[:, (MODB + i * 8192) // 2:(MODB + (i + 1) * 8192) // 2] for i in range(2)]
    MODT = AR[:, (MODB + 16384) // 2:(MODB + 16384 + 144 * ncols * 4) // 2].bitcast(F32)
    BM = AR[:, (MODB + 16384 + 1728) // 2:(MODB + 16384 + 1728 + 576) // 2].bitcast(F32)
    assert MODB + 16384 + 1728 + 576 <= ABYTES
    tWM = [T(), T()]
    tMODT = T()
    tBM = T()
    sWM = [b.slot("wm"), b.slot("wm")]
    sBM = b.slot("bm")
    wmk = [0]

    def mod_gen(li, l):
        pm = PS[7]
        for s_ in range(72):
            sl = wmk[0] % 2
            wmk[0] += 1
            b.dma("pool", [(WM[sl], W[f"wmod{l}"][s_])], [], [tWM[sl]], sWM[sl], max_dma_last_dim=8192)
            fns = []
            for q in range(2):
                oc = 2 * s_ + q
                for kc in range(16):
                    fns.append(lambda e, q=q, kc=kc, oc=oc, sl=sl: e.matmul(
                        pm[:, oc * ncols:(oc + 1) * ncols],
                        WM[sl][:, kc * 256 + q * 128: kc * 256 + (q + 1) * 128],
                        CB[:, kc * ncols:(kc + 1) * ncols], start=(kc == 0), stop=(kc == 15)))
            b.group("pe", fns, reads=[tWM[sl], tC], writes=[tPS[7]])
            yield
        b.dma("sp", [(BM, W[f"bmod{l}"][:, :])], [], [tBM], sBM)
        for col in range(ncols):
            b.op("dve", lambda e, col=col: e.tensor_tensor(
                out=MODT[:, col:144 * ncols:ncols], in0=pm[:, col:144 * ncols:ncols], in1=BM, op=ALU.add),
                reads=[tPS[7], tBM], writes=[tMODT])
        for n in range(3):
            for col in range(ncols):
                sh0 = ((3 * n + 0) * 16) * ncols + col
                sc0 = ((3 * n + 1) * 16) * ncols + col
                g0 = ((3 * n + 2) * 16) * ncols + col
                o = ((li * 3 + n) * ncols + col) * 3 * NCH
                b.op("dve", lambda e, sc0=sc0, o=o, n=n, li=li: e.scalar_tensor_tensor(
                    out=DER[:, o:o + NCH], in0=MODT[:, sc0:sc0 + 15 * ncols + 1:ncols], scalar=1.0,
                    in1=GAIN[:, li, n * 16:(n + 1) * 16], op0=ALU.add, op1=ALU.mult),
                    reads=[tMODT, tC], writes=[tDER])
                b.op("dve", lambda e, g0=g0, o=o, n=n: e.tensor_scalar(
                    out=DER[:, o + NCH:o + 2 * NCH], in0=MODT[:, g0:g0 + 15 * ncols + 1:ncols],
                    scalar1=(1.0 if n == 1 else 0.5), scalar2=None, op0=ALU.mult),
                    reads=[tMODT], writes=[tDER])
                b.op("dve", lambda e, sh0=sh0, o=o: e.tensor_copy(
                    out=DER[:, o + 2 * NCH:o + 3 * NCH], in_=MODT[:, sh0:sh0 + 15 * ncols + 1:ncols]),
                    reads=[tMODT], writes=[tDER])
        yield

    ticker = [None]

    def tick():
        if ticker[0] is not None:
            try:
                next(ticker[0])
            except StopIteration:
                ticker[0] = None

    def drain():
        while ticker[0] is not None:
            tick()

    ticker[0] = mod_gen(0, layers[0])
    drain()

    def TS(t):
        return slice(t * 512, (t + 1) * 512)

    def norm_rstd(nt):
        for t in range(nt):
            pb = 6
            for m in range(NCH):
                b.op("act", lambda e, m=m: e.activation(out=SG[m % 2], in_=X[:, m, TS(t)], func=AF.Square),
                     reads=[tX[m][t]], writes=[tSG[m % 2]])
                b.group("pe", [lambda e, m=m: e.matmul(PS[pb][:], ones[:], SG[m % 2], start=(m == 0), stop=(m == 15))],
                        reads=[tSG[m % 2], tC], writes=[tPS[pb]])
            b.op("act", lambda e: e.activation(out=RS[:, TS(t)], in_=PS[pb][:], func=AF.Sqrt, bias=EPS, scale=1.0 / D),
                 reads=[tPS[pb]], writes=[tRS[t]])
            b.op("dve", lambda e: e.reciprocal(out=RS[:, TS(t)], in_=RS[:, TS(t)]), reads=[tRS[t]], writes=[tRS[t]])

    def adaln(li, n, col, nt):
        norm_rstd(nt)
        for t in range(nt):
            for m in range(NCH):
                b.op("dve", lambda e, m=m: e.tensor_tensor(out=TMP[m % 2], in0=X[:, m, TS(t)], in1=RS[:, TS(t)], op=ALU.mult),
                     reads=[tX[m][t], tRS[t]], writes=[tTMP[m % 2]])
                b.op("act", lambda e, m=m: e.activation(
                    out=H[:, m, TS(t)], in_=TMP[m % 2], func=AF.Identity, bias=der(li, n, col, 2, m), scale=der(li, n, col, 0, m)),
                    reads=[tTMP[m % 2], tDER], writes=[tH[m][t]])

    pcount = [0]

    def ffn(li, l, f, col, nt):
        n = 0 if f == 0 else 2
        Tn = nt * 512
        b.barrier()
        adaln(li, n, col, nt)
        cv = Carver(BASE)
        WIN = [cv.take(8192) for _ in range(3)]
        WOUT = [cv.take(16384) for _ in range(2)]
        HID = cv.take(8192)
        tWIN = [T() for _ in range(3)]
        tWOUT = [T(), T()]
        tHID = [T() for _ in range(2)]
        win = W[f"win{l}_{f}"]
        wout = W[f"wout{l}_{f}"]
        for g in range(11):
            gs = g % 2
            b.dma("pool", [(WOUT[gs], wout[g])], [], [tWOUT[gs]], sFW[2 + gs], max_dma_last_dim=8192)
            for c in range(4):
                j = 4 * g + c
                ws = j % 3
                b.dma("pool", [(WIN[ws], win[j])], [], [tWIN[ws]], sFWI[ws], max_dma_last_dim=8192)
                for t in range(nt):
                    kk = pcount[0]
                    pcount[0] += 1
                    pg = kk % 2
                    pu = 2 + kk % 2
                    rd = [tWIN[ws]] + [tH[m][t] for m in range(NCH)]
                    b.group("pe", [lambda e, kc=kc: e.matmul(PS[pg][:], WIN[ws][:, kc * 256: kc * 256 + 128], H[:, kc, TS(t)],
                                                              start=(kc == 0), stop=(kc == 15)) for kc in range(16)],
                            reads=rd, writes=[tPS[pg]])
                    b.group("pe", [lambda e, kc=kc: e.matmul(PS[pu][:], WIN[ws][:, kc * 256 + 128: kc * 256 + 256], H[:, kc, TS(t)],
                                                              start=(kc == 0), stop=(kc == 15)) for kc in range(16)],
                            reads=rd, writes=[tPS[pu]])
                    b.op("act", lambda e: e.activation(out=SG[kk % 2], in_=PS[pg][:], func=AF.Silu),
                         reads=[tPS[pg]], writes=[tSG[kk % 2]])
                    b.op("dve", lambda e: e.tensor_tensor(
                        out=HID[:, c * Tn + t * 512: c * Tn + (t + 1) * 512], in0=SG[kk % 2], in1=PS[pu][:], op=ALU.mult),
                        reads=[tSG[kk % 2], tPS[pu]], writes=[tHID[t]])
                tick()
            for m in range(NCH):
                for t in range(nt):
                    kk = pcount[0]
                    pcount[0] += 1
                    po = 4 + kk % 2
                    b.group("pe", [lambda e, c=c: e.matmul(PS[po][:], WOUT[gs][:, c * 2048 + m * 128: c * 2048 + (m + 1) * 128],
                                                            HID[:, c * Tn + t * 512: c * Tn + (t + 1) * 512],
                                                            start=(c == 0), stop=(c == 3)) for c in range(4)],
                            reads=[tWOUT[gs], tHID[t]], writes=[tPS[po]])
                    b.op("dve", lambda e: e.scalar_tensor_tensor(
                        out=X[:, m, TS(t)], in0=PS[po][:], scalar=der(li, n, col, 1, m), in1=X[:, m, TS(t)],
                        op0=ALU.mult, op1=ALU.add), reads=[tPS[po], tDER, tX[m][t]], writes=[tX[m][t]])

    sFW = [b.slot("fw") for _ in range(4)]
    sFWI = [b.slot("fwi") for _ in range(4)]
    sWP = [b.slot("wp") for _ in range(3)]
    sIO = [b.slot("io") for _ in range(6)]
    sWH = [b.slot("wh") for _ in range(2)]
    wpk = [0]

    def proj(WP, tWP, wchunk, nt, evac, tiles=None):
        sl = wpk[0] % len(WP)
        wpk[0] += 1
        b.dma("pool", [(WP[sl], wchunk)], [], [tWP[sl]], sWP[sl], max_dma_last_dim=8192)
        for t in (range(nt) if tiles is None else tiles):
            kk = pcount[0]
            pcount[0] += 1
            pb = kk % 2
            b.group("pe", [lambda e, kc=kc: e.matmul(PS[pb][:], WP[sl][:, kc * 128:(kc + 1) * 128], H[:, kc, TS(t)],
                                                      start=(kc == 0), stop=(kc == 15)) for kc in range(16)],
                    reads=[tWP[sl]] + [tH[m][t] for m in range(NCH)], writes=[tPS[pb]])
            evac(pb, t)

    def outproj(li, col, nt, Y, tY, WP, tWP, wo, kcs=range(16)):
        for m in range(NCH):
            sl = wpk[0] % len(WP)
            wpk[0] += 1
            b.dma("pool", [(WP[sl], wo[m])], [], [tWP[sl]], sWP[sl], max_dma_last_dim=8192)
            for t in range(nt):
                kk = pcount[0]
                pcount[0] += 1
                po = 4 + kk % 2
                k0 = kcs[0]
                b.group("pe", [lambda e, kc=kc: e.matmul(PS[po][:], WP[sl][:, kc * 128:(kc + 1) * 128], Y[:, kc - k0, TS(t)],
                                                          start=(kc == kcs[0]), stop=(kc == kcs[-1])) for kc in kcs],
                        reads=[tWP[sl], tY], writes=[tPS[po]])
                b.op("dve", lambda e: e.scalar_tensor_tensor(
                    out=X[:, m, TS(t)], in0=PS[po][:], scalar=der(li, 1, col, 1, m), in1=X[:, m, TS(t)],
                    op0=ALU.mult, op1=ALU.add), reads=[tPS[po], tDER, tX[m][t]], writes=[tX[m][t]])

    def attention(qparts, kparts, vfn, nkb, q0s, scale, rows, ydst, rtiles, wtile, PT, tPT, RD, tRD, tab=None):
        its = [(q0, n, kb) for (q0, n) in q0s for kb in range(nkb)]
        SBK = [0, 1, 4, 5]
        LA = 3

        def qk(idx):
            q0, n, kb = its[idx]
            sb = SBK[idx % 4]
            b.group("pe", [lambda e, i=i: e.matmul(PS[sb][:, 0:n], kparts[i](kb, q0), qparts[i](q0, n),
                                                    start=(i == 0), stop=(i == len(qparts) - 1)) for i in range(len(qparts))],
                    reads=rtiles, writes=[tPS[sb]])

        for j in range(min(LA, len(its))):
            qk(j)
        for idx, (q0, n, kb) in enumerate(its):
            if idx + LA < len(its):
                qk(idx + LA)
            sb = SBK[idx % 4]
            kk = idx
            pt = PT[kk % len(PT)]
            tb_ = tab(kb, q0, n) if tab is not None else None
            if tb_ is not None:
                tap, ttile = tb_
                b.op("dve", lambda e: e.scalar_tensor_tensor(out=TMP[kk % 2][:, 0:n], in0=PS[sb][:, 0:n], scalar=scale,
                                                             in1=tap, op0=ALU.mult, op1=ALU.add),
                     reads=[tPS[sb], ttile], writes=[tTMP[kk % 2]])
                b.op("act", lambda e: e.activation(out=pt[:, 0:n], in_=TMP[kk % 2][:, 0:n], func=AF.Exp),
                     reads=[tTMP[kk % 2]], writes=[tPT[kk % len(PT)]])
            else:
                b.op("act", lambda e: e.activation(out=pt[:, 0:n], in_=PS[sb][:, 0:n], func=AF.Exp, scale=scale),
                     reads=[tPS[sb]], writes=[tPT[kk % len(PT)]])
            b.group("pe", [lambda e: e.matmul(PS[2][:, 0:n], vfn(kb, q0), pt[:, 0:n], start=(kb == 0), stop=(kb == nkb - 1)),
                           lambda e: e.matmul(PS[3][:, 0:n], ones[:], pt[:, 0:n], start=(kb == 0), stop=(kb == nkb - 1))],
                    reads=[tPT[kk % len(PT)], tC] + rtiles, writes=[tPS[2], tPS[3]])
            if kb == nkb - 1:
                b.op("dve", lambda e: e.reciprocal(out=RD[rows, 0:n], in_=PS[3][rows, 0:n]), reads=[tPS[3]], writes=[tRD])
                b.op("dve", lambda e: e.tensor_tensor(out=ydst(q0, n), in0=PS[2][rows, 0:n], in1=RD[rows, 0:n], op=ALU.mult),
                     reads=[tPS[2], tRD], writes=[wtile])

    def mixer_ab(li, l, kind, ui, col, nt):
        Tn = nt * 512
        lat = kind == "lat"
        seqs = [(0, 1024)] if lat else [(0, 256), (256, 512)]
        b.barrier()
        adaln(li, 1, col, nt)
        cv = Carver(BASE)
        Y = cv.take(16 * Tn * 2, BF16, [16, Tn])
        tY = T()
        WP = [cv.take(4096) for _ in range(3)]
        tWP = [T() for _ in range(3)]
        LC = cv.take(8 * 11 * 4, F32, [8, 11])
        NSP = cv.take(8 * 4 * 4, F32, [8, 4])
        ST = cv.take(64, F32)
        SO = cv.take(128, F32)
        tLC = T()
        tSO = T()
        wab = W[f"wab{l}"]
        prs = [(LC, W[f"lc{l}"][:, :].rearrange("p (c k) -> p c k", k=11))]
        if lat:
            prs.append((ST, W[f"st{l}"][ui]))
        b.dma("sp", prs, [], [tLC], sIO[0])
        if DBG.get('skip_nsp'):
            return
        b.op("act", lambda e: e.activation(out=NSP[:, :, 0:2], in_=LC[:, :, 9:11], func=AF.Exp, scale=-1.0), reads=[tLC], writes=[tLC])
        b.op("act", lambda e: e.activation(out=NSP[:, :, 0:2], in_=NSP[:, :, 0:2], func=AF.Ln, bias=1.0), reads=[tLC], writes=[tLC])
        b.op("dve", lambda e: e.tensor_scalar(out=NSP[:, :, 2:4], in0=NSP[:, :, 0:2], scalar1=-16.0, scalar2=None, op0=ALU.mult), reads=[tLC], writes=[tLC])
        b.op("dve", lambda e: e.tensor_scalar(out=NSP[:, :, 0:2], in0=NSP[:, :, 0:2], scalar1=-8.0, scalar2=None, op0=ALU.mult), reads=[tLC], writes=[tLC])
        mark = cv.off
        XA2 = [cv.take(Tn * 4, F32) for _ in range(2)]
        GA2 = [cv.take(Tn * 4, F32) for _ in range(2)]
        XC = cv.take(Tn * 4, F32)
        R = cv.take(Tn * 4, F32)
        GI = cv.take(Tn * 4, F32)
        HF = cv.take(Tn * 4, F32)
        HB = cv.take(Tn * 4, F32)
        XCB = cv.take(Tn * 2)
        BD = cv.take(1024, BF16, [4, 128])
        tXA2 = [T(), T()]
        tGA2 = [T(), T()]
        tXC, tR, tGI, tHF, tHB, tXCB, tBD = [T() for _ in range(7)]
        for c in range(8 if not DBG.get('skip_lru') else 0):
            XA, tXA, GA, tGA = XA2[c % 2], tXA2[c % 2], GA2[c % 2], tGA2[c % 2]
            Sb, tS = XA, tXA
            b.dma("pool", [(BD, W[f"bd{l}"][c].rearrange("p (k n) -> p k n", n=128))], [], [tBD], sIO[1], max_dma_last_dim=8192)
            proj(WP, tWP, wab[c], nt, lambda pb, t: b.op("dve", lambda e: e.tensor_copy(out=XA[:, TS(t)], in_=PS[pb][:]),
                                                       reads=[tPS[pb]], writes=[tXA]))
            proj(WP, tWP, wab[8 + c], nt, lambda pb, t: b.op("dve", lambda e: e.tensor_copy(out=GA[:, TS(t)], in_=PS[pb][:]),
                                                           reads=[tPS[pb]], writes=[tGA]))
            for (s0, s1) in seqs:
                b.op("act", lambda e: e.activation(out=XC[:, s0:s1], in_=XA[:, s0:s1], func=AF.Identity,
                                                   bias=LC[:, c, 4:5], scale=LC[:, c, 2:3]), reads=[tXA, tLC], writes=[tXC])
                for (jj, sh) in ((0, -2), (1, -1), (3, 1)):
                    if sh < 0:
                        o_ = XC[:, s0 - sh:s1]
                        i_ = XA[:, s0:s1 + sh]
                    else:
                        o_ = XC[:, s0:s1 - sh]
                        i_ = XA[:, s0 + sh:s1]
                    b.op("dve", lambda e, o_=o_, i_=i_, jj=jj: e.scalar_tensor_tensor(
                        out=o_, in0=i_, scalar=LC[:, c, jj:jj + 1], in1=o_, op0=ALU.mult, op1=ALU.add),
                        reads=[tXA, tXC, tLC], writes=[tXC])
            b.op("dve", lambda e: e.tensor_copy(out=XCB, in_=XC), reads=[tXC], writes=[tXCB])
            for d in range(2):
                Hd, tHd = (HF, tHF) if d == 0 else (HB, tHB)
                for t in range(nt):
                    for (a, dst, tdst, bcol) in ((0, R, tR, 5 + d), (1, GI, tGI, 7 + d)):
                        kk = pcount[0]
                        pcount[0] += 1
                        pb = kk % 2
                        b.group("pe", [lambda e: e.matmul(PS[pb][:], BD[:, d * 2 + a, :], XCB[:, TS(t)], start=True, stop=True)],
                                reads=[tBD, tXCB], writes=[tPS[pb]])
                        b.op("act", lambda e: e.activation(out=dst[:, TS(t)], in_=PS[pb][:], func=AF.Sigmoid, bias=LC[:, c, bcol:bcol + 1]),
                             reads=[tPS[pb], tLC], writes=[tdst])
                b.op("act", lambda e: e.activation(out=Sb, in_=R, func=AF.Exp, scale=NSP[:, c, 2 + d:3 + d]), reads=[tR, tLC, tXC], writes=[tS])
                b.op("act", lambda e: e.activation(out=Sb, in_=Sb, func=AF.Sqrt, bias=1.0, scale=-1.0), reads=[tS], writes=[tS])
                b.op("act", lambda e: e.activation(out=R, in_=R, func=AF.Exp, scale=NSP[:, c, d:d + 1]), reads=[tR, tLC], writes=[tR])
                b.op("dve", lambda e: e.tensor_tensor(out=GI, in0=GI, in1=Sb, op=ALU.mult), reads=[tGI, tS], writes=[tGI])
                b.op("dve", lambda e: e.tensor_tensor(out=GI, in0=GI, in1=XC, op=ALU.mult), reads=[tGI, tXC], writes=[tGI])
                for si, (s0, s1) in enumerate(seqs):
                    init = ST[:, c * 2 + d:c * 2 + d + 1] if lat else 0.0
                    if d == 0:
                        b.op("dve", lambda e: e.tensor_tensor_scan(out=Hd[:, s0:s1], data0=R[:, s0:s1], data1=GI[:, s0:s1],
                                                                   initial=init, op0=ALU.mult, op1=ALU.add),
                             reads=[tR, tGI, tLC], writes=[tHd])
                    else:
                        b.op("dve", lambda e: e.tensor_tensor_scan(out=rev(Hd[:, s0:s1]), data0=rev(R[:, s0:s1]), data1=rev(GI[:, s0:s1]),
                                                                   initial=init, op0=ALU.mult, op1=ALU.add),
                             reads=[tR, tGI, tLC], writes=[tHd])
                    if not lat:
                        src = Hd[:, s1 - 1:s1] if d == 0 else Hd[:, s0:s0 + 1]
                        k_ = (c * 2 + d) * 2 + si
                        b.op("act", lambda e: e.activation(out=SO[:, k_:k_ + 1], in_=src, func=AF.Identity), reads=[tHd], writes=[tSO])
            b.op("dve", lambda e: e.tensor_tensor(out=R, in0=GA, in1=GA, op=ALU.mult), reads=[tGA, tGI], writes=[tR])
            b.op("dve", lambda e: e.tensor_scalar(out=R, in0=R, scalar1=0.044715, scalar2=1.0, op0=ALU.mult, op1=ALU.add), reads=[tR], writes=[tR])
            b.op("dve", lambda e: e.tensor_tensor(out=R, in0=R, in1=GA, op=ALU.mult), reads=[tR, tGA], writes=[tR])
            b.op("act", lambda e: e.activation(out=R, in_=R, func=AF.Sigmoid, scale=1.5957691216057308), reads=[tR], writes=[tR])
            b.op("dve", lambda e: e.tensor_tensor(out=R, in0=R, in1=GA, op=ALU.mult), reads=[tR, tGA], writes=[tR])
            b.op("dve", lambda e: e.tensor_tensor(out=HF, in0=HF, in1=HB, op=ALU.add), reads=[tHF, tHB], writes=[tHF])
            b.op("dve", lambda e: e.tensor_tensor(out=Y[:, c, :], in0=HF, in1=R, op=ALU.mult), reads=[tHF, tR], writes=[tY])
        if not lat:
            b.dma("sp", [(O[f"ost{l}"][ui], SO)], [tSO], [], sOut[0])
        b.barrier()
        cv.off = mark
        QT = cv.take(Tn * 2)
        KT = cv.take((Tn + 512) * 2)
        QM = cv.take(Tn * 2)
        KF = cv.take(Tn * 4, F32) if not lat else None
        VF = cv.take(Tn * 4, F32) if not lat else None
        nvb = (Tn + (512 if lat else 0)) // 128
        VT = cv.take(nvb * 128 * 2, BF16, [nvb, 128])
        PT = [cv.take(1024) for _ in range(4)]
        RD = cv.take(2048, F32)
        tQT, tKT, tQM, tKF, tVF, tVT, tRD = [T() for _ in range(7)]
        tPT = [T() for _ in range(4)]
        if lat:
            KC32 = cv.take(2048, F32)
            VC32 = cv.take(2048, F32, [4, 128])
            tKC32 = T()
            TAB = [cv.take(2048, BF16) for _ in range(8)]
            tTAB = [T() for _ in range(8)]
            for kb in range(8):
                b.op("pool", lambda e, kb=kb: e.memset(TAB[kb], NEG), writes=[tTAB[kb]])
        for c in range(8 if not DBG.get('skip_qkv') else 0):
            proj(WP, tWP, wab[16 + c], nt, lambda pb, t: b.op("act", lambda e: e.activation(out=QT[:, TS(t)], in_=PS[pb][:], func=AF.Identity),
                                                            reads=[tPS[pb]], writes=[tQT]))

            def evk(pb, t):
                if not lat:
                    b.op("act", lambda e: e.activation(out=KF[:, TS(t)], in_=PS[pb][:], func=AF.Identity), reads=[tPS[pb]], writes=[tKF])
                    b.op("dve", lambda e: e.tensor_copy(out=KT[:, TS(t)], in_=KF[:, TS(t)]), reads=[tKF], writes=[tKT])
                else:
                    b.op("dve", lambda e: e.tensor_copy(out=KT[:, TS(t)], in_=PS[pb][:]), reads=[tPS[pb]], writes=[tKT])
            proj(WP, tWP, wab[24 + c], nt, evk)
            if not lat:
                proj(WP, tWP, wab[32 + c], nt, lambda pb, t: b.op("act", lambda e: e.activation(out=VF[:, TS(t)], in_=PS[pb][:], func=AF.Identity),
                                                                reads=[tPS[pb]], writes=[tVF]))
                b.dma("sp", [(O[f"onak{l}"][ui, c], KF), (O[f"onav{l}"][ui, c], VF)], [tKF, tVF], [], sOut[1])
            sl = wpk[0] % 3
            wpk[0] += 1
            b.dma("pool", [(WP[sl], wab[32 + c])], [], [tWP[sl]], sWP[sl], max_dma_last_dim=8192)
            for tb in range(Tn // 128 if not DBG.get('skip_vt') else 0):
                kk = pcount[0]
                pcount[0] += 1
                pb = kk % 2
                t = tb // 4
                b.group("pe", [lambda e, kc=kc: e.matmul(PS[pb][:, 0:128], H[:, kc, tb * 128:(tb + 1) * 128], WP[sl][:, kc * 128:(kc + 1) * 128],
                                                          start=(kc == 0), stop=(kc == 15)) for kc in range(16)],
                        reads=[tWP[sl]] + [tH[m][t] for m in range(NCH)], writes=[tPS[pb]])
                b.op("act", lambda e: e.activation(out=VT[:, tb, :], in_=PS[pb][:, 0:128], func=AF.Identity), reads=[tPS[pb]], writes=[tVT])
            if lat:
                b.dma("sp", [(KC32, W[f"nakc{l}"][ui, c]),
                             (VC32, W[f"navc{l}"][ui][:, :, c * 128:(c + 1) * 128].rearrange("k p n -> p k n"))],
                      [], [tKC32], sIO[2])
                b.op("dve", lambda e: e.tensor_copy(out=KT[:, Tn:Tn + 512], in_=KC32), reads=[tKC32], writes=[tKT])
                b.op("dve", lambda e: e.tensor_copy(out=VT[:, 8:12, :], in_=VC32), reads=[tKC32], writes=[tVT])
            for hh in range(2 if not DBG.get('skip_att') else 0):
                rows = slice(hh * 64, (hh + 1) * 64)
                b.op("dve", lambda e: e.tensor_scalar(out=QM, in0=QT, scalar1=HM[:, hh:hh + 1], scalar2=None, op0=ALU.mult),
                     reads=[tQT, tC], writes=[tQM])
                if lat:
                    hd = 2 * c + hh
                    bz = W[f"bz{l}"]
                    for i in range(16):
                        r_lo = 0 if i <= 7 else i - 3
                        r_hi = 15 if i >= 8 else i + 4
                        nr = r_hi - r_lo + 1
                        rr0 = r_lo - i + 7
                        kb = i // 2
                        src = bass.AP(bz.tensor, bz.offset + ((hd * 15 + rr0) * 64) * 64, [[64, 64], [4096, nr], [1, 64]])
                        dst = TAB[kb][(i % 2) * 64:(i % 2) * 64 + 64, r_lo * 64:(r_hi + 1) * 64].rearrange("p (r q) -> p r q", q=64)
                        b.dma("pool", [(dst, src)], [], [tTAB[kb]], sTAB[kb])
                    attention([lambda q0, n: QM[:, q0:q0 + n]], [lambda kb, q0: KT[:, kb * 128:(kb + 1) * 128]],
                              lambda kb, q0: VT[:, kb, :], 12, [(0, 512), (512, 512)], NA_SCALE, rows,
                              lambda q0, n: Y[rows, 8 + c, q0:q0 + n], [tQM, tKT, tVT], tY, PT, tPT, RD, tRD,
                              tab=lambda kb, q0, n: (TAB[kb][:, q0:q0 + n], tTAB[kb]) if kb < 8 else None)
                else:
                    if True:
                        attention([lambda q0, n: QM[:, q0:q0 + n]], [lambda kb, q0: KT[:, (q0 // 128 + kb) * 128:(q0 // 128 + kb + 1) * 128]],
                                  lambda kb, q0: VT[:, q0 // 128 + kb, :], 2, [(0, 256), (256, 256)], NA_SCALE, rows,
                                  lambda q0, n: Y[rows, 8 + c, q0:q0 + n], [tQM, tKT, tVT], tY, PT, tPT, RD, tRD)
        if not DBG.get('skip_outproj'):
            outproj(li, col, nt, Y, tY, WP, tWP, W[f"wabo{l}"])

    sOut = [b.slot("out") for _ in range(6)]
    outslots.extend(sOut)
    sTAB = [b.slot("tab") for _ in range(8)]

    def mixer_cd(li, l, kind, ui, col, nt):
        Tn = nt * 512
        lat = kind == "lat"
        Tk = Tn + (512 if lat else 0)
        seqs = [(0, 1024)] if lat else [(0, 256), (256, 512)]
        wcd = W[f"wcd{l}"]
        b.barrier()
        adaln(li, 1, col, nt)
        cv = Carver(BASE)
        Y8 = cv.take(8 * Tn * 2, BF16, [8, Tn])
        tY = T()
        WP = [cv.take(4096) for _ in range(2)]
        tWP = [T() for _ in range(2)]
        CDC = cv.take(40, F32)
        tCDC = T()
        b.dma("sp", [(CDC, W[f"cdc{l}"][:, :])], [], [tCDC], sIO[0])
        mark = cv.off

        def rope_from(src, t0, n, dst, rd, wr):
            b.op("act", lambda e: e.activation(out=SG[0][:, 0:n], in_=src, func=AF.Identity), reads=rd, writes=[tSG[0]])
            b.group("pe", [lambda e: e.matmul(PS[5][:, 0:n], perm, SG[0][:, 0:n], start=True, stop=True)], reads=[tSG[0], tMat], writes=[tPS[5]])
            b.op("dve", lambda e: e.tensor_tensor(out=TMP[0][:, 0:n], in0=src, in1=COS[:, t0:t0 + n], op=ALU.mult), reads=rd + [tC], writes=[tTMP[0]])
            b.op("dve", lambda e: e.tensor_tensor(out=TMP[1][:, 0:n], in0=PS[5][:, 0:n], in1=SIN[:, t0:t0 + n], op=ALU.mult), reads=[tPS[5], tC], writes=[tTMP[1]])
            b.op("dve", lambda e: e.tensor_tensor(out=dst, in0=TMP[0][:, 0:n], in1=TMP[1][:, 0:n], op=ALU.add), reads=[tTMP[0], tTMP[1]], writes=wr)

        K2 = [cv.take(Tk * 2) for _ in range(4)]
        tK2 = [T() for _ in range(4)]
        nkbT = Tk // 128
        VD = cv.take(nkbT * 512 * 2, BF16, [nkbT, 512])
        tVD = T()
        WV = cv.take(16384)
        tWV = T()
        QT = cv.take(Tn * 2)
        QM = cv.take(Tn * 2)
        STG = cv.take(2048, F32)
        RSQ = cv.take(2048, F32)
        PT = [cv.take(1024) for _ in range(4)]
        RD = cv.take(2048, F32)
        tQT, tQM, tSTG, tRSQ, tRD = [T() for _ in range(5)]
        tPT = [T() for _ in range(4)]

        def headnorm(pb, t, gcol, dst, wr, rope, f32dma=None):
            b.op("act", lambda e: e.activation(out=STG, in_=PS[pb][:], func=AF.Identity), reads=[tPS[pb]], writes=[tSTG])
            b.op("act", lambda e: e.activation(out=SG[1], in_=PS[pb][:], func=AF.Square), reads=[tPS[pb]], writes=[tSG[1]])
            b.group("pe", [lambda e: e.matmul(PS[6][:], bones, SG[1], start=True, stop=True)], reads=[tSG[1], tMat], writes=[tPS[6]])
            b.op("act", lambda e: e.activation(out=RSQ, in_=PS[6][:], func=AF.Sqrt, bias=EPS, scale=1.0 / 64), reads=[tPS[6]], writes=[tRSQ])
            b.op("dve", lambda e: e.reciprocal(out=RSQ, in_=RSQ), reads=[tRSQ], writes=[tRSQ])
            if rope or f32dma is not None:
                b.op("dve", lambda e: e.scalar_tensor_tensor(out=STG, in0=STG, scalar=CDC[:, gcol:gcol + 1], in1=RSQ, op0=ALU.mult, op1=ALU.mult),
                     reads=[tSTG, tRSQ, tCDC], writes=[tSTG])
                if f32dma is not None:
                    f32dma()
                if rope:
                    rope_from(STG, t * 512, 512, dst, [tSTG], wr)
                else:
                    b.op("act", lambda e: e.activation(out=dst, in_=STG, func=AF.Identity), reads=[tSTG], writes=wr)
            else:
                b.op("dve", lambda e: e.scalar_tensor_tensor(out=dst, in0=STG, scalar=CDC[:, gcol:gcol + 1], in1=RSQ, op0=ALU.mult, op1=ALU.mult),
                     reads=[tSTG, tRSQ, tCDC], writes=wr)

        for kvh in range(4):
            def evk(pb, t, kvh=kvh):
                f = None
                if not lat:
                    f = lambda: b.dma("sp", [(O[f"ogk{l}"][ui, kvh], STG[0:64, :])], [tSTG], [], sOut[2])
                headnorm(pb, t, 1, K2[kvh][:, TS(t)], [tK2[kvh]], lat, f)
            proj(WP, tWP, wcd[8 + kvh], nt, evk)
        if lat:
            b.dma("pool", [(K2[kvh][:, Tn:Tn + 512], W[f"gqk{l}"][ui, kvh]) for kvh in range(4)]
                  + [(VD[:, 8:12, :], W[f"gqv{l}"][ui].rearrange("k p n -> p k n"))], [], tK2 + [tVD], sIO[2], max_dma_last_dim=8192)
        else:
            for i in range(2):
                def evv(pb, t, i=i):
                    b.op("act", lambda e: e.activation(out=STG, in_=PS[pb][:], func=AF.Identity), reads=[tPS[pb]], writes=[tSTG])
                    b.dma("sp", [(O[f"ogv{l}"][ui, i], STG)], [tSTG], [], sOut[2])
                proj(WP, tWP, wcd[12 + i], nt, evv)
        b.dma("pool", [(WV, W[f"wvd{l}"][:, :])], [], [tWV], sIO[3], max_dma_last_dim=8192)
        for tb in range(Tn // 128):
            kk = pcount[0]
            pcount[0] += 1
            pb = kk % 2
            b.group("pe", [lambda e, kc=kc: e.matmul(PS[pb][:], H[:, kc, tb * 128:(tb + 1) * 128], WV[:, kc * 512:(kc + 1) * 512],
                                                      start=(kc == 0), stop=(kc == 15)) for kc in range(16)],
                    reads=[tWV] + [tH[m][tb // 4] for m in range(NCH)], writes=[tPS[pb]])
            b.op("act", lambda e: e.activation(out=VD[:, tb, :], in_=PS[pb][:], func=AF.Identity), reads=[tPS[pb]], writes=[tVD])
        for c in range(8):
            kvh = c // 2
            proj(WP, tWP, wcd[c], nt, lambda pb, t: headnorm(pb, t, 0, QT[:, TS(t)], [tQT], lat))
            for hh in range(2):
                rows = slice(hh * 64, (hh + 1) * 64)
                b.op("dve", lambda e: e.tensor_scalar(out=QM, in0=QT, scalar1=HM[:, hh:hh + 1], scalar2=None, op0=ALU.mult),
                     reads=[tQT, tC], writes=[tQM])
                if lat:
                    attention([lambda q0, n: QM[:, q0:q0 + n]], [lambda kb, q0: K2[kvh][:, kb * 128:(kb + 1) * 128]],
                              lambda kb, q0: VD[:, kb, kvh * 128:(kvh + 1) * 128], 12, [(0, 512), (512, 512)], NA_SCALE, rows,
                              lambda q0, n: Y8[rows, c, q0:q0 + n], [tQM, tK2[kvh], tVD], tY, PT, tPT, RD, tRD)
                else:
                    if True:
                        attention([lambda q0, n: QM[:, q0:q0 + n]], [lambda kb, q0: K2[kvh][:, (q0 // 128 + kb) * 128:(q0 // 128 + kb + 1) * 128]],
                                  lambda kb, q0: VD[:, q0 // 128 + kb, kvh * 128:(kvh + 1) * 128], 2, [(0, 256), (256, 256)], NA_SCALE, rows,
                                  lambda q0, n: Y8[rows, c, q0:q0 + n], [tQM, tK2[kvh], tVD], tY, PT, tPT, RD, tRD)
        outproj(li, col, nt, Y8, tY, WP, tWP, W[f"wcdo{l}"], kcs=range(0, 8))

        b.barrier()
        cv.off = mark
        STG4 = cv.take(8192, F32, [4, 512])
        QAN = cv.take(4 * Tn * 2, BF16, [4, Tn])
        CKB = cv.take(4 * Tk * 2, BF16, [4, Tk])
        KR2 = cv.take(Tk * 2)
        RS2 = cv.take(2048, F32)
        WH2 = [cv.take(4096, BF16, [4, 4, 128]) for _ in range(2)]
        QN = cv.take(Tn * 2)
        QRh = cv.take(Tn * 2)
        KN = cv.take(Tk * 2)
        VM = cv.take(nkbT * 128 * 2, BF16, [nkbT, 128])
        PT = [cv.take(1024) for _ in range(4)]
        RD = cv.take(2048, F32)
        tSTG4, tQAN, tCKB, tKR2, tRS2, tQN, tQRh, tKN, tVM, tRD = [T() for _ in range(10)]
        tWH2 = [T(), T()]
        tPT = [T() for _ in range(4)]

        def norm512(wbase, gcol0, t, after):
            for i in range(4):
                def ev(pb, t_, i=i):
                    b.op("act", lambda e: e.activation(out=STG4[:, i, :], in_=PS[pb][:], func=AF.Identity), reads=[tPS[pb]], writes=[tSTG4])
                    b.op("act", lambda e: e.activation(out=SG[i % 2], in_=PS[pb][:], func=AF.Square), reads=[tPS[pb]], writes=[tSG[i % 2]])
                    b.group("pe", [lambda e: e.matmul(PS[6][:], ones[:], SG[i % 2], start=(i == 0), stop=(i == 3))],
                            reads=[tSG[i % 2], tC], writes=[tPS[6]])
                proj(WP, tWP, wcd[wbase + i], nt, ev, tiles=[t])
            b.op("act", lambda e: e.activation(out=RS2, in_=PS[6][:], func=AF.Sqrt, bias=EPS, scale=1.0 / 512), reads=[tPS[6]], writes=[tRS2])
            b.op("dve", lambda e: e.reciprocal(out=RS2, in_=RS2), reads=[tRS2], writes=[tRS2])
            for i in range(4):
                b.op("dve", lambda e: e.scalar_tensor_tensor(out=STG4[:, i, :], in0=STG4[:, i, :], scalar=CDC[:, gcol0 + i:gcol0 + i + 1], in1=RS2,
                                                             op0=ALU.mult, op1=ALU.mult), reads=[tSTG4, tRS2, tCDC], writes=[tSTG4])
                after(i)

        for t in range(nt):
            norm512(14, 2, t, lambda i: b.op("act", lambda e: e.activation(out=QAN[:, i, TS(t)], in_=STG4[:, i, :], func=AF.Identity),
                                             reads=[tSTG4], writes=[tQAN]))

            def after_ckv(i):
                if not lat:
                    b.dma("sp", [(O[f"omc{l}"][ui, i], STG4[:, i, :])], [tSTG4], [], sOut[3])
                b.op("act", lambda e: e.activation(out=CKB[:, i, TS(t)], in_=STG4[:, i, :], func=AF.Identity), reads=[tSTG4], writes=[tCKB])
            norm512(18, 6, t, after_ckv)

        def evkr(pb, t):
            b.op("act", lambda e: e.activation(out=RS2, in_=PS[pb][:], func=AF.Identity), reads=[tPS[pb]], writes=[tRS2])
            if lat:
                rope_from(RS2, t * 512, 512, KR2[:, TS(t)], [tRS2], [tKR2])
            else:
                b.dma("sp", [(O[f"omr{l}"][ui], RS2[0:64, :])], [tRS2], [], sOut[4])
                b.op("act", lambda e: e.activation(out=KR2[:, TS(t)], in_=RS2, func=AF.Identity), reads=[tRS2], writes=[tKR2])
        proj(WP, tWP, wcd[22], nt, evkr)
        if lat:
            b.dma("pool", [(CKB[:, :, Tn:Tn + 512], W[f"mlc{l}"][ui].rearrange("k p n -> p k n")), (KR2[:, Tn:Tn + 512], W[f"mlr{l}"][ui])],
                  [], [tCKB, tKR2], sIO[4], max_dma_last_dim=8192)
        wuqn = W[f"wuqn{l}"][:, :].rearrange("p (k n) -> p k n", n=1024)
        wuk = W[f"wuk{l}"][:, :].rearrange("p (k n) -> p k n", n=1024)
        wuv = W[f"wuv{l}"][:, :].rearrange("p (k n) -> p k n", n=1024)
        wuqr = W[f"wuqr{l}"][:, :].rearrange("p (k n) -> p k n", n=512)
        for h in range(8):
            hs = slice(h * 128, (h + 1) * 128)
            WH, tWH = WH2[h % 2], tWH2[h % 2]
            b.op("pool", lambda e: e.memset(WH[:, 3, :, :], 0.0), writes=[tWH])
            b.dma("pool", [(WH[:, 0, :, :], wuqn[:, :, hs]), (WH[:, 1, :, :], wuk[:, :, hs]), (WH[:, 2, :, :], wuv[:, :, hs]),
                           (WH[:, 3, :, (h % 2) * 64:(h % 2) * 64 + 64], wuqr[:, :, h * 64:(h + 1) * 64])], [], [tWH], sWH[h % 2])
            for t in range(nt):
                for (wi, dst, tdst, rp) in ((0, QN, tQN, False), (3, QRh, tQRh, lat)):
                    kk = pcount[0]
                    pcount[0] += 1
                    pb = kk % 2
                    b.group("pe", [lambda e, kc=kc: e.matmul(PS[pb][:], WH[:, wi, kc, :], QAN[:, kc, TS(t)], start=(kc == 0), stop=(kc == 3)) for kc in range(4)],
                            reads=[tWH, tQAN], writes=[tPS[pb]])
                    if rp:
                        b.op("act", lambda e: e.activation(out=RS2, in_=PS[pb][:], func=AF.Identity), reads=[tPS[pb]], writes=[tRS2])
                        rope_from(RS2, t * 512, 512, dst[:, TS(t)], [tRS2], [tdst])
                    else:
                        b.op("act", lambda e: e.activation(out=dst[:, TS(t)], in_=PS[pb][:], func=AF.Identity), reads=[tPS[pb]], writes=[tdst])
            for kt in range(Tk // 512):
                kk = pcount[0]
                pcount[0] += 1
                pb = kk % 2
                b.group("pe", [lambda e, kc=kc: e.matmul(PS[pb][:], WH[:, 1, kc, :], CKB[:, kc, TS(kt)], start=(kc == 0), stop=(kc == 3)) for kc in range(4)],
                        reads=[tWH, tCKB], writes=[tPS[pb]])
                b.op("act", lambda e: e.activation(out=KN[:, TS(kt)], in_=PS[pb][:], func=AF.Identity), reads=[tPS[pb]], writes=[tKN])
            for kb in range(nkbT):
                kk = pcount[0]
                pcount[0] += 1
                pb = kk % 2
                b.group("pe", [lambda e, kc=kc: e.matmul(PS[pb][:, 0:128], CKB[:, kc, kb * 128:(kb + 1) * 128], WH[:, 2, kc, :], start=(kc == 0), stop=(kc == 3)) for kc in range(4)],
                        reads=[tWH, tCKB], writes=[tPS[pb]])
                b.op("act", lambda e: e.activation(out=VM[:, kb, :], in_=PS[pb][:, 0:128], func=AF.Identity), reads=[tPS[pb]], writes=[tVM])
            rows = slice(0, 128)
            rt = [tQN, tQRh, tKN, tKR2, tVM]
            if lat:
                attention([lambda q0, n: QN[:, q0:q0 + n], lambda q0, n: QRh[:, q0:q0 + n]],
                          [lambda kb, q0: KN[:, kb * 128:(kb + 1) * 128], lambda kb, q0: KR2[:, kb * 128:(kb + 1) * 128]],
                          lambda kb, q0: VM[:, kb, :], 12, [(0, 512), (512, 512)], MLA_SCALE, rows,
                          lambda q0, n: Y8[:, h, q0:q0 + n], rt, tY, PT, tPT, RD, tRD)
            else:
                if True:
                    attention([lambda q0, n: QN[:, q0:q0 + n], lambda q0, n: QRh[:, q0:q0 + n]],
                              [lambda kb, q0: KN[:, (q0 // 128 + kb) * 128:(q0 // 128 + kb + 1) * 128], lambda kb, q0: KR2[:, (q0 // 128 + kb) * 128:(q0 // 128 + kb + 1) * 128]],
                              lambda kb, q0: VM[:, q0 // 128 + kb, :], 2, [(0, 256), (256, 256)], MLA_SCALE, rows,
                              lambda q0, n: Y8[:, h, q0:q0 + n], rt, tY, PT, tPT, RD, tRD)
        outproj(li, col, nt, Y8, tY, WP, tWP, W[f"wcdo{l}"], kcs=range(8, 16))

    for ui in range(nu):
        for kind in ("ctx", "lat"):
            if DBG.get('only_' + ('lat' if kind == 'ctx' else 'ctx')):
                continue
            nt = 1 if kind == "ctx" else 2
            Tn = nt * 512
            col = 0 if kind == "ctx" else 1 + ui
            src = xc if kind == "ctx" else xl
            dst = yc if kind == "ctx" else yl
            b.dma("sp", [(X[:, m, 0:Tn], src[ui, m]) for m in range(NCH)], [], allX, sX)
            for li, l in enumerate(layers):
                if ui == 0 and kind == "ctx" and li + 1 < len(layers) and not DBG.get('only_lat'):
                    ticker[0] = mod_gen(li + 1, layers[li + 1])
                ffn(li, l, 0, col, nt)
                if mixers:
                    if l % 2 == 0:
                        mixer_ab(li, l, kind, ui, col, nt)
                    else:
                        mixer_cd(li, l, kind, ui, col, nt)
                    ffn(li, l, 1, col, nt)
                if ticker[0] is not None:
                    b.barrier()
                    drain()
            norm_rstd(nt)
            for t in range(nt):
                for m in range(NCH):
                    b.op("dve", lambda e, m=m: e.scalar_tensor_tensor(
                        out=X[:, m, TS(t)], in0=X[:, m, TS(t)], scalar=FG[:, m:m + 1], in1=RS[:, TS(t)], op0=ALU.mult, op1=ALU.mult),
                        reads=[tX[m][t], tRS[t], tC], writes=[tX[m][t]])
            b.dma("sp", [(dst[ui, m], X[:, m, 0:Tn]) for m in range(NCH)], allX, [], sY)
    b.finish(outslots)
    return b


def _chunks(w, nchunk):
    return np.ascontiguousarray(w.reshape(16, 128, nchunk, 128).transpose(2, 1, 0, 3)).reshape(nchunk, 128, 2048)


def _rows4(w, nk, ncol):
    return np.ascontiguousarray(w.reshape(nk, 128, ncol).transpose(1, 0, 2)).reshape(128, nk * ncol)


def _const_tables():
    t = np.arange(1024)
    nf = 16
    inv = 1.0 / (10000.0 ** (np.arange(nf, dtype=np.float32) / nf))
    d = np.arange(128) % 64
    half = d // 32
    i = d % 16
    pos = np.where(half[:, None] == 0, (t // 64)[None, :], (t % 64)[None, :]).astype(np.float32)
    ang = pos * inv[i][:, None]
    cos = np.cos(ang).astype(np.float32)
    sin = np.sin(ang).astype(np.float32)
    perm = np.zeros((128, 128), np.float32)
    for m in range(128):
        j = (m % 64) % 32
        if j < 16:
            perm[m + 16, m] = -1.0
        else:
            perm[m - 16, m] = 1.0
    bones = np.zeros((128, 128), np.float32)
    bones[:64, :64] = 1.0
    bones[64:, 64:] = 1.0
    matd = np.concatenate([perm, bones, np.zeros((128, 128), np.float32)], axis=1)
    hm = np.zeros((128, 2), np.float32)
    hm[:64, 0] = 1.0
    hm[64:, 1] = 1.0
    return cos, sin, matd, hm


def _layout_common(inp, layers, mixers=True):
    out = {}
    cos, sin, matd, hm = _const_tables()
    out["cosd"], out["sind"], out["matd"] = cos, sin, matd
    fg = np.asarray(inp["final_gain"]).reshape(16, 128).T
    out["cst"] = np.ascontiguousarray(np.concatenate([fg, hm], axis=1).astype(np.float32))
    for l in layers:
        j = l // 2
        wm = np.asarray(inp["w_mod"][l])
        out[f"wmod{l}"] = np.ascontiguousarray(wm.reshape(16, 128, 72, 256).transpose(2, 1, 0, 3)).reshape(72, 128, 4096)
        out[f"bmod{l}"] = np.ascontiguousarray(np.asarray(inp["b_mod"][l]).reshape(144, 128).T)
        out[f"gain{l}"] = np.ascontiguousarray(np.asarray(inp["norm_gain"][l]).reshape(3, 16, 128).transpose(2, 0, 1)).reshape(128, 48)
        for f in range(2):
            w = np.asarray(inp["w_ffn_in"][l][f])
            g = w[:, :DFF].reshape(16, 128, NJ, 128)
            u = w[:, DFF:].reshape(16, 128, NJ, 128)
            cat = np.concatenate([g, u], axis=-1)
            out[f"win{l}_{f}"] = np.ascontiguousarray(cat.transpose(2, 1, 0, 3)).reshape(NJ, 128, 4096)
            wo = np.asarray(inp["w_ffn_out"][l][f])
            out[f"wout{l}_{f}"] = np.ascontiguousarray(wo.reshape(11, 4, 128, 2048).transpose(0, 2, 1, 3)).reshape(11, 128, 8192)
        if not mixers:
            continue
        if l % 2 == 0:
            out[f"wab{l}"] = _chunks(np.asarray(inp["w_in_ab"][j]), 40)
            out[f"wabo{l}"] = _chunks(np.asarray(inp["w_out_ab"][j]), 16)
            wa = np.asarray(inp["lru_wa"][j])
            wx = np.asarray(inp["lru_wx"][j])
            bd = np.zeros((8, 128, 4, 128), np.float32)
            for c in range(8):
                for d in range(2):
                    for a, ww in enumerate((wa, wx)):
                        bd[c, :64, d * 2 + a, :64] = ww[d, 2 * c]
                        bd[c, 64:, d * 2 + a, 64:] = ww[d, 2 * c + 1]
            out[f"bd{l}"] = bd.reshape(8, 128, 512)
            cols = [np.asarray(inp["conv_w"][j])[k] for k in range(4)] + [np.asarray(inp["conv_b"][j])]
            cols += [np.asarray(inp["lru_ba"][j])[0], np.asarray(inp["lru_ba"][j])[1],
                     np.asarray(inp["lru_bx"][j])[0], np.asarray(inp["lru_bx"][j])[1],
                     np.asarray(inp["lru_lambda"][j])[0], np.asarray(inp["lru_lambda"][j])[1]]
            lc = np.stack(cols, axis=-1)
            out[f"lc{l}"] = np.ascontiguousarray(lc.reshape(8, 128, 11).transpose(1, 0, 2)).reshape(128, 88)
            nb = np.asarray(inp["na_bias"][j])
            kc = np.arange(64)[:, None]
            qc = np.arange(64)[None, :]
            relc = np.clip(kc - qc + 15, 0, 30)
            cstart = np.clip(qc - 8, 0, 48)
            ok = (kc >= cstart) & (kc < cstart + 16)
            bz = np.empty((16, 15, 64, 64), np.float32)
            for rr in range(15):
                g = nb[:, 14 - rr][:, relc]
                bz[:, rr] = np.where(ok[None], g, np.float32(NEG))
            out[f"bz{l}"] = bz
        else:
            w = np.asarray(inp["w_in_cd"][j])
            cols = [w[:, c * 128:(c + 1) * 128] for c in range(8)]
            for kvh in range(4):
                kk = w[:, 1024 + kvh * 64:1024 + (kvh + 1) * 64]
                cols.append(np.concatenate([kk, kk], axis=1))
            cols += [w[:, 1280:1408], w[:, 1408:1536]]
            cols += [w[:, 1536 + i * 128:1536 + (i + 1) * 128] for i in range(4)]
            cols += [w[:, 2048 + i * 128:2048 + (i + 1) * 128] for i in range(4)]
            cols.append(np.concatenate([w[:, 2560:2624], w[:, 2560:2624]], axis=1))
            out[f"wcd{l}"] = _chunks(np.concatenate(cols, axis=1), 23)
            vd = np.concatenate([np.concatenate([w[:, 1280 + k * 64:1280 + (k + 1) * 64]] * 2, axis=1) for k in range(4)], axis=1)
            out[f"wvd{l}"] = _rows4(vd, 16, 512)
            out[f"wcdo{l}"] = _chunks(np.asarray(inp["w_out_cd"][j]), 16)
            gq = np.tile(np.asarray(inp["gqa_q_gain"][j]), 2)[:, None]
            gk = np.tile(np.asarray(inp["gqa_k_gain"][j]), 2)[:, None]
            mq = np.asarray(inp["mla_q_gain"][j]).reshape(4, 128).T
            mk = np.asarray(inp["mla_kv_gain"][j]).reshape(4, 128).T
            out[f"cdc{l}"] = np.ascontiguousarray(np.concatenate([gq, gk, mq, mk], axis=1).astype(np.float32))
            uq = np.asarray(inp["mla_w_uq"][j]).reshape(512, 8, 192)
            out[f"wuqn{l}"] = _rows4(np.ascontiguousarray(uq[:, :, :128]).reshape(512, 1024), 4, 1024)
            out[f"wuqr{l}"] = _rows4(np.ascontiguousarray(uq[:, :, 128:]).reshape(512, 512), 4, 512)
            out[f"wuk{l}"] = _rows4(np.asarray(inp["mla_w_uk"][j]), 4, 1024)
            out[f"wuv{l}"] = _rows4(np.asarray(inp["mla_w_uv"][j]), 4, 1024)
    return out


def _layout_unit(inp, units, layers, mixers=True):
    nu = len(units)
    xp = np.asarray(inp["x_prompt"])
    xs = np.asarray(inp["x_sample"])
    c = np.asarray(inp["c"])
    cond = np.empty((1 + nu, D), np.float32)
    cond[0] = np.asarray(inp["c_ctx"])
    xc = np.empty((nu, NCH, 128, 512), np.float32)
    xl = np.empty((nu, NCH, 128, 1024), np.float32)
    for k, u in enumerate(units):
        tok = np.concatenate([xp[2 * u], xp[2 * u + 1]], axis=0)
        xc[k] = tok.T.reshape(NCH, 128, 512)
        xl[k] = xs[u].T.reshape(NCH, 128, 1024)
        cond[1 + k] = c[u]
    ncols = 1 + nu
    out = {"xc": xc, "xl": xl,
           "condT": np.ascontiguousarray(cond.T.reshape(NCH, 128, ncols).transpose(1, 0, 2)).reshape(128, NCH * ncols)}
    if not mixers:
        return out
    for l in layers:
        j = l // 2
        if l % 2 == 0:
            sf = np.asarray(inp["state_lru_fwd"])[units, j]
            sb = np.asarray(inp["state_lru_bwd"])[units, j]
            st = np.stack([sf, sb], axis=-1).reshape(nu, 8, 128, 2).transpose(0, 2, 1, 3)
            out[f"st{l}"] = np.ascontiguousarray(st).reshape(nu, 128, 16)
            kk = np.asarray(inp["cache_na_k"])[units, j].reshape(nu, 512, 1024)
            out[f"nakc{l}"] = np.ascontiguousarray(kk.transpose(0, 2, 1)).reshape(nu, 8, 128, 512)
            out[f"navc{l}"] = np.ascontiguousarray(np.asarray(inp["cache_na_v"])[units, j].reshape(nu, 4, 128, 1024))
        else:
            gk = np.asarray(inp["cache_gqa_k"])[units, j]
            kt = gk.transpose(0, 2, 3, 1)
            out[f"gqk{l}"] = np.ascontiguousarray(np.concatenate([kt, kt], axis=2))
            gv = np.asarray(inp["cache_gqa_v"])[units, j]
            out[f"gqv{l}"] = np.ascontiguousarray(np.concatenate([gv, gv], axis=3)).reshape(nu, 4, 128, 512)
            mc = np.asarray(inp["cache_mla_ckv"])[units, j]
            out[f"mlc{l}"] = np.ascontiguousarray(mc.transpose(0, 2, 1)).reshape(nu, 4, 128, 512)
            mr = np.asarray(inp["cache_mla_krope"])[units, j].transpose(0, 2, 1)
            out[f"mlr{l}"] = np.ascontiguousarray(np.concatenate([mr, mr], axis=1))
    return out


NCORES = 8


def kernel(**inp):
    ncores = NCORES
    nu = 8 // ncores
    layers = [0, 1, 2, 3]
    bld = build(nu, layers, True)
    common = _layout_common(inp, layers, True)
    in_maps = []
    for core in range(ncores):
        d = dict(common)
        d.update(_layout_unit(inp, [core * nu + k for k in range(nu)], layers, True))
        in_maps.append(d)
    res = run_bass_kernel_spmd(bld.nc, in_maps, core_ids=list(range(ncores)))
    y_prompt = np.empty((16, 256, D), np.float32)
    y_sample = np.empty((8, 1024, D), np.float32)
    st_f = np.empty((16, 2, 1024), np.float32)
    st_b = np.empty((16, 2, 1024), np.float32)
    na_k = np.empty((16, 2, 256, 16, 64), np.float32)
    na_v = np.empty((16, 2, 256, 16, 64), np.float32)
    gq_k = np.empty((16, 2, 256, 4, 64), np.float32)
    gq_v = np.empty((16, 2, 256, 4, 64), np.float32)
    ml_c = np.empty((16, 2, 256, 512), np.float32)
    ml_r = np.empty((16, 2, 256, 64), np.float32)
    for core in range(ncores):
        r = res.results[core]
        for k in range(nu):
            u = core * nu + k
            yp = r["yc"][k].reshape(D, 512).T
            y_prompt[2 * u] = yp[:256]
            y_prompt[2 * u + 1] = yp[256:]
            y_sample[u] = r["yl"][k].reshape(D, 1024).T
            for l in layers:
                j = l // 2
                if l % 2 == 0:
                    so = r[f"ost{l}"][k].reshape(128, 8, 2, 2).transpose(3, 2, 1, 0).reshape(2, 2, 1024)
                    kk = r[f"onak{l}"][k].reshape(1024, 512).T.reshape(2, 256, 16, 64)
                    vv = r[f"onav{l}"][k].reshape(1024, 512).T.reshape(2, 256, 16, 64)
                    for s in range(2):
                        st_f[2 * u + s, j] = so[s, 0]
                        st_b[2 * u + s, j] = so[s, 1]
                        na_k[2 * u + s, j] = kk[s]
                        na_v[2 * u + s, j] = vv[s]
                else:
                    kk = r[f"ogk{l}"][k].reshape(256, 512).T.reshape(2, 256, 4, 64)
                    vv = r[f"ogv{l}"][k].reshape(256, 512).T.reshape(2, 256, 4, 64)
                    cc = r[f"omc{l}"][k].reshape(512, 512).T.reshape(2, 256, 512)
                    rr = r[f"omr{l}"][k].T.reshape(2, 256, 64)
                    for s in range(2):
                        gq_k[2 * u + s, j] = kk[s]
                        gq_v[2 * u + s, j] = vv[s]
                        ml_c[2 * u + s, j] = cc[s]
                        ml_r[2 * u + s, j] = rr[s]
    return (y_prompt, y_sample, st_f, st_b, na_k, na_v, gq_k, gq_v, ml_c, ml_r)
```
